# Optimizing a Trainium2 kernel written in Bass

```python
import jax, jax.numpy as jnp
from jax import lax
import numpy as np

D_MODEL = 4096
BATCH = 2
SEQ = 8192
DEPTH = 1

HEAD_DIM = 64
N_ATTN_HEADS = 32
N_RWKV_HEADS = 32
ATTN_WIDTH = N_ATTN_HEADS * HEAD_DIM
RWKV_WIDTH = N_RWKV_HEADS * HEAD_DIM
MIX_WIDTH = ATTN_WIDTH + RWKV_WIDTH
DILATION_PAIRS = ((128, 1), (512, 4), (2048, 16))
ATTN_BLOCK = 128
ROPE_THETA = 10000.0
DECAY_LORA = 128
ICLR_LORA = 128
GATE_LORA = 480
D_FF = 11008
CONV_WIDTH = 3
PLE_DIM = 256
NORM_EPS = 1e-6
RWKV_GN_EPS = 64e-5
RWKV_COLS = 3 * RWKV_WIDTH + DECAY_LORA + ICLR_LORA + GATE_LORA
IN_COLS = 3 * ATTN_WIDTH + RWKV_COLS

kernel_name = "hybrid_dilated_attn_rwkv7_convffn_ple"


def rmsnorm(x, g, eps=NORM_EPS):
    xf = x.astype(jnp.float32)
    y = xf * lax.rsqrt(jnp.mean(xf * xf, axis=-1, keepdims=True) + eps)
    return (y * g.astype(jnp.float32)).astype(x.dtype)


def apply_rope(t, positions):
    half = HEAD_DIM // 2
    inv_freq = ROPE_THETA ** (-jnp.arange(half, dtype=jnp.float32) / half)
    ang = positions.astype(jnp.float32)[:, :, None] * inv_freq
    cos = jnp.cos(ang)[:, :, None, :]
    sin = jnp.sin(ang)[:, :, None, :]
    tf = t.astype(jnp.float32)
    t1, t2 = tf[..., :half], tf[..., half:]
    return jnp.concatenate([t1 * cos - t2 * sin, t2 * cos + t1 * sin], axis=-1).astype(t.dtype)


def dilated_branch(q, k, v, window, dil):
    B, H, S, hd = q.shape
    n_back = window // dil
    span = dil * ATTN_BLOCK
    s_pad = -(-S // span) * span
    M = s_pad // dil
    nb = M // ATTN_BLOCK

    def to_strided(t):
        t = jnp.pad(t, ((0, 0), (0, 0), (0, s_pad - S), (0, 0)))
        t = t.reshape(B, H, M, dil, hd).transpose(0, 1, 3, 2, 4)
        return t.reshape(B, H, dil, nb, ATTN_BLOCK, hd)

    def with_prev_block(t):
        prev = jnp.pad(t[:, :, :, :-1], ((0, 0), (0, 0), (0, 0), (1, 0), (0, 0), (0, 0)))
        return jnp.concatenate([prev, t], axis=4)

    qs = to_strided(q)
    kc = with_prev_block(to_strided(k))
    vc = with_prev_block(to_strided(v))
    scores = jnp.einsum('bhrnqd,bhrnkd->bhrnqk', qs, kc).astype(jnp.float32) * (HEAD_DIM ** -0.5)

    qi = jnp.arange(ATTN_BLOCK)[:, None]
    ki = jnp.arange(2 * ATTN_BLOCK)[None, :]
    dist = ATTN_BLOCK + qi - ki
    band = (dist >= 0) & (dist <= n_back)
    has_prev = (jnp.arange(nb) > 0)[:, None, None] | (ki >= ATTN_BLOCK)[None]
    mask = band[None] & has_prev
    scores = jnp.where(mask[None, None, None], scores, -jnp.inf)

    lse = jax.nn.logsumexp(scores, axis=-1)
    probs = jnp.exp(scores - lse[..., None])
    o = jnp.einsum('bhrnqk,bhrnkd->bhrnqd', probs.astype(vc.dtype), vc)
    o = o.reshape(B, H, dil, M, hd).transpose(0, 1, 3, 2, 4).reshape(B, H, s_pad, hd)[:, :, :S]
    lse = lse.reshape(B, H, dil, M).transpose(0, 1, 3, 2).reshape(B, H, s_pad)[:, :, :S]
    return o, lse


def dilated_attention(q, k, v):
    outs, lses = [], []
    for window, dil in DILATION_PAIRS:
        o, lse = dilated_branch(q, k, v, window, dil)
        outs.append(o.astype(jnp.float32))
        lses.append(lse)
    wts = jax.nn.softmax(jnp.stack(lses, axis=0), axis=0)
    return jnp.sum(wts[..., None] * jnp.stack(outs, axis=0), axis=0)


def rwkv7_step(state, inp):
    r, w, k, v, a, b = inp
    sa = jnp.einsum('bhvk,bhk->bhv', state, a)
    state = state * w[:, :, None, :] + sa[..., None] * b[:, :, None, :] + v[..., None] * k[:, :, None, :]
    y = jnp.einsum('bhvk,bhk->bhv', state, r)
    return state, y


def rwkv7_time_mix(P, mu, w0, w_decay_up, a0, w_iclr_up, w_gate_up, k_k, k_a, r_k, ln_x_w, ln_x_b):
    B, S, _ = P.shape
    P = P.astype(jnp.float32)
    prev = jnp.pad(P[:, :-1], ((0, 0), (1, 0), (0, 0)))
    mixed = P + (prev - P) * mu
    cuts = [RWKV_WIDTH, 2 * RWKV_WIDTH, 3 * RWKV_WIDTH,
            3 * RWKV_WIDTH + DECAY_LORA, 3 * RWKV_WIDTH + DECAY_LORA + ICLR_LORA]
    r, k, v, wd, ad, gd = jnp.split(mixed, cuts, axis=-1)
    w_raw = -jax.nn.softplus(-(w0 + jnp.tanh(wd) @ w_decay_up)) - 0.5
    decay = jnp.exp(-jnp.exp(w_raw))
    a = jax.nn.sigmoid(a0 + ad @ w_iclr_up)
    g = jax.nn.sigmoid(gd) @ w_gate_up

    def heads(t):
        return t.reshape(B, S, N_RWKV_HEADS, HEAD_DIM)

    kk = heads(k * k_k)
    kk = kk / jnp.maximum(jnp.sqrt(jnp.sum(kk * kk, axis=-1, keepdims=True)), 1e-12)
    k = k * (1.0 + (a - 1.0) * k_a)
    r_h, k_h, v_h, w_h, a_h = heads(r), heads(k), heads(v), heads(decay), heads(a)
    xs = tuple(t.transpose(1, 0, 2, 3) for t in (r_h, w_h, k_h, v_h, -kk, kk * a_h))
    state0 = jnp.zeros((B, N_RWKV_HEADS, HEAD_DIM, HEAD_DIM), jnp.float32)
    _, y = lax.scan(rwkv7_step, state0, xs)
    y = y.transpose(1, 0, 2, 3)
    mean = jnp.mean(y, axis=-1, keepdims=True)
    var = jnp.mean(jnp.square(y - mean), axis=-1, keepdims=True)
    y = ((y - mean) * lax.rsqrt(var + RWKV_GN_EPS)).reshape(B, S, RWKV_WIDTH) * ln_x_w + ln_x_b
    bonus = (jnp.sum(r_h * k_h * r_k, axis=-1, keepdims=True) * v_h).reshape(B, S, RWKV_WIDTH)
    return (y + bonus) * g


def setup_inputs(seed: int = 0) -> dict:
    key = jax.random.key(seed)
    ks = jax.random.split(key, 32)
    L = DEPTH
    f32 = jnp.float32

    def nrm(k, shape, scale):
        return jax.random.normal(k, shape, f32) * scale

    def gain(k, shape):
        return 1.0 + 0.05 * jax.random.normal(k, shape, f32)

    x = jax.random.normal(ks[0], (BATCH, SEQ, D_MODEL), f32)
    p = jax.random.normal(ks[1], (L, BATCH, SEQ, PLE_DIM), f32)
    offset = jax.random.randint(ks[2], (BATCH, 1), 0, 4096, dtype=jnp.int32)
    positions = offset + jnp.arange(SEQ, dtype=jnp.int32)[None, :]
    return dict(
        x=x,
        p=p,
        positions=positions,
        attn_norm_g=gain(ks[3], (L, D_MODEL)),
        w_in=nrm(ks[4], (L, D_MODEL, IN_COLS), D_MODEL ** -0.5),
        q_norm_g=gain(ks[5], (L, HEAD_DIM)),
        k_norm_g=gain(ks[6], (L, HEAD_DIM)),
        rwkv_mu=jax.random.uniform(ks[7], (L, RWKV_COLS), f32),
        w0=jax.random.uniform(ks[8], (L, RWKV_WIDTH), f32, -3.0, 1.0),
        w_decay_up=nrm(ks[9], (L, DECAY_LORA, RWKV_WIDTH), 0.5 * DECAY_LORA ** -0.5),
        a0=nrm(ks[10], (L, RWKV_WIDTH), 0.1),
        w_iclr_up=nrm(ks[11], (L, ICLR_LORA, RWKV_WIDTH), ICLR_LORA ** -0.5),
        w_gate_up=nrm(ks[12], (L, GATE_LORA, RWKV_WIDTH), GATE_LORA ** -0.5),
        k_k=0.85 + 0.05 * jax.random.normal(ks[13], (L, RWKV_WIDTH), f32),
        k_a=gain(ks[14], (L, RWKV_WIDTH)),
        r_k=nrm(ks[15], (L, N_RWKV_HEADS, HEAD_DIM), 0.1),
        ln_x_w=gain(ks[16], (L, RWKV_WIDTH)),
        ln_x_b=nrm(ks[17], (L, RWKV_WIDTH), 0.01),
        w_out=nrm(ks[18], (L, MIX_WIDTH, D_MODEL), MIX_WIDTH ** -0.5),
        mlp_norm_g=gain(ks[19], (L, D_MODEL)),
        w_mlp_up=nrm(ks[20], (L, D_MODEL, 2 * D_FF), D_MODEL ** -0.5),
        conv_w=nrm(ks[21], (L, CONV_WIDTH, 2 * D_FF), CONV_WIDTH ** -0.5),
        conv_b=nrm(ks[22], (L, 2 * D_FF), 0.01),
        w_mlp_down=nrm(ks[23], (L, D_FF, D_MODEL), D_FF ** -0.5),
        w_ple_proj=nrm(ks[24], (L, PLE_DIM, D_MODEL), PLE_DIM ** -0.5),
        ple_norm_g=gain(ks[25], (L, D_MODEL)),
        w_ple_gate=nrm(ks[26], (L, D_MODEL, D_MODEL), D_MODEL ** -0.5),
    )


def reference(x, p, positions, attn_norm_g, w_in, q_norm_g, k_norm_g, rwkv_mu, w0, w_decay_up,
              a0, w_iclr_up, w_gate_up, k_k, k_a, r_k, ln_x_w, ln_x_b, w_out, mlp_norm_g,
              w_mlp_up, conv_w, conv_b, w_mlp_down, w_ple_proj, ple_norm_g, w_ple_gate):
    B, S, _ = x.shape
    h = x
    for i in range(DEPTH):
        xn = rmsnorm(h, attn_norm_g[i])
        proj = xn @ w_in[i]
        q, k, v = jnp.split(proj[..., :3 * ATTN_WIDTH], 3, axis=-1)
        q = q.reshape(B, S, N_ATTN_HEADS, HEAD_DIM)
        k = k.reshape(B, S, N_ATTN_HEADS, HEAD_DIM)
        v = v.reshape(B, S, N_ATTN_HEADS, HEAD_DIM)
        q = apply_rope(rmsnorm(q, q_norm_g[i]), positions)
        k = apply_rope(rmsnorm(k, k_norm_g[i]), positions)
        attn = dilated_attention(q.transpose(0, 2, 1, 3), k.transpose(0, 2, 1, 3),
                                 v.transpose(0, 2, 1, 3))
        attn = attn.transpose(0, 2, 1, 3).reshape(B, S, ATTN_WIDTH)
        rwkv = rwkv7_time_mix(proj[..., 3 * ATTN_WIDTH:], rwkv_mu[i], w0[i], w_decay_up[i], a0[i],
                              w_iclr_up[i], w_gate_up[i], k_k[i], k_a[i], r_k[i], ln_x_w[i], ln_x_b[i])
        mix = jnp.concatenate([attn.astype(x.dtype), rwkv.astype(x.dtype)], axis=-1)
        h = h + mix @ w_out[i]

        hn = rmsnorm(h, mlp_norm_g[i])
        u = hn @ w_mlp_up[i]
        u_pad = jnp.pad(u, ((0, 0), (CONV_WIDTH - 1, 0), (0, 0)))
        cw = conv_w[i].astype(u.dtype)
        u = sum(u_pad[:, j:j + S] * cw[j] for j in range(CONV_WIDTH)) + conv_b[i].astype(u.dtype)
        gate, up = jnp.split(u, 2, axis=-1)
        h = h + (jax.nn.silu(gate) * up) @ w_mlp_down[i]

        e = rmsnorm(p[i] @ w_ple_proj[i], ple_norm_g[i])
        h = h + jax.nn.sigmoid(h @ w_ple_gate[i]) * e
    return h
```

```python
import contextlib
import math
import os
import numpy as np
import concourse.bass as bass
import concourse.mybir as mybir
from concourse.bass_utils import run_bass_kernel_spmd

F32 = mybir.dt.float32
BF16 = mybir.dt.bfloat16
I32 = mybir.dt.int32
AF = mybir.ActivationFunctionType
ALU = mybir.AluOpType

D_MODEL = 4096
HEAD_DIM = 64
ATTN_W = 2048
RWKV_W = 2048
D_FF = 11008
PLE_DIM = 256
NORM_EPS = 1e-6
GN_EPS = 64e-5
C0 = math.exp(-0.5)

ENGS = ("pe", "act", "dve", "pool", "sp")
EPOCH = 8000


class Buf:
    __slots__ = ("name", "w", "r")

    def __init__(self, name=""):
        self.name = name
        self.w = None
        self.r = []


class Op:
    __slots__ = ("eng", "fn", "deps", "dma", "sig", "idx", "dsem", "dcnt", "blk", "inc")

    def __init__(self, eng, fn, dma=False):
        self.eng = eng
        self.fn = fn
        self.deps = []
        self.dma = dma
        self.sig = False
        self.idx = None
        self.dsem = None
        self.dcnt = None
        self.blk = 0


class Prog:
    def __init__(self, nc):
        self.nc = nc
        self.ops = {e: [] for e in ENGS}
        self.sigcount = {e: 0 for e in ENGS}
        self.esems = {e: [] for e in ENGS}
        self.dma_sems = {}
        self.waited = {e: {} for e in ENGS}
        self._ctx = []
        self.blk = 0
        self.keybufs = {}

    def _sem(self, name):
        cm = self.nc.semaphore(name)
        s = cm.__enter__()
        self._ctx.append(cm)
        return s

    def dsem(self, key):
        if key not in self.dma_sems:
            self.dma_sems[key] = [self._sem("d_" + str(key)), 0]
        return self.dma_sems[key]

    def close(self):
        for cm in reversed(self._ctx):
            cm.__exit__(None, None, None)

    def _deps(self, op, reads, writes):
        deps = set()
        for b in reads:
            if b.w is not None:
                deps.add(b.w)
        for b in writes:
            if b.w is not None:
                deps.add(b.w)
            for r in b.r:
                deps.add(r)
        deps.discard(op)
        for d in deps:
            if op.dma or d.dma or d.eng != op.eng:
                d.sig = True
        op.deps = list(deps)
        for b in reads:
            b.r.append(op)
        for b in writes:
            b.w = op
            b.r = []

    def op(self, eng, fn, reads=(), writes=()):
        o = Op(eng, fn)
        o.blk = self.blk
        self._deps(o, reads, writes)
        self.ops[eng].append(o)
        return o

    def dma(self, eng, fn, semkey, reads=(), writes=(), inc=16):
        o = Op(eng, fn, dma=True)
        o.blk = self.blk
        o.inc = inc
        ds = self.dsem(semkey)
        ds[1] += inc
        o.dsem, o.dcnt = ds[0], ds[1]
        kb = self.keybufs.setdefault(semkey, Buf(semkey))
        self._deps(o, reads, list(writes) + [kb])
        self.ops[eng].append(o)
        return o

    def _assign(self):
        for e in ENGS:
            for o in self.ops[e]:
                if o.sig and not o.dma and o.idx is None:
                    self.sigcount[e] += 1
                    o.idx = self.sigcount[e]
            need = (self.sigcount[e] + EPOCH - 1) // EPOCH
            while len(self.esems[e]) < need:
                self.esems[e].append(self._sem("e_%s_%d" % (e, len(self.esems[e]))))

    def _wait_for(self, eng_name, eng, d):
        w = self.waited[eng_name]
        if d.blk < self.blk:
            return
        if d.dma:
            key = ("d", id(d.dsem))
            if w.get(key, 0) >= d.dcnt:
                return
            w[key] = d.dcnt
            eng.wait_ge(d.dsem, d.dcnt)
        else:
            ep, v = divmod(d.idx - 1, EPOCH)
            key = (d.eng, ep)
            if w.get(key, 0) >= v + 1:
                return
            w[key] = v + 1
            for e2 in range(ep):
                w[(d.eng, e2)] = EPOCH
            eng.wait_ge(self.esems[d.eng][ep], v + 1)

    def emit_engine(self, eng_name, eng):
        for o in self.ops[eng_name]:
            for d in sorted(o.deps, key=lambda x: (x.dma, x.idx or 0, x.dcnt or 0)):
                if (not o.dma) and (not d.dma) and d.eng == eng_name:
                    continue
                self._wait_for(eng_name, eng, d)
            ins = o.fn(eng)
            if o.dma:
                ins.then_inc(o.dsem, o.inc)
            elif o.sig:
                ep, v = divmod(o.idx - 1, EPOCH)
                ins.then_inc(self.esems[eng_name][ep], 1)

    def finish_waits(self, eng_name, eng):
        for e in ENGS:
            if self.sigcount[e] > 0:
                ep, v = divmod(self.sigcount[e] - 1, EPOCH)
                key = (e, ep)
                if self.waited[eng_name].get(key, 0) < v + 1:
                    self.waited[eng_name][key] = v + 1
                    eng.wait_ge(self.esems[e][ep], v + 1)
        for key, (sem, cnt) in self.dma_sems.items():
            k = ("d", id(sem))
            if cnt > 0 and self.waited[eng_name].get(k, 0) < cnt:
                self.waited[eng_name][k] = cnt
                eng.wait_ge(sem, cnt)

    def run_block(self):
        nc = self.nc
        for e in ENGS:
            lst = [o for o in self.ops[e] if not o.dma]
            if lst:
                lst[-1].sig = True
        self._assign()
        with nc.Block() as block:
            @block.tensor
            def _(t):
                self.emit_engine("pe", t)
                self.finish_waits("pe", t)

            @block.scalar
            def _(a):
                self.emit_engine("act", a)
                self.finish_waits("act", a)

            @block.vector
            def _(v):
                self.emit_engine("dve", v)
                self.finish_waits("dve", v)

            @block.gpsimd
            def _(g):
                self.emit_engine("pool", g)
                self.finish_waits("pool", g)

            @block.sync
            def _(s):
                self.emit_engine("sp", s)
                self.finish_waits("sp", s)
        for e in ENGS:
            self.ops[e] = []
        self.blk += 1


class TB:
    __slots__ = ("t", "b")

    def __init__(self, t, name=""):
        self.t = t
        self.b = Buf(name)


class Builder:
    def __init__(self):
        self.nc = bass.Bass("TRN2", target_bir_lowering=False)
        self.P = Prog(self.nc)
        self.es = contextlib.ExitStack()
        self.n = 0
        self.psb = []
        for i in range(8):
            t = self.es.enter_context(self.nc.psum_tensor("psb%d" % i, [128, 512], F32))
            self.psb.append(TB(t, "psb%d" % i))

    def dram(self, name, shape, dt, kind):
        return self.nc.dram_tensor(name, list(shape), dt, kind=kind).ap()

    def sb(self, es, name, shape, dt=F32):
        self.n += 1
        t = es.enter_context(self.nc.sbuf_tensor("%s_%d" % (name, self.n), list(shape), dt))
        return TB(t, name)

    def finish(self):
        self.P.close()
        self.es.close()


def _bind(f, *a):
    return lambda e: f(e, *a)


NW_A = 3
TS = 64
CH = 64


def build_A(S, phases=("A1", "A2", "A3"), debug=False, B=None, fused=False):
    own = B is None
    if own:
        B = Builder()
    nc, P = B.nc, B.P
    T = 512
    NT = S // T
    IN, OUT, INT = "ExternalInput", "ExternalOutput", "Internal"
    xT = B.dram("xT", [128, 32, S], F32, IN)
    pos = B.dram("pos", [1, S], I32, IN)
    wA = B.dram("wA", [30, 128, 32 * 128], F32, IN)
    gA = B.dram("gA", [128, 32], F32, IN)
    c128 = B.dram("c128", [128, 8 * 128 + 256 + 4], F32, IN)
    hp_d = B.dram("hp", [64, 10 * 8], F32, IN)
    lp_d = B.dram("lp", [128, 6], F32, IN)
    wdu_d = B.dram("wdu", [128, 512], F32, IN)
    wau_d = B.dram("wau", [128, 512], F32, IN)
    wgu_d = B.dram("wgu", [128, 4 * 512], F32, IN)
    c64_d = B.dram("c64", [64, 64 + 8 * TS], F32, IN)
    if fused:
        mixA = B.dram("mixx", [S // 512, 1024, 512], BF16, INT)
        MIX = dict(ap=mixA, off=0, dt=BF16, dst=lambda r0, nr, t0, n: mixA[t0 // 512, r0:r0 + nr, t0 % 512:t0 % 512 + n])
    else:
        mixA = B.dram("mixA", [1024, S], F32, OUT)
        MIX = dict(ap=mixA, off=0, dt=F32, dst=lambda r0, nr, t0, n: mixA[r0:r0 + nr, t0:t0 + n])
    SK = OUT if debug else INT
    qk_s = B.dram("qk_s", [8, 128, S], BF16, SK)
    vv_s = B.dram("vv_s", [4, 128, S], BF16, SK)
    pr_s = B.dram("pr_s", [18, 128, S], F32, SK)

    with contextlib.ExitStack() as es0:
        cf = B.sb(es0, "cf", [128, 8 * 128 + 256 + 4], F32)
        cb = B.sb(es0, "cb", [128, 128 + 128 + 256], BF16)
        hp = B.sb(es0, "hp", [64, 10, 8], F32)
        lp = B.sb(es0, "lp", [128, 6], F32)
        c64 = B.sb(es0, "c64", [64, 64 + 8 * TS], F32)
        P.dma("sp", lambda e: e.dma_start(out=cf.t[:], in_=c128), "cf", writes=[cf.b])
        P.dma("sp", lambda e: e.dma_start(out=hp.t[:].rearrange("p a h -> p (a h)"), in_=hp_d), "hp", writes=[hp.b])
        P.dma("sp", lambda e: e.dma_start(out=lp.t[:], in_=lp_d), "lp", writes=[lp.b])
        P.dma("sp", lambda e: e.dma_start(out=c64.t[:], in_=c64_d), "c64", writes=[c64.b])
        P.dma("pool", lambda e: e.dma_start(out=cb.t[:, 0:128], in_=c128[:, 0:128]), "cb0", writes=[cb.b])
        P.dma("pool", lambda e: e.dma_start(out=cb.t[:, 128:256], in_=c128[:, 384:512]), "cb1", writes=[cb.b])
        P.dma("pool", lambda e: e.dma_start(out=cb.t[:, 256:512], in_=c128[:, 1024:1280]), "cb2", writes=[cb.b])
        ONESF = cf.t[:, 0:128]
        BD = cf.t[:, 128:256]
        ROTT = cf.t[:, 256:384]
        IDENTF = cf.t[:, 384:512]
        MASKG = cf.t[:, 512:640]
        GQ = cf.t[:, 1280:1281]
        GK = cf.t[:, 1281:1282]
        INVF = cf.t[:, 1282:1283]
        ONESB = cb.t[:, 0:128]
        IDENTB = cb.t[:, 128:256]
        MASKB = cb.t[:, 256:512]
        MASKN = c64.t[:, 0:64]
        RESETM = c64.t[:, 64:64 + 8 * TS]
        CR = [cf.b, cb.b, hp.b, lp.b, c64.b]

        if "A1" in phases:
          phase_A1(B, S, T, NT, xT, pos, wA, gA, qk_s, vv_s, pr_s,
                 dict(ONESB=ONESB, BD=BD, ROTT=ROTT, GQ=GQ, GK=GK, INVF=INVF, CR=CR))
        if "A2" in phases:
          phase_A2(B, S, qk_s, vv_s, MIX,
                 dict(ONESF=ONESF, IDENTB=IDENTB, MASKB=MASKB, CR=CR))
        if "A3" in phases:
          phase_A3(B, S, pr_s, MIX, wdu_d, wau_d, wgu_d,
                 dict(ONESF=ONESF, IDENTF=IDENTF, MASKG=MASKG, MASKN=MASKN, RESETM=RESETM,
                      hp=hp, lp=lp, CR=CR))
    if own:
        B.finish()
        return nc
    return mixA


def phase_A1(B, S, T, NT, xT, pos, wA, gA, qk_s, vv_s, pr_s, K):
    nc, P = B.nc, B.P
    CR = K["CR"]
    with contextlib.ExitStack() as es:
        x_sb = B.sb(es, "x", [128, 32, T], F32)
        xn = [B.sb(es, "xn%d" % i, [128, 32, T], BF16) for i in range(2)]
        w_sb = [B.sb(es, "w%d" % i, [128, 32, 128], BF16) for i in range(NW_A)]
        sqb = [B.sb(es, "sqb%d" % i, [128, T], BF16) for i in range(2)]
        rstd = B.sb(es, "rstd", [128, T], F32)
        g_sb = B.sb(es, "g", [128, 32], F32)
        posi = B.sb(es, "posi", [128, T], I32)
        cs = [dict(sin=B.sb(es, "sin%d" % i, [128, T], F32), cos=B.sb(es, "cos%d" % i, [128, T], F32)) for i in range(2)]
        tri = B.sb(es, "tri", [128, T], I32)
        sqf = [B.sb(es, "sqf%d" % i, [128, T], F32) for i in range(2)]
        rs2 = [B.sb(es, "rs2%d" % i, [128, T], F32) for i in range(2)]
        qn = [B.sb(es, "qn%d" % i, [128, T], F32) for i in range(2)]
        t1 = [B.sb(es, "t1%d" % i, [128, T], F32) for i in range(2)]
        tr = [t1[0], t1[1], rs2[0], rs2[1]]
        stb = [B.sb(es, "stb%d" % i, [128, T], BF16) for i in range(3)]
        stf = [B.sb(es, "stf%d" % i, [128, T], F32) for i in range(2)]
        P.dma("sp", lambda e: e.dma_start(out=g_sb.t[:], in_=gA), "gA", writes=[g_sb.b])
        xparts = [Buf() for _ in range(4)]

        psm = [B.psb[i] for i in range(4)]
        pss = [B.psb[i] for i in range(4, 8)]
        cnt = dict(w=0, m=0, s=0, sq=0, stb=0, stf=0, q=0)

        def prep(tt):
            t0 = tt * T
            xnb = xn[tt % 2]
            csb = cs[tt % 2]
            for q4 in range(4):
                P.dma("sp", _bind(lambda e, q4: e.dma_start(out=x_sb.t[:, 8 * q4:8 * q4 + 8, :], in_=xT[:, 8 * q4:8 * q4 + 8, t0:t0 + T]), q4),
                      "x%d" % q4, writes=[xparts[q4]])
            P.dma("sp", lambda e: e.dma_start(out=posi.t[:], in_=pos[:, t0:t0 + T].partition_broadcast(128)), "posi", writes=[posi.b])
            ssp = pss[cnt["s"] % 4]
            cnt["s"] += 1
            for kc in range(32):
                sq = sqb[cnt["sq"] % 2]
                cnt["sq"] += 1
                P.op("act", _bind(lambda e, sq, kc: e.activation(out=sq.t[:], in_=x_sb.t[:, kc, :], func=AF.Square), sq, kc),
                     reads=[xparts[kc // 8]], writes=[sq.b])
                P.op("pe", _bind(lambda e, sq, kc: e.matmul(ssp.t[:], K["ONESB"], sq.t[:], start=(kc == 0), stop=(kc == 31)), sq, kc),
                     reads=[sq.b] + CR, writes=[ssp.b])
            P.op("act", lambda e: e.activation(out=rstd.t[:], in_=ssp.t[:], func=AF.Sqrt, bias=NORM_EPS, scale=1.0 / D_MODEL),
                 reads=[ssp.b], writes=[rstd.b])
            P.op("dve", lambda e: e.reciprocal(out=rstd.t[:], in_=rstd.t[:]), reads=[rstd.b], writes=[rstd.b])
            for kc in range(32):
                P.op("dve", _bind(lambda e, kc: e.scalar_tensor_tensor(out=xnb.t[:, kc, :], in0=x_sb.t[:, kc, :], scalar=g_sb.t[:, kc:kc + 1],
                                                                       in1=rstd.t[:], op0=ALU.mult, op1=ALU.mult), kc),
                     reads=[xparts[kc // 8], rstd.b, g_sb.b], writes=[xnb.b])
            a, k_, r_, rc = tr
            TWO_PI = 2.0 * math.pi
            C1 = 6.28125
            C2 = 0.0019340515136718750
            C3 = TWO_PI - C1 - C2

            def trig(e):
                e.tensor_copy(out=a.t[:], in_=posi.t[:])
                e.tensor_scalar(out=a.t[:], in0=a.t[:], scalar1=K["INVF"], scalar2=None, op0=ALU.mult)
                e.tensor_scalar(out=k_.t[:], in0=a.t[:], scalar1=1.0 / TWO_PI, scalar2=None, op0=ALU.mult)
                e.tensor_copy(out=tri.t[:], in_=k_.t[:])
                e.tensor_copy(out=k_.t[:], in_=tri.t[:])
                e.scalar_tensor_tensor(out=r_.t[:], in0=k_.t[:], scalar=-C1, in1=a.t[:], op0=ALU.mult, op1=ALU.add)
                e.scalar_tensor_tensor(out=r_.t[:], in0=k_.t[:], scalar=-C2, in1=r_.t[:], op0=ALU.mult, op1=ALU.add)
                e.scalar_tensor_tensor(out=r_.t[:], in0=k_.t[:], scalar=-C3, in1=r_.t[:], op0=ALU.mult, op1=ALU.add)
                e.tensor_scalar(out=r_.t[:], in0=r_.t[:], scalar1=-math.pi, scalar2=math.pi, op0=ALU.max, op1=ALU.min)
                e.tensor_scalar(out=rc.t[:], in0=r_.t[:], scalar1=math.pi / 2, scalar2=None, op0=ALU.add)
                e.tensor_scalar(out=k_.t[:], in0=rc.t[:], scalar1=math.pi, scalar2=-TWO_PI, op0=ALU.is_gt, op1=ALU.mult)
                e.tensor_tensor(out=rc.t[:], in0=rc.t[:], in1=k_.t[:], op=ALU.add)
                return e.tensor_scalar(out=rc.t[:], in0=rc.t[:], scalar1=-math.pi, scalar2=math.pi, op0=ALU.max, op1=ALU.min)
            P.op("dve", trig, reads=[posi.b] + CR, writes=[tb.b for tb in tr] + [])
            P.op("act", lambda e: e.activation(out=csb["sin"].t[:], in_=r_.t[:], func=AF.Sin), reads=[r_.b], writes=[csb["sin"].b])
            P.op("act", lambda e: e.activation(out=csb["cos"].t[:], in_=rc.t[:], func=AF.Sin), reads=[rc.b], writes=[csb["cos"].b])

        pending = []

        def flush(now):
            keep = []
            for due, fn in pending:
                if due <= now:
                    fn()
                else:
                    keep.append((due, fn))
            pending[:] = keep

        def chunk(tt, j, seq):
            t0 = tt * T
            xnb = xn[tt % 2]
            csb = cs[tt % 2]
            slot = w_sb[cnt["w"] % NW_A]
            cnt["w"] += 1
            P.dma("pool", _bind(lambda e, slot, j: e.dma_start(out=slot.t[:].rearrange("p k c -> p (k c)"), in_=wA[j]), slot, j),
                  "w%d" % (cnt["w"] % NW_A), writes=[slot.b])
            ps = psm[cnt["m"] % 4]
            cnt["m"] += 1

            def mm(e, slot=slot, ps=ps):
                for kc in range(32):
                    r = e.matmul(ps.t[:], slot.t[:, kc, :], xnb.t[:, kc, :], start=(kc == 0), stop=(kc == 31))
                return r
            P.op("pe", mm, reads=[slot.b, xnb.b], writes=[ps.b])
            if j < 8:
                gv = K["GQ"] if j < 4 else K["GK"]
                i2 = cnt["q"] % 2
                cnt["q"] += 1
                sq, r2, qq, tt1 = sqf[i2], rs2[i2], qn[i2], t1[i2]
                P.op("act", lambda e: e.activation(out=sq.t[:], in_=ps.t[:], func=AF.Square), reads=[ps.b], writes=[sq.b])

                def st2():
                    hs = pss[cnt["s"] % 4]
                    cnt["s"] += 1
                    P.op("pe", lambda e: e.matmul(hs.t[:], K["BD"], sq.t[:], start=True, stop=True), reads=[sq.b] + CR, writes=[hs.b])
                    P.op("act", lambda e: e.activation(out=r2.t[:], in_=hs.t[:], func=AF.Sqrt, bias=NORM_EPS, scale=1.0 / HEAD_DIM),
                         reads=[hs.b], writes=[r2.b])
                    P.op("dve", lambda e: e.reciprocal(out=r2.t[:], in_=r2.t[:]), reads=[r2.b], writes=[r2.b])
                    P.op("dve", lambda e: e.scalar_tensor_tensor(out=qq.t[:], in0=ps.t[:], scalar=gv, in1=r2.t[:], op0=ALU.mult, op1=ALU.mult),
                         reads=[ps.b, r2.b] + CR, writes=[qq.b])

                    def st3():
                        rp = pss[cnt["s"] % 4]
                        cnt["s"] += 1
                        sb_ = stb[cnt["stb"] % 3]
                        cnt["stb"] += 1
                        P.op("pe", lambda e: e.matmul(rp.t[:], K["ROTT"], qq.t[:], start=True, stop=True), reads=[qq.b] + CR, writes=[rp.b])
                        P.op("dve", lambda e: e.tensor_tensor(out=tt1.t[:], in0=qq.t[:], in1=csb["cos"].t[:], op=ALU.mult),
                             reads=[qq.b, csb["cos"].b], writes=[tt1.b])
                        P.op("dve", lambda e: e.tensor_tensor(out=qq.t[:], in0=rp.t[:], in1=csb["sin"].t[:], op=ALU.mult),
                             reads=[rp.b, csb["sin"].b], writes=[qq.b])
                        P.op("dve", lambda e: e.tensor_tensor(out=sb_.t[:], in0=tt1.t[:], in1=qq.t[:], op=ALU.add),
                             reads=[tt1.b, qq.b], writes=[sb_.b])
                        P.dma("sp", lambda e: e.dma_start(out=qk_s[j, :, t0:t0 + T], in_=sb_.t[:]), "stb%d" % (cnt["stb"] % 3), reads=[sb_.b])
                    pending.append((seq + 2, st3))
                pending.append((seq + 1, st2))
            elif j < 12:
                sb_ = stb[cnt["stb"] % 3]
                cnt["stb"] += 1
                P.op("act", lambda e: e.copy(out=sb_.t[:], in_=ps.t[:]), reads=[ps.b], writes=[sb_.b])
                P.dma("sp", lambda e: e.dma_start(out=vv_s[j - 8, :, t0:t0 + T], in_=sb_.t[:]), "stb%d" % (cnt["stb"] % 3), reads=[sb_.b])
            else:
                sf = stf[cnt["stf"] % 2]
                cnt["stf"] += 1
                P.op("act", lambda e: e.copy(out=sf.t[:], in_=ps.t[:]), reads=[ps.b], writes=[sf.b])
                P.dma("sp", lambda e: e.dma_start(out=pr_s[j - 12, :, t0:t0 + T], in_=sf.t[:]), "stf%d" % (cnt["stf"] % 2), reads=[sf.b])

        prep(0)
        seq = 0
        for tt in range(NT):
            for j in range(30):
                chunk(tt, j, seq)
                seq += 1
                flush(seq)
                if j == 14 and tt + 1 < NT:
                    prep(tt + 1)
        flush(seq + 10)
        P.run_block()


def phase_A2(B, S, qk_s, vv_s, MIX, K):
    nc, P = B.nc, B.P
    CR = K["CR"]
    with contextlib.ExitStack() as es:
        qn_ = B.sb(es, "qn", [128, S], BF16)
        kn_ = B.sb(es, "kn", [128, S], BF16)
        vn_ = B.sb(es, "vn", [128, S], BF16)
        qd_ = B.sb(es, "qd", [128, S], BF16)
        kd_ = B.sb(es, "kd", [128, S], BF16)
        vd_ = B.sb(es, "vd", [128, S], BF16)
        acc = [B.sb(es, "acc%d" % h, [65, S], F32) for h in range(2)]
        vp = [B.sb(es, "vp%d" % i, [128, 2, 65], BF16) for i in range(3)]
        pT = [B.sb(es, "pT%d" % i, [128, 256], BF16) for i in range(4)]
        rec = [B.sb(es, "rec%d" % i, [64, 512], F32) for i in range(2)]
        ost = [B.sb(es, "ost%d" % i, [64, 512], MIX["dt"]) for i in range(2)]
        MDST = MIX["dst"]
        for v_ in vp:
            P.op("pool", _bind(lambda e, v_: e.memset(v_.t[:], 1.0), v_), writes=[v_.b])
        sc_ps = [B.psb[0], B.psb[1]]
        o_ps = [[B.psb[2 + h * 2 + i] for i in range(2)] for h in range(2)]
        vt_ps = [B.psb[6], B.psb[7]]
        fin_ps = [B.psb[0], B.psb[1]]
        c = dict(vp=0, pT=0, sc=0, vt=0, fin=0, kb=0)

        def o_ap(h, i):
            return o_ps[h][i].t[0:65, 0:128]

        def vt_ap(i):
            return vt_ps[i].t[:, 0:64].bitcast(BF16)

        def do_kb(hp_i, bi, d, r, kb, nb, M, q3, k3, v3):
            nq = 256 if kb < nb - 1 else 128
            k0 = r * M + kb * 128
            vti = c["vt"] % 2
            c["vt"] += 1
            vtb = vt_ps[vti]
            vpt = vp[c["vp"] % 3]
            c["vp"] += 1
            P.op("pe", lambda e: e.transpose(out=vt_ap(vti), in_=v3.t[:, k0:k0 + 128], identity=K["IDENTB"]),
                 reads=[v3.b] + CR, writes=[vtb.b])
            P.op("dve", lambda e: e.tensor_copy(out=vpt.t[:, :, 0:64], in_=vt_ap(vti).rearrange("p (h c) -> p h c", h=2)),
                 reads=[vtb.b], writes=[vpt.b])
            for h in range(2):
                do_head(bi, d, r, kb, nq, k0, h, q3, k3, vpt)

        def do_head(bi, d, r, kb, nq, k0, h, q3, k3, vpt):
            hs = slice(h * 64, (h + 1) * 64)
            sc = sc_ps[c["sc"] % 2]
            c["sc"] += 1
            pt = pT[c["pT"] % 4]
            c["pT"] += 1

            def scf(e):
                e.matmul(sc.t[:, 0:nq], k3.t[hs, k0:k0 + 128], q3.t[hs, k0:k0 + nq], start=True, stop=False)
                return e.matmul(sc.t[:, 0:nq], K["IDENTB"], K["MASKB"][:, 0:nq], start=False, stop=True)
            P.op("pe", scf, reads=[q3.b, k3.b] + CR, writes=[sc.b])
            P.op("act", lambda e: e.activation(out=pt.t[:, 0:nq], in_=sc.t[:, 0:nq], func=AF.Exp, scale=0.125),
                 reads=[sc.b], writes=[pt.b])
            oa = o_ps[h][kb % 2]
            ob = o_ps[h][(kb + 1) % 2]

            def pv(e):
                r_ = e.matmul(o_ap(h, kb % 2), vpt.t[:, h, :], pt.t[:, 0:128], start=(kb == 0), stop=True, skip_group_check=True)
                if nq == 256:
                    r_ = e.matmul(o_ap(h, (kb + 1) % 2), vpt.t[:, h, :], pt.t[:, 128:256], start=True, stop=False, skip_group_check=True)
                return r_
            P.op("pe", pv, reads=[pt.b, vpt.b, oa.b], writes=[oa.b] + ([ob.b] if nq == 256 else []))
            tpos = (kb * 128) * d + r

            def ev(e):
                dst = acc[h].t[:, tpos:tpos + 127 * d + 1:d] if d > 1 else acc[h].t[:, tpos:tpos + 128]
                if bi == 0:
                    return e.tensor_copy(out=dst, in_=o_ap(h, kb % 2))
                return e.tensor_tensor(out=dst, in0=dst, in1=o_ap(h, kb % 2), op=ALU.add)
            P.op("dve", ev, reads=[oa.b, acc[h].b], writes=[acc[h].b, oa.b])

        def do_branch(hp_i, bi, d):
            M = S // d
            nb = M // 128
            if d == 1:
                q3, k3, v3 = qn_, kn_, vn_
            else:
                def cp(src, dst, eng):
                    P.op(eng, lambda e: e.tensor_copy(out=dst.t[:].rearrange("p (r m) -> p r m", r=d),
                                                      in_=src.t[:].rearrange("p (m r) -> p r m", r=d)),
                         reads=[src.b], writes=[dst.b])
                cp(qn_, qd_, "dve")
                cp(kn_, kd_, "pool")
                cp(vn_, vd_, "dve")
                q3, k3, v3 = qd_, kd_, vd_
            for r in range(d):
                for kb in range(nb):
                    do_kb(hp_i, bi, d, r, kb, nb, M, q3, k3, v3)

        def do_fin(hp_i, h, s0):
            fp = fin_ps[c["fin"] % 2]
            rc_ = rec[c["fin"] % 2]
            os_ = ost[c["fin"] % 2]
            key = "ost%d" % (c["fin"] % 2)
            c["fin"] += 1
            P.op("pe", lambda e: e.matmul(fp.t[0:64, :], K["ONESF"][64:65, 0:64], acc[h].t[64:65, s0:s0 + 512], start=True, stop=True),
                 reads=[acc[h].b] + CR, writes=[fp.b])
            P.op("dve", lambda e: e.reciprocal(out=rc_.t[:], in_=fp.t[0:64, :]), reads=[fp.b], writes=[rc_.b])
            P.op("pool", lambda e: e.tensor_tensor(out=os_.t[:], in0=acc[h].t[0:64, s0:s0 + 512], in1=rc_.t[:], op=ALU.mult),
                 reads=[rc_.b, acc[h].b], writes=[os_.b])
            row = (hp_i * 2 + h) * 64
            P.dma("sp", lambda e: e.dma_start(out=MDST(row, 64, s0, 512), in_=os_.t[:]), key, reads=[os_.b])

        def do_pair(hp_i):
            P.dma("sp", lambda e: e.dma_start(out=qn_.t[:], in_=qk_s[hp_i]), "a2q", writes=[qn_.b])
            P.dma("sp", lambda e: e.dma_start(out=kn_.t[:], in_=qk_s[4 + hp_i]), "a2k", writes=[kn_.b])
            P.dma("sp", lambda e: e.dma_start(out=vn_.t[:], in_=vv_s[hp_i]), "a2v", writes=[vn_.b])
            for bi, d in enumerate((1, 4, 16)):
                do_branch(hp_i, bi, d)
            for h in range(2):
                for s0 in range(0, S, 512):
                    do_fin(hp_i, h, s0)

        for hp_i in range(4):
            do_pair(hp_i)
        P.run_block()


def phase_A3(B, S, pr_s, MIX, wdu_d, wau_d, wgu_d, K):
    nc, P = B.nc, B.P
    CR = K["CR"]
    hp, lp = K["hp"], K["lp"]
    NS = S // TS
    NC = TS // CH
    W = TS + 1
    with contextlib.ExitStack() as es:
        def H(name, n=TS, extra=()):
            return B.sb(es, name, [64, 8] + list(extra) + [n], F32)

        def S64(name, rows=64):
            return B.sb(es, name, [rows, 8, 64], F32)
        wdu = B.sb(es, "wdu", [128, 512], F32)
        wau = B.sb(es, "wau", [128, 512], F32)
        wgu = B.sb(es, "wgu", [128, 4, 512], F32)
        P.dma("sp", lambda e: e.dma_start(out=wdu.t[:], in_=wdu_d), "wdu", writes=[wdu.b])
        P.dma("sp", lambda e: e.dma_start(out=wau.t[:], in_=wau_d), "wau", writes=[wau.b])
        P.dma("sp", lambda e: e.dma_start(out=wgu.t[:].rearrange("p k c -> p (k c)"), in_=wgu_d), "wgu", writes=[wgu.b])
        RX = [H("RX%d" % i, W) for i in range(2)]
        KX = [H("KX%d" % i, W) for i in range(2)]
        VX = [H("VX%d" % i, W) for i in range(2)]
        WX = [B.sb(es, "WX%d" % i, [128, W], F32) for i in range(2)]
        AXl = [B.sb(es, "AX%d" % i, [128, W], F32) for i in range(2)]
        GX = [B.sb(es, "GX%d" % i, [128, 4, W], F32) for i in range(2)]
        parts = {}

        def part(tb, i):
            k = (id(tb), i)
            if k not in parts:
                parts[k] = Buf()
            return parts[k]
        D1 = H("D1")
        r_ = H("r")
        k_ = H("k")
        VZ = [H("VZ%d" % i, TS, extra=(2,)) for i in range(2)]
        wdm = B.sb(es, "wdm", [128, TS], F32)
        adm = B.sb(es, "adm", [128, TS], F32)
        gdm = B.sb(es, "gdm", [128, 4, TS], F32)
        dl = B.sb(es, "dl", [128, 4, TS], F32)
        sw = H("sw")
        a_ = H("a")
        g_ = [H("g%d" % i) for i in range(2)]
        kk = H("kk")
        sq = H("sq")
        kmod = H("kmod")
        ba = H("ba")
        cs_ = H("cs")
        E1 = [H("E1%d" % i) for i in range(2)]
        E2 = H("E2")
        E3 = H("E3")
        E4 = H("E4")
        tmpH = H("tmpH")
        AR = [H("AR%d" % i, TS, extra=(2,)) for i in range(2)]
        BK = [H("BK%d" % i, TS, extra=(2,)) for i in range(2)]
        BKh = [H("BKh%d" % i, TS, extra=(2,)) for i in range(2)]
        bonus = [H("bon%d" % i) for i in range(2)]
        Y = [H("Y%d" % i) for i in range(2)]
        Gm = [B.sb(es, "Gm%d" % i, [128, 8, 128], F32) for i in range(2)]
        QN0 = [S64("QN0%d" % i) for i in range(2)]
        QP = [S64("QP%d" % i) for i in range(2)]
        QtP = [S64("QtP%d" % i) for i in range(2)]
        IQ = S64("IQ")
        X = [S64("X%d" % i) for i in range(2)]
        Atm = S64("Atm")
        BKt = [S64("BKt%d" % i, 128) for i in range(2)]
        UV = [S64("UV%d" % i, 128) for i in range(2)]
        Wsb = S64("Wsb")
        Uhat = [S64("Uhat%d" % i) for i in range(2)]
        AhT = [S64("AhT%d" % i) for i in range(2)]
        ST = [S64("ST%d" % i) for i in range(2)]
        STd = S64("STd")
        yc = H("yc")
        ysq = H("ysq")
        rsd = H("rsd")
        ostg = [B.sb(es, "ostg%d" % i, [64, 8, TS], MIX["dt"]) for i in range(2)]
        MDST = MIX["dst"]
        P.op("pool", lambda e: e.memset(ST[0].t[:], 0.0), writes=[ST[0].b])
        P.op("pool", lambda e: e.memset(VZ[0].t[:], 0.0), writes=[VZ[0].b])
        P.op("pool", lambda e: e.memset(VZ[1].t[:], 0.0), writes=[VZ[1].b])
        P.op("pool", lambda e: e.memset(UV[0].t[:], 0.0), writes=[UV[0].b])
        P.op("pool", lambda e: e.memset(UV[1].t[:], 0.0), writes=[UV[1].b])

        ONES64 = K["ONESF"][0:64, 0:64]
        ID64 = K["IDENTF"][0:64, 0:64]
        ID64B = ID64.unsqueeze(1).to_broadcast([64, 8, 64])
        MASKNB = K["MASKN"].unsqueeze(1).to_broadcast([64, 8, 64])
        MASKGB = K["MASKG"].unsqueeze(1).to_broadcast([128, 4, 128])
        st = dict(ps=0, sti=0, chunk=0)

        def nps():
            p = B.psb[st["ps"] % 8]
            st["ps"] += 1
            return p

        def hpv(i):
            return hp.t[:, i, :].unsqueeze(2)

        def bc(ap, n=TS):
            return ap.to_broadcast([64, 8, n])

        def pv8(p, rows=64):
            return p.t[0:rows, :].rearrange("p (h t) -> p h t", h=8)

        def mm8(out_fn, lhs_fn, rhs_fn):
            def f(e):
                for h in range(8):
                    r = e.matmul(out_fn(h), lhs_fn(h), rhs_fn(h), start=True, stop=True)
                return r
            return f

        def do_chunk(s_i, c, ar, bk, bkh, vz, e1, yy):
            cs0 = c * CH
            csl = slice(cs0, cs0 + CH)
            ci = st["chunk"] % 2
            st["chunk"] += 1
            gm, qn0, bkt, uv, uh, aht = Gm[ci], QN0[ci], BKt[ci], UV[ci], Uhat[ci], AhT[ci]
            for half in range(2):
                def ghalf(half):
                    pg_ = nps()

                    def gmm(e):
                        for hh in range(4):
                            h = half * 4 + hh
                            r = e.matmul(pg_.t[:, hh * 128:(hh + 1) * 128], bk.t[:, h, :, csl], ar.t[:, h, :, csl], start=True, stop=True)
                        return r
                    P.op("pe", gmm, reads=[bk.b, ar.b], writes=[pg_.b])
                    P.op("dve", lambda e: e.tensor_tensor(out=gm.t[:, half * 4:half * 4 + 4, :], in0=pg_.t[:].rearrange("p (h t) -> p h t", h=4),
                                                          in1=MASKGB, op=ALU.mult),
                         reads=[pg_.b] + CR, writes=[gm.b])
                ghalf(half)
            pn = nps()
            P.op("pe", mm8(lambda h: pv8(pn)[:, h, :], lambda h: ar.t[:, h, 0, csl], lambda h: bk.t[:, h, 0, csl]), reads=[ar.b, bk.b], writes=[pn.b])
            P.op("dve", lambda e: e.tensor_tensor(out=qn0.t[:], in0=pv8(pn), in1=MASKNB, op=ALU.mult), reads=[pn.b] + CR, writes=[qn0.b])
            if DBG < 5:
                return
            P.op("pool", lambda e: e.tensor_tensor(out=X[0].t[:], in0=gm.t[0:64, :, 0:64], in1=ID64B, op=ALU.add), reads=[gm.b] + CR, writes=[X[0].b])

            def level(lvl, q_cur, qt_ap, qt_b, x_cur):
                pq = nps()
                P.op("pe", mm8(lambda h: pv8(pq)[:, h, :], qt_ap, lambda h: q_cur.t[:, h, :]), reads=[qt_b, q_cur.b], writes=[pq.b])
                q_new = QP[lvl % 2]
                qt_new = QtP[lvl % 2]
                if lvl < 5:
                    pqt = nps()
                    P.op("pe", mm8(lambda h: pv8(pqt)[:, h, :], lambda h: q_cur.t[:, h, :], qt_ap), reads=[qt_b, q_cur.b], writes=[pqt.b])
                    P.op("act", lambda e: e.copy(out=q_new.t[:], in_=pv8(pq)), reads=[pq.b], writes=[q_new.b])
                    P.op("dve", lambda e: e.tensor_copy(out=qt_new.t[:], in_=pv8(pqt)), reads=[pqt.b], writes=[qt_new.b])
                    P.op("pool", lambda e: e.tensor_tensor(out=IQ.t[:], in0=q_new.t[:], in1=ID64B, op=ALU.add), reads=[q_new.b] + CR, writes=[IQ.b])
                else:
                    P.op("dve", lambda e: e.tensor_tensor(out=IQ.t[:], in0=pv8(pq), in1=ID64B, op=ALU.add), reads=[pq.b] + CR, writes=[IQ.b])
                px = nps()
                x_new = X[lvl % 2]
                P.op("pe", mm8(lambda h: pv8(px)[:, h, :], lambda h: IQ.t[:, h, :], lambda h: x_cur.t[:, h, :]), reads=[IQ.b, x_cur.b], writes=[px.b])
                P.op("act", lambda e: e.copy(out=x_new.t[:], in_=pv8(px)), reads=[px.b], writes=[x_new.b])
                return q_new, (lambda h: qt_new.t[:, h, :]), qt_new.b, x_new

            q_cur, qt_ap, qt_b, x_cur = qn0, (lambda h: gm.t[0:64, h, 0:64]), gm.b, X[0]
            for lvl in range(1, 6):
                q_cur, qt_ap, qt_b, x_cur = level(lvl, q_cur, qt_ap, qt_b, x_cur)
            TT = x_cur
            if DBG < 6:
                return
            pa_, pb_, pv_ = nps(), nps(), nps()

            def trs(e):
                for h in range(8):
                    e.transpose(out=pv8(pa_)[:, h, :], in_=ar.t[:, h, 0, csl], identity=ID64)
                for h in range(8):
                    e.transpose(out=pv8(pb_, 128)[:, h, :], in_=bkh.t[:, h, :, csl], identity=ID64)
                for h in range(8):
                    r = e.transpose(out=pv8(pv_, 128)[:, h, :], in_=vz.t[:, h, :, csl], identity=ID64)
                return r
            P.op("pe", trs, reads=[ar.b, bkh.b, vz.b] + CR, writes=[pa_.b, pb_.b, pv_.b])
            P.op("act", lambda e: e.copy(out=Atm.t[:], in_=pv8(pa_)), reads=[pa_.b], writes=[Atm.b])
            P.op("dve", lambda e: e.tensor_copy(out=bkt.t[:], in_=pv8(pb_, 128)), reads=[pb_.b], writes=[bkt.b])
            P.op("act", lambda e: e.copy(out=uv.t[64:128, :, :], in_=pv8(pv_, 128)[64:128, :, :]), reads=[pv_.b], writes=[uv.b])
            pw_ = nps()
            P.op("pe", mm8(lambda h: pv8(pw_)[:, h, :], lambda h: gm.t[64:128, h, 0:64], lambda h: uv.t[64:128, h, :]), reads=[gm.b, uv.b], writes=[pw_.b])
            P.op("act", lambda e: e.copy(out=Wsb.t[:], in_=pv8(pw_)), reads=[pw_.b], writes=[Wsb.b])
            pu_, ph_ = nps(), nps()
            P.op("pe", mm8(lambda h: pv8(pu_)[:, h, :], lambda h: TT.t[:, h, :], lambda h: Wsb.t[:, h, :]), reads=[TT.b, Wsb.b], writes=[pu_.b])
            P.op("pe", mm8(lambda h: pv8(ph_)[:, h, :], lambda h: Atm.t[:, h, :], lambda h: TT.t[:, h, :]), reads=[TT.b, Atm.b], writes=[ph_.b])
            P.op("act", lambda e: e.copy(out=uh.t[:], in_=pv8(pu_)), reads=[pu_.b], writes=[uh.b])
            P.op("dve", lambda e: e.tensor_copy(out=aht.t[:], in_=pv8(ph_)), reads=[ph_.b], writes=[aht.b])
            if DBG < 7:
                return
            st_old = ST[st["sti"] % 2]
            st_new = ST[(st["sti"] + 1) % 2]
            st["sti"] += 1
            pc_ap = e1.t[:, :, cs0 + CH - 1:cs0 + CH]
            P.op("pool", lambda e: e.tensor_tensor(out=STd.t[:], in0=st_old.t[:], in1=pc_ap.to_broadcast([64, 8, 64]), op=ALU.mult),
                 reads=[st_old.b, e1.b], writes=[STd.b])
            pU = nps()
            P.op("pe", mm8(lambda h: pv8(pU)[:, h, :], lambda h: aht.t[:, h, :], lambda h: st_old.t[:, h, :]), reads=[aht.b, st_old.b], writes=[pU.b])
            P.op("dve", lambda e: e.tensor_tensor(out=uv.t[0:64, :, :], in0=pv8(pU), in1=uh.t[:], op=ALU.add), reads=[pU.b, uh.b], writes=[uv.b])
            pS = nps()
            P.op("pe", mm8(lambda h: pv8(pS)[:, h, :], lambda h: bkt.t[:, h, :], lambda h: uv.t[:, h, :]), reads=[bkt.b, uv.b], writes=[pS.b])
            P.op("dve", lambda e: e.tensor_tensor(out=st_new.t[:], in0=pv8(pS), in1=STd.t[:], op=ALU.add), reads=[pS.b, STd.b], writes=[st_new.b])
            if DBG < 8:
                return
            pY = nps()

            def y1(e):
                for h in range(8):
                    e.matmul(pv8(pY)[:, h, :], st_old.t[:, h, :], ar.t[:, h, 1, csl], start=True, stop=False)
                    r = e.matmul(pv8(pY)[:, h, :], uv.t[:, h, :], gm.t[:, h, 64:128], start=False, stop=True)
                return r
            P.op("pe", y1, reads=[st_old.b, ar.b, uv.b, gm.b], writes=[pY.b])
            P.op("act", lambda e: e.copy(out=yy.t[:, :, csl], in_=pv8(pY)), reads=[pY.b], writes=[yy.b])

        DBG = int(os.environ.get('A3_DBG', '100'))

        def do_super(s_i):
            t0 = s_i * TS
            i2 = s_i % 2
            rx, kx, vx, wx, ax, gx = RX[i2], KX[i2], VX[i2], WX[i2], AXl[i2], GX[i2]
            vz, ar, bk, bkh, e1, bon, gg, yy = VZ[i2], AR[i2], BK[i2], BKh[i2], E1[i2], bonus[i2], g_[i2], Y[i2]
            lo = 1 if s_i == 0 else 0
            src0 = t0 - 1 + lo
            n = W - lo

            def ldH(dst, c0, key):
                if s_i == 0:
                    P.op("pool", lambda e: e.memset(dst.t[:, :, 0:1], 0.0), writes=[part(dst, cc) for cc in range(4)])
                for cc in range(4):
                    def one(cc):
                        P.dma("sp", lambda e: e.dma_start(out=dst.t[:, 2 * cc:2 * cc + 2, lo:W],
                                                          in_=pr_s[c0 + cc, :, src0:src0 + n].rearrange("(h p) t -> p h t", h=2)),
                              "%s%d" % (key, i2), writes=[part(dst, cc)])
                    one(cc)
            ldH(rx, 0, "rx")
            ldH(kx, 4, "kx")
            ldH(vx, 8, "vx")
            if s_i == 0:
                P.op("pool", lambda e: e.memset(wx.t[:, 0:1], 0.0), writes=[wx.b])
                P.op("pool", lambda e: e.memset(ax.t[:, 0:1], 0.0), writes=[ax.b])
                P.op("pool", lambda e: e.memset(gx.t[:, :, 0:1], 0.0), writes=[part(gx, cc) for cc in range(4)])
            P.dma("sp", lambda e: e.dma_start(out=wx.t[:, lo:W], in_=pr_s[12, :, src0:src0 + n]), "wx%d" % i2, writes=[wx.b])
            P.dma("sp", lambda e: e.dma_start(out=ax.t[:, lo:W], in_=pr_s[13, :, src0:src0 + n]), "ax%d" % i2, writes=[ax.b])
            for cc in range(4):
                def oneg(cc):
                    P.dma("sp", lambda e: e.dma_start(out=gx.t[:, cc, lo:W], in_=pr_s[14 + cc, :, src0:src0 + n]),
                          "gx%d" % i2, writes=[part(gx, cc)])
                oneg(cc)
            allp = lambda tb: [part(tb, cc) for cc in range(4)]

            def mixH(src, dst_ap, mi):
                def f(e):
                    e.tensor_tensor(out=D1.t[:], in0=src.t[:, :, 0:TS], in1=src.t[:, :, 1:W], op=ALU.subtract)
                    e.tensor_tensor(out=D1.t[:], in0=D1.t[:], in1=bc(hpv(mi)), op=ALU.mult)
                    return e.tensor_tensor(out=dst_ap, in0=D1.t[:], in1=src.t[:, :, 1:W], op=ALU.add)
                return f
            P.op("dve", mixH(rx, r_.t[:], 0), reads=allp(rx) + CR, writes=[D1.b, r_.b])
            P.op("dve", mixH(kx, k_.t[:], 1), reads=allp(kx) + CR, writes=[D1.b, k_.b])
            P.op("dve", mixH(vx, vz.t[:, :, 1, :], 2), reads=allp(vx) + CR, writes=[D1.b, vz.b])

            def mixL(e):
                e.tensor_tensor(out=dl.t[:, 0, :], in0=wx.t[:, 0:TS], in1=wx.t[:, 1:W], op=ALU.subtract)
                e.scalar_tensor_tensor(out=wdm.t[:], in0=dl.t[:, 0, :], scalar=lp.t[:, 0:1], in1=wx.t[:, 1:W], op0=ALU.mult, op1=ALU.add)
                e.tensor_tensor(out=dl.t[:, 0, :], in0=ax.t[:, 0:TS], in1=ax.t[:, 1:W], op=ALU.subtract)
                e.scalar_tensor_tensor(out=adm.t[:], in0=dl.t[:, 0, :], scalar=lp.t[:, 1:2], in1=ax.t[:, 1:W], op0=ALU.mult, op1=ALU.add)
                e.tensor_tensor(out=dl.t[:], in0=gx.t[:, :, 0:TS], in1=gx.t[:, :, 1:W], op=ALU.subtract)
                for cc in range(4):
                    r = e.scalar_tensor_tensor(out=gdm.t[:, cc, :], in0=dl.t[:, cc, :], scalar=lp.t[:, 2 + cc:3 + cc], in1=gx.t[:, cc, 1:W],
                                               op0=ALU.mult, op1=ALU.add)
                return r
            P.op("dve", mixL, reads=[wx.b, ax.b] + allp(gx) + CR, writes=[dl.b, wdm.b, adm.b, gdm.b])
            P.op("act", lambda e: e.activation(out=wdm.t[:], in_=wdm.t[:], func=AF.Tanh), reads=[wdm.b], writes=[wdm.b])
            P.op("act", lambda e: e.activation(out=gdm.t[:], in_=gdm.t[:], func=AF.Sigmoid), reads=[gdm.b], writes=[gdm.b])

            def lora_all():
                pw, pa, pg = nps(), nps(), nps()

                def lora(e):
                    for h in range(8):
                        e.matmul(pv8(pw)[:, h, :], wdu.t[:, h * 64:(h + 1) * 64], wdm.t[:], start=True, stop=True)
                    for h in range(8):
                        e.matmul(pv8(pa)[:, h, :], wau.t[:, h * 64:(h + 1) * 64], adm.t[:], start=True, stop=True)
                    for h in range(8):
                        for cc in range(4):
                            r = e.matmul(pv8(pg)[:, h, :], wgu.t[:, cc, h * 64:(h + 1) * 64], gdm.t[:, cc, :], start=(cc == 0), stop=(cc == 3))
                    return r
                P.op("pe", lora, reads=[wdu.b, wau.b, wgu.b, wdm.b, adm.b, gdm.b], writes=[pw.b, pa.b, pg.b])

                def sig(e):
                    for h in range(8):
                        e.activation(out=sw.t[:, h, :], in_=pv8(pw)[:, h, :], func=AF.Sigmoid, bias=hp.t[:, 3, h:h + 1], scale=1.0)
                    for h in range(8):
                        r = e.activation(out=a_.t[:, h, :], in_=pv8(pa)[:, h, :], func=AF.Sigmoid, bias=hp.t[:, 4, h:h + 1], scale=1.0)
                    return r
                P.op("act", sig, reads=[pw.b, pa.b] + CR, writes=[sw.b, a_.b])
                P.op("act", lambda e: e.copy(out=gg.t[:], in_=pv8(pg)), reads=[pg.b], writes=[gg.b])
            if DBG < 2:
                return
            lora_all()
            if DBG < 3:
                return
            P.op("dve", lambda e: e.tensor_tensor(out=kk.t[:], in0=k_.t[:], in1=bc(hpv(5)), op=ALU.mult), reads=[k_.b] + CR, writes=[kk.b])
            P.op("act", lambda e: e.activation(out=sq.t[:], in_=kk.t[:], func=AF.Square), reads=[kk.b], writes=[sq.b])

            def sum8(src, consume):
                pk = nps()
                P.op("pe", mm8(lambda h: pv8(pk)[:, h, :], lambda h: ONES64, lambda h: src.t[:, h, :]), reads=[src.b] + CR, writes=[pk.b])
                consume(pk)
            sum8(sq, lambda pk: P.op("act", lambda e: e.activation(out=tmpH.t[:], in_=pv8(pk), func=AF.Sqrt), reads=[pk.b], writes=[tmpH.b]))

            def kkn(e):
                e.tensor_scalar(out=tmpH.t[:], in0=tmpH.t[:], scalar1=1e-12, scalar2=None, op0=ALU.max)
                e.reciprocal(out=tmpH.t[:], in_=tmpH.t[:])
                e.tensor_tensor(out=kk.t[:], in0=kk.t[:], in1=tmpH.t[:], op=ALU.mult)
                e.scalar_tensor_tensor(out=tmpH.t[:], in0=a_.t[:], scalar=-1.0, in1=bc(hpv(6)), op0=ALU.add, op1=ALU.mult)
                e.scalar_tensor_tensor(out=kmod.t[:], in0=tmpH.t[:], scalar=1.0, in1=k_.t[:], op0=ALU.add, op1=ALU.mult)
                return e.tensor_tensor(out=ba.t[:], in0=kk.t[:], in1=a_.t[:], op=ALU.mult)
            P.op("dve", kkn, reads=[tmpH.b, kk.b, a_.b, k_.b] + CR, writes=[tmpH.b, kk.b, kmod.b, ba.b])
            flat = lambda t: t.t[:].rearrange("p h t -> p (h t)")
            P.op("dve", lambda e: e.tensor_tensor_scan(out=flat(cs_), data0=K["RESETM"], data1=flat(sw), initial=0.0, op0=ALU.mult, op1=ALU.add),
                 reads=[sw.b] + CR, writes=[cs_.b])
            P.op("act", lambda e: e.activation(out=e1.t[:], in_=cs_.t[:], func=AF.Exp, scale=-C0), reads=[cs_.b], writes=[e1.b])
            P.op("act", lambda e: e.activation(out=E2.t[:], in_=cs_.t[:], func=AF.Exp, scale=C0), reads=[cs_.b], writes=[E2.b])
            P.op("pool", lambda e: e.tensor_tensor(out=E3.t[:], in0=cs_.t[:], in1=sw.t[:], op=ALU.subtract), reads=[cs_.b, sw.b], writes=[E3.b])
            P.op("act", lambda e: e.activation(out=E3.t[:], in_=E3.t[:], func=AF.Exp, scale=-C0), reads=[E3.b], writes=[E3.b])

            def e4f(e):
                c4 = cs_.t[:].rearrange("p h (c t) -> p h c t", t=CH)
                return e.tensor_tensor(out=E4.t[:].rearrange("p h (c t) -> p h c t", t=CH), in0=c4,
                                       in1=c4[:, :, :, CH - 1:CH].to_broadcast([64, 8, NC, CH]), op=ALU.subtract)
            P.op("pool", e4f, reads=[cs_.b], writes=[E4.b])
            P.op("act", lambda e: e.activation(out=E4.t[:], in_=E4.t[:], func=AF.Exp, scale=C0), reads=[E4.b], writes=[E4.b])

            def tild(e):
                e.tensor_tensor(out=ar.t[:, :, 1, :], in0=r_.t[:], in1=e1.t[:], op=ALU.mult)
                e.scalar_tensor_tensor(out=ar.t[:, :, 0, :], in0=kk.t[:], scalar=-1.0, in1=E3.t[:], op0=ALU.mult, op1=ALU.mult)
                e.tensor_tensor(out=bk.t[:, :, 0, :], in0=ba.t[:], in1=E2.t[:], op=ALU.mult)
                return e.tensor_tensor(out=bk.t[:, :, 1, :], in0=kmod.t[:], in1=E2.t[:], op=ALU.mult)
            P.op("dve", tild, reads=[r_.b, e1.b, kk.b, E3.b, ba.b, E2.b, kmod.b], writes=[ar.b, bk.b])

            def hatf(e):
                e.tensor_tensor(out=bkh.t[:, :, 0, :], in0=ba.t[:], in1=E4.t[:], op=ALU.mult)
                return e.tensor_tensor(out=bkh.t[:, :, 1, :], in0=kmod.t[:], in1=E4.t[:], op=ALU.mult)
            P.op("pool", hatf, reads=[ba.b, kmod.b, E4.b], writes=[bkh.b])

            def rkf(e):
                e.tensor_tensor(out=tmpH.t[:], in0=r_.t[:], in1=kmod.t[:], op=ALU.mult)
                return e.tensor_tensor(out=sq.t[:], in0=tmpH.t[:], in1=bc(hpv(7)), op=ALU.mult)
            P.op("pool", rkf, reads=[r_.b, kmod.b, tmpH.b, sq.b] + CR, writes=[tmpH.b, sq.b])
            sum8(sq, lambda pb: P.op("dve", lambda e: e.tensor_tensor(out=bon.t[:], in0=pv8(pb), in1=vz.t[:, :, 1, :], op=ALU.mult),
                                     reads=[pb.b, vz.b], writes=[bon.b]))
            if DBG < 4:
                return
            for c in range(NC):
                do_chunk(s_i, c, ar, bk, bkh, vz, e1, yy)
            if DBG < 9:
                return
            sum8(yy, lambda pm: P.op("dve", lambda e: e.scalar_tensor_tensor(out=yc.t[:], in0=pv8(pm), scalar=-1.0 / 64, in1=yy.t[:],
                                                                             op0=ALU.mult, op1=ALU.add),
                                     reads=[pm.b, yy.b], writes=[yc.b]))
            P.op("act", lambda e: e.activation(out=ysq.t[:], in_=yc.t[:], func=AF.Square), reads=[yc.b], writes=[ysq.b])
            sum8(ysq, lambda pvv: P.op("act", lambda e: e.activation(out=rsd.t[:], in_=pv8(pvv), func=AF.Sqrt, bias=GN_EPS, scale=1.0 / 64),
                                       reads=[pvv.b], writes=[rsd.b]))
            og = ostg[i2]

            def fin(e):
                e.reciprocal(out=rsd.t[:], in_=rsd.t[:])
                e.tensor_tensor(out=yc.t[:], in0=yc.t[:], in1=rsd.t[:], op=ALU.mult)
                e.tensor_tensor(out=yc.t[:], in0=yc.t[:], in1=bc(hpv(8)), op=ALU.mult)
                e.tensor_tensor(out=yc.t[:], in0=yc.t[:], in1=bc(hpv(9)), op=ALU.add)
                e.tensor_tensor(out=yc.t[:], in0=yc.t[:], in1=bon.t[:], op=ALU.add)
                return e.tensor_tensor(out=og.t[:], in0=yc.t[:], in1=gg.t[:], op=ALU.mult)
            P.op("dve", fin, reads=[rsd.b, yc.b, bon.b, gg.b] + CR, writes=[rsd.b, yc.b, og.b])
            P.dma("sp", lambda e: e.dma_start(out=MDST(512, 512, t0, TS).rearrange("(h p) t -> p h t", h=8), in_=og.t[:]),
                  "ostg%d" % i2, reads=[og.b])

        for s_i in range(min(NS, int(os.environ.get('A3_NS', '100000')))):
            do_super(s_i)
        P.run_block()


def _consts_A(qg, kg):
    c = np.zeros((128, 8 * 128 + 256 + 4), np.float32)
    p = np.arange(128)
    c[:, 0:128] = 1.0
    c[:, 128:256] = (p[:, None] // 64 == p[None, :] // 64).astype(np.float32)
    rot = np.zeros((128, 128), np.float32)
    for m in range(128):
        if m % 64 < 32:
            rot[m + 32, m] = -1.0
        else:
            rot[m - 32, m] = 1.0
    c[:, 256:384] = rot
    c[:, 384:512] = np.eye(128, dtype=np.float32)
    i = (p % 64)[:, None]
    t = np.arange(64)[None, :]
    c[:, 512:576] = (i < t).astype(np.float32)
    c[:, 576:640] = (i <= t).astype(np.float32)
    kk = p[:, None]
    qq = np.arange(256)[None, :]
    dist = qq - kk
    c[:, 1024:1280] = np.where((dist >= 0) & (dist <= 128), 0.0, -262144.0)
    c[:, 1280] = np.tile(qg, 2)
    c[:, 1281] = np.tile(kg, 2)
    c[:, 1282] = (np.float32(10000.0) ** (-(np.arange(32, dtype=np.float32)) / np.float32(32)))[p % 32]
    return c


def _consts_64():
    c = np.zeros((64, 64 + 8 * TS), np.float32)
    t = np.arange(64)
    c[:, 0:64] = (t[:, None] > t[None, :]).astype(np.float32)
    m = np.ones((8, TS), np.float32)
    m[:, ::CH] = 0.0
    c[:, 64:] = m.reshape(1, -1)
    return c


def _wchunk(w, cols):
    blk = np.zeros((4096, 128), np.float32)
    blk[:, :len(cols)] = w[:, cols]
    return np.ascontiguousarray(blk.reshape(32, 128, 128).transpose(1, 0, 2)).reshape(128, 32 * 128)


def prep_A(inp, S):
    x = inp["x"]
    w_in = inp["w_in"][0]
    mu = inp["rwkv_mu"][0]
    maps = []
    xTs = [np.ascontiguousarray(x[b].T.reshape(32, 128, S).transpose(1, 0, 2)) for b in range(2)]
    cA = _consts_A(inp["q_norm_g"][0], inp["k_norm_g"][0])
    c64 = _consts_64()
    gA = np.ascontiguousarray(inp["attn_norm_g"][0].reshape(32, 128).T)
    RB = 3 * ATTN_W
    for core in range(8):
        b, g = core // 4, core % 4
        cols = []
        for base in (0, 2048, 4096, RB, RB + 2048, RB + 4096):
            for jj in range(4):
                cols.append(np.arange(base + 512 * g + 128 * jj, base + 512 * g + 128 * jj + 128))
        cols.append(np.arange(RB + 6144, RB + 6272))
        cols.append(np.arange(RB + 6272, RB + 6400))
        for jj in range(4):
            lo = RB + 6400 + 128 * jj
            cols.append(np.arange(lo, min(lo + 128, RB + 6880)))
        wA = np.stack([_wchunk(w_in, cc) for cc in cols])

        def hsl(v):
            return v[512 * g:512 * g + 512].reshape(8, 64).T
        hp = np.zeros((64, 10, 8), np.float32)
        hp[:, 0] = hsl(mu[0:2048])
        hp[:, 1] = hsl(mu[2048:4096])
        hp[:, 2] = hsl(mu[4096:6144])
        hp[:, 3] = hsl(inp["w0"][0])
        hp[:, 4] = hsl(inp["a0"][0])
        hp[:, 5] = hsl(inp["k_k"][0])
        hp[:, 6] = hsl(inp["k_a"][0])
        hp[:, 7] = hsl(inp["r_k"][0].reshape(-1))
        hp[:, 8] = hsl(inp["ln_x_w"][0])
        hp[:, 9] = hsl(inp["ln_x_b"][0])
        lp = np.zeros((128, 6), np.float32)
        lp[:, 0] = mu[6144:6272]
        lp[:, 1] = mu[6272:6400]
        mg = np.zeros(512, np.float32)
        mg[:480] = mu[6400:6880]
        lp[:, 2:6] = mg.reshape(4, 128).T
        wg = np.zeros((512, 512), np.float32)
        wg[:480] = inp["w_gate_up"][0][:, 512 * g:512 * g + 512]
        maps.append(dict(
            xT=xTs[b], pos=np.ascontiguousarray(inp["positions"][b][None, :].astype(np.int32)),
            wA=wA, gA=gA, c128=cA, hp=np.ascontiguousarray(hp.reshape(64, 80)), lp=lp,
            wdu=np.ascontiguousarray(inp["w_decay_up"][0][:, 512 * g:512 * g + 512]),
            wau=np.ascontiguousarray(inp["w_iclr_up"][0][:, 512 * g:512 * g + 512]),
            wgu=np.ascontiguousarray(wg.reshape(4, 128, 512).transpose(1, 0, 2)).reshape(128, 2048),
            c64=c64))
    return maps


def gather_mix(resA, S):
    mixT = np.zeros((2, 4096, S), np.float32)
    for core in range(8):
        b, g = core // 4, core % 4
        m = resA[core]["mixA"]
        mixT[b, 512 * g:512 * g + 512] = m[0:512]
        mixT[b, 2048 + 512 * g:2048 + 512 * g + 512] = m[512:1024]
    return mixT


NWB = 3
MPAD = 64
NFF = D_FF // 128


def build_B(NTOK, mix_dram=None, B=None, fused=False, mixx=None):
    own = B is None
    if own:
        B = Builder()
    nc, P = B.nc, B.P
    T = 512
    NTB = NTOK // T
    IN, OUT = "ExternalInput", "ExternalOutput"
    if fused:
        qm_d = B.dram("qmask", [128, 4], F32, IN)
    elif mix_dram is None:
        mix_dram = B.dram("mixin", [4, 1024, NTOK + 2], F32, IN)
    xT = B.dram("xTb", [128, 32, NTOK + 2], F32, IN)
    pT = B.dram("pTb", [128, 2, NTOK], F32, IN)
    wout = B.dram("wout", [32, 128, 32 * 128], F32, IN)
    wup = B.dram("wup", [2 * NFF, 128, 32 * 128], F32, IN)
    wdn = B.dram("wdn", [NFF, 128, 4096], F32, IN)
    wple = B.dram("wple", [128, 2 * 4096], F32, IN)
    wgt = B.dram("wgt", [32, 128, 32 * 128], F32, IN)
    cst = B.dram("cstb", [128, 64 + 2 * NFF * 4 + 128], F32, IN)
    outT = B.dram("outT", [128, 32, NTOK], F32, OUT)
    NU = 2 * NFF

    with contextlib.ExitStack() as es:
        c_sb = B.sb(es, "cstb", [128, 64 + NU * 4 + 128], F32)
        ones_b = B.sb(es, "onesb", [128, 128], BF16)
        P.dma("sp", lambda e: e.dma_start(out=c_sb.t[:], in_=cst), "cstb", writes=[c_sb.b])
        P.dma("pool", lambda e: e.dma_start(out=ones_b.t[:], in_=cst[:, 64 + NU * 4:64 + NU * 4 + 128]), "onesb", writes=[ones_b.b])
        GM = c_sb.t[:, 0:32]
        GP = c_sb.t[:, 32:64]
        CW = c_sb.t[:, 64:64 + NU * 3].rearrange("p (u j) -> p u j", j=3)
        CB = c_sb.t[:, 64 + NU * 3:64 + NU * 4]
        CR = [c_sb.b, ones_b.b]

        h = B.sb(es, "h", [128, 32, T], F32)
        a16 = B.sb(es, "a16", [128, 32, T], BF16)
        hh = B.sb(es, "hh", [128, 32, 2], F32)
        a16h = B.sb(es, "a16h", [128, 32, 2], BF16)
        wr = [B.sb(es, "wr%d" % i, [128, 32, 128], BF16) for i in range(NWB)]
        wd = [B.sb(es, "wd%d" % i, [128, 4096], BF16) for i in range(2)]
        ue = [[B.sb(es, "ue%d%d" % (i, j), [128, T + 2], F32) for j in range(2)] for i in range(2)]
        cv = [[B.sb(es, "cv%d%d" % (i, j), [128, T], F32) for j in range(2)] for i in range(2)]
        actg = [B.sb(es, "actg%d" % i, [128, 2, T], BF16) for i in range(2)]
        uhalo = B.sb(es, "uhalo", [128, NU, 2], F32)
        rstd = B.sb(es, "rstdb", [128, T], F32)
        rstdh = B.sb(es, "rstdh", [128, 2], F32)
        rstde = B.sb(es, "rstde", [128, T], F32)
        p16 = B.sb(es, "p16", [128, 2, T], BF16)
        sqb = [B.sb(es, "sqbb%d" % i, [128, T], BF16) for i in range(2)]
        sqh = B.sb(es, "sqh", [128, 2], BF16)
        sg = [B.sb(es, "sg%d" % i, [128, T], F32) for i in range(2)]
        te = [B.sb(es, "te%d" % i, [128, T], F32) for i in range(2)]
        hparts = [Buf() for _ in range(32)]
        mixg_b = Buf("mixg")
        if fused:
            qm = B.sb(es, "qm", [128, 4], F32)
            cand = [B.sb(es, "cand%d" % i, [128, 4, T], BF16) for i in range(4)]
            candh = [B.sb(es, "candh%d" % i, [128, 32, 2], BF16) for i in range(4)]
            P.dma("sp", lambda e: e.dma_start(out=qm.t[:], in_=qm_d), "qm", writes=[qm.b])
            NCHK = 4 * NTOK // 512
            chunk_b = [Buf("mixg%d" % i) for i in range(NCHK)]
            order = [qq * (NTOK // 512) + tt for tt in range(NTOK // 512) for qq in range(4)]
            for ci in order:
                def ag(ci):
                    P.dma("pool", lambda e: e.collective_compute("AllGather", ALU.bypass, replica_groups=[[0, 1, 2, 3], [4, 5, 6, 7]],
                                                                 ins=[mixx[ci]], outs=[mix_dram[ci]]), "ag", writes=[chunk_b[ci]], inc=1)
                ag(ci)
        psm = [B.psb[i] for i in range(6)]
        pss = [B.psb[6], B.psb[7]]
        cnt = dict(w=0, d=0, m=0, s=0, sq=0, ue=0, ag=0, sg=0)

        def wslot():
            s_ = wr[cnt["w"] % NWB]
            key = "wr%d" % (cnt["w"] % NWB)
            cnt["w"] += 1
            return s_, key

        def pmain():
            p = psm[cnt["m"] % 6]
            cnt["m"] += 1
            return p

        def psmall():
            p = pss[cnt["s"] % 2]
            cnt["s"] += 1
            return p

        def load_w(src_ap):
            s_, key = wslot()
            P.dma("pool", lambda e: e.dma_start(out=s_.t[:].rearrange("p k c -> p (k c)"), in_=src_ap), key, writes=[s_.b])
            return s_

        def mm32(ps_ap, s_, rhs_fn):
            def f(e):
                for kc in range(32):
                    r = e.matmul(ps_ap, s_.t[:, kc, :], rhs_fn(kc), start=(kc == 0), stop=(kc == 31))
                return r
            return f

        def rms(src, src_tokens, n, dst16, g_ap, rs, halo):
            ssp = psmall()
            for kc in range(32):
                def one(kc):
                    sq = sqh if halo else sqb[cnt["sq"] % 2]
                    cnt["sq"] += 1
                    sqa = sq.t[:, 0:n]
                    P.op("act", lambda e: e.activation(out=sqa, in_=src.t[:, kc, 0:n], func=AF.Square), reads=[src_tokens[kc]], writes=[sq.b])
                    P.op("pe", lambda e: e.matmul(ssp.t[:, 0:n], ones_b.t[:], sqa, start=(kc == 0), stop=(kc == 31)), reads=[sq.b] + CR, writes=[ssp.b])
                one(kc)
            P.op("act", lambda e: e.activation(out=rs.t[:, 0:n], in_=ssp.t[:, 0:n], func=AF.Sqrt, bias=NORM_EPS, scale=1.0 / D_MODEL), reads=[ssp.b], writes=[rs.b])
            P.op("dve", lambda e: e.reciprocal(out=rs.t[:, 0:n], in_=rs.t[:, 0:n]), reads=[rs.b], writes=[rs.b])
            for kc in range(32):
                def two(kc):
                    P.op("dve", lambda e: e.scalar_tensor_tensor(out=dst16.t[:, kc, 0:n], in0=src.t[:, kc, 0:n], scalar=g_ap[:, kc:kc + 1], in1=rs.t[:, 0:n],
                                                                 op0=ALU.mult, op1=ALU.mult),
                         reads=[src_tokens[kc], rs.b] + CR, writes=[dst16.b])
                two(kc)

        def do_tile(tt):
            t0 = tt * T
            first = tt == 0
            hhp = [hh.b] * 32
            for q4 in range(4):
                def ld(q4):
                    P.dma("sp", lambda e: e.dma_start(out=h.t[:, 8 * q4:8 * q4 + 8, :], in_=xT[:, 8 * q4:8 * q4 + 8, 2 + t0:2 + t0 + T]),
                          "hx%d" % q4, writes=hparts[8 * q4:8 * q4 + 8])
                    if not fused:
                        P.dma("pool", lambda e: e.dma_start(out=a16.t[:, 8 * q4:8 * q4 + 8, :],
                                                            in_=mix_dram[q4, :, 2 + t0:2 + t0 + T].rearrange("(k p) t -> p k t", p=128)),
                              "mx%d" % q4, writes=[a16.b])
                ld(q4)
            if fused:
                for g8 in range(8):
                    def selg(g8):
                        for qq in range(4):
                            def ldc(qq):
                                ci = (qq * NTOK + t0) // 512
                                P.dma("sp", lambda e: e.dma_start(out=cand[qq].t[:], in_=mix_dram[ci, g8 * 512:(g8 + 1) * 512, :].rearrange("(k p) t -> p k t", p=128)),
                                      "cd%d" % qq, reads=[chunk_b[ci]], writes=[cand[qq].b])
                            ldc(qq)
                        dst = a16.t[:, 4 * g8:4 * g8 + 4, :]

                        def sel(e):
                            r = e.tensor_scalar(out=dst, in0=cand[0].t[:], scalar1=qm.t[:, 0:1], scalar2=None, op0=ALU.mult)
                            for qq in range(1, 4):
                                r = e.scalar_tensor_tensor(out=dst, in0=cand[qq].t[:], scalar=qm.t[:, qq:qq + 1], in1=dst, op0=ALU.mult, op1=ALU.add)
                            return r
                        P.op("dve", sel, reads=[c_.b for c_ in cand] + [qm.b], writes=[a16.b])
                    selg(g8)
            P.dma("pool", lambda e: e.dma_start(out=p16.t[:], in_=pT[:, :, t0:t0 + T]), "p16", writes=[p16.b])
            if first:
                P.dma("sp", lambda e: e.dma_start(out=hh.t[:], in_=xT[:, :, 0:2]), "hhx", writes=[hh.b])
                if fused:
                    P.op("pool", lambda e: e.memset(candh[0].t[:], 0.0), writes=[candh[0].b])
                    for qq in range(1, 4):
                        def ldch(qq):
                            ci = qq * NTOK // 512 - 1
                            P.dma("sp", lambda e: e.dma_start(out=candh[qq].t[:], in_=mix_dram[ci, :, 510:512].rearrange("(k p) t -> p k t", p=128)),
                                  "cdh%d" % qq, reads=[chunk_b[ci]], writes=[candh[qq].b])
                        ldch(qq)

                    def selh(e):
                        r = e.tensor_scalar(out=a16h.t[:], in0=candh[0].t[:], scalar1=qm.t[:, 0:1], scalar2=None, op0=ALU.mult)
                        for qq in range(1, 4):
                            r = e.scalar_tensor_tensor(out=a16h.t[:], in0=candh[qq].t[:], scalar=qm.t[:, qq:qq + 1], in1=a16h.t[:], op0=ALU.mult, op1=ALU.add)
                        return r
                    P.op("dve", selh, reads=[c_.b for c_ in candh] + [qm.b], writes=[a16h.b])
                else:
                    for q4 in range(4):
                        def ldh(q4):
                            P.dma("pool", lambda e: e.dma_start(out=a16h.t[:, 8 * q4:8 * q4 + 8, :],
                                                                in_=mix_dram[q4, :, 0:2].rearrange("(k p) t -> p k t", p=128)),
                                  "mxh%d" % q4, writes=[a16h.b])
                        ldh(q4)
            for c in range(32):
                def oc(c):
                    s_ = load_w(wout[c])
                    ps = pmain()
                    P.op("pe", mm32(ps.t[:], s_, lambda kc: a16.t[:, kc, :]), reads=[s_.b, a16.b], writes=[ps.b])
                    P.op("dve", lambda e: e.tensor_tensor(out=h.t[:, c, :], in0=ps.t[:], in1=h.t[:, c, :], op=ALU.add), reads=[ps.b, hparts[c]], writes=[hparts[c]])
                    if first:
                        ph = psmall()
                        P.op("pe", mm32(ph.t[:, 0:2], s_, lambda kc: a16h.t[:, kc, :]), reads=[s_.b, a16h.b], writes=[ph.b])
                        P.op("dve", lambda e: e.tensor_tensor(out=hh.t[:, c, :], in0=ph.t[:, 0:2], in1=hh.t[:, c, :], op=ALU.add), reads=[ph.b, hh.b], writes=[hh.b])
                oc(c)
            BD_ = int(os.environ.get('B_DBG', '100'))

            def dump():
                for c in range(32):
                    P.dma("sp", _bind(lambda e, c: e.dma_start(out=outT[:, c, t0:t0 + T], in_=h.t[:, c, :]), c), "oc%d" % (c % 8), reads=[hparts[c]])
            if BD_ == 1:
                dump()
                return
            rms(h, hparts, T, a16, GM, rstd, False)
            if first:
                rms(hh, hhp, 2, a16h, GM, rstdh, True)
            def do_pairf(f0):
                ag = actg[cnt["ag"] % 2]
                cnt["ag"] += 1
                for fi in range(2):
                    def ff(f, fi):
                        cvs = []
                        for gi in range(2):
                            def gu(gi):
                                uc = gi * NFF + f
                                s_ = load_w(wup[uc])
                                ps = pmain()
                                P.op("pe", mm32(ps.t[:], s_, lambda kc: a16.t[:, kc, :]), reads=[s_.b, a16.b], writes=[ps.b])
                                u = ue[gi][cnt["ue"] % 2]
                                c1 = cv[gi][cnt["ue"] % 2]
                                if first:
                                    ph = psmall()
                                    P.op("pe", mm32(ph.t[:, 0:2], s_, lambda kc: a16h.t[:, kc, :]), reads=[s_.b, a16h.b], writes=[ph.b])
                                    P.op("dve", lambda e: e.tensor_copy(out=u.t[:, 0:2], in_=ph.t[:, 0:2]), reads=[ph.b], writes=[u.b])
                                else:
                                    P.op("dve", lambda e: e.tensor_copy(out=u.t[:, 0:2], in_=uhalo.t[:, uc, :]), reads=[uhalo.b], writes=[u.b])
                                P.op("act", lambda e: e.copy(out=u.t[:, 2:T + 2], in_=ps.t[:]), reads=[ps.b], writes=[u.b])
                                P.op("dve", lambda e: e.tensor_copy(out=uhalo.t[:, uc, :], in_=u.t[:, T:T + 2]), reads=[u.b], writes=[uhalo.b])
                                P.op("act", lambda e: e.activation(out=c1.t[:], in_=u.t[:, 2:T + 2], func=AF.Identity, bias=CB[:, uc:uc + 1], scale=CW[:, uc, 2:3]),
                                     reads=[u.b] + CR, writes=[c1.b])

                                def cvf(e):
                                    e.scalar_tensor_tensor(out=c1.t[:], in0=u.t[:, 1:T + 1], scalar=CW[:, uc, 1:2], in1=c1.t[:], op0=ALU.mult, op1=ALU.add)
                                    return e.scalar_tensor_tensor(out=c1.t[:], in0=u.t[:, 0:T], scalar=CW[:, uc, 0:1], in1=c1.t[:], op0=ALU.mult, op1=ALU.add)
                                P.op("dve", cvf, reads=[u.b, c1.b] + CR, writes=[c1.b])
                                cvs.append(c1)
                            gu(gi)
                        cnt["ue"] += 1
                        sgt = sg[cnt["sg"] % 2]
                        cnt["sg"] += 1
                        P.op("act", lambda e: e.activation(out=sgt.t[:], in_=cvs[0].t[:], func=AF.Silu), reads=[cvs[0].b], writes=[sgt.b])
                        P.op("dve", lambda e: e.tensor_tensor(out=ag.t[:, fi, :], in0=sgt.t[:], in1=cvs[1].t[:], op=ALU.mult), reads=[sgt.b, cvs[1].b], writes=[ag.b])
                    ff(f0 + fi, fi)
                wds = []
                for fi in range(2):
                    def ldd(fi):
                        s_ = wd[fi]
                        P.dma("pool", lambda e: e.dma_start(out=s_.t[:], in_=wdn[f0 + fi]), "wd%d" % fi, writes=[s_.b])
                        wds.append(s_)
                    ldd(fi)
                for c in range(32):
                    def dc(c):
                        ps = pmain()

                        def f(e):
                            e.matmul(ps.t[:], wds[0].t[:, c * 128:(c + 1) * 128], ag.t[:, 0, :], start=True, stop=False)
                            return e.matmul(ps.t[:], wds[1].t[:, c * 128:(c + 1) * 128], ag.t[:, 1, :], start=False, stop=True)
                        P.op("pe", f, reads=[wds[0].b, wds[1].b, ag.b], writes=[ps.b])
                        P.op("dve", lambda e: e.tensor_tensor(out=h.t[:, c, :], in0=ps.t[:], in1=h.t[:, c, :], op=ALU.add), reads=[ps.b, hparts[c]], writes=[hparts[c]])
                    dc(c)
            for f0 in range(0, NFF, 2):
                do_pairf(f0)
            if BD_ == 2:
                dump()
                return
            for c in range(32):
                P.op("act", _bind(lambda e, c: e.copy(out=a16.t[:, c, :], in_=h.t[:, c, :]), c), reads=[hparts[c]], writes=[a16.b])
            P.dma("pool", lambda e: e.dma_start(out=wd[0].t[:], in_=wple[:, 0:4096]), "wd0", writes=[wd[0].b])
            P.dma("pool", lambda e: e.dma_start(out=wd[1].t[:], in_=wple[:, 4096:8192]), "wd1", writes=[wd[1].b])
            ssp = psmall()
            for c in range(32):
                def e1(c):
                    ps = pmain()

                    def f(e):
                        e.matmul(ps.t[:], wd[0].t[:, c * 128:(c + 1) * 128], p16.t[:, 0, :], start=True, stop=False)
                        return e.matmul(ps.t[:], wd[1].t[:, c * 128:(c + 1) * 128], p16.t[:, 1, :], start=False, stop=True)
                    P.op("pe", f, reads=[wd[0].b, wd[1].b, p16.b], writes=[ps.b])
                    sq = sqb[cnt["sq"] % 2]
                    cnt["sq"] += 1
                    P.op("act", lambda e: e.activation(out=sq.t[:], in_=ps.t[:], func=AF.Square), reads=[ps.b], writes=[sq.b])
                    P.op("pe", lambda e: e.matmul(ssp.t[:], ones_b.t[:], sq.t[:], start=(c == 0), stop=(c == 31)), reads=[sq.b] + CR, writes=[ssp.b])
                e1(c)
            P.op("act", lambda e: e.activation(out=rstde.t[:], in_=ssp.t[:], func=AF.Sqrt, bias=NORM_EPS, scale=1.0 / D_MODEL), reads=[ssp.b], writes=[rstde.b])
            P.op("dve", lambda e: e.reciprocal(out=rstde.t[:], in_=rstde.t[:]), reads=[rstde.b], writes=[rstde.b])
            for c in range(32):
                def e2(c):
                    s_ = load_w(wgt[c])
                    pg = pmain()
                    P.op("pe", mm32(pg.t[:], s_, lambda kc: a16.t[:, kc, :]), reads=[s_.b, a16.b], writes=[pg.b])
                    pe_ = pmain()

                    def f(e):
                        e.matmul(pe_.t[:], wd[0].t[:, c * 128:(c + 1) * 128], p16.t[:, 0, :], start=True, stop=False)
                        return e.matmul(pe_.t[:], wd[1].t[:, c * 128:(c + 1) * 128], p16.t[:, 1, :], start=False, stop=True)
                    P.op("pe", f, reads=[wd[0].b, wd[1].b, p16.b], writes=[pe_.b])
                    sgt = sg[cnt["sg"] % 2]
                    tet = te[cnt["sg"] % 2]
                    cnt["sg"] += 1
                    P.op("act", lambda e: e.activation(out=sgt.t[:], in_=pg.t[:], func=AF.Sigmoid), reads=[pg.b], writes=[sgt.b])

                    def comb(e):
                        e.scalar_tensor_tensor(out=tet.t[:], in0=pe_.t[:], scalar=GP[:, c:c + 1], in1=rstde.t[:], op0=ALU.mult, op1=ALU.mult)
                        e.tensor_tensor(out=tet.t[:], in0=tet.t[:], in1=sgt.t[:], op=ALU.mult)
                        return e.tensor_tensor(out=h.t[:, c, :], in0=h.t[:, c, :], in1=tet.t[:], op=ALU.add)
                    P.op("dve", comb, reads=[pe_.b, rstde.b, sgt.b, hparts[c]] + CR, writes=[tet.b, hparts[c]])
                    P.dma("sp", lambda e: e.dma_start(out=outT[:, c, t0:t0 + T], in_=h.t[:, c, :]), "oc%d" % (c % 8), reads=[hparts[c]])
                e2(c)

        for tt in range(NTB):
            do_tile(tt)
        P.run_block()
    if own:
        B.finish()
    return nc


def _wchunks_all(w, nchunk):
    return np.ascontiguousarray(w.reshape(32, 128, nchunk, 128).transpose(2, 1, 0, 3)).reshape(nchunk, 128, 32 * 128)


def prep_B_weights(inp):
    perm = np.concatenate([np.concatenate([np.arange(512 * g, 512 * g + 512), np.arange(2048 + 512 * g, 2048 + 512 * g + 512)]) for g in range(4)])
    wout = _wchunks_all(inp["w_out"][0][perm, :], 32)
    wup = _wchunks_all(inp["w_mlp_up"][0], 2 * NFF)
    wdn = np.ascontiguousarray(inp["w_mlp_down"][0].reshape(NFF, 128, 4096))
    wple = np.ascontiguousarray(inp["w_ple_proj"][0].reshape(2, 128, 4096).transpose(1, 0, 2)).reshape(128, 8192)
    wgt = _wchunks_all(inp["w_ple_gate"][0], 32)
    NU = 2 * NFF
    cst = np.zeros((128, 64 + NU * 4 + 128), np.float32)
    cst[:, 0:32] = inp["mlp_norm_g"][0].reshape(32, 128).T
    cst[:, 32:64] = inp["ple_norm_g"][0].reshape(32, 128).T
    cw = inp["conv_w"][0].reshape(3, NU, 128).transpose(2, 1, 0)
    cst[:, 64:64 + NU * 3] = cw.reshape(128, NU * 3)
    cst[:, 64 + NU * 3:64 + NU * 4] = inp["conv_b"][0].reshape(NU, 128).T
    cst[:, 64 + NU * 4:] = 1.0
    return dict(wout=wout, wup=wup, wdn=wdn, wple=wple, wgt=wgt, cstb=cst)


def prep_B_acts(inp, S, core):
    NTOK = S // 4
    b, q = core // 4, core % 4
    lo = q * NTOK
    xe = np.zeros((NTOK + 2, 4096), np.float32)
    xe[2:] = inp["x"][b, lo:lo + NTOK]
    if q > 0:
        xe[0:2] = inp["x"][b, lo - 2:lo]
    xTb = np.ascontiguousarray(xe.T.reshape(32, 128, NTOK + 2).transpose(1, 0, 2))
    pTb = np.ascontiguousarray(inp["p"][0, b, lo:lo + NTOK].T.reshape(2, 128, NTOK).transpose(1, 0, 2))
    return dict(xTb=xTb, pTb=pTb)


def mix_for_core(resA, S, core):
    NTOK = S // 4
    b, q = core // 4, core % 4
    lo = q * NTOK
    m = np.zeros((4, 1024, NTOK + 2), np.float32)
    for g in range(4):
        src = resA[4 * b + g]["mixA"]
        m[g, :, 2:] = src[:, lo:lo + NTOK]
        if q > 0:
            m[g, :, 0:2] = src[:, lo - 2:lo]
    return m


_CACHE = {}


def kernel_unfused(**inp):
    inp = {k: np.asarray(v) for k, v in inp.items()}
    S = inp["x"].shape[1]
    NTOK = S // 4
    if ("A", S) not in _CACHE:
        _CACHE[("A", S)] = build_A(S)
        _CACHE[("B", S)] = build_B(NTOK)
    ncA, ncB = _CACHE[("A", S)], _CACHE[("B", S)]
    mapsA = prep_A(inp, S)
    resA = run_bass_kernel_spmd(ncA, mapsA, core_ids=list(range(8))).results
    del mapsA
    wB = prep_B_weights(inp)
    mapsB = []
    for core in range(8):
        d = dict(wB)
        d.update(prep_B_acts(inp, S, core))
        d["mixin"] = mix_for_core(resA, S, core)
        mapsB.append(d)
    resB = run_bass_kernel_spmd(ncB, mapsB, core_ids=list(range(8))).results
    out = np.zeros((2, S, 4096), np.float32)
    for core in range(8):
        b, q = core // 4, core % 4
        o = resB[core]["outT"]
        out[b, q * NTOK:(q + 1) * NTOK] = o.transpose(2, 1, 0).reshape(NTOK, 4096)
    return out


def build_fused(S):
    B = Builder()
    mixx = build_A(S, B=B, fused=True)
    mixg = B.dram("mixg", [S // 512, 4096, 512], BF16, "Internal")
    build_B(S // 4, mix_dram=mixg, B=B, fused=True, mixx=mixx)
    B.finish()
    return B.nc


def kernel(**inp):
    inp = {k: np.asarray(v) for k, v in inp.items()}
    S = inp["x"].shape[1]
    NTOK = S // 4
    if ("F", S) not in _CACHE:
        _CACHE[("F", S)] = build_fused(S)
    nc = _CACHE[("F", S)]
    maps = prep_A(inp, S)
    wB = prep_B_weights(inp)
    for core in range(8):
        maps[core].update(wB)
        maps[core].update(prep_B_acts(inp, S, core))
        qm = np.zeros((128, 4), np.float32)
        qm[:, core % 4] = 1.0
        maps[core]["qmask"] = qm
    res = run_bass_kernel_spmd(nc, maps, core_ids=list(range(8))).results
    out = np.zeros((2, S, 4096), np.float32)
    for core in range(8):
        b, q = core // 4, core % 4
        o = res[core]["outT"]
        out[b, q * NTOK:(q + 1) * NTOK] = o.transpose(2, 1, 0).reshape(NTOK, 4096)
    return out
```

```python
import contextlib
import math
import os
import numpy as np
import concourse.bass as bass
import concourse.mybir as mybir
from concourse.bass_utils import run_bass_kernel_spmd

F32 = mybir.dt.float32
BF16 = mybir.dt.bfloat16
I32 = mybir.dt.int32
AF = mybir.ActivationFunctionType
ALU = mybir.AluOpType

D_MODEL = 4096
HEAD_DIM = 64
ATTN_W = 2048
RWKV_W = 2048
D_FF = 11008
PLE_DIM = 256
NORM_EPS = 1e-6
GN_EPS = 64e-5
C0 = math.exp(-0.5)

ENGS = ("pe", "act", "dve", "pool", "sp")
EPOCH = 8000


class Buf:
    __slots__ = ("name", "w", "r")

    def __init__(self, name=""):
        self.name = name
        self.w = None
        self.r = []


class Op:
    __slots__ = ("eng", "fn", "deps", "dma", "sig", "idx", "dsem", "dcnt", "blk", "inc")

    def __init__(self, eng, fn, dma=False):
        self.eng = eng
        self.fn = fn
        self.deps = []
        self.dma = dma
        self.sig = False
        self.idx = None
        self.dsem = None
        self.dcnt = None
        self.blk = 0


class Prog:
    def __init__(self, nc):
        self.nc = nc
        self.ops = {e: [] for e in ENGS}
        self.sigcount = {e: 0 for e in ENGS}
        self.esems = {e: [] for e in ENGS}
        self.dma_sems = {}
        self.waited = {e: {} for e in ENGS}
        self._ctx = []
        self.blk = 0
        self.keybufs = {}

    def _sem(self, name):
        cm = self.nc.semaphore(name)
        s = cm.__enter__()
        self._ctx.append(cm)
        return s

    def dsem(self, key):
        if key not in self.dma_sems:
            self.dma_sems[key] = [self._sem("d_" + str(key)), 0]
        return self.dma_sems[key]

    def close(self):
        for cm in reversed(self._ctx):
            cm.__exit__(None, None, None)

    def _deps(self, op, reads, writes):
        deps = set()
        for b in reads:
            if b.w is not None:
                deps.add(b.w)
        for b in writes:
            if b.w is not None:
                deps.add(b.w)
            for r in b.r:
                deps.add(r)
        deps.discard(op)
        for d in deps:
            if op.dma or d.dma or d.eng != op.eng:
                d.sig = True
        op.deps = list(deps)
        for b in reads:
            b.r.append(op)
        for b in writes:
            b.w = op
            b.r = []

    def op(self, eng, fn, reads=(), writes=()):
        o = Op(eng, fn)
        o.blk = self.blk
        self._deps(o, reads, writes)
        self.ops[eng].append(o)
        return o

    def dma(self, eng, fn, semkey, reads=(), writes=(), inc=16):
        o = Op(eng, fn, dma=True)
        o.blk = self.blk
        o.inc = inc
        ds = self.dsem(semkey)
        ds[1] += inc
        o.dsem, o.dcnt = ds[0], ds[1]
        kb = self.keybufs.setdefault(semkey, Buf(semkey))
        self._deps(o, reads, list(writes) + [kb])
        self.ops[eng].append(o)
        return o

    def _assign(self):
        for e in ENGS:
            for o in self.ops[e]:
                if o.sig and not o.dma and o.idx is None:
                    self.sigcount[e] += 1
                    o.idx = self.sigcount[e]
            need = (self.sigcount[e] + EPOCH - 1) // EPOCH
            while len(self.esems[e]) < need:
                self.esems[e].append(self._sem("e_%s_%d" % (e, len(self.esems[e]))))

    def _wait_for(self, eng_name, eng, d):
        w = self.waited[eng_name]
        if d.blk < self.blk:
            return
        if d.dma:
            key = ("d", id(d.dsem))
            if w.get(key, 0) >= d.dcnt:
                return
            w[key] = d.dcnt
            eng.wait_ge(d.dsem, d.dcnt)
        else:
            ep, v = divmod(d.idx - 1, EPOCH)
            key = (d.eng, ep)
            if w.get(key, 0) >= v + 1:
                return
            w[key] = v + 1
            for e2 in range(ep):
                w[(d.eng, e2)] = EPOCH
            eng.wait_ge(self.esems[d.eng][ep], v + 1)

    def emit_engine(self, eng_name, eng):
        for o in self.ops[eng_name]:
            for d in sorted(o.deps, key=lambda x: (x.dma, x.idx or 0, x.dcnt or 0)):
                if (not o.dma) and (not d.dma) and d.eng == eng_name:
                    continue
                self._wait_for(eng_name, eng, d)
            ins = o.fn(eng)
            if o.dma:
                ins.then_inc(o.dsem, o.inc)
            elif o.sig:
                ep, v = divmod(o.idx - 1, EPOCH)
                ins.then_inc(self.esems[eng_name][ep], 1)

    def finish_waits(self, eng_name, eng):
        for e in ENGS:
            if self.sigcount[e] > 0:
                ep, v = divmod(self.sigcount[e] - 1, EPOCH)
                key = (e, ep)
                if self.waited[eng_name].get(key, 0) < v + 1:
                    self.waited[eng_name][key] = v + 1
                    eng.wait_ge(self.esems[e][ep], v + 1)
        for key, (sem, cnt) in self.dma_sems.items():
            k = ("d", id(sem))
            if cnt > 0 and self.waited[eng_name].get(k, 0) < cnt:
                self.waited[eng_name][k] = cnt
                eng.wait_ge(sem, cnt)

    def run_block(self):
        nc = self.nc
        for e in ENGS:
            lst = [o for o in self.ops[e] if not o.dma]
            if lst:
                lst[-1].sig = True
        self._assign()
        with nc.Block() as block:
            @block.tensor
            def _(t):
                self.emit_engine("pe", t)
                self.finish_waits("pe", t)

            @block.scalar
            def _(a):
                self.emit_engine("act", a)
                self.finish_waits("act", a)

            @block.vector
            def _(v):
                self.emit_engine("dve", v)
                self.finish_waits("dve", v)

            @block.gpsimd
            def _(g):
                self.emit_engine("pool", g)
                self.finish_waits("pool", g)

            @block.sync
            def _(s):
                self.emit_engine("sp", s)
                self.finish_waits("sp", s)
        for e in ENGS:
            self.ops[e] = []
        self.blk += 1


class TB:
    __slots__ = ("t", "b")

    def __init__(self, t, name=""):
        self.t = t
        self.b = Buf(name)


class Builder:
    def __init__(self):
        self.nc = bass.Bass("TRN2", target_bir_lowering=False)
        self.P = Prog(self.nc)
        self.es = contextlib.ExitStack()
        self.n = 0
        self.psb = []
        for i in range(8):
            t = self.es.enter_context(self.nc.psum_tensor("psb%d" % i, [128, 512], F32))
            self.psb.append(TB(t, "psb%d" % i))

    def dram(self, name, shape, dt, kind):
        return self.nc.dram_tensor(name, list(shape), dt, kind=kind).ap()

    def sb(self, es, name, shape, dt=F32):
        self.n += 1
        t = es.enter_context(self.nc.sbuf_tensor("%s_%d" % (name, self.n), list(shape), dt))
        return TB(t, name)

    def finish(self):
        self.P.close()
        self.es.close()


def _bind(f, *a):
    return lambda e: f(e, *a)


NW_A = 3
TS = 64
CH = 64


def build_A(S, phases=("A1", "A2", "A3"), debug=False, B=None, fused=False, conv=()):
    own = B is None
    if own:
        B = Builder()
    nc, P = B.nc, B.P
    T = 512
    NT = S // T
    IN, OUT, INT = "ExternalInput", "ExternalOutput", "Internal"
    xT = B.dram("xT", [128, 32, S], F32, IN)
    pos = B.dram("pos", [1, S], I32, IN)
    wA = B.dram("wA", [30, 128, 32 * 128], F32, IN)
    gA = B.dram("gA", [128, 32], F32, IN)
    c128 = B.dram("c128", [128, 8 * 128 + 256 + 4], F32, IN)
    hp_d = B.dram("hp", [64, 10 * 8], F32, IN)
    lp_d = B.dram("lp", [128, 6], F32, IN)
    wdu_d = B.dram("wdu", [128, 512], F32, IN)
    wau_d = B.dram("wau", [128, 512], F32, IN)
    wgu_d = B.dram("wgu", [128, 4 * 512], F32, IN)
    c64_d = B.dram("c64", [64, 64 + 8 * TS], F32, IN)
    if fused:
        mixA = B.dram("mixx", [S // 512, 1024, 512], BF16, INT)
        MIX = dict(ap=mixA, off=0, dt=BF16, dst=lambda r0, nr, t0, n: mixA[t0 // 512, r0:r0 + nr, t0 % 512:t0 % 512 + n])
    else:
        mixA = B.dram("mixA", [1024, S], F32, OUT)
        MIX = dict(ap=mixA, off=0, dt=F32, dst=lambda r0, nr, t0, n: mixA[r0:r0 + nr, t0:t0 + n])
    SK = OUT if debug else INT
    qk_s = B.dram("qk_s", [8, 128, S], BF16, SK)
    vv_s = B.dram("vv_s", [4, 128, S], BF16, SK)
    pr_s = B.dram("pr_s", [18, 128, S], F32, SK)

    with contextlib.ExitStack() as es0:
        cf = B.sb(es0, "cf", [128, 8 * 128 + 256 + 4], F32)
        cb = B.sb(es0, "cb", [128, 128 + 128 + 256], BF16)
        hp = B.sb(es0, "hp", [64, 10, 8], F32)
        lp = B.sb(es0, "lp", [128, 6], F32)
        c64 = B.sb(es0, "c64", [64, 64 + 8 * TS], F32)
        P.dma("sp", lambda e: e.dma_start(out=cf.t[:], in_=c128), "cf", writes=[cf.b])
        P.dma("sp", lambda e: e.dma_start(out=hp.t[:].rearrange("p a h -> p (a h)"), in_=hp_d), "hp", writes=[hp.b])
        P.dma("sp", lambda e: e.dma_start(out=lp.t[:], in_=lp_d), "lp", writes=[lp.b])
        P.dma("sp", lambda e: e.dma_start(out=c64.t[:], in_=c64_d), "c64", writes=[c64.b])
        P.dma("pool", lambda e: e.dma_start(out=cb.t[:, 0:128], in_=c128[:, 0:128]), "cb0", writes=[cb.b])
        P.dma("pool", lambda e: e.dma_start(out=cb.t[:, 128:256], in_=c128[:, 384:512]), "cb1", writes=[cb.b])
        P.dma("pool", lambda e: e.dma_start(out=cb.t[:, 256:512], in_=c128[:, 1024:1280]), "cb2", writes=[cb.b])
        ONESF = cf.t[:, 0:128]
        BD = cf.t[:, 128:256]
        ROTT = cf.t[:, 256:384]
        IDENTF = cf.t[:, 384:512]
        MASKG = cf.t[:, 512:640]
        GQ = cf.t[:, 1280:1281]
        GK = cf.t[:, 1281:1282]
        INVF = cf.t[:, 1282:1283]
        ONESB = cb.t[:, 0:128]
        IDENTB = cb.t[:, 128:256]
        MASKB = cb.t[:, 256:512]
        MASKN = c64.t[:, 0:64]
        RESETM = c64.t[:, 64:64 + 8 * TS]
        CR = [cf.b, cb.b, hp.b, lp.b, c64.b]

        if "A1" in phases:
          phase_A1(B, S, T, NT, xT, pos, wA, gA, qk_s, vv_s, pr_s,
                 dict(ONESB=ONESB, BD=BD, ROTT=ROTT, GQ=GQ, GK=GK, INVF=INVF, CR=CR))
        if "A2" in phases:
          phase_A2(B, S, qk_s, vv_s, MIX,
                 dict(ONESF=ONESF, IDENTB=IDENTB, MASKB=MASKB, CR=CR))
        if "A3" in phases:
          phase_A3(B, S, pr_s, MIX, wdu_d, wau_d, wgu_d,
                 dict(ONESF=ONESF, IDENTF=IDENTF, MASKG=MASKG, MASKN=MASKN, RESETM=RESETM,
                      hp=hp, lp=lp, CR=CR), conv=conv)
    if own:
        B.finish()
        return nc
    return mixA


def phase_A1(B, S, T, NT, xT, pos, wA, gA, qk_s, vv_s, pr_s, K):
    nc, P = B.nc, B.P
    CR = K["CR"]
    with contextlib.ExitStack() as es:
        x_sb = B.sb(es, "x", [128, 32, T], F32)
        xn = [B.sb(es, "xn%d" % i, [128, 32, T], BF16) for i in range(2)]
        w_sb = [B.sb(es, "w%d" % i, [128, 32, 128], BF16) for i in range(NW_A)]
        sqb = [B.sb(es, "sqb%d" % i, [128, T], BF16) for i in range(2)]
        rstd = B.sb(es, "rstd", [128, T], F32)
        g_sb = B.sb(es, "g", [128, 32], F32)
        posi = B.sb(es, "posi", [128, T], I32)
        cs = [dict(sin=B.sb(es, "sin%d" % i, [128, T], F32), cos=B.sb(es, "cos%d" % i, [128, T], F32)) for i in range(2)]
        tri = B.sb(es, "tri", [128, T], I32)
        sqf = [B.sb(es, "sqf%d" % i, [128, T], F32) for i in range(2)]
        rs2 = [B.sb(es, "rs2%d" % i, [128, T], F32) for i in range(2)]
        qn = [B.sb(es, "qn%d" % i, [128, T], F32) for i in range(2)]
        t1 = [B.sb(es, "t1%d" % i, [128, T], F32) for i in range(2)]
        tr = [t1[0], t1[1], rs2[0], rs2[1]]
        stb = [B.sb(es, "stb%d" % i, [128, T], BF16) for i in range(3)]
        stf = [B.sb(es, "stf%d" % i, [128, T], F32) for i in range(2)]
        P.dma("sp", lambda e: e.dma_start(out=g_sb.t[:], in_=gA), "gA", writes=[g_sb.b])
        xparts = [Buf() for _ in range(4)]

        psm = [B.psb[i] for i in range(4)]
        pss = [B.psb[i] for i in range(4, 8)]
        cnt = dict(w=0, m=0, s=0, sq=0, stb=0, stf=0, q=0)

        def prep(tt):
            t0 = tt * T
            xnb = xn[tt % 2]
            csb = cs[tt % 2]
            for q4 in range(4):
                P.dma("sp", _bind(lambda e, q4: e.dma_start(out=x_sb.t[:, 8 * q4:8 * q4 + 8, :], in_=xT[:, 8 * q4:8 * q4 + 8, t0:t0 + T]), q4),
                      "x%d" % q4, writes=[xparts[q4]])
            P.dma("sp", lambda e: e.dma_start(out=posi.t[:], in_=pos[:, t0:t0 + T].partition_broadcast(128)), "posi", writes=[posi.b])
            ssp = pss[cnt["s"] % 4]
            cnt["s"] += 1
            for kc in range(32):
                sq = sqb[cnt["sq"] % 2]
                cnt["sq"] += 1
                P.op("act", _bind(lambda e, sq, kc: e.activation(out=sq.t[:], in_=x_sb.t[:, kc, :], func=AF.Square), sq, kc),
                     reads=[xparts[kc // 8]], writes=[sq.b])
                P.op("pe", _bind(lambda e, sq, kc: e.matmul(ssp.t[:], K["ONESB"], sq.t[:], start=(kc == 0), stop=(kc == 31)), sq, kc),
                     reads=[sq.b] + CR, writes=[ssp.b])
            P.op("act", lambda e: e.activation(out=rstd.t[:], in_=ssp.t[:], func=AF.Sqrt, bias=NORM_EPS, scale=1.0 / D_MODEL),
                 reads=[ssp.b], writes=[rstd.b])
            P.op("dve", lambda e: e.reciprocal(out=rstd.t[:], in_=rstd.t[:]), reads=[rstd.b], writes=[rstd.b])
            for kc in range(32):
                P.op("dve", _bind(lambda e, kc: e.scalar_tensor_tensor(out=xnb.t[:, kc, :], in0=x_sb.t[:, kc, :], scalar=g_sb.t[:, kc:kc + 1],
                                                                       in1=rstd.t[:], op0=ALU.mult, op1=ALU.mult), kc),
                     reads=[xparts[kc // 8], rstd.b, g_sb.b], writes=[xnb.b])
            a, k_, r_, rc = tr
            TWO_PI = 2.0 * math.pi
            C1 = 6.28125
            C2 = 0.0019340515136718750
            C3 = TWO_PI - C1 - C2

            def trig(e):
                e.tensor_copy(out=a.t[:], in_=posi.t[:])
                e.tensor_scalar(out=a.t[:], in0=a.t[:], scalar1=K["INVF"], scalar2=None, op0=ALU.mult)
                e.tensor_scalar(out=k_.t[:], in0=a.t[:], scalar1=1.0 / TWO_PI, scalar2=None, op0=ALU.mult)
                e.tensor_copy(out=tri.t[:], in_=k_.t[:])
                e.tensor_copy(out=k_.t[:], in_=tri.t[:])
                e.scalar_tensor_tensor(out=r_.t[:], in0=k_.t[:], scalar=-C1, in1=a.t[:], op0=ALU.mult, op1=ALU.add)
                e.scalar_tensor_tensor(out=r_.t[:], in0=k_.t[:], scalar=-C2, in1=r_.t[:], op0=ALU.mult, op1=ALU.add)
                e.scalar_tensor_tensor(out=r_.t[:], in0=k_.t[:], scalar=-C3, in1=r_.t[:], op0=ALU.mult, op1=ALU.add)
                e.tensor_scalar(out=r_.t[:], in0=r_.t[:], scalar1=-math.pi, scalar2=math.pi, op0=ALU.max, op1=ALU.min)
                e.tensor_scalar(out=rc.t[:], in0=r_.t[:], scalar1=math.pi / 2, scalar2=None, op0=ALU.add)
                e.tensor_scalar(out=k_.t[:], in0=rc.t[:], scalar1=math.pi, scalar2=-TWO_PI, op0=ALU.is_gt, op1=ALU.mult)
                e.tensor_tensor(out=rc.t[:], in0=rc.t[:], in1=k_.t[:], op=ALU.add)
                return e.tensor_scalar(out=rc.t[:], in0=rc.t[:], scalar1=-math.pi, scalar2=math.pi, op0=ALU.max, op1=ALU.min)
            P.op("dve", trig, reads=[posi.b] + CR, writes=[tb.b for tb in tr] + [])
            P.op("act", lambda e: e.activation(out=csb["sin"].t[:], in_=r_.t[:], func=AF.Sin), reads=[r_.b], writes=[csb["sin"].b])
            P.op("act", lambda e: e.activation(out=csb["cos"].t[:], in_=rc.t[:], func=AF.Sin), reads=[rc.b], writes=[csb["cos"].b])

        pending = []

        def flush(now):
            keep = []
            for due, fn in pending:
                if due <= now:
                    fn()
                else:
                    keep.append((due, fn))
            pending[:] = keep

        def chunk(tt, j, seq):
            t0 = tt * T
            xnb = xn[tt % 2]
            csb = cs[tt % 2]
            slot = w_sb[cnt["w"] % NW_A]
            cnt["w"] += 1
            P.dma("pool", _bind(lambda e, slot, j: e.dma_start(out=slot.t[:].rearrange("p k c -> p (k c)"), in_=wA[j]), slot, j),
                  "w%d" % (cnt["w"] % NW_A), writes=[slot.b])
            ps = psm[cnt["m"] % 4]
            cnt["m"] += 1

            def mm(e, slot=slot, ps=ps):
                for kc in range(32):
                    r = e.matmul(ps.t[:], slot.t[:, kc, :], xnb.t[:, kc, :], start=(kc == 0), stop=(kc == 31))
                return r
            P.op("pe", mm, reads=[slot.b, xnb.b], writes=[ps.b])
            if j < 8:
                gv = K["GQ"] if j < 4 else K["GK"]
                i2 = cnt["q"] % 2
                cnt["q"] += 1
                sq, r2, qq, tt1 = sqf[i2], rs2[i2], qn[i2], t1[i2]
                P.op("act", lambda e: e.activation(out=sq.t[:], in_=ps.t[:], func=AF.Square), reads=[ps.b], writes=[sq.b])

                def st2():
                    hs = pss[cnt["s"] % 4]
                    cnt["s"] += 1
                    P.op("pe", lambda e: e.matmul(hs.t[:], K["BD"], sq.t[:], start=True, stop=True), reads=[sq.b] + CR, writes=[hs.b])
                    P.op("act", lambda e: e.activation(out=r2.t[:], in_=hs.t[:], func=AF.Sqrt, bias=NORM_EPS, scale=1.0 / HEAD_DIM),
                         reads=[hs.b], writes=[r2.b])
                    P.op("dve", lambda e: e.reciprocal(out=r2.t[:], in_=r2.t[:]), reads=[r2.b], writes=[r2.b])
                    P.op("dve", lambda e: e.scalar_tensor_tensor(out=qq.t[:], in0=ps.t[:], scalar=gv, in1=r2.t[:], op0=ALU.mult, op1=ALU.mult),
                         reads=[ps.b, r2.b] + CR, writes=[qq.b])

                    def st3():
                        rp = pss[cnt["s"] % 4]
                        cnt["s"] += 1
                        sb_ = stb[cnt["stb"] % 3]
                        cnt["stb"] += 1
                        P.op("pe", lambda e: e.matmul(rp.t[:], K["ROTT"], qq.t[:], start=True, stop=True), reads=[qq.b] + CR, writes=[rp.b])
                        P.op("dve", lambda e: e.tensor_tensor(out=tt1.t[:], in0=qq.t[:], in1=csb["cos"].t[:], op=ALU.mult),
                             reads=[qq.b, csb["cos"].b], writes=[tt1.b])
                        P.op("dve", lambda e: e.tensor_tensor(out=qq.t[:], in0=rp.t[:], in1=csb["sin"].t[:], op=ALU.mult),
                             reads=[rp.b, csb["sin"].b], writes=[qq.b])
                        P.op("dve", lambda e: e.tensor_tensor(out=sb_.t[:], in0=tt1.t[:], in1=qq.t[:], op=ALU.add),
                             reads=[tt1.b, qq.b], writes=[sb_.b])
                        P.dma("sp", lambda e: e.dma_start(out=qk_s[j, :, t0:t0 + T], in_=sb_.t[:]), "stb%d" % (cnt["stb"] % 3), reads=[sb_.b])
                    pending.append((seq + 2, st3))
                pending.append((seq + 1, st2))
            elif j < 12:
                sb_ = stb[cnt["stb"] % 3]
                cnt["stb"] += 1
                P.op("act", lambda e: e.copy(out=sb_.t[:], in_=ps.t[:]), reads=[ps.b], writes=[sb_.b])
                P.dma("sp", lambda e: e.dma_start(out=vv_s[j - 8, :, t0:t0 + T], in_=sb_.t[:]), "stb%d" % (cnt["stb"] % 3), reads=[sb_.b])
            else:
                sf = stf[cnt["stf"] % 2]
                cnt["stf"] += 1
                P.op("act", lambda e: e.copy(out=sf.t[:], in_=ps.t[:]), reads=[ps.b], writes=[sf.b])
                P.dma("sp", lambda e: e.dma_start(out=pr_s[j - 12, :, t0:t0 + T], in_=sf.t[:]), "stf%d" % (cnt["stf"] % 2), reads=[sf.b])

        prep(0)
        seq = 0
        for tt in range(NT):
            for j in range(30):
                chunk(tt, j, seq)
                seq += 1
                flush(seq)
                if j == 14 and tt + 1 < NT:
                    prep(tt + 1)
        flush(seq + 10)
        P.run_block()


def phase_A2(B, S, qk_s, vv_s, MIX, K):
    nc, P = B.nc, B.P
    CR = K["CR"]
    with contextlib.ExitStack() as es:
        qn_ = B.sb(es, "qn", [128, S], BF16)
        kn_ = B.sb(es, "kn", [128, S], BF16)
        vn_ = B.sb(es, "vn", [128, S], BF16)
        qd_ = B.sb(es, "qd", [128, S], BF16)
        kd_ = B.sb(es, "kd", [128, S], BF16)
        vd_ = B.sb(es, "vd", [128, S], BF16)
        acc = [B.sb(es, "acc%d" % h, [65, S], F32) for h in range(2)]
        vp = [B.sb(es, "vp%d" % i, [128, 2, 65], BF16) for i in range(3)]
        pT = [B.sb(es, "pT%d" % i, [128, 256], BF16) for i in range(4)]
        rec = [B.sb(es, "rec%d" % i, [64, 512], F32) for i in range(2)]
        ost = [B.sb(es, "ost%d" % i, [64, 512], MIX["dt"]) for i in range(2)]
        MDST = MIX["dst"]
        for v_ in vp:
            P.op("pool", _bind(lambda e, v_: e.memset(v_.t[:], 1.0), v_), writes=[v_.b])
        sc_ps = [B.psb[0], B.psb[1]]
        o_ps = [[B.psb[2 + h * 2 + i] for i in range(2)] for h in range(2)]
        vt_ps = [B.psb[6], B.psb[7]]
        fin_ps = [B.psb[0], B.psb[1]]
        c = dict(vp=0, pT=0, sc=0, vt=0, fin=0, kb=0)

        def o_ap(h, i):
            return o_ps[h][i].t[0:65, 0:128]

        def vt_ap(i):
            return vt_ps[i].t[:, 0:64].bitcast(BF16)

        def do_kb(hp_i, bi, d, r, kb, nb, M, q3, k3, v3):
            nq = 256 if kb < nb - 1 else 128
            k0 = r * M + kb * 128
            vti = c["vt"] % 2
            c["vt"] += 1
            vtb = vt_ps[vti]
            vpt = vp[c["vp"] % 3]
            c["vp"] += 1
            P.op("pe", lambda e: e.transpose(out=vt_ap(vti), in_=v3.t[:, k0:k0 + 128], identity=K["IDENTB"]),
                 reads=[v3.b] + CR, writes=[vtb.b])
            P.op("dve", lambda e: e.tensor_copy(out=vpt.t[:, :, 0:64], in_=vt_ap(vti).rearrange("p (h c) -> p h c", h=2)),
                 reads=[vtb.b], writes=[vpt.b])
            for h in range(2):
                do_head(bi, d, r, kb, nq, k0, h, q3, k3, vpt)

        def do_head(bi, d, r, kb, nq, k0, h, q3, k3, vpt):
            hs = slice(h * 64, (h + 1) * 64)
            sc = sc_ps[c["sc"] % 2]
            c["sc"] += 1
            pt = pT[c["pT"] % 4]
            c["pT"] += 1

            def scf(e):
                e.matmul(sc.t[:, 0:nq], k3.t[hs, k0:k0 + 128], q3.t[hs, k0:k0 + nq], start=True, stop=False)
                return e.matmul(sc.t[:, 0:nq], K["IDENTB"], K["MASKB"][:, 0:nq], start=False, stop=True)
            P.op("pe", scf, reads=[q3.b, k3.b] + CR, writes=[sc.b])
            P.op("act", lambda e: e.activation(out=pt.t[:, 0:nq], in_=sc.t[:, 0:nq], func=AF.Exp, scale=0.125),
                 reads=[sc.b], writes=[pt.b])
            oa = o_ps[h][kb % 2]
            ob = o_ps[h][(kb + 1) % 2]

            def pv(e):
                r_ = e.matmul(o_ap(h, kb % 2), vpt.t[:, h, :], pt.t[:, 0:128], start=(kb == 0), stop=True, skip_group_check=True)
                if nq == 256:
                    r_ = e.matmul(o_ap(h, (kb + 1) % 2), vpt.t[:, h, :], pt.t[:, 128:256], start=True, stop=False, skip_group_check=True)
                return r_
            P.op("pe", pv, reads=[pt.b, vpt.b, oa.b], writes=[oa.b] + ([ob.b] if nq == 256 else []))
            tpos = (kb * 128) * d + r

            def ev(e):
                dst = acc[h].t[:, tpos:tpos + 127 * d + 1:d] if d > 1 else acc[h].t[:, tpos:tpos + 128]
                if bi == 0:
                    return e.tensor_copy(out=dst, in_=o_ap(h, kb % 2))
                return e.tensor_tensor(out=dst, in0=dst, in1=o_ap(h, kb % 2), op=ALU.add)
            P.op("dve", ev, reads=[oa.b, acc[h].b], writes=[acc[h].b, oa.b])

        def do_branch(hp_i, bi, d):
            M = S // d
            nb = M // 128
            if d == 1:
                q3, k3, v3 = qn_, kn_, vn_
            else:
                def cp(src, dst, eng):
                    P.op(eng, lambda e: e.tensor_copy(out=dst.t[:].rearrange("p (r m) -> p r m", r=d),
                                                      in_=src.t[:].rearrange("p (m r) -> p r m", r=d)),
                         reads=[src.b], writes=[dst.b])
                cp(qn_, qd_, "dve")
                cp(kn_, kd_, "pool")
                cp(vn_, vd_, "dve")
                q3, k3, v3 = qd_, kd_, vd_
            for r in range(d):
                for kb in range(nb):
                    do_kb(hp_i, bi, d, r, kb, nb, M, q3, k3, v3)

        def do_fin(hp_i, h, s0):
            fp = fin_ps[c["fin"] % 2]
            rc_ = rec[c["fin"] % 2]
            os_ = ost[c["fin"] % 2]
            key = "ost%d" % (c["fin"] % 2)
            c["fin"] += 1
            P.op("pe", lambda e: e.matmul(fp.t[0:64, :], K["ONESF"][64:65, 0:64], acc[h].t[64:65, s0:s0 + 512], start=True, stop=True),
                 reads=[acc[h].b] + CR, writes=[fp.b])
            P.op("dve", lambda e: e.reciprocal(out=rc_.t[:], in_=fp.t[0:64, :]), reads=[fp.b], writes=[rc_.b])
            P.op("pool", lambda e: e.tensor_tensor(out=os_.t[:], in0=acc[h].t[0:64, s0:s0 + 512], in1=rc_.t[:], op=ALU.mult),
                 reads=[rc_.b, acc[h].b], writes=[os_.b])
            row = (hp_i * 2 + h) * 64
            P.dma("sp", lambda e: e.dma_start(out=MDST(row, 64, s0, 512), in_=os_.t[:]), key, reads=[os_.b])

        def do_pair(hp_i):
            P.dma("sp", lambda e: e.dma_start(out=qn_.t[:], in_=qk_s[hp_i]), "a2q", writes=[qn_.b])
            P.dma("sp", lambda e: e.dma_start(out=kn_.t[:], in_=qk_s[4 + hp_i]), "a2k", writes=[kn_.b])
            P.dma("sp", lambda e: e.dma_start(out=vn_.t[:], in_=vv_s[hp_i]), "a2v", writes=[vn_.b])
            for bi, d in enumerate((1, 4, 16)):
                do_branch(hp_i, bi, d)
            for h in range(2):
                for s0 in range(0, S, 512):
                    do_fin(hp_i, h, s0)

        for hp_i in range(4):
            do_pair(hp_i)
        P.run_block()


def phase_A3(B, S, pr_s, MIX, wdu_d, wau_d, wgu_d, K, conv=()):
    nc, P = B.nc, B.P
    CR = K["CR"]
    hp, lp = K["hp"], K["lp"]
    NS = S // TS
    NC = TS // CH
    W = TS + 1
    with contextlib.ExitStack() as es:
        def H(name, n=TS, extra=()):
            return B.sb(es, name, [64, 8] + list(extra) + [n], F32)

        def S64(name, rows=64):
            return B.sb(es, name, [rows, 8, 64], F32)
        wdu = B.sb(es, "wdu", [128, 512], F32)
        wau = B.sb(es, "wau", [128, 512], F32)
        wgu = B.sb(es, "wgu", [128, 4, 512], F32)
        P.dma("sp", lambda e: e.dma_start(out=wdu.t[:], in_=wdu_d), "wdu", writes=[wdu.b])
        P.dma("sp", lambda e: e.dma_start(out=wau.t[:], in_=wau_d), "wau", writes=[wau.b])
        P.dma("sp", lambda e: e.dma_start(out=wgu.t[:].rearrange("p k c -> p (k c)"), in_=wgu_d), "wgu", writes=[wgu.b])
        RX = [H("RX%d" % i, W) for i in range(2)]
        KX = [H("KX%d" % i, W) for i in range(2)]
        VX = [H("VX%d" % i, W) for i in range(2)]
        WX = [B.sb(es, "WX%d" % i, [128, W], F32) for i in range(2)]
        AXl = [B.sb(es, "AX%d" % i, [128, W], F32) for i in range(2)]
        GX = [B.sb(es, "GX%d" % i, [128, 4, W], F32) for i in range(2)]
        parts = {}

        def part(tb, i):
            k = (id(tb), i)
            if k not in parts:
                parts[k] = Buf()
            return parts[k]
        D1 = H("D1")
        r_ = H("r")
        k_ = H("k")
        VZ = [H("VZ%d" % i, TS, extra=(2,)) for i in range(2)]
        wdm = B.sb(es, "wdm", [128, TS], F32)
        adm = B.sb(es, "adm", [128, TS], F32)
        gdm = B.sb(es, "gdm", [128, 4, TS], F32)
        dl = B.sb(es, "dl", [128, 4, TS], F32)
        sw = H("sw")
        a_ = H("a")
        g_ = [H("g%d" % i) for i in range(2)]
        kk = H("kk")
        sq = H("sq")
        kmod = H("kmod")
        ba = H("ba")
        cs_ = H("cs")
        E1 = [H("E1%d" % i) for i in range(2)]
        E2 = H("E2")
        E3 = H("E3")
        E4 = H("E4")
        tmpH = H("tmpH")
        AR = [H("AR%d" % i, TS, extra=(2,)) for i in range(2)]
        BK = [H("BK%d" % i, TS, extra=(2,)) for i in range(2)]
        BKh = [H("BKh%d" % i, TS, extra=(2,)) for i in range(2)]
        bonus = [H("bon%d" % i) for i in range(2)]
        Y = [H("Y%d" % i) for i in range(2)]
        Gm = [B.sb(es, "Gm%d" % i, [128, 8, 128], F32) for i in range(2)]
        QN0 = [S64("QN0%d" % i) for i in range(2)]
        QP = [S64("QP%d" % i) for i in range(2)]
        QtP = [S64("QtP%d" % i) for i in range(2)]
        IQ = S64("IQ")
        X = [S64("X%d" % i) for i in range(2)]
        Atm = S64("Atm")
        BKt = [S64("BKt%d" % i, 128) for i in range(2)]
        UV = [S64("UV%d" % i, 128) for i in range(2)]
        Wsb = S64("Wsb")
        Uhat = [S64("Uhat%d" % i) for i in range(2)]
        AhT = [S64("AhT%d" % i) for i in range(2)]
        ST = [S64("ST%d" % i) for i in range(2)]
        STd = S64("STd")
        yc = H("yc")
        ysq = H("ysq")
        rsd = H("rsd")
        ostg = [B.sb(es, "ostg%d" % i, [64, 8, TS], MIX["dt"]) for i in range(2)]
        MDST = MIX["dst"]
        P.op("pool", lambda e: e.memset(ST[0].t[:], 0.0), writes=[ST[0].b])
        P.op("pool", lambda e: e.memset(VZ[0].t[:], 0.0), writes=[VZ[0].b])
        P.op("pool", lambda e: e.memset(VZ[1].t[:], 0.0), writes=[VZ[1].b])
        P.op("pool", lambda e: e.memset(UV[0].t[:], 0.0), writes=[UV[0].b])
        P.op("pool", lambda e: e.memset(UV[1].t[:], 0.0), writes=[UV[1].b])

        ONES64 = K["ONESF"][0:64, 0:64]
        ID64 = K["IDENTF"][0:64, 0:64]
        ID64B = ID64.unsqueeze(1).to_broadcast([64, 8, 64])
        MASKNB = K["MASKN"].unsqueeze(1).to_broadcast([64, 8, 64])
        MASKGB = K["MASKG"].unsqueeze(1).to_broadcast([128, 4, 128])
        st = dict(ps=0, sti=0, chunk=0)

        def nps():
            p = B.psb[st["ps"] % 8]
            st["ps"] += 1
            return p

        def hpv(i):
            return hp.t[:, i, :].unsqueeze(2)

        def bc(ap, n=TS):
            return ap.to_broadcast([64, 8, n])

        def pv8(p, rows=64):
            return p.t[0:rows, :].rearrange("p (h t) -> p h t", h=8)

        def mm8(out_fn, lhs_fn, rhs_fn):
            def f(e):
                for h in range(8):
                    r = e.matmul(out_fn(h), lhs_fn(h), rhs_fn(h), start=True, stop=True)
                return r
            return f

        def do_chunk(s_i, c, ar, bk, bkh, vz, e1, yy):
            cs0 = c * CH
            csl = slice(cs0, cs0 + CH)
            ci = st["chunk"] % 2
            st["chunk"] += 1
            gm, qn0, bkt, uv, uh, aht = Gm[ci], QN0[ci], BKt[ci], UV[ci], Uhat[ci], AhT[ci]
            for half in range(2):
                def ghalf(half):
                    pg_ = nps()

                    def gmm(e):
                        for hh in range(4):
                            h = half * 4 + hh
                            r = e.matmul(pg_.t[:, hh * 128:(hh + 1) * 128], bk.t[:, h, :, csl], ar.t[:, h, :, csl], start=True, stop=True)
                        return r
                    P.op("pe", gmm, reads=[bk.b, ar.b], writes=[pg_.b])
                    P.op("dve", lambda e: e.tensor_tensor(out=gm.t[:, half * 4:half * 4 + 4, :], in0=pg_.t[:].rearrange("p (h t) -> p h t", h=4),
                                                          in1=MASKGB, op=ALU.mult),
                         reads=[pg_.b] + CR, writes=[gm.b])
                ghalf(half)
            pn = nps()
            P.op("pe", mm8(lambda h: pv8(pn)[:, h, :], lambda h: ar.t[:, h, 0, csl], lambda h: bk.t[:, h, 0, csl]), reads=[ar.b, bk.b], writes=[pn.b])
            P.op("dve", lambda e: e.tensor_tensor(out=qn0.t[:], in0=pv8(pn), in1=MASKNB, op=ALU.mult), reads=[pn.b] + CR, writes=[qn0.b])
            if DBG < 5:
                return
            P.op("pool", lambda e: e.tensor_tensor(out=X[0].t[:], in0=gm.t[0:64, :, 0:64], in1=ID64B, op=ALU.add), reads=[gm.b] + CR, writes=[X[0].b])

            def level(lvl, q_cur, qt_ap, qt_b, x_cur):
                pq = nps()
                P.op("pe", mm8(lambda h: pv8(pq)[:, h, :], qt_ap, lambda h: q_cur.t[:, h, :]), reads=[qt_b, q_cur.b], writes=[pq.b])
                q_new = QP[lvl % 2]
                qt_new = QtP[lvl % 2]
                if lvl < 5:
                    pqt = nps()
                    P.op("pe", mm8(lambda h: pv8(pqt)[:, h, :], lambda h: q_cur.t[:, h, :], qt_ap), reads=[qt_b, q_cur.b], writes=[pqt.b])
                    P.op("act", lambda e: e.copy(out=q_new.t[:], in_=pv8(pq)), reads=[pq.b], writes=[q_new.b])
                    P.op("dve", lambda e: e.tensor_copy(out=qt_new.t[:], in_=pv8(pqt)), reads=[pqt.b], writes=[qt_new.b])
                    P.op("pool", lambda e: e.tensor_tensor(out=IQ.t[:], in0=q_new.t[:], in1=ID64B, op=ALU.add), reads=[q_new.b] + CR, writes=[IQ.b])
                else:
                    P.op("dve", lambda e: e.tensor_tensor(out=IQ.t[:], in0=pv8(pq), in1=ID64B, op=ALU.add), reads=[pq.b] + CR, writes=[IQ.b])
                px = nps()
                x_new = X[lvl % 2]
                P.op("pe", mm8(lambda h: pv8(px)[:, h, :], lambda h: IQ.t[:, h, :], lambda h: x_cur.t[:, h, :]), reads=[IQ.b, x_cur.b], writes=[px.b])
                P.op("act", lambda e: e.copy(out=x_new.t[:], in_=pv8(px)), reads=[px.b], writes=[x_new.b])
                return q_new, (lambda h: qt_new.t[:, h, :]), qt_new.b, x_new

            q_cur, qt_ap, qt_b, x_cur = qn0, (lambda h: gm.t[0:64, h, 0:64]), gm.b, X[0]
            for lvl in range(1, 6):
                q_cur, qt_ap, qt_b, x_cur = level(lvl, q_cur, qt_ap, qt_b, x_cur)
            TT = x_cur
            if DBG < 6:
                return
            pa_, pb_, pv_ = nps(), nps(), nps()

            def trs(e):
                for h in range(8):
                    e.transpose(out=pv8(pa_)[:, h, :], in_=ar.t[:, h, 0, csl], identity=ID64)
                for h in range(8):
                    e.transpose(out=pv8(pb_, 128)[:, h, :], in_=bkh.t[:, h, :, csl], identity=ID64)
                for h in range(8):
                    r = e.transpose(out=pv8(pv_, 128)[:, h, :], in_=vz.t[:, h, :, csl], identity=ID64)
                return r
            P.op("pe", trs, reads=[ar.b, bkh.b, vz.b] + CR, writes=[pa_.b, pb_.b, pv_.b])
            P.op("act", lambda e: e.copy(out=Atm.t[:], in_=pv8(pa_)), reads=[pa_.b], writes=[Atm.b])
            P.op("dve", lambda e: e.tensor_copy(out=bkt.t[:], in_=pv8(pb_, 128)), reads=[pb_.b], writes=[bkt.b])
            P.op("act", lambda e: e.copy(out=uv.t[64:128, :, :], in_=pv8(pv_, 128)[64:128, :, :]), reads=[pv_.b], writes=[uv.b])
            pw_ = nps()
            P.op("pe", mm8(lambda h: pv8(pw_)[:, h, :], lambda h: gm.t[64:128, h, 0:64], lambda h: uv.t[64:128, h, :]), reads=[gm.b, uv.b], writes=[pw_.b])
            P.op("act", lambda e: e.copy(out=Wsb.t[:], in_=pv8(pw_)), reads=[pw_.b], writes=[Wsb.b])
            pu_, ph_ = nps(), nps()
            P.op("pe", mm8(lambda h: pv8(pu_)[:, h, :], lambda h: TT.t[:, h, :], lambda h: Wsb.t[:, h, :]), reads=[TT.b, Wsb.b], writes=[pu_.b])
            P.op("pe", mm8(lambda h: pv8(ph_)[:, h, :], lambda h: Atm.t[:, h, :], lambda h: TT.t[:, h, :]), reads=[TT.b, Atm.b], writes=[ph_.b])
            P.op("act", lambda e: e.copy(out=uh.t[:], in_=pv8(pu_)), reads=[pu_.b], writes=[uh.b])
            P.op("dve", lambda e: e.tensor_copy(out=aht.t[:], in_=pv8(ph_)), reads=[ph_.b], writes=[aht.b])
            if DBG < 7:
                return
            st_old = ST[st["sti"] % 2]
            st_new = ST[(st["sti"] + 1) % 2]
            st["sti"] += 1
            pc_ap = e1.t[:, :, cs0 + CH - 1:cs0 + CH]
            P.op("pool", lambda e: e.tensor_tensor(out=STd.t[:], in0=st_old.t[:], in1=pc_ap.to_broadcast([64, 8, 64]), op=ALU.mult),
                 reads=[st_old.b, e1.b], writes=[STd.b])
            pU = nps()
            P.op("pe", mm8(lambda h: pv8(pU)[:, h, :], lambda h: aht.t[:, h, :], lambda h: st_old.t[:, h, :]), reads=[aht.b, st_old.b], writes=[pU.b])
            P.op("dve", lambda e: e.tensor_tensor(out=uv.t[0:64, :, :], in0=pv8(pU), in1=uh.t[:], op=ALU.add), reads=[pU.b, uh.b], writes=[uv.b])
            pS = nps()
            P.op("pe", mm8(lambda h: pv8(pS)[:, h, :], lambda h: bkt.t[:, h, :], lambda h: uv.t[:, h, :]), reads=[bkt.b, uv.b], writes=[pS.b])
            P.op("dve", lambda e: e.tensor_tensor(out=st_new.t[:], in0=pv8(pS), in1=STd.t[:], op=ALU.add), reads=[pS.b, STd.b], writes=[st_new.b])
            if DBG < 8:
                return
            pY = nps()

            def y1(e):
                for h in range(8):
                    e.matmul(pv8(pY)[:, h, :], st_old.t[:, h, :], ar.t[:, h, 1, csl], start=True, stop=False)
                    r = e.matmul(pv8(pY)[:, h, :], uv.t[:, h, :], gm.t[:, h, 64:128], start=False, stop=True)
                return r
            P.op("pe", y1, reads=[st_old.b, ar.b, uv.b, gm.b], writes=[pY.b])
            P.op("act", lambda e: e.copy(out=yy.t[:, :, csl], in_=pv8(pY)), reads=[pY.b], writes=[yy.b])

        DBG = int(os.environ.get('A3_DBG', '100'))

        def do_super(s_i):
            t0 = s_i * TS
            i2 = s_i % 2
            rx, kx, vx, wx, ax, gx = RX[i2], KX[i2], VX[i2], WX[i2], AXl[i2], GX[i2]
            vz, ar, bk, bkh, e1, bon, gg, yy = VZ[i2], AR[i2], BK[i2], BKh[i2], E1[i2], bonus[i2], g_[i2], Y[i2]
            lo = 1 if s_i == 0 else 0
            src0 = t0 - 1 + lo
            n = W - lo

            def ldH(dst, c0, key):
                if s_i == 0:
                    P.op("pool", lambda e: e.memset(dst.t[:, :, 0:1], 0.0), writes=[part(dst, cc) for cc in range(4)])
                for cc in range(4):
                    def one(cc):
                        P.dma("sp", lambda e: e.dma_start(out=dst.t[:, 2 * cc:2 * cc + 2, lo:W],
                                                          in_=pr_s[c0 + cc, :, src0:src0 + n].rearrange("(h p) t -> p h t", h=2)),
                              "%s%d" % (key, i2), writes=[part(dst, cc)])
                    one(cc)
            ldH(rx, 0, "rx")
            ldH(kx, 4, "kx")
            ldH(vx, 8, "vx")
            if s_i == 0:
                P.op("pool", lambda e: e.memset(wx.t[:, 0:1], 0.0), writes=[wx.b])
                P.op("pool", lambda e: e.memset(ax.t[:, 0:1], 0.0), writes=[ax.b])
                P.op("pool", lambda e: e.memset(gx.t[:, :, 0:1], 0.0), writes=[part(gx, cc) for cc in range(4)])
            P.dma("sp", lambda e: e.dma_start(out=wx.t[:, lo:W], in_=pr_s[12, :, src0:src0 + n]), "wx%d" % i2, writes=[wx.b])
            P.dma("sp", lambda e: e.dma_start(out=ax.t[:, lo:W], in_=pr_s[13, :, src0:src0 + n]), "ax%d" % i2, writes=[ax.b])
            for cc in range(4):
                def oneg(cc):
                    P.dma("sp", lambda e: e.dma_start(out=gx.t[:, cc, lo:W], in_=pr_s[14 + cc, :, src0:src0 + n]),
                          "gx%d" % i2, writes=[part(gx, cc)])
                oneg(cc)
            allp = lambda tb: [part(tb, cc) for cc in range(4)]

            def mixH(src, dst_ap, mi):
                def f(e):
                    e.tensor_tensor(out=D1.t[:], in0=src.t[:, :, 0:TS], in1=src.t[:, :, 1:W], op=ALU.subtract)
                    e.tensor_tensor(out=D1.t[:], in0=D1.t[:], in1=bc(hpv(mi)), op=ALU.mult)
                    return e.tensor_tensor(out=dst_ap, in0=D1.t[:], in1=src.t[:, :, 1:W], op=ALU.add)
                return f
            P.op("dve", mixH(rx, r_.t[:], 0), reads=allp(rx) + CR, writes=[D1.b, r_.b])
            P.op("dve", mixH(kx, k_.t[:], 1), reads=allp(kx) + CR, writes=[D1.b, k_.b])
            P.op("dve", mixH(vx, vz.t[:, :, 1, :], 2), reads=allp(vx) + CR, writes=[D1.b, vz.b])

            def mixL(e):
                e.tensor_tensor(out=dl.t[:, 0, :], in0=wx.t[:, 0:TS], in1=wx.t[:, 1:W], op=ALU.subtract)
                e.scalar_tensor_tensor(out=wdm.t[:], in0=dl.t[:, 0, :], scalar=lp.t[:, 0:1], in1=wx.t[:, 1:W], op0=ALU.mult, op1=ALU.add)
                e.tensor_tensor(out=dl.t[:, 0, :], in0=ax.t[:, 0:TS], in1=ax.t[:, 1:W], op=ALU.subtract)
                e.scalar_tensor_tensor(out=adm.t[:], in0=dl.t[:, 0, :], scalar=lp.t[:, 1:2], in1=ax.t[:, 1:W], op0=ALU.mult, op1=ALU.add)
                e.tensor_tensor(out=dl.t[:], in0=gx.t[:, :, 0:TS], in1=gx.t[:, :, 1:W], op=ALU.subtract)
                for cc in range(4):
                    r = e.scalar_tensor_tensor(out=gdm.t[:, cc, :], in0=dl.t[:, cc, :], scalar=lp.t[:, 2 + cc:3 + cc], in1=gx.t[:, cc, 1:W],
                                               op0=ALU.mult, op1=ALU.add)
                return r
            P.op("dve", mixL, reads=[wx.b, ax.b] + allp(gx) + CR, writes=[dl.b, wdm.b, adm.b, gdm.b])
            P.op("act", lambda e: e.activation(out=wdm.t[:], in_=wdm.t[:], func=AF.Tanh), reads=[wdm.b], writes=[wdm.b])
            P.op("act", lambda e: e.activation(out=gdm.t[:], in_=gdm.t[:], func=AF.Sigmoid), reads=[gdm.b], writes=[gdm.b])

            def lora_all():
                pw, pa, pg = nps(), nps(), nps()

                def lora(e):
                    for h in range(8):
                        e.matmul(pv8(pw)[:, h, :], wdu.t[:, h * 64:(h + 1) * 64], wdm.t[:], start=True, stop=True)
                    for h in range(8):
                        e.matmul(pv8(pa)[:, h, :], wau.t[:, h * 64:(h + 1) * 64], adm.t[:], start=True, stop=True)
                    for h in range(8):
                        for cc in range(4):
                            r = e.matmul(pv8(pg)[:, h, :], wgu.t[:, cc, h * 64:(h + 1) * 64], gdm.t[:, cc, :], start=(cc == 0), stop=(cc == 3))
                    return r
                P.op("pe", lora, reads=[wdu.b, wau.b, wgu.b, wdm.b, adm.b, gdm.b], writes=[pw.b, pa.b, pg.b])

                def sig(e):
                    for h in range(8):
                        e.activation(out=sw.t[:, h, :], in_=pv8(pw)[:, h, :], func=AF.Sigmoid, bias=hp.t[:, 3, h:h + 1], scale=1.0)
                    for h in range(8):
                        r = e.activation(out=a_.t[:, h, :], in_=pv8(pa)[:, h, :], func=AF.Sigmoid, bias=hp.t[:, 4, h:h + 1], scale=1.0)
                    return r
                P.op("act", sig, reads=[pw.b, pa.b] + CR, writes=[sw.b, a_.b])
                P.op("act", lambda e: e.copy(out=gg.t[:], in_=pv8(pg)), reads=[pg.b], writes=[gg.b])
            if DBG < 2:
                return
            lora_all()
            if DBG < 3:
                return
            P.op("dve", lambda e: e.tensor_tensor(out=kk.t[:], in0=k_.t[:], in1=bc(hpv(5)), op=ALU.mult), reads=[k_.b] + CR, writes=[kk.b])
            P.op("act", lambda e: e.activation(out=sq.t[:], in_=kk.t[:], func=AF.Square), reads=[kk.b], writes=[sq.b])

            def sum8(src, consume):
                pk = nps()
                P.op("pe", mm8(lambda h: pv8(pk)[:, h, :], lambda h: ONES64, lambda h: src.t[:, h, :]), reads=[src.b] + CR, writes=[pk.b])
                consume(pk)
            sum8(sq, lambda pk: P.op("act", lambda e: e.activation(out=tmpH.t[:], in_=pv8(pk), func=AF.Sqrt), reads=[pk.b], writes=[tmpH.b]))

            def kkn(e):
                e.tensor_scalar(out=tmpH.t[:], in0=tmpH.t[:], scalar1=1e-12, scalar2=None, op0=ALU.max)
                e.reciprocal(out=tmpH.t[:], in_=tmpH.t[:])
                e.tensor_tensor(out=kk.t[:], in0=kk.t[:], in1=tmpH.t[:], op=ALU.mult)
                e.scalar_tensor_tensor(out=tmpH.t[:], in0=a_.t[:], scalar=-1.0, in1=bc(hpv(6)), op0=ALU.add, op1=ALU.mult)
                e.scalar_tensor_tensor(out=kmod.t[:], in0=tmpH.t[:], scalar=1.0, in1=k_.t[:], op0=ALU.add, op1=ALU.mult)
                return e.tensor_tensor(out=ba.t[:], in0=kk.t[:], in1=a_.t[:], op=ALU.mult)
            P.op("dve", kkn, reads=[tmpH.b, kk.b, a_.b, k_.b] + CR, writes=[tmpH.b, kk.b, kmod.b, ba.b])
            flat = lambda t: t.t[:].rearrange("p h t -> p (h t)")
            P.op("dve", lambda e: e.tensor_tensor_scan(out=flat(cs_), data0=K["RESETM"], data1=flat(sw), initial=0.0, op0=ALU.mult, op1=ALU.add),
                 reads=[sw.b] + CR, writes=[cs_.b])
            P.op("act", lambda e: e.activation(out=e1.t[:], in_=cs_.t[:], func=AF.Exp, scale=-C0), reads=[cs_.b], writes=[e1.b])
            P.op("act", lambda e: e.activation(out=E2.t[:], in_=cs_.t[:], func=AF.Exp, scale=C0), reads=[cs_.b], writes=[E2.b])
            P.op("pool", lambda e: e.tensor_tensor(out=E3.t[:], in0=cs_.t[:], in1=sw.t[:], op=ALU.subtract), reads=[cs_.b, sw.b], writes=[E3.b])
            P.op("act", lambda e: e.activation(out=E3.t[:], in_=E3.t[:], func=AF.Exp, scale=-C0), reads=[E3.b], writes=[E3.b])

            def e4f(e):
                c4 = cs_.t[:].rearrange("p h (c t) -> p h c t", t=CH)
                return e.tensor_tensor(out=E4.t[:].rearrange("p h (c t) -> p h c t", t=CH), in0=c4,
                                       in1=c4[:, :, :, CH - 1:CH].to_broadcast([64, 8, NC, CH]), op=ALU.subtract)
            P.op("pool", e4f, reads=[cs_.b], writes=[E4.b])
            P.op("act", lambda e: e.activation(out=E4.t[:], in_=E4.t[:], func=AF.Exp, scale=C0), reads=[E4.b], writes=[E4.b])

            def tild(e):
                e.tensor_tensor(out=ar.t[:, :, 1, :], in0=r_.t[:], in1=e1.t[:], op=ALU.mult)
                e.scalar_tensor_tensor(out=ar.t[:, :, 0, :], in0=kk.t[:], scalar=-1.0, in1=E3.t[:], op0=ALU.mult, op1=ALU.mult)
                e.tensor_tensor(out=bk.t[:, :, 0, :], in0=ba.t[:], in1=E2.t[:], op=ALU.mult)
                return e.tensor_tensor(out=bk.t[:, :, 1, :], in0=kmod.t[:], in1=E2.t[:], op=ALU.mult)
            P.op("dve", tild, reads=[r_.b, e1.b, kk.b, E3.b, ba.b, E2.b, kmod.b], writes=[ar.b, bk.b])

            def hatf(e):
                e.tensor_tensor(out=bkh.t[:, :, 0, :], in0=ba.t[:], in1=E4.t[:], op=ALU.mult)
                return e.tensor_tensor(out=bkh.t[:, :, 1, :], in0=kmod.t[:], in1=E4.t[:], op=ALU.mult)
            P.op("pool", hatf, reads=[ba.b, kmod.b, E4.b], writes=[bkh.b])

            def rkf(e):
                e.tensor_tensor(out=tmpH.t[:], in0=r_.t[:], in1=kmod.t[:], op=ALU.mult)
                return e.tensor_tensor(out=sq.t[:], in0=tmpH.t[:], in1=bc(hpv(7)), op=ALU.mult)
            P.op("pool", rkf, reads=[r_.b, kmod.b, tmpH.b, sq.b] + CR, writes=[tmpH.b, sq.b])
            sum8(sq, lambda pb: P.op("dve", lambda e: e.tensor_tensor(out=bon.t[:], in0=pv8(pb), in1=vz.t[:, :, 1, :], op=ALU.mult),
                                     reads=[pb.b, vz.b], writes=[bon.b]))
            if DBG < 4:
                return
            for c in range(NC):
                do_chunk(s_i, c, ar, bk, bkh, vz, e1, yy)
            if DBG < 9:
                return
            sum8(yy, lambda pm: P.op("dve", lambda e: e.scalar_tensor_tensor(out=yc.t[:], in0=pv8(pm), scalar=-1.0 / 64, in1=yy.t[:],
                                                                             op0=ALU.mult, op1=ALU.add),
                                     reads=[pm.b, yy.b], writes=[yc.b]))
            P.op("act", lambda e: e.activation(out=ysq.t[:], in_=yc.t[:], func=AF.Square), reads=[yc.b], writes=[ysq.b])
            sum8(ysq, lambda pvv: P.op("act", lambda e: e.activation(out=rsd.t[:], in_=pv8(pvv), func=AF.Sqrt, bias=GN_EPS, scale=1.0 / 64),
                                       reads=[pvv.b], writes=[rsd.b]))
            og = ostg[i2]

            def fin(e):
                e.reciprocal(out=rsd.t[:], in_=rsd.t[:])
                e.tensor_tensor(out=yc.t[:], in0=yc.t[:], in1=rsd.t[:], op=ALU.mult)
                e.tensor_tensor(out=yc.t[:], in0=yc.t[:], in1=bc(hpv(8)), op=ALU.mult)
                e.tensor_tensor(out=yc.t[:], in0=yc.t[:], in1=bc(hpv(9)), op=ALU.add)
                e.tensor_tensor(out=yc.t[:], in0=yc.t[:], in1=bon.t[:], op=ALU.add)
                return e.tensor_tensor(out=og.t[:], in0=yc.t[:], in1=gg.t[:], op=ALU.mult)
            P.op("dve", fin, reads=[rsd.b, yc.b, bon.b, gg.b] + CR, writes=[rsd.b, yc.b, og.b])
            P.dma("sp", lambda e: e.dma_start(out=MDST(512, 512, t0, TS).rearrange("(h p) t -> p h t", h=8), in_=og.t[:]),
                  "ostg%d" % i2, reads=[og.b])

        conv = list(conv)
        per = -(-len(conv) // NS) if conv else 0
        cvi = 0
        for s_i in range(min(NS, int(os.environ.get('A3_NS', '100000')))):
            do_super(s_i)
            for _ in range(per):
                if cvi < len(conv):
                    def cvt(k):
                        dst, src = conv[k]
                        P.dma("pool", lambda e: e.dma_start(out=dst, in_=src), "cv%d" % (k % 8))
                    cvt(cvi)
                    cvi += 1
        P.run_block()


def _consts_A(qg, kg):
    c = np.zeros((128, 8 * 128 + 256 + 4), np.float32)
    p = np.arange(128)
    c[:, 0:128] = 1.0
    c[:, 128:256] = (p[:, None] // 64 == p[None, :] // 64).astype(np.float32)
    rot = np.zeros((128, 128), np.float32)
    for m in range(128):
        if m % 64 < 32:
            rot[m + 32, m] = -1.0
        else:
            rot[m - 32, m] = 1.0
    c[:, 256:384] = rot
    c[:, 384:512] = np.eye(128, dtype=np.float32)
    i = (p % 64)[:, None]
    t = np.arange(64)[None, :]
    c[:, 512:576] = (i < t).astype(np.float32)
    c[:, 576:640] = (i <= t).astype(np.float32)
    kk = p[:, None]
    qq = np.arange(256)[None, :]
    dist = qq - kk
    c[:, 1024:1280] = np.where((dist >= 0) & (dist <= 128), 0.0, -262144.0)
    c[:, 1280] = np.tile(qg, 2)
    c[:, 1281] = np.tile(kg, 2)
    c[:, 1282] = (np.float32(10000.0) ** (-(np.arange(32, dtype=np.float32)) / np.float32(32)))[p % 32]
    return c


def _consts_64():
    c = np.zeros((64, 64 + 8 * TS), np.float32)
    t = np.arange(64)
    c[:, 0:64] = (t[:, None] > t[None, :]).astype(np.float32)
    m = np.ones((8, TS), np.float32)
    m[:, ::CH] = 0.0
    c[:, 64:] = m.reshape(1, -1)
    return c


def _wchunk(w, cols):
    blk = np.zeros((4096, 128), np.float32)
    blk[:, :len(cols)] = w[:, cols]
    return np.ascontiguousarray(blk.reshape(32, 128, 128).transpose(1, 0, 2)).reshape(128, 32 * 128)


def prep_A(inp, S):
    x = inp["x"]
    w_in = inp["w_in"][0]
    mu = inp["rwkv_mu"][0]
    maps = []
    xTs = [np.ascontiguousarray(x[b].T.reshape(32, 128, S).transpose(1, 0, 2)) for b in range(2)]
    cA = _consts_A(inp["q_norm_g"][0], inp["k_norm_g"][0])
    c64 = _consts_64()
    gA = np.ascontiguousarray(inp["attn_norm_g"][0].reshape(32, 128).T)
    RB = 3 * ATTN_W
    for core in range(8):
        b, g = core // 4, core % 4
        cols = []
        for base in (0, 2048, 4096, RB, RB + 2048, RB + 4096):
            for jj in range(4):
                cols.append(np.arange(base + 512 * g + 128 * jj, base + 512 * g + 128 * jj + 128))
        cols.append(np.arange(RB + 6144, RB + 6272))
        cols.append(np.arange(RB + 6272, RB + 6400))
        for jj in range(4):
            lo = RB + 6400 + 128 * jj
            cols.append(np.arange(lo, min(lo + 128, RB + 6880)))
        wA = np.stack([_wchunk(w_in, cc) for cc in cols])

        def hsl(v):
            return v[512 * g:512 * g + 512].reshape(8, 64).T
        hp = np.zeros((64, 10, 8), np.float32)
        hp[:, 0] = hsl(mu[0:2048])
        hp[:, 1] = hsl(mu[2048:4096])
        hp[:, 2] = hsl(mu[4096:6144])
        hp[:, 3] = hsl(inp["w0"][0])
        hp[:, 4] = hsl(inp["a0"][0])
        hp[:, 5] = hsl(inp["k_k"][0])
        hp[:, 6] = hsl(inp["k_a"][0])
        hp[:, 7] = hsl(inp["r_k"][0].reshape(-1))
        hp[:, 8] = hsl(inp["ln_x_w"][0])
        hp[:, 9] = hsl(inp["ln_x_b"][0])
        lp = np.zeros((128, 6), np.float32)
        lp[:, 0] = mu[6144:6272]
        lp[:, 1] = mu[6272:6400]
        mg = np.zeros(512, np.float32)
        mg[:480] = mu[6400:6880]
        lp[:, 2:6] = mg.reshape(4, 128).T
        wg = np.zeros((512, 512), np.float32)
        wg[:480] = inp["w_gate_up"][0][:, 512 * g:512 * g + 512]
        maps.append(dict(
            xT=xTs[b], pos=np.ascontiguousarray(inp["positions"][b][None, :].astype(np.int32)),
            wA=wA, gA=gA, c128=cA, hp=np.ascontiguousarray(hp.reshape(64, 80)), lp=lp,
            wdu=np.ascontiguousarray(inp["w_decay_up"][0][:, 512 * g:512 * g + 512]),
            wau=np.ascontiguousarray(inp["w_iclr_up"][0][:, 512 * g:512 * g + 512]),
            wgu=np.ascontiguousarray(wg.reshape(4, 128, 512).transpose(1, 0, 2)).reshape(128, 2048),
            c64=c64))
    return maps


def gather_mix(resA, S):
    mixT = np.zeros((2, 4096, S), np.float32)
    for core in range(8):
        b, g = core // 4, core % 4
        m = resA[core]["mixA"]
        mixT[b, 512 * g:512 * g + 512] = m[0:512]
        mixT[b, 2048 + 512 * g:2048 + 512 * g + 512] = m[512:1024]
    return mixT


NWB = 3
MPAD = 64
NFF = D_FF // 128


def build_B(NTOK, mix_dram=None, B=None, fused=False, mixx=None, wdecl=None):
    own = B is None
    if own:
        B = Builder()
    nc, P = B.nc, B.P
    T = 512
    NTB = NTOK // T
    IN, OUT = "ExternalInput", "ExternalOutput"
    if fused:
        qm_d = B.dram("qmask", [128, 4], F32, IN)
    elif mix_dram is None:
        mix_dram = B.dram("mixin", [4, 1024, NTOK + 2], F32, IN)
    xT = B.dram("xTb", [128, 32, NTOK + 2], F32, IN)
    pT = B.dram("pTb", [128, 2, NTOK], F32, IN)
    if wdecl is None:
        wdecl = declare_B_weights(B)[0]
    wout, wup, wdn, wple, wgt = (wdecl[k] for k in ("wout", "wup", "wdn", "wple", "wgt"))
    WQ = "sp" if fused else "pool"
    SQ = "pool" if fused else "sp"
    cst = B.dram("cstb", [128, 64 + 2 * NFF * 4 + 128], F32, IN)
    outT = B.dram("outT", [128, 32, NTOK], F32, OUT)
    NU = 2 * NFF

    with contextlib.ExitStack() as es:
        c_sb = B.sb(es, "cstb", [128, 64 + NU * 4 + 128], F32)
        ones_b = B.sb(es, "onesb", [128, 128], BF16)
        P.dma("sp", lambda e: e.dma_start(out=c_sb.t[:], in_=cst), "cstb", writes=[c_sb.b])
        P.dma("pool", lambda e: e.dma_start(out=ones_b.t[:], in_=cst[:, 64 + NU * 4:64 + NU * 4 + 128]), "onesb", writes=[ones_b.b])
        GM = c_sb.t[:, 0:32]
        GP = c_sb.t[:, 32:64]
        CW = c_sb.t[:, 64:64 + NU * 3].rearrange("p (u j) -> p u j", j=3)
        CB = c_sb.t[:, 64 + NU * 3:64 + NU * 4]
        CR = [c_sb.b, ones_b.b]

        h = B.sb(es, "h", [128, 32, T], F32)
        a16 = B.sb(es, "a16", [128, 32, T], BF16)
        hh = B.sb(es, "hh", [128, 32, 2], F32)
        a16h = B.sb(es, "a16h", [128, 32, 2], BF16)
        wr = [B.sb(es, "wr%d" % i, [128, 32, 128], BF16) for i in range(NWB)]
        wd = [B.sb(es, "wd%d" % i, [128, 4096], BF16) for i in range(2)]
        ue = [[B.sb(es, "ue%d%d" % (i, j), [128, T + 2], F32) for j in range(2)] for i in range(2)]
        cv = [[B.sb(es, "cv%d%d" % (i, j), [128, T], F32) for j in range(2)] for i in range(2)]
        actg = [B.sb(es, "actg%d" % i, [128, 2, T], BF16) for i in range(2)]
        uhalo = B.sb(es, "uhalo", [128, NU, 2], F32)
        rstd = B.sb(es, "rstdb", [128, T], F32)
        rstdh = B.sb(es, "rstdh", [128, 2], F32)
        rstde = B.sb(es, "rstde", [128, T], F32)
        p16 = B.sb(es, "p16", [128, 2, T], BF16)
        sqb = [B.sb(es, "sqbb%d" % i, [128, T], BF16) for i in range(2)]
        sqh = B.sb(es, "sqh", [128, 2], BF16)
        sg = [B.sb(es, "sg%d" % i, [128, T], F32) for i in range(2)]
        te = [B.sb(es, "te%d" % i, [128, T], F32) for i in range(2)]
        hparts = [Buf() for _ in range(32)]
        mixg_b = Buf("mixg")
        if fused:
            qm = B.sb(es, "qm", [128, 4], F32)
            cand = [B.sb(es, "cand%d" % i, [128, 4, T], BF16) for i in range(4)]
            candh = [B.sb(es, "candh%d" % i, [128, 32, 2], BF16) for i in range(4)]
            P.dma("sp", lambda e: e.dma_start(out=qm.t[:], in_=qm_d), "qm", writes=[qm.b])
            NCHK = 4 * NTOK // 512
            chunk_b = [Buf("mixg%d" % i) for i in range(NCHK)]
            order = [qq * (NTOK // 512) + tt for tt in range(NTOK // 512) for qq in range(4)]
            for ci in order:
                def ag(ci):
                    P.dma("pool", lambda e: e.collective_compute("AllGather", ALU.bypass, replica_groups=[[0, 1, 2, 3], [4, 5, 6, 7]],
                                                                 ins=[mixx[ci]], outs=[mix_dram[ci]]), "ag", writes=[chunk_b[ci]], inc=1)
                ag(ci)
        psm = [B.psb[i] for i in range(6)]
        pss = [B.psb[6], B.psb[7]]
        cnt = dict(w=0, d=0, m=0, s=0, sq=0, ue=0, ag=0, sg=0)

        def wslot():
            s_ = wr[cnt["w"] % NWB]
            key = "wr%d" % (cnt["w"] % NWB)
            cnt["w"] += 1
            return s_, key

        def pmain():
            p = psm[cnt["m"] % 6]
            cnt["m"] += 1
            return p

        def psmall():
            p = pss[cnt["s"] % 2]
            cnt["s"] += 1
            return p

        def load_w(src_ap):
            s_, key = wslot()
            P.dma(WQ, lambda e: e.dma_start(out=s_.t[:].rearrange("p k c -> p (k c)"), in_=src_ap), key, writes=[s_.b])
            return s_

        def mm32(ps_ap, s_, rhs_fn):
            def f(e):
                for kc in range(32):
                    r = e.matmul(ps_ap, s_.t[:, kc, :], rhs_fn(kc), start=(kc == 0), stop=(kc == 31))
                return r
            return f

        def rms(src, src_tokens, n, dst16, g_ap, rs, halo):
            ssp = psmall()
            for kc in range(32):
                def one(kc):
                    sq = sqh if halo else sqb[cnt["sq"] % 2]
                    cnt["sq"] += 1
                    sqa = sq.t[:, 0:n]
                    P.op("act", lambda e: e.activation(out=sqa, in_=src.t[:, kc, 0:n], func=AF.Square), reads=[src_tokens[kc]], writes=[sq.b])
                    P.op("pe", lambda e: e.matmul(ssp.t[:, 0:n], ones_b.t[:], sqa, start=(kc == 0), stop=(kc == 31)), reads=[sq.b] + CR, writes=[ssp.b])
                one(kc)
            P.op("act", lambda e: e.activation(out=rs.t[:, 0:n], in_=ssp.t[:, 0:n], func=AF.Sqrt, bias=NORM_EPS, scale=1.0 / D_MODEL), reads=[ssp.b], writes=[rs.b])
            P.op("dve", lambda e: e.reciprocal(out=rs.t[:, 0:n], in_=rs.t[:, 0:n]), reads=[rs.b], writes=[rs.b])
            for kc in range(32):
                def two(kc):
                    P.op("dve", lambda e: e.scalar_tensor_tensor(out=dst16.t[:, kc, 0:n], in0=src.t[:, kc, 0:n], scalar=g_ap[:, kc:kc + 1], in1=rs.t[:, 0:n],
                                                                 op0=ALU.mult, op1=ALU.mult),
                         reads=[src_tokens[kc], rs.b] + CR, writes=[dst16.b])
                two(kc)

        def do_tile(tt):
            t0 = tt * T
            first = tt == 0
            hhp = [hh.b] * 32
            for q4 in range(4):
                def ld(q4):
                    P.dma("sp", lambda e: e.dma_start(out=h.t[:, 8 * q4:8 * q4 + 8, :], in_=xT[:, 8 * q4:8 * q4 + 8, 2 + t0:2 + t0 + T]),
                          "hx%d" % q4, writes=hparts[8 * q4:8 * q4 + 8])
                    if not fused:
                        P.dma("pool", lambda e: e.dma_start(out=a16.t[:, 8 * q4:8 * q4 + 8, :],
                                                            in_=mix_dram[q4, :, 2 + t0:2 + t0 + T].rearrange("(k p) t -> p k t", p=128)),
                              "mx%d" % q4, writes=[a16.b])
                ld(q4)
            if fused:
                for g8 in range(8):
                    def selg(g8):
                        for qq in range(4):
                            def ldc(qq):
                                ci = (qq * NTOK + t0) // 512
                                P.dma("sp", lambda e: e.dma_start(out=cand[qq].t[:], in_=mix_dram[ci, g8 * 512:(g8 + 1) * 512, :].rearrange("(k p) t -> p k t", p=128)),
                                      "cd%d" % qq, reads=[chunk_b[ci]], writes=[cand[qq].b])
                            ldc(qq)
                        dst = a16.t[:, 4 * g8:4 * g8 + 4, :]

                        def sel(e):
                            r = e.tensor_scalar(out=dst, in0=cand[0].t[:], scalar1=qm.t[:, 0:1], scalar2=None, op0=ALU.mult)
                            for qq in range(1, 4):
                                r = e.scalar_tensor_tensor(out=dst, in0=cand[qq].t[:], scalar=qm.t[:, qq:qq + 1], in1=dst, op0=ALU.mult, op1=ALU.add)
                            return r
                        P.op("dve", sel, reads=[c_.b for c_ in cand] + [qm.b], writes=[a16.b])
                    selg(g8)
            P.dma("pool", lambda e: e.dma_start(out=p16.t[:], in_=pT[:, :, t0:t0 + T]), "p16", writes=[p16.b])
            if first:
                P.dma("sp", lambda e: e.dma_start(out=hh.t[:], in_=xT[:, :, 0:2]), "hhx", writes=[hh.b])
                if fused:
                    P.op("pool", lambda e: e.memset(candh[0].t[:], 0.0), writes=[candh[0].b])
                    for qq in range(1, 4):
                        def ldch(qq):
                            ci = qq * NTOK // 512 - 1
                            P.dma("sp", lambda e: e.dma_start(out=candh[qq].t[:], in_=mix_dram[ci, :, 510:512].rearrange("(k p) t -> p k t", p=128)),
                                  "cdh%d" % qq, reads=[chunk_b[ci]], writes=[candh[qq].b])
                        ldch(qq)

                    def selh(e):
                        r = e.tensor_scalar(out=a16h.t[:], in0=candh[0].t[:], scalar1=qm.t[:, 0:1], scalar2=None, op0=ALU.mult)
                        for qq in range(1, 4):
                            r = e.scalar_tensor_tensor(out=a16h.t[:], in0=candh[qq].t[:], scalar=qm.t[:, qq:qq + 1], in1=a16h.t[:], op0=ALU.mult, op1=ALU.add)
                        return r
                    P.op("dve", selh, reads=[c_.b for c_ in candh] + [qm.b], writes=[a16h.b])
                else:
                    for q4 in range(4):
                        def ldh(q4):
                            P.dma("pool", lambda e: e.dma_start(out=a16h.t[:, 8 * q4:8 * q4 + 8, :],
                                                                in_=mix_dram[q4, :, 0:2].rearrange("(k p) t -> p k t", p=128)),
                                  "mxh%d" % q4, writes=[a16h.b])
                        ldh(q4)
            for c in range(32):
                def oc(c):
                    s_ = load_w(wout[c])
                    ps = pmain()
                    P.op("pe", mm32(ps.t[:], s_, lambda kc: a16.t[:, kc, :]), reads=[s_.b, a16.b], writes=[ps.b])
                    P.op("dve", lambda e: e.tensor_tensor(out=h.t[:, c, :], in0=ps.t[:], in1=h.t[:, c, :], op=ALU.add), reads=[ps.b, hparts[c]], writes=[hparts[c]])
                    if first:
                        ph = psmall()
                        P.op("pe", mm32(ph.t[:, 0:2], s_, lambda kc: a16h.t[:, kc, :]), reads=[s_.b, a16h.b], writes=[ph.b])
                        P.op("dve", lambda e: e.tensor_tensor(out=hh.t[:, c, :], in0=ph.t[:, 0:2], in1=hh.t[:, c, :], op=ALU.add), reads=[ph.b, hh.b], writes=[hh.b])
                oc(c)
            BD_ = int(os.environ.get('B_DBG', '100'))

            def dump():
                for c in range(32):
                    P.dma("sp", _bind(lambda e, c: e.dma_start(out=outT[:, c, t0:t0 + T], in_=h.t[:, c, :]), c), "oc%d" % (c % 8), reads=[hparts[c]])
            if BD_ == 1:
                dump()
                return
            rms(h, hparts, T, a16, GM, rstd, False)
            if first:
                rms(hh, hhp, 2, a16h, GM, rstdh, True)
            def do_pairf(f0):
                ag = actg[cnt["ag"] % 2]
                cnt["ag"] += 1
                for fi in range(2):
                    def ff(f, fi):
                        cvs = []
                        for gi in range(2):
                            def gu(gi):
                                uc = gi * NFF + f
                                s_ = load_w(wup[uc])
                                ps = pmain()
                                P.op("pe", mm32(ps.t[:], s_, lambda kc: a16.t[:, kc, :]), reads=[s_.b, a16.b], writes=[ps.b])
                                u = ue[gi][cnt["ue"] % 2]
                                c1 = cv[gi][cnt["ue"] % 2]
                                if first:
                                    ph = psmall()
                                    P.op("pe", mm32(ph.t[:, 0:2], s_, lambda kc: a16h.t[:, kc, :]), reads=[s_.b, a16h.b], writes=[ph.b])
                                    P.op("dve", lambda e: e.tensor_copy(out=u.t[:, 0:2], in_=ph.t[:, 0:2]), reads=[ph.b], writes=[u.b])
                                else:
                                    P.op("dve", lambda e: e.tensor_copy(out=u.t[:, 0:2], in_=uhalo.t[:, uc, :]), reads=[uhalo.b], writes=[u.b])
                                P.op("act", lambda e: e.copy(out=u.t[:, 2:T + 2], in_=ps.t[:]), reads=[ps.b], writes=[u.b])
                                P.op("dve", lambda e: e.tensor_copy(out=uhalo.t[:, uc, :], in_=u.t[:, T:T + 2]), reads=[u.b], writes=[uhalo.b])
                                P.op("act", lambda e: e.activation(out=c1.t[:], in_=u.t[:, 2:T + 2], func=AF.Identity, bias=CB[:, uc:uc + 1], scale=CW[:, uc, 2:3]),
                                     reads=[u.b] + CR, writes=[c1.b])

                                def cvf(e):
                                    e.scalar_tensor_tensor(out=c1.t[:], in0=u.t[:, 1:T + 1], scalar=CW[:, uc, 1:2], in1=c1.t[:], op0=ALU.mult, op1=ALU.add)
                                    return e.scalar_tensor_tensor(out=c1.t[:], in0=u.t[:, 0:T], scalar=CW[:, uc, 0:1], in1=c1.t[:], op0=ALU.mult, op1=ALU.add)
                                P.op("dve", cvf, reads=[u.b, c1.b] + CR, writes=[c1.b])
                                cvs.append(c1)
                            gu(gi)
                        cnt["ue"] += 1
                        sgt = sg[cnt["sg"] % 2]
                        cnt["sg"] += 1
                        P.op("act", lambda e: e.activation(out=sgt.t[:], in_=cvs[0].t[:], func=AF.Silu), reads=[cvs[0].b], writes=[sgt.b])
                        P.op("dve", lambda e: e.tensor_tensor(out=ag.t[:, fi, :], in0=sgt.t[:], in1=cvs[1].t[:], op=ALU.mult), reads=[sgt.b, cvs[1].b], writes=[ag.b])
                    ff(f0 + fi, fi)
                wds = []
                for fi in range(2):
                    def ldd(fi):
                        s_ = wd[fi]
                        P.dma(WQ, lambda e: e.dma_start(out=s_.t[:], in_=wdn[f0 + fi]), "wd%d" % fi, writes=[s_.b])
                        wds.append(s_)
                    ldd(fi)
                for c in range(32):
                    def dc(c):
                        ps = pmain()

                        def f(e):
                            e.matmul(ps.t[:], wds[0].t[:, c * 128:(c + 1) * 128], ag.t[:, 0, :], start=True, stop=False)
                            return e.matmul(ps.t[:], wds[1].t[:, c * 128:(c + 1) * 128], ag.t[:, 1, :], start=False, stop=True)
                        P.op("pe", f, reads=[wds[0].b, wds[1].b, ag.b], writes=[ps.b])
                        P.op("dve", lambda e: e.tensor_tensor(out=h.t[:, c, :], in0=ps.t[:], in1=h.t[:, c, :], op=ALU.add), reads=[ps.b, hparts[c]], writes=[hparts[c]])
                    dc(c)
            for f0 in range(0, NFF, 2):
                do_pairf(f0)
            if BD_ == 2:
                dump()
                return
            for c in range(32):
                P.op("act", _bind(lambda e, c: e.copy(out=a16.t[:, c, :], in_=h.t[:, c, :]), c), reads=[hparts[c]], writes=[a16.b])
            P.dma(WQ, lambda e: e.dma_start(out=wd[0].t[:], in_=wple[:, 0:4096]), "wd0", writes=[wd[0].b])
            P.dma(WQ, lambda e: e.dma_start(out=wd[1].t[:], in_=wple[:, 4096:8192]), "wd1", writes=[wd[1].b])
            ssp = psmall()
            for c in range(32):
                def e1(c):
                    ps = pmain()

                    def f(e):
                        e.matmul(ps.t[:], wd[0].t[:, c * 128:(c + 1) * 128], p16.t[:, 0, :], start=True, stop=False)
                        return e.matmul(ps.t[:], wd[1].t[:, c * 128:(c + 1) * 128], p16.t[:, 1, :], start=False, stop=True)
                    P.op("pe", f, reads=[wd[0].b, wd[1].b, p16.b], writes=[ps.b])
                    sq = sqb[cnt["sq"] % 2]
                    cnt["sq"] += 1
                    P.op("act", lambda e: e.activation(out=sq.t[:], in_=ps.t[:], func=AF.Square), reads=[ps.b], writes=[sq.b])
                    P.op("pe", lambda e: e.matmul(ssp.t[:], ones_b.t[:], sq.t[:], start=(c == 0), stop=(c == 31)), reads=[sq.b] + CR, writes=[ssp.b])
                e1(c)
            P.op("act", lambda e: e.activation(out=rstde.t[:], in_=ssp.t[:], func=AF.Sqrt, bias=NORM_EPS, scale=1.0 / D_MODEL), reads=[ssp.b], writes=[rstde.b])
            P.op("dve", lambda e: e.reciprocal(out=rstde.t[:], in_=rstde.t[:]), reads=[rstde.b], writes=[rstde.b])
            for c in range(32):
                def e2(c):
                    s_ = load_w(wgt[c])
                    pg = pmain()
                    P.op("pe", mm32(pg.t[:], s_, lambda kc: a16.t[:, kc, :]), reads=[s_.b, a16.b], writes=[pg.b])
                    pe_ = pmain()

                    def f(e):
                        e.matmul(pe_.t[:], wd[0].t[:, c * 128:(c + 1) * 128], p16.t[:, 0, :], start=True, stop=False)
                        return e.matmul(pe_.t[:], wd[1].t[:, c * 128:(c + 1) * 128], p16.t[:, 1, :], start=False, stop=True)
                    P.op("pe", f, reads=[wd[0].b, wd[1].b, p16.b], writes=[pe_.b])
                    sgt = sg[cnt["sg"] % 2]
                    tet = te[cnt["sg"] % 2]
                    cnt["sg"] += 1
                    P.op("act", lambda e: e.activation(out=sgt.t[:], in_=pg.t[:], func=AF.Sigmoid), reads=[pg.b], writes=[sgt.b])

                    def comb(e):
                        e.scalar_tensor_tensor(out=tet.t[:], in0=pe_.t[:], scalar=GP[:, c:c + 1], in1=rstde.t[:], op0=ALU.mult, op1=ALU.mult)
                        e.tensor_tensor(out=tet.t[:], in0=tet.t[:], in1=sgt.t[:], op=ALU.mult)
                        return e.tensor_tensor(out=h.t[:, c, :], in0=h.t[:, c, :], in1=tet.t[:], op=ALU.add)
                    P.op("dve", comb, reads=[pe_.b, rstde.b, sgt.b, hparts[c]] + CR, writes=[tet.b, hparts[c]])
                    P.dma(SQ, lambda e: e.dma_start(out=outT[:, c, t0:t0 + T], in_=h.t[:, c, :]), "oc%d" % (c % 8), reads=[hparts[c]])
                e2(c)

        for tt in range(NTB):
            do_tile(tt)
        P.run_block()
    if own:
        B.finish()
    return nc


def _wchunks_all(w, nchunk):
    return np.ascontiguousarray(w.reshape(32, 128, nchunk, 128).transpose(2, 1, 0, 3)).reshape(nchunk, 128, 32 * 128)


def prep_B_weights(inp):
    perm = np.concatenate([np.concatenate([np.arange(512 * g, 512 * g + 512), np.arange(2048 + 512 * g, 2048 + 512 * g + 512)]) for g in range(4)])
    wout = _wchunks_all(inp["w_out"][0][perm, :], 32)
    wup = _wchunks_all(inp["w_mlp_up"][0], 2 * NFF)
    wdn = np.ascontiguousarray(inp["w_mlp_down"][0].reshape(NFF, 128, 4096))
    wple = np.ascontiguousarray(inp["w_ple_proj"][0].reshape(2, 128, 4096).transpose(1, 0, 2)).reshape(128, 8192)
    wgt = _wchunks_all(inp["w_ple_gate"][0], 32)
    NU = 2 * NFF
    cst = np.zeros((128, 64 + NU * 4 + 128), np.float32)
    cst[:, 0:32] = inp["mlp_norm_g"][0].reshape(32, 128).T
    cst[:, 32:64] = inp["ple_norm_g"][0].reshape(32, 128).T
    cw = inp["conv_w"][0].reshape(3, NU, 128).transpose(2, 1, 0)
    cst[:, 64:64 + NU * 3] = cw.reshape(128, NU * 3)
    cst[:, 64 + NU * 3:64 + NU * 4] = inp["conv_b"][0].reshape(NU, 128).T
    cst[:, 64 + NU * 4:] = 1.0
    return dict(wout=wout, wup=wup, wdn=wdn, wple=wple, wgt=wgt, cstb=cst)


def prep_B_acts(inp, S, core):
    NTOK = S // 4
    b, q = core // 4, core % 4
    lo = q * NTOK
    xe = np.zeros((NTOK + 2, 4096), np.float32)
    xe[2:] = inp["x"][b, lo:lo + NTOK]
    if q > 0:
        xe[0:2] = inp["x"][b, lo - 2:lo]
    xTb = np.ascontiguousarray(xe.T.reshape(32, 128, NTOK + 2).transpose(1, 0, 2))
    pTb = np.ascontiguousarray(inp["p"][0, b, lo:lo + NTOK].T.reshape(2, 128, NTOK).transpose(1, 0, 2))
    return dict(xTb=xTb, pTb=pTb)


def mix_for_core(resA, S, core):
    NTOK = S // 4
    b, q = core // 4, core % 4
    lo = q * NTOK
    m = np.zeros((4, 1024, NTOK + 2), np.float32)
    for g in range(4):
        src = resA[4 * b + g]["mixA"]
        m[g, :, 2:] = src[:, lo:lo + NTOK]
        if q > 0:
            m[g, :, 0:2] = src[:, lo - 2:lo]
    return m


_CACHE = {}


def kernel_unfused(**inp):
    inp = {k: np.asarray(v) for k, v in inp.items()}
    S = inp["x"].shape[1]
    NTOK = S // 4
    if ("A", S) not in _CACHE:
        _CACHE[("A", S)] = build_A(S)
        _CACHE[("B", S)] = build_B(NTOK)
    ncA, ncB = _CACHE[("A", S)], _CACHE[("B", S)]
    mapsA = prep_A(inp, S)
    resA = run_bass_kernel_spmd(ncA, mapsA, core_ids=list(range(8))).results
    del mapsA
    wB = prep_B_weights(inp)
    mapsB = []
    for core in range(8):
        d = dict(wB)
        d.update(prep_B_acts(inp, S, core))
        d["mixin"] = mix_for_core(resA, S, core)
        mapsB.append(d)
    resB = run_bass_kernel_spmd(ncB, mapsB, core_ids=list(range(8))).results
    out = np.zeros((2, S, 4096), np.float32)
    for core in range(8):
        b, q = core // 4, core % 4
        o = resB[core]["outT"]
        out[b, q * NTOK:(q + 1) * NTOK] = o.transpose(2, 1, 0).reshape(NTOK, 4096)
    return out


def declare_B_weights(B, bf16_copy=False):
    IN = "ExternalInput"
    d = dict(wout=B.dram("wout", [32, 128, 32 * 128], F32, IN),
             wup=B.dram("wup", [2 * NFF, 128, 32 * 128], F32, IN),
             wdn=B.dram("wdn", [NFF, 128, 4096], F32, IN),
             wple=B.dram("wple", [128, 2 * 4096], F32, IN),
             wgt=B.dram("wgt", [32, 128, 32 * 128], F32, IN))
    if not bf16_copy:
        return d, ()
    c = dict(wout=B.dram("wout_b", [32, 128, 4096], BF16, "Internal"),
             wup=B.dram("wup_b", [2 * NFF, 128, 4096], BF16, "Internal"),
             wdn=B.dram("wdn_b", [NFF, 128, 4096], BF16, "Internal"),
             wple=B.dram("wple_b", [128, 8192], BF16, "Internal"),
             wgt=B.dram("wgt_b", [32, 128, 4096], BF16, "Internal"))
    conv = [(c["wout"][i], d["wout"][i]) for i in range(32)]
    for f in range(NFF):
        conv += [(c["wup"][f], d["wup"][f]), (c["wup"][NFF + f], d["wup"][NFF + f])]
        if f % 2 == 1:
            conv += [(c["wdn"][f - 1], d["wdn"][f - 1]), (c["wdn"][f], d["wdn"][f])]
    conv += [(c["wple"][:, 0:4096], d["wple"][:, 0:4096]), (c["wple"][:, 4096:8192], d["wple"][:, 4096:8192])]
    conv += [(c["wgt"][i], d["wgt"][i]) for i in range(32)]
    return c, conv


def build_fused(S):
    B = Builder()
    wdecl, conv = declare_B_weights(B, bf16_copy=True)
    mixx = build_A(S, B=B, fused=True, conv=conv)
    mixg = B.dram("mixg", [S // 512, 4096, 512], BF16, "Internal")
    build_B(S // 4, mix_dram=mixg, B=B, fused=True, mixx=mixx, wdecl=wdecl)
    B.finish()
    return B.nc


def kernel(**inp):
    inp = {k: np.asarray(v) for k, v in inp.items()}
    S = inp["x"].shape[1]
    NTOK = S // 4
    if ("F", S) not in _CACHE:
        _CACHE[("F", S)] = build_fused(S)
    nc = _CACHE[("F", S)]
    maps = prep_A(inp, S)
    wB = prep_B_weights(inp)
    for core in range(8):
        maps[core].update(wB)
        maps[core].update(prep_B_acts(inp, S, core))
        qm = np.zeros((128, 4), np.float32)
        qm[:, core % 4] = 1.0
        maps[core]["qmask"] = qm
    res = run_bass_kernel_spmd(nc, maps, core_ids=list(range(8))).results
    out = np.zeros((2, S, 4096), np.float32)
    for core in range(8):
        b, q = core // 4, core % 4
        o = res[core]["outT"]
        out[b, q * NTOK:(q + 1) * NTOK] = o.transpose(2, 1, 0).reshape(NTOK, 4096)
    return out
```

```python
import contextlib
import math
import os
import numpy as np
import concourse.bass as bass
import concourse.mybir as mybir
from concourse.bass_utils import run_bass_kernel_spmd

F32 = mybir.dt.float32
BF16 = mybir.dt.bfloat16
I32 = mybir.dt.int32
AF = mybir.ActivationFunctionType
ALU = mybir.AluOpType

D_MODEL = 4096
HEAD_DIM = 64
ATTN_W = 2048
RWKV_W = 2048
D_FF = 11008
PLE_DIM = 256
NORM_EPS = 1e-6
GN_EPS = 64e-5
C0 = math.exp(-0.5)

ENGS = ("pe", "act", "dve", "pool", "sp")
EPOCH = 8000


class Buf:
    __slots__ = ("name", "w", "r")

    def __init__(self, name=""):
        self.name = name
        self.w = None
        self.r = []


class Op:
    __slots__ = ("eng", "fn", "deps", "dma", "sig", "idx", "dsem", "dcnt", "blk", "inc")

    def __init__(self, eng, fn, dma=False):
        self.eng = eng
        self.fn = fn
        self.deps = []
        self.dma = dma
        self.sig = False
        self.idx = None
        self.dsem = None
        self.dcnt = None
        self.blk = 0


class Prog:
    def __init__(self, nc):
        self.nc = nc
        self.ops = {e: [] for e in ENGS}
        self.sigcount = {e: 0 for e in ENGS}
        self.esems = {e: [] for e in ENGS}
        self.dma_sems = {}
        self.waited = {e: {} for e in ENGS}
        self._ctx = []
        self.blk = 0
        self.keybufs = {}

    def _sem(self, name):
        cm = self.nc.semaphore(name)
        s = cm.__enter__()
        self._ctx.append(cm)
        return s

    def dsem(self, key):
        if key not in self.dma_sems:
            self.dma_sems[key] = [self._sem("d_" + str(key)), 0]
        return self.dma_sems[key]

    def close(self):
        for cm in reversed(self._ctx):
            cm.__exit__(None, None, None)

    def _deps(self, op, reads, writes):
        deps = set()
        for b in reads:
            if b.w is not None:
                deps.add(b.w)
        for b in writes:
            if b.w is not None:
                deps.add(b.w)
            for r in b.r:
                deps.add(r)
        deps.discard(op)
        for d in deps:
            if op.dma or d.dma or d.eng != op.eng:
                d.sig = True
        op.deps = list(deps)
        for b in reads:
            b.r.append(op)
        for b in writes:
            b.w = op
            b.r = []

    def op(self, eng, fn, reads=(), writes=()):
        o = Op(eng, fn)
        o.blk = self.blk
        self._deps(o, reads, writes)
        self.ops[eng].append(o)
        return o

    def dma(self, eng, fn, semkey, reads=(), writes=(), inc=16):
        o = Op(eng, fn, dma=True)
        o.blk = self.blk
        o.inc = inc
        ds = self.dsem(semkey)
        ds[1] += inc
        o.dsem, o.dcnt = ds[0], ds[1]
        kb = self.keybufs.setdefault(semkey, Buf(semkey))
        self._deps(o, reads, list(writes) + [kb])
        self.ops[eng].append(o)
        return o

    def _assign(self):
        for e in ENGS:
            for o in self.ops[e]:
                if o.sig and not o.dma and o.idx is None:
                    self.sigcount[e] += 1
                    o.idx = self.sigcount[e]
            need = (self.sigcount[e] + EPOCH - 1) // EPOCH
            while len(self.esems[e]) < need:
                self.esems[e].append(self._sem("e_%s_%d" % (e, len(self.esems[e]))))

    def _wait_for(self, eng_name, eng, d):
        w = self.waited[eng_name]
        if d.blk < self.blk:
            return
        if d.dma:
            key = ("d", id(d.dsem))
            if w.get(key, 0) >= d.dcnt:
                return
            w[key] = d.dcnt
            eng.wait_ge(d.dsem, d.dcnt)
        else:
            ep, v = divmod(d.idx - 1, EPOCH)
            key = (d.eng, ep)
            if w.get(key, 0) >= v + 1:
                return
            w[key] = v + 1
            for e2 in range(ep):
                w[(d.eng, e2)] = EPOCH
            eng.wait_ge(self.esems[d.eng][ep], v + 1)

    def emit_engine(self, eng_name, eng):
        for o in self.ops[eng_name]:
            for d in sorted(o.deps, key=lambda x: (x.dma, x.idx or 0, x.dcnt or 0)):
                if (not o.dma) and (not d.dma) and d.eng == eng_name:
                    continue
                self._wait_for(eng_name, eng, d)
            ins = o.fn(eng)
            if o.dma:
                ins.then_inc(o.dsem, o.inc)
            elif o.sig:
                ep, v = divmod(o.idx - 1, EPOCH)
                ins.then_inc(self.esems[eng_name][ep], 1)

    def finish_waits(self, eng_name, eng):
        for e in ENGS:
            if self.sigcount[e] > 0:
                ep, v = divmod(self.sigcount[e] - 1, EPOCH)
                key = (e, ep)
                if self.waited[eng_name].get(key, 0) < v + 1:
                    self.waited[eng_name][key] = v + 1
                    eng.wait_ge(self.esems[e][ep], v + 1)
        for key, (sem, cnt) in self.dma_sems.items():
            k = ("d", id(sem))
            if cnt > 0 and self.waited[eng_name].get(k, 0) < cnt:
                self.waited[eng_name][k] = cnt
                eng.wait_ge(sem, cnt)

    def run_block(self):
        nc = self.nc
        for e in ENGS:
            lst = [o for o in self.ops[e] if not o.dma]
            if lst:
                lst[-1].sig = True
        self._assign()
        with nc.Block() as block:
            @block.tensor
            def _(t):
                self.emit_engine("pe", t)
                self.finish_waits("pe", t)

            @block.scalar
            def _(a):
                self.emit_engine("act", a)
                self.finish_waits("act", a)

            @block.vector
            def _(v):
                self.emit_engine("dve", v)
                self.finish_waits("dve", v)

            @block.gpsimd
            def _(g):
                self.emit_engine("pool", g)
                self.finish_waits("pool", g)

            @block.sync
            def _(s):
                self.emit_engine("sp", s)
                self.finish_waits("sp", s)
        for e in ENGS:
            self.ops[e] = []
        self.blk += 1


class TB:
    __slots__ = ("t", "b")

    def __init__(self, t, name=""):
        self.t = t
        self.b = Buf(name)


class Builder:
    def __init__(self):
        self.nc = bass.Bass("TRN2", target_bir_lowering=False)
        self.P = Prog(self.nc)
        self.es = contextlib.ExitStack()
        self.n = 0
        self.psb = []
        for i in range(8):
            t = self.es.enter_context(self.nc.psum_tensor("psb%d" % i, [128, 512], F32))
            self.psb.append(TB(t, "psb%d" % i))

    def dram(self, name, shape, dt, kind):
        return self.nc.dram_tensor(name, list(shape), dt, kind=kind).ap()

    def sb(self, es, name, shape, dt=F32):
        self.n += 1
        t = es.enter_context(self.nc.sbuf_tensor("%s_%d" % (name, self.n), list(shape), dt))
        return TB(t, name)

    def finish(self):
        self.P.close()
        self.es.close()


def _bind(f, *a):
    return lambda e: f(e, *a)


NW_A = 3
TS = 64
CH = 64


def build_A(S, phases=("A1", "A2", "A3"), debug=False, B=None, fused=False, conv=()):
    own = B is None
    if own:
        B = Builder()
    nc, P = B.nc, B.P
    T = 512
    NT = S // T
    IN, OUT, INT = "ExternalInput", "ExternalOutput", "Internal"
    xT = B.dram("xT", [128, 32, S], F32, IN)
    pos = B.dram("pos", [1, S], I32, IN)
    wA = B.dram("wA", [30, 128, 32 * 128], F32, IN)
    gA = B.dram("gA", [128, 32], F32, IN)
    c128 = B.dram("c128", [128, 8 * 128 + 256 + 4], F32, IN)
    hp_d = B.dram("hp", [64, 10 * 8], F32, IN)
    lp_d = B.dram("lp", [128, 6], F32, IN)
    wdu_d = B.dram("wdu", [128, 512], F32, IN)
    wau_d = B.dram("wau", [128, 512], F32, IN)
    wgu_d = B.dram("wgu", [128, 4 * 512], F32, IN)
    c64_d = B.dram("c64", [64, 64 + 8 * TS], F32, IN)
    if fused:
        mixA = B.dram("mixx", [S // 512, 1024, 512], BF16, INT)
        MIX = dict(ap=mixA, off=0, dt=BF16, dst=lambda r0, nr, t0, n: mixA[t0 // 512, r0:r0 + nr, t0 % 512:t0 % 512 + n])
    else:
        mixA = B.dram("mixA", [1024, S], F32, OUT)
        MIX = dict(ap=mixA, off=0, dt=F32, dst=lambda r0, nr, t0, n: mixA[r0:r0 + nr, t0:t0 + n])
    SK = OUT if debug else INT
    qk_s = B.dram("qk_s", [8, 128, S], BF16, SK)
    vv_s = B.dram("vv_s", [4, 128, S], BF16, SK)
    pr_s = B.dram("pr_s", [18, 128, S], F32, SK)

    with contextlib.ExitStack() as es0:
        cf = B.sb(es0, "cf", [128, 8 * 128 + 256 + 4], F32)
        cb = B.sb(es0, "cb", [128, 128 + 128 + 256], BF16)
        hp = B.sb(es0, "hp", [64, 10, 8], F32)
        lp = B.sb(es0, "lp", [128, 6], F32)
        c64 = B.sb(es0, "c64", [64, 64 + 8 * TS], F32)
        P.dma("sp", lambda e: e.dma_start(out=cf.t[:], in_=c128), "cf", writes=[cf.b])
        P.dma("sp", lambda e: e.dma_start(out=hp.t[:].rearrange("p a h -> p (a h)"), in_=hp_d), "hp", writes=[hp.b])
        P.dma("sp", lambda e: e.dma_start(out=lp.t[:], in_=lp_d), "lp", writes=[lp.b])
        P.dma("sp", lambda e: e.dma_start(out=c64.t[:], in_=c64_d), "c64", writes=[c64.b])
        P.dma("pool", lambda e: e.dma_start(out=cb.t[:, 0:128], in_=c128[:, 0:128]), "cb0", writes=[cb.b])
        P.dma("pool", lambda e: e.dma_start(out=cb.t[:, 128:256], in_=c128[:, 384:512]), "cb1", writes=[cb.b])
        P.dma("pool", lambda e: e.dma_start(out=cb.t[:, 256:512], in_=c128[:, 1024:1280]), "cb2", writes=[cb.b])
        ONESF = cf.t[:, 0:128]
        BD = cf.t[:, 128:256]
        ROTT = cf.t[:, 256:384]
        IDENTF = cf.t[:, 384:512]
        MASKG = cf.t[:, 512:640]
        GQ = cf.t[:, 1280:1281]
        GK = cf.t[:, 1281:1282]
        INVF = cf.t[:, 1282:1283]
        ONESB = cb.t[:, 0:128]
        IDENTB = cb.t[:, 128:256]
        MASKB = cb.t[:, 256:512]
        MASKN = c64.t[:, 0:64]
        RESETM = c64.t[:, 64:64 + 8 * TS]
        CR = [cf.b, cb.b, hp.b, lp.b, c64.b]

        if "A1" in phases:
          phase_A1(B, S, T, NT, xT, pos, wA, gA, qk_s, vv_s, pr_s,
                 dict(ONESB=ONESB, BD=BD, ROTT=ROTT, GQ=GQ, GK=GK, INVF=INVF, CR=CR))
        if "A2" in phases:
          phase_A2(B, S, qk_s, vv_s, MIX,
                 dict(ONESF=ONESF, IDENTB=IDENTB, MASKB=MASKB, CR=CR))
        if "A3" in phases:
          phase_A3(B, S, pr_s, MIX, wdu_d, wau_d, wgu_d,
                 dict(ONESF=ONESF, IDENTF=IDENTF, MASKG=MASKG, MASKN=MASKN, RESETM=RESETM,
                      hp=hp, lp=lp, CR=CR), conv=conv)
    if own:
        B.finish()
        return nc
    return mixA


def phase_A1(B, S, T, NT, xT, pos, wA, gA, qk_s, vv_s, pr_s, K):
    nc, P = B.nc, B.P
    CR = K["CR"]
    with contextlib.ExitStack() as es:
        x_sb = B.sb(es, "x", [128, 32, T], F32)
        xn = [B.sb(es, "xn%d" % i, [128, 32, T], BF16) for i in range(2)]
        w_sb = [B.sb(es, "w%d" % i, [128, 32, 128], BF16) for i in range(NW_A)]
        sqb = [B.sb(es, "sqb%d" % i, [128, T], BF16) for i in range(2)]
        rstd = B.sb(es, "rstd", [128, T], F32)
        g_sb = B.sb(es, "g", [128, 32], F32)
        posi = B.sb(es, "posi", [128, T], I32)
        cs = [dict(sin=B.sb(es, "sin%d" % i, [128, T], F32), cos=B.sb(es, "cos%d" % i, [128, T], F32)) for i in range(2)]
        tri = B.sb(es, "tri", [128, T], I32)
        sqf = [B.sb(es, "sqf%d" % i, [128, T], F32) for i in range(2)]
        rs2 = [B.sb(es, "rs2%d" % i, [128, T], F32) for i in range(2)]
        qn = [B.sb(es, "qn%d" % i, [128, T], F32) for i in range(2)]
        t1 = [B.sb(es, "t1%d" % i, [128, T], F32) for i in range(2)]
        tr = [t1[0], t1[1], rs2[0], rs2[1]]
        stb = [B.sb(es, "stb%d" % i, [128, T], BF16) for i in range(3)]
        stf = [B.sb(es, "stf%d" % i, [128, T], F32) for i in range(2)]
        P.dma("sp", lambda e: e.dma_start(out=g_sb.t[:], in_=gA), "gA", writes=[g_sb.b])
        xparts = [Buf() for _ in range(4)]

        psm = [B.psb[i] for i in range(4)]
        pss = [B.psb[i] for i in range(4, 8)]
        cnt = dict(w=0, m=0, s=0, sq=0, stb=0, stf=0, q=0)

        def prep(tt):
            t0 = tt * T
            xnb = xn[tt % 2]
            csb = cs[tt % 2]
            for q4 in range(4):
                P.dma("sp", _bind(lambda e, q4: e.dma_start(out=x_sb.t[:, 8 * q4:8 * q4 + 8, :], in_=xT[:, 8 * q4:8 * q4 + 8, t0:t0 + T]), q4),
                      "x%d" % q4, writes=[xparts[q4]])
            P.dma("sp", lambda e: e.dma_start(out=posi.t[:], in_=pos[:, t0:t0 + T].partition_broadcast(128)), "posi", writes=[posi.b])
            ssp = pss[cnt["s"] % 4]
            cnt["s"] += 1
            for kc in range(32):
                sq = sqb[cnt["sq"] % 2]
                cnt["sq"] += 1
                P.op("act", _bind(lambda e, sq, kc: e.activation(out=sq.t[:], in_=x_sb.t[:, kc, :], func=AF.Square), sq, kc),
                     reads=[xparts[kc // 8]], writes=[sq.b])
                P.op("pe", _bind(lambda e, sq, kc: e.matmul(ssp.t[:], K["ONESB"], sq.t[:], start=(kc == 0), stop=(kc == 31)), sq, kc),
                     reads=[sq.b] + CR, writes=[ssp.b])
            P.op("act", lambda e: e.activation(out=rstd.t[:], in_=ssp.t[:], func=AF.Sqrt, bias=NORM_EPS, scale=1.0 / D_MODEL),
                 reads=[ssp.b], writes=[rstd.b])
            P.op("dve", lambda e: e.reciprocal(out=rstd.t[:], in_=rstd.t[:]), reads=[rstd.b], writes=[rstd.b])
            for kc in range(32):
                P.op("dve", _bind(lambda e, kc: e.scalar_tensor_tensor(out=xnb.t[:, kc, :], in0=x_sb.t[:, kc, :], scalar=g_sb.t[:, kc:kc + 1],
                                                                       in1=rstd.t[:], op0=ALU.mult, op1=ALU.mult), kc),
                     reads=[xparts[kc // 8], rstd.b, g_sb.b], writes=[xnb.b])
            a, k_, r_, rc = tr
            TWO_PI = 2.0 * math.pi
            C1 = 6.28125
            C2 = 0.0019340515136718750
            C3 = TWO_PI - C1 - C2

            def trig(e):
                e.tensor_copy(out=a.t[:], in_=posi.t[:])
                e.tensor_scalar(out=a.t[:], in0=a.t[:], scalar1=K["INVF"], scalar2=None, op0=ALU.mult)
                e.tensor_scalar(out=k_.t[:], in0=a.t[:], scalar1=1.0 / TWO_PI, scalar2=None, op0=ALU.mult)
                e.tensor_copy(out=tri.t[:], in_=k_.t[:])
                e.tensor_copy(out=k_.t[:], in_=tri.t[:])
                e.scalar_tensor_tensor(out=r_.t[:], in0=k_.t[:], scalar=-C1, in1=a.t[:], op0=ALU.mult, op1=ALU.add)
                e.scalar_tensor_tensor(out=r_.t[:], in0=k_.t[:], scalar=-C2, in1=r_.t[:], op0=ALU.mult, op1=ALU.add)
                e.scalar_tensor_tensor(out=r_.t[:], in0=k_.t[:], scalar=-C3, in1=r_.t[:], op0=ALU.mult, op1=ALU.add)
                e.tensor_scalar(out=r_.t[:], in0=r_.t[:], scalar1=-math.pi, scalar2=math.pi, op0=ALU.max, op1=ALU.min)
                e.tensor_scalar(out=rc.t[:], in0=r_.t[:], scalar1=math.pi / 2, scalar2=None, op0=ALU.add)
                e.tensor_scalar(out=k_.t[:], in0=rc.t[:], scalar1=math.pi, scalar2=-TWO_PI, op0=ALU.is_gt, op1=ALU.mult)
                e.tensor_tensor(out=rc.t[:], in0=rc.t[:], in1=k_.t[:], op=ALU.add)
                return e.tensor_scalar(out=rc.t[:], in0=rc.t[:], scalar1=-math.pi, scalar2=math.pi, op0=ALU.max, op1=ALU.min)
            P.op("dve", trig, reads=[posi.b] + CR, writes=[tb.b for tb in tr] + [])
            P.op("act", lambda e: e.activation(out=csb["sin"].t[:], in_=r_.t[:], func=AF.Sin), reads=[r_.b], writes=[csb["sin"].b])
            P.op("act", lambda e: e.activation(out=csb["cos"].t[:], in_=rc.t[:], func=AF.Sin), reads=[rc.b], writes=[csb["cos"].b])

        pending = []

        def flush(now):
            keep = []
            for due, fn in pending:
                if due <= now:
                    fn()
                else:
                    keep.append((due, fn))
            pending[:] = keep

        def chunk(tt, j, seq):
            t0 = tt * T
            xnb = xn[tt % 2]
            csb = cs[tt % 2]
            slot = w_sb[cnt["w"] % NW_A]
            cnt["w"] += 1
            P.dma("pool", _bind(lambda e, slot, j: e.dma_start(out=slot.t[:].rearrange("p k c -> p (k c)"), in_=wA[j]), slot, j),
                  "w%d" % (cnt["w"] % NW_A), writes=[slot.b])
            ps = psm[cnt["m"] % 4]
            cnt["m"] += 1

            def mm(e, slot=slot, ps=ps):
                for kc in range(32):
                    r = e.matmul(ps.t[:], slot.t[:, kc, :], xnb.t[:, kc, :], start=(kc == 0), stop=(kc == 31))
                return r
            P.op("pe", mm, reads=[slot.b, xnb.b], writes=[ps.b])
            if j < 8:
                gv = K["GQ"] if j < 4 else K["GK"]
                i2 = cnt["q"] % 2
                cnt["q"] += 1
                sq, r2, qq, tt1 = sqf[i2], rs2[i2], qn[i2], t1[i2]
                P.op("act", lambda e: e.activation(out=sq.t[:], in_=ps.t[:], func=AF.Square), reads=[ps.b], writes=[sq.b])

                def st2():
                    hs = pss[cnt["s"] % 4]
                    cnt["s"] += 1
                    P.op("pe", lambda e: e.matmul(hs.t[:], K["BD"], sq.t[:], start=True, stop=True), reads=[sq.b] + CR, writes=[hs.b])
                    P.op("act", lambda e: e.activation(out=r2.t[:], in_=hs.t[:], func=AF.Sqrt, bias=NORM_EPS, scale=1.0 / HEAD_DIM),
                         reads=[hs.b], writes=[r2.b])
                    P.op("dve", lambda e: e.reciprocal(out=r2.t[:], in_=r2.t[:]), reads=[r2.b], writes=[r2.b])
                    P.op("dve", lambda e: e.scalar_tensor_tensor(out=qq.t[:], in0=ps.t[:], scalar=gv, in1=r2.t[:], op0=ALU.mult, op1=ALU.mult),
                         reads=[ps.b, r2.b] + CR, writes=[qq.b])

                    def st3():
                        rp = pss[cnt["s"] % 4]
                        cnt["s"] += 1
                        sb_ = stb[cnt["stb"] % 3]
                        cnt["stb"] += 1
                        P.op("pe", lambda e: e.matmul(rp.t[:], K["ROTT"], qq.t[:], start=True, stop=True), reads=[qq.b] + CR, writes=[rp.b])
                        P.op("dve", lambda e: e.tensor_tensor(out=tt1.t[:], in0=qq.t[:], in1=csb["cos"].t[:], op=ALU.mult),
                             reads=[qq.b, csb["cos"].b], writes=[tt1.b])
                        P.op("dve", lambda e: e.tensor_tensor(out=qq.t[:], in0=rp.t[:], in1=csb["sin"].t[:], op=ALU.mult),
                             reads=[rp.b, csb["sin"].b], writes=[qq.b])
                        P.op("dve", lambda e: e.tensor_tensor(out=sb_.t[:], in0=tt1.t[:], in1=qq.t[:], op=ALU.add),
                             reads=[tt1.b, qq.b], writes=[sb_.b])
                        P.dma("sp", lambda e: e.dma_start(out=qk_s[j, :, t0:t0 + T], in_=sb_.t[:]), "stb%d" % (cnt["stb"] % 3), reads=[sb_.b])
                    pending.append((seq + 2, st3))
                pending.append((seq + 1, st2))
            elif j < 12:
                sb_ = stb[cnt["stb"] % 3]
                cnt["stb"] += 1
                P.op("act", lambda e: e.copy(out=sb_.t[:], in_=ps.t[:]), reads=[ps.b], writes=[sb_.b])
                P.dma("sp", lambda e: e.dma_start(out=vv_s[j - 8, :, t0:t0 + T], in_=sb_.t[:]), "stb%d" % (cnt["stb"] % 3), reads=[sb_.b])
            else:
                sf = stf[cnt["stf"] % 2]
                cnt["stf"] += 1
                P.op("act", lambda e: e.copy(out=sf.t[:], in_=ps.t[:]), reads=[ps.b], writes=[sf.b])
                P.dma("sp", lambda e: e.dma_start(out=pr_s[j - 12, :, t0:t0 + T], in_=sf.t[:]), "stf%d" % (cnt["stf"] % 2), reads=[sf.b])

        prep(0)
        seq = 0
        for tt in range(NT):
            for j in range(30):
                chunk(tt, j, seq)
                seq += 1
                flush(seq)
                if j == 14 and tt + 1 < NT:
                    prep(tt + 1)
        flush(seq + 10)
        P.run_block()


def phase_A2(B, S, qk_s, vv_s, MIX, K):
    nc, P = B.nc, B.P
    CR = K["CR"]
    with contextlib.ExitStack() as es:
        qn_ = B.sb(es, "qn", [128, S], BF16)
        kn_ = B.sb(es, "kn", [128, S], BF16)
        vn_ = B.sb(es, "vn", [128, S], BF16)
        qd_ = B.sb(es, "qd", [128, S], BF16)
        kd_ = B.sb(es, "kd", [128, S], BF16)
        vd_ = B.sb(es, "vd", [128, S], BF16)
        acc = [B.sb(es, "acc%d" % h, [65, S], F32) for h in range(2)]
        vp = [B.sb(es, "vp%d" % i, [128, 2, 65], BF16) for i in range(3)]
        pT = [B.sb(es, "pT%d" % i, [128, 256], BF16) for i in range(4)]
        rec = [B.sb(es, "rec%d" % i, [64, 512], F32) for i in range(2)]
        ost = [B.sb(es, "ost%d" % i, [64, 512], MIX["dt"]) for i in range(2)]
        MDST = MIX["dst"]
        for v_ in vp:
            P.op("pool", _bind(lambda e, v_: e.memset(v_.t[:], 1.0), v_), writes=[v_.b])
        sc_ps = [B.psb[0], B.psb[1]]
        o_ps = [[B.psb[2 + h * 2 + i] for i in range(2)] for h in range(2)]
        vt_ps = [B.psb[6], B.psb[7]]
        fin_ps = [B.psb[0], B.psb[1]]
        c = dict(vp=0, pT=0, sc=0, vt=0, fin=0, kb=0)

        def o_ap(h, i):
            return o_ps[h][i].t[0:65, 0:128]

        def vt_ap(i):
            return vt_ps[i].t[:, 0:64].bitcast(BF16)

        def do_kb(hp_i, bi, d, r, kb, nb, M, q3, k3, v3):
            nq = 256 if kb < nb - 1 else 128
            k0 = r * M + kb * 128
            vti = c["vt"] % 2
            c["vt"] += 1
            vtb = vt_ps[vti]
            vpt = vp[c["vp"] % 3]
            c["vp"] += 1
            P.op("pe", lambda e: e.transpose(out=vt_ap(vti), in_=v3.t[:, k0:k0 + 128], identity=K["IDENTB"]),
                 reads=[v3.b] + CR, writes=[vtb.b])
            P.op("dve", lambda e: e.tensor_copy(out=vpt.t[:, :, 0:64], in_=vt_ap(vti).rearrange("p (h c) -> p h c", h=2)),
                 reads=[vtb.b], writes=[vpt.b])
            for h in range(2):
                do_head(bi, d, r, kb, nq, k0, h, q3, k3, vpt)

        def do_head(bi, d, r, kb, nq, k0, h, q3, k3, vpt):
            hs = slice(h * 64, (h + 1) * 64)
            sc = sc_ps[c["sc"] % 2]
            c["sc"] += 1
            pt = pT[c["pT"] % 4]
            c["pT"] += 1

            def scf(e):
                e.matmul(sc.t[:, 0:nq], k3.t[hs, k0:k0 + 128], q3.t[hs, k0:k0 + nq], start=True, stop=False)
                return e.matmul(sc.t[:, 0:nq], K["IDENTB"], K["MASKB"][:, 0:nq], start=False, stop=True)
            P.op("pe", scf, reads=[q3.b, k3.b] + CR, writes=[sc.b])
            P.op("act", lambda e: e.activation(out=pt.t[:, 0:nq], in_=sc.t[:, 0:nq], func=AF.Exp, scale=0.125),
                 reads=[sc.b], writes=[pt.b])
            oa = o_ps[h][kb % 2]
            ob = o_ps[h][(kb + 1) % 2]

            def pv(e):
                r_ = e.matmul(o_ap(h, kb % 2), vpt.t[:, h, :], pt.t[:, 0:128], start=(kb == 0), stop=True, skip_group_check=True)
                if nq == 256:
                    r_ = e.matmul(o_ap(h, (kb + 1) % 2), vpt.t[:, h, :], pt.t[:, 128:256], start=True, stop=False, skip_group_check=True)
                return r_
            P.op("pe", pv, reads=[pt.b, vpt.b, oa.b], writes=[oa.b] + ([ob.b] if nq == 256 else []))
            tpos = (kb * 128) * d + r

            def ev(e):
                dst = acc[h].t[:, tpos:tpos + 127 * d + 1:d] if d > 1 else acc[h].t[:, tpos:tpos + 128]
                if bi == 0:
                    return e.tensor_copy(out=dst, in_=o_ap(h, kb % 2))
                return e.tensor_tensor(out=dst, in0=dst, in1=o_ap(h, kb % 2), op=ALU.add)
            P.op("dve", ev, reads=[oa.b, acc[h].b], writes=[acc[h].b, oa.b])

        def do_branch(hp_i, bi, d):
            M = S // d
            nb = M // 128
            if d == 1:
                q3, k3, v3 = qn_, kn_, vn_
            else:
                def cp(src, dst, eng):
                    P.op(eng, lambda e: e.tensor_copy(out=dst.t[:].rearrange("p (r m) -> p r m", r=d),
                                                      in_=src.t[:].rearrange("p (m r) -> p r m", r=d)),
                         reads=[src.b], writes=[dst.b])
                cp(qn_, qd_, "dve")
                cp(kn_, kd_, "pool")
                cp(vn_, vd_, "dve")
                q3, k3, v3 = qd_, kd_, vd_
            for r in range(d):
                for kb in range(nb):
                    do_kb(hp_i, bi, d, r, kb, nb, M, q3, k3, v3)

        def do_fin(hp_i, h, s0):
            fp = fin_ps[c["fin"] % 2]
            rc_ = rec[c["fin"] % 2]
            os_ = ost[c["fin"] % 2]
            key = "ost%d" % (c["fin"] % 2)
            c["fin"] += 1
            P.op("pe", lambda e: e.matmul(fp.t[0:64, :], K["ONESF"][64:65, 0:64], acc[h].t[64:65, s0:s0 + 512], start=True, stop=True),
                 reads=[acc[h].b] + CR, writes=[fp.b])
            P.op("dve", lambda e: e.reciprocal(out=rc_.t[:], in_=fp.t[0:64, :]), reads=[fp.b], writes=[rc_.b])
            P.op("pool", lambda e: e.tensor_tensor(out=os_.t[:], in0=acc[h].t[0:64, s0:s0 + 512], in1=rc_.t[:], op=ALU.mult),
                 reads=[rc_.b, acc[h].b], writes=[os_.b])
            row = (hp_i * 2 + h) * 64
            P.dma("sp", lambda e: e.dma_start(out=MDST(row, 64, s0, 512), in_=os_.t[:]), key, reads=[os_.b])

        def do_pair(hp_i):
            P.dma("sp", lambda e: e.dma_start(out=qn_.t[:], in_=qk_s[hp_i]), "a2q", writes=[qn_.b])
            P.dma("sp", lambda e: e.dma_start(out=kn_.t[:], in_=qk_s[4 + hp_i]), "a2k", writes=[kn_.b])
            P.dma("sp", lambda e: e.dma_start(out=vn_.t[:], in_=vv_s[hp_i]), "a2v", writes=[vn_.b])
            for bi, d in enumerate((1, 4, 16)):
                do_branch(hp_i, bi, d)
            for h in range(2):
                for s0 in range(0, S, 512):
                    do_fin(hp_i, h, s0)

        for hp_i in range(4):
            do_pair(hp_i)
        P.run_block()


def phase_A3(B, S, pr_s, MIX, wdu_d, wau_d, wgu_d, K, conv=()):
    nc, P = B.nc, B.P
    CR = K["CR"]
    hp, lp = K["hp"], K["lp"]
    NS = S // TS
    NC = TS // CH
    W = TS + 1
    with contextlib.ExitStack() as es:
        def H(name, n=TS, extra=()):
            return B.sb(es, name, [64, 8] + list(extra) + [n], F32)

        def S64(name, rows=64):
            return B.sb(es, name, [rows, 8, 64], F32)
        wdu = B.sb(es, "wdu", [128, 512], F32)
        wau = B.sb(es, "wau", [128, 512], F32)
        wgu = B.sb(es, "wgu", [128, 4, 512], F32)
        P.dma("sp", lambda e: e.dma_start(out=wdu.t[:], in_=wdu_d), "wdu", writes=[wdu.b])
        P.dma("sp", lambda e: e.dma_start(out=wau.t[:], in_=wau_d), "wau", writes=[wau.b])
        P.dma("sp", lambda e: e.dma_start(out=wgu.t[:].rearrange("p k c -> p (k c)"), in_=wgu_d), "wgu", writes=[wgu.b])
        RX = [H("RX%d" % i, W) for i in range(2)]
        KX = [H("KX%d" % i, W) for i in range(2)]
        VX = [H("VX%d" % i, W) for i in range(2)]
        WX = [B.sb(es, "WX%d" % i, [128, W], F32) for i in range(2)]
        AXl = [B.sb(es, "AX%d" % i, [128, W], F32) for i in range(2)]
        GX = [B.sb(es, "GX%d" % i, [128, 4, W], F32) for i in range(2)]
        parts = {}

        def part(tb, i):
            k = (id(tb), i)
            if k not in parts:
                parts[k] = Buf()
            return parts[k]
        D1 = H("D1")
        r_ = H("r")
        k_ = H("k")
        VZ = [H("VZ%d" % i, TS, extra=(2,)) for i in range(2)]
        wdm = B.sb(es, "wdm", [128, TS], F32)
        adm = B.sb(es, "adm", [128, TS], F32)
        gdm = B.sb(es, "gdm", [128, 4, TS], F32)
        dl = B.sb(es, "dl", [128, 4, TS], F32)
        sw = H("sw")
        a_ = H("a")
        g_ = [H("g%d" % i) for i in range(2)]
        kk = H("kk")
        sq = H("sq")
        kmod = H("kmod")
        ba = H("ba")
        cs_ = H("cs")
        E1 = [H("E1%d" % i) for i in range(2)]
        E2 = H("E2")
        E3 = H("E3")
        E4 = H("E4")
        tmpH = H("tmpH")
        AR = [H("AR%d" % i, TS, extra=(2,)) for i in range(2)]
        BK = [H("BK%d" % i, TS, extra=(2,)) for i in range(2)]
        BKh = [H("BKh%d" % i, TS, extra=(2,)) for i in range(2)]
        bonus = [H("bon%d" % i) for i in range(2)]
        Y = [H("Y%d" % i) for i in range(2)]
        Gm = [B.sb(es, "Gm%d" % i, [128, 8, 128], F32) for i in range(2)]
        QN0 = [S64("QN0%d" % i) for i in range(2)]
        QP = [S64("QP%d" % i) for i in range(2)]
        QtP = [S64("QtP%d" % i) for i in range(2)]
        IQ = S64("IQ")
        X = [S64("X%d" % i) for i in range(2)]
        Atm = S64("Atm")
        BKt = [S64("BKt%d" % i, 128) for i in range(2)]
        UV = [S64("UV%d" % i, 128) for i in range(2)]
        Wsb = S64("Wsb")
        Uhat = [S64("Uhat%d" % i) for i in range(2)]
        AhT = [S64("AhT%d" % i) for i in range(2)]
        ST = [S64("ST%d" % i) for i in range(2)]
        STd = S64("STd")
        yc = H("yc")
        ysq = H("ysq")
        rsd = H("rsd")
        ostg = [B.sb(es, "ostg%d" % i, [64, 8, TS], MIX["dt"]) for i in range(2)]
        MDST = MIX["dst"]
        P.op("pool", lambda e: e.memset(ST[0].t[:], 0.0), writes=[ST[0].b])
        P.op("pool", lambda e: e.memset(VZ[0].t[:], 0.0), writes=[VZ[0].b])
        P.op("pool", lambda e: e.memset(VZ[1].t[:], 0.0), writes=[VZ[1].b])
        P.op("pool", lambda e: e.memset(UV[0].t[:], 0.0), writes=[UV[0].b])
        P.op("pool", lambda e: e.memset(UV[1].t[:], 0.0), writes=[UV[1].b])

        ONES64 = K["ONESF"][0:64, 0:64]
        ID64 = K["IDENTF"][0:64, 0:64]
        ID64B = ID64.unsqueeze(1).to_broadcast([64, 8, 64])
        MASKNB = K["MASKN"].unsqueeze(1).to_broadcast([64, 8, 64])
        MASKGB = K["MASKG"].unsqueeze(1).to_broadcast([128, 4, 128])
        st = dict(ps=0, sti=0, chunk=0)

        def nps():
            pool = st.get("pool", 0)
            key = "ps%d" % pool
            p = B.psb[pool * 4 + st.get(key, 0) % 4]
            st[key] = st.get(key, 0) + 1
            return p

        def hpv(i):
            return hp.t[:, i, :].unsqueeze(2)

        def bc(ap, n=TS):
            return ap.to_broadcast([64, 8, n])

        def pv8(p, rows=64):
            return p.t[0:rows, :].rearrange("p (h t) -> p h t", h=8)

        def mm8(out_fn, lhs_fn, rhs_fn):
            def f(e):
                for h in range(8):
                    r = e.matmul(out_fn(h), lhs_fn(h), rhs_fn(h), start=True, stop=True)
                return r
            return f

        def sum8(src, consume):
            pk = nps()
            P.op("pe", mm8(lambda h: pv8(pk)[:, h, :], lambda h: ONES64, lambda h: src.t[:, h, :]), reads=[src.b] + CR, writes=[pk.b])
            consume(pk)

        def chunk_pre(s_i, c, ar, bk, bkh, vz, e1, yy, cx):
            cs0 = c * CH
            csl = slice(cs0, cs0 + CH)
            ci = st["chunk"] % 2
            st["chunk"] += 1
            gm, qn0, bkt, uv, uh, aht = Gm[ci], QN0[ci], BKt[ci], UV[ci], Uhat[ci], AhT[ci]
            for half in range(2):
                def ghalf(half):
                    pg_ = nps()

                    def gmm(e):
                        for hh in range(4):
                            h = half * 4 + hh
                            r = e.matmul(pg_.t[:, hh * 128:(hh + 1) * 128], bk.t[:, h, :, csl], ar.t[:, h, :, csl], start=True, stop=True)
                        return r
                    P.op("pe", gmm, reads=[bk.b, ar.b], writes=[pg_.b])
                    P.op("dve", lambda e: e.tensor_tensor(out=gm.t[:, half * 4:half * 4 + 4, :], in0=pg_.t[:].rearrange("p (h t) -> p h t", h=4),
                                                          in1=MASKGB, op=ALU.mult),
                         reads=[pg_.b] + CR, writes=[gm.b])
                ghalf(half)
                yield
            pn = nps()
            P.op("pe", mm8(lambda h: pv8(pn)[:, h, :], lambda h: ar.t[:, h, 0, csl], lambda h: bk.t[:, h, 0, csl]), reads=[ar.b, bk.b], writes=[pn.b])
            yield
            P.op("dve", lambda e: e.tensor_tensor(out=qn0.t[:], in0=pv8(pn), in1=MASKNB, op=ALU.mult), reads=[pn.b] + CR, writes=[qn0.b])
            yield
            P.op("pool", lambda e: e.tensor_tensor(out=X[0].t[:], in0=gm.t[0:64, :, 0:64], in1=ID64B, op=ALU.add), reads=[gm.b] + CR, writes=[X[0].b])
            yield

            def level(lvl, q_cur, qt_ap, qt_b, x_cur):
                pq = nps()
                P.op("pe", mm8(lambda h: pv8(pq)[:, h, :], qt_ap, lambda h: q_cur.t[:, h, :]), reads=[qt_b, q_cur.b], writes=[pq.b])
                yield
                q_new = QP[lvl % 2]
                qt_new = QtP[lvl % 2]
                if lvl < 5:
                    pqt = nps()
                    P.op("pe", mm8(lambda h: pv8(pqt)[:, h, :], lambda h: q_cur.t[:, h, :], qt_ap), reads=[qt_b, q_cur.b], writes=[pqt.b])
                    P.op("act", lambda e: e.copy(out=q_new.t[:], in_=pv8(pq)), reads=[pq.b], writes=[q_new.b])
                    P.op("dve", lambda e: e.tensor_copy(out=qt_new.t[:], in_=pv8(pqt)), reads=[pqt.b], writes=[qt_new.b])
                    P.op("pool", lambda e: e.tensor_tensor(out=IQ.t[:], in0=q_new.t[:], in1=ID64B, op=ALU.add), reads=[q_new.b] + CR, writes=[IQ.b])
                else:
                    P.op("dve", lambda e: e.tensor_tensor(out=IQ.t[:], in0=pv8(pq), in1=ID64B, op=ALU.add), reads=[pq.b] + CR, writes=[IQ.b])
                px = nps()
                x_new = X[lvl % 2]
                P.op("pe", mm8(lambda h: pv8(px)[:, h, :], lambda h: IQ.t[:, h, :], lambda h: x_cur.t[:, h, :]), reads=[IQ.b, x_cur.b], writes=[px.b])
                yield
                P.op("act", lambda e: e.copy(out=x_new.t[:], in_=pv8(px)), reads=[px.b], writes=[x_new.b])
                yield
                return q_new, (lambda h: qt_new.t[:, h, :]), qt_new.b, x_new

            q_cur, qt_ap, qt_b, x_cur = qn0, (lambda h: gm.t[0:64, h, 0:64]), gm.b, X[0]
            for lvl in range(1, 6):
                q_cur, qt_ap, qt_b, x_cur = yield from level(lvl, q_cur, qt_ap, qt_b, x_cur)
            TT = x_cur
            pa_, pb_, pv_ = nps(), nps(), nps()

            def trs(e):
                for h in range(8):
                    e.transpose(out=pv8(pa_)[:, h, :], in_=ar.t[:, h, 0, csl], identity=ID64)
                for h in range(8):
                    e.transpose(out=pv8(pb_, 128)[:, h, :], in_=bkh.t[:, h, :, csl], identity=ID64)
                for h in range(8):
                    r = e.transpose(out=pv8(pv_, 128)[:, h, :], in_=vz.t[:, h, :, csl], identity=ID64)
                return r
            P.op("pe", trs, reads=[ar.b, bkh.b, vz.b] + CR, writes=[pa_.b, pb_.b, pv_.b])
            yield
            P.op("act", lambda e: e.copy(out=Atm.t[:], in_=pv8(pa_)), reads=[pa_.b], writes=[Atm.b])
            yield
            P.op("dve", lambda e: e.tensor_copy(out=bkt.t[:], in_=pv8(pb_, 128)), reads=[pb_.b], writes=[bkt.b])
            yield
            P.op("act", lambda e: e.copy(out=uv.t[64:128, :, :], in_=pv8(pv_, 128)[64:128, :, :]), reads=[pv_.b], writes=[uv.b])
            yield
            pw_ = nps()
            P.op("pe", mm8(lambda h: pv8(pw_)[:, h, :], lambda h: gm.t[64:128, h, 0:64], lambda h: uv.t[64:128, h, :]), reads=[gm.b, uv.b], writes=[pw_.b])
            yield
            P.op("act", lambda e: e.copy(out=Wsb.t[:], in_=pv8(pw_)), reads=[pw_.b], writes=[Wsb.b])
            yield
            pu_, ph_ = nps(), nps()
            P.op("pe", mm8(lambda h: pv8(pu_)[:, h, :], lambda h: TT.t[:, h, :], lambda h: Wsb.t[:, h, :]), reads=[TT.b, Wsb.b], writes=[pu_.b])
            yield
            P.op("pe", mm8(lambda h: pv8(ph_)[:, h, :], lambda h: Atm.t[:, h, :], lambda h: TT.t[:, h, :]), reads=[TT.b, Atm.b], writes=[ph_.b])
            yield
            P.op("act", lambda e: e.copy(out=uh.t[:], in_=pv8(pu_)), reads=[pu_.b], writes=[uh.b])
            yield
            P.op("dve", lambda e: e.tensor_copy(out=aht.t[:], in_=pv8(ph_)), reads=[ph_.b], writes=[aht.b])
            yield
            cx.update(gm=gm, bkt=bkt, uv=uv, uh=uh, aht=aht, csl=csl, cs0=cs0)

        def chunk_scan(cx, ar, e1, yy):
            gm, bkt, uv, uh, aht, csl, cs0 = (cx[k] for k in ("gm", "bkt", "uv", "uh", "aht", "csl", "cs0"))
            st_old = ST[st["sti"] % 2]
            st_new = ST[(st["sti"] + 1) % 2]
            st["sti"] += 1
            pc_ap = e1.t[:, :, cs0 + CH - 1:cs0 + CH]
            P.op("pool", lambda e: e.tensor_tensor(out=STd.t[:], in0=st_old.t[:], in1=pc_ap.to_broadcast([64, 8, 64]), op=ALU.mult),
                 reads=[st_old.b, e1.b], writes=[STd.b])
            yield
            pU = nps()
            P.op("pe", mm8(lambda h: pv8(pU)[:, h, :], lambda h: aht.t[:, h, :], lambda h: st_old.t[:, h, :]), reads=[aht.b, st_old.b], writes=[pU.b])
            yield
            P.op("dve", lambda e: e.tensor_tensor(out=uv.t[0:64, :, :], in0=pv8(pU), in1=uh.t[:], op=ALU.add), reads=[pU.b, uh.b], writes=[uv.b])
            yield
            pS = nps()
            P.op("pe", mm8(lambda h: pv8(pS)[:, h, :], lambda h: bkt.t[:, h, :], lambda h: uv.t[:, h, :]), reads=[bkt.b, uv.b], writes=[pS.b])
            yield
            P.op("dve", lambda e: e.tensor_tensor(out=st_new.t[:], in0=pv8(pS), in1=STd.t[:], op=ALU.add), reads=[pS.b, STd.b], writes=[st_new.b])
            yield
            pY = nps()

            def y1(e):
                for h in range(8):
                    e.matmul(pv8(pY)[:, h, :], st_old.t[:, h, :], ar.t[:, h, 1, csl], start=True, stop=False)
                    r = e.matmul(pv8(pY)[:, h, :], uv.t[:, h, :], gm.t[:, h, 64:128], start=False, stop=True)
                return r
            P.op("pe", y1, reads=[st_old.b, ar.b, uv.b, gm.b], writes=[pY.b])
            yield
            P.op("act", lambda e: e.copy(out=yy.t[:, :, csl], in_=pv8(pY)), reads=[pY.b], writes=[yy.b])
            yield

        def super_pre(s_i, sx):
            t0 = s_i * TS
            i2 = s_i % 2
            rx, kx, vx, wx, ax, gx = RX[i2], KX[i2], VX[i2], WX[i2], AXl[i2], GX[i2]
            vz, ar, bk, bkh, e1, bon, gg, yy = VZ[i2], AR[i2], BK[i2], BKh[i2], E1[i2], bonus[i2], g_[i2], Y[i2]
            lo = 1 if s_i == 0 else 0
            src0 = t0 - 1 + lo
            n = W - lo

            def ldH(dst, c0, key):
                if s_i == 0:
                    P.op("pool", lambda e: e.memset(dst.t[:, :, 0:1], 0.0), writes=[part(dst, cc) for cc in range(4)])
                for cc in range(4):
                    def one(cc):
                        P.dma("sp", lambda e: e.dma_start(out=dst.t[:, 2 * cc:2 * cc + 2, lo:W],
                                                          in_=pr_s[c0 + cc, :, src0:src0 + n].rearrange("(h p) t -> p h t", h=2)),
                              "%s%d" % (key, i2), writes=[part(dst, cc)])
                    one(cc)
            ldH(rx, 0, "rx")
            ldH(kx, 4, "kx")
            ldH(vx, 8, "vx")
            if s_i == 0:
                P.op("pool", lambda e: e.memset(wx.t[:, 0:1], 0.0), writes=[wx.b])
                P.op("pool", lambda e: e.memset(ax.t[:, 0:1], 0.0), writes=[ax.b])
                P.op("pool", lambda e: e.memset(gx.t[:, :, 0:1], 0.0), writes=[part(gx, cc) for cc in range(4)])
            P.dma("sp", lambda e: e.dma_start(out=wx.t[:, lo:W], in_=pr_s[12, :, src0:src0 + n]), "wx%d" % i2, writes=[wx.b])
            yield
            P.dma("sp", lambda e: e.dma_start(out=ax.t[:, lo:W], in_=pr_s[13, :, src0:src0 + n]), "ax%d" % i2, writes=[ax.b])
            yield
            for cc in range(4):
                def oneg(cc):
                    P.dma("sp", lambda e: e.dma_start(out=gx.t[:, cc, lo:W], in_=pr_s[14 + cc, :, src0:src0 + n]),
                          "gx%d" % i2, writes=[part(gx, cc)])
                oneg(cc)
            allp = lambda tb: [part(tb, cc) for cc in range(4)]

            def mixH(src, dst_ap, mi):
                def f(e):
                    e.tensor_tensor(out=D1.t[:], in0=src.t[:, :, 0:TS], in1=src.t[:, :, 1:W], op=ALU.subtract)
                    e.tensor_tensor(out=D1.t[:], in0=D1.t[:], in1=bc(hpv(mi)), op=ALU.mult)
                    return e.tensor_tensor(out=dst_ap, in0=D1.t[:], in1=src.t[:, :, 1:W], op=ALU.add)
                return f
            P.op("dve", mixH(rx, r_.t[:], 0), reads=allp(rx) + CR, writes=[D1.b, r_.b])
            yield
            P.op("dve", mixH(kx, k_.t[:], 1), reads=allp(kx) + CR, writes=[D1.b, k_.b])
            yield
            P.op("dve", mixH(vx, vz.t[:, :, 1, :], 2), reads=allp(vx) + CR, writes=[D1.b, vz.b])
            yield

            def mixL(e):
                e.tensor_tensor(out=dl.t[:, 0, :], in0=wx.t[:, 0:TS], in1=wx.t[:, 1:W], op=ALU.subtract)
                e.scalar_tensor_tensor(out=wdm.t[:], in0=dl.t[:, 0, :], scalar=lp.t[:, 0:1], in1=wx.t[:, 1:W], op0=ALU.mult, op1=ALU.add)
                e.tensor_tensor(out=dl.t[:, 0, :], in0=ax.t[:, 0:TS], in1=ax.t[:, 1:W], op=ALU.subtract)
                e.scalar_tensor_tensor(out=adm.t[:], in0=dl.t[:, 0, :], scalar=lp.t[:, 1:2], in1=ax.t[:, 1:W], op0=ALU.mult, op1=ALU.add)
                e.tensor_tensor(out=dl.t[:], in0=gx.t[:, :, 0:TS], in1=gx.t[:, :, 1:W], op=ALU.subtract)
                for cc in range(4):
                    r = e.scalar_tensor_tensor(out=gdm.t[:, cc, :], in0=dl.t[:, cc, :], scalar=lp.t[:, 2 + cc:3 + cc], in1=gx.t[:, cc, 1:W],
                                               op0=ALU.mult, op1=ALU.add)
                return r
            P.op("dve", mixL, reads=[wx.b, ax.b] + allp(gx) + CR, writes=[dl.b, wdm.b, adm.b, gdm.b])
            yield
            P.op("act", lambda e: e.activation(out=wdm.t[:], in_=wdm.t[:], func=AF.Tanh), reads=[wdm.b], writes=[wdm.b])
            yield
            P.op("act", lambda e: e.activation(out=gdm.t[:], in_=gdm.t[:], func=AF.Sigmoid), reads=[gdm.b], writes=[gdm.b])
            yield

            def lora_all():
                pw, pa, pg = nps(), nps(), nps()

                def lora(e):
                    for h in range(8):
                        e.matmul(pv8(pw)[:, h, :], wdu.t[:, h * 64:(h + 1) * 64], wdm.t[:], start=True, stop=True)
                    for h in range(8):
                        e.matmul(pv8(pa)[:, h, :], wau.t[:, h * 64:(h + 1) * 64], adm.t[:], start=True, stop=True)
                    for h in range(8):
                        for cc in range(4):
                            r = e.matmul(pv8(pg)[:, h, :], wgu.t[:, cc, h * 64:(h + 1) * 64], gdm.t[:, cc, :], start=(cc == 0), stop=(cc == 3))
                    return r
                P.op("pe", lora, reads=[wdu.b, wau.b, wgu.b, wdm.b, adm.b, gdm.b], writes=[pw.b, pa.b, pg.b])

                def sig(e):
                    for h in range(8):
                        e.activation(out=sw.t[:, h, :], in_=pv8(pw)[:, h, :], func=AF.Sigmoid, bias=hp.t[:, 3, h:h + 1], scale=1.0)
                    for h in range(8):
                        r = e.activation(out=a_.t[:, h, :], in_=pv8(pa)[:, h, :], func=AF.Sigmoid, bias=hp.t[:, 4, h:h + 1], scale=1.0)
                    return r
                P.op("act", sig, reads=[pw.b, pa.b] + CR, writes=[sw.b, a_.b])
                P.op("act", lambda e: e.copy(out=gg.t[:], in_=pv8(pg)), reads=[pg.b], writes=[gg.b])
            lora_all()
            yield
            P.op("dve", lambda e: e.tensor_tensor(out=kk.t[:], in0=k_.t[:], in1=bc(hpv(5)), op=ALU.mult), reads=[k_.b] + CR, writes=[kk.b])
            yield
            P.op("act", lambda e: e.activation(out=sq.t[:], in_=kk.t[:], func=AF.Square), reads=[kk.b], writes=[sq.b])
            yield

            sum8(sq, lambda pk: P.op("act", lambda e: e.activation(out=tmpH.t[:], in_=pv8(pk), func=AF.Sqrt), reads=[pk.b], writes=[tmpH.b]))
            yield

            def kkn(e):
                e.tensor_scalar(out=tmpH.t[:], in0=tmpH.t[:], scalar1=1e-12, scalar2=None, op0=ALU.max)
                e.reciprocal(out=tmpH.t[:], in_=tmpH.t[:])
                e.tensor_tensor(out=kk.t[:], in0=kk.t[:], in1=tmpH.t[:], op=ALU.mult)
                e.scalar_tensor_tensor(out=tmpH.t[:], in0=a_.t[:], scalar=-1.0, in1=bc(hpv(6)), op0=ALU.add, op1=ALU.mult)
                e.scalar_tensor_tensor(out=kmod.t[:], in0=tmpH.t[:], scalar=1.0, in1=k_.t[:], op0=ALU.add, op1=ALU.mult)
                return e.tensor_tensor(out=ba.t[:], in0=kk.t[:], in1=a_.t[:], op=ALU.mult)
            P.op("dve", kkn, reads=[tmpH.b, kk.b, a_.b, k_.b] + CR, writes=[tmpH.b, kk.b, kmod.b, ba.b])
            yield
            flat = lambda t: t.t[:].rearrange("p h t -> p (h t)")
            P.op("dve", lambda e: e.tensor_tensor_scan(out=flat(cs_), data0=K["RESETM"], data1=flat(sw), initial=0.0, op0=ALU.mult, op1=ALU.add),
                 reads=[sw.b] + CR, writes=[cs_.b])
            yield
            P.op("act", lambda e: e.activation(out=e1.t[:], in_=cs_.t[:], func=AF.Exp, scale=-C0), reads=[cs_.b], writes=[e1.b])
            yield
            P.op("act", lambda e: e.activation(out=E2.t[:], in_=cs_.t[:], func=AF.Exp, scale=C0), reads=[cs_.b], writes=[E2.b])
            yield
            P.op("pool", lambda e: e.tensor_tensor(out=E3.t[:], in0=cs_.t[:], in1=sw.t[:], op=ALU.subtract), reads=[cs_.b, sw.b], writes=[E3.b])
            yield
            P.op("act", lambda e: e.activation(out=E3.t[:], in_=E3.t[:], func=AF.Exp, scale=-C0), reads=[E3.b], writes=[E3.b])
            yield

            def e4f(e):
                c4 = cs_.t[:].rearrange("p h (c t) -> p h c t", t=CH)
                return e.tensor_tensor(out=E4.t[:].rearrange("p h (c t) -> p h c t", t=CH), in0=c4,
                                       in1=c4[:, :, :, CH - 1:CH].to_broadcast([64, 8, NC, CH]), op=ALU.subtract)
            P.op("pool", e4f, reads=[cs_.b], writes=[E4.b])
            yield
            P.op("act", lambda e: e.activation(out=E4.t[:], in_=E4.t[:], func=AF.Exp, scale=C0), reads=[E4.b], writes=[E4.b])
            yield

            def tild(e):
                e.tensor_tensor(out=ar.t[:, :, 1, :], in0=r_.t[:], in1=e1.t[:], op=ALU.mult)
                e.scalar_tensor_tensor(out=ar.t[:, :, 0, :], in0=kk.t[:], scalar=-1.0, in1=E3.t[:], op0=ALU.mult, op1=ALU.mult)
                e.tensor_tensor(out=bk.t[:, :, 0, :], in0=ba.t[:], in1=E2.t[:], op=ALU.mult)
                return e.tensor_tensor(out=bk.t[:, :, 1, :], in0=kmod.t[:], in1=E2.t[:], op=ALU.mult)
            P.op("dve", tild, reads=[r_.b, e1.b, kk.b, E3.b, ba.b, E2.b, kmod.b], writes=[ar.b, bk.b])
            yield

            def hatf(e):
                e.tensor_tensor(out=bkh.t[:, :, 0, :], in0=ba.t[:], in1=E4.t[:], op=ALU.mult)
                return e.tensor_tensor(out=bkh.t[:, :, 1, :], in0=kmod.t[:], in1=E4.t[:], op=ALU.mult)
            P.op("pool", hatf, reads=[ba.b, kmod.b, E4.b], writes=[bkh.b])
            yield

            def rkf(e):
                e.tensor_tensor(out=tmpH.t[:], in0=r_.t[:], in1=kmod.t[:], op=ALU.mult)
                return e.tensor_tensor(out=sq.t[:], in0=tmpH.t[:], in1=bc(hpv(7)), op=ALU.mult)
            P.op("pool", rkf, reads=[r_.b, kmod.b, tmpH.b, sq.b] + CR, writes=[tmpH.b, sq.b])
            yield
            sum8(sq, lambda pb: P.op("dve", lambda e: e.tensor_tensor(out=bon.t[:], in0=pv8(pb), in1=vz.t[:, :, 1, :], op=ALU.mult),
                                     reads=[pb.b, vz.b], writes=[bon.b]))
            yield
            cxs = []
            for c in range(NC):
                cx = {}
                yield from chunk_pre(s_i, c, ar, bk, bkh, vz, e1, yy, cx)
                cxs.append(cx)
            sx.update(cxs=cxs, ar=ar, e1=e1, yy=yy, bon=bon, gg=gg, i2=i2, t0=t0)

        def super_post(s_i, sx):
            ar, e1, yy, bon, gg, i2, t0 = (sx[k] for k in ("ar", "e1", "yy", "bon", "gg", "i2", "t0"))
            for cx in sx["cxs"]:
                yield from chunk_scan(cx, ar, e1, yy)
            sum8(yy, lambda pm: P.op("dve", lambda e: e.scalar_tensor_tensor(out=yc.t[:], in0=pv8(pm), scalar=-1.0 / 64, in1=yy.t[:],
                                                                             op0=ALU.mult, op1=ALU.add),
                                     reads=[pm.b, yy.b], writes=[yc.b]))
            yield
            P.op("act", lambda e: e.activation(out=ysq.t[:], in_=yc.t[:], func=AF.Square), reads=[yc.b], writes=[ysq.b])
            yield
            sum8(ysq, lambda pvv: P.op("act", lambda e: e.activation(out=rsd.t[:], in_=pv8(pvv), func=AF.Sqrt, bias=GN_EPS, scale=1.0 / 64),
                                       reads=[pvv.b], writes=[rsd.b]))
            yield
            og = ostg[i2]

            def fin(e):
                e.reciprocal(out=rsd.t[:], in_=rsd.t[:])
                e.tensor_tensor(out=yc.t[:], in0=yc.t[:], in1=rsd.t[:], op=ALU.mult)
                e.tensor_tensor(out=yc.t[:], in0=yc.t[:], in1=bc(hpv(8)), op=ALU.mult)
                e.tensor_tensor(out=yc.t[:], in0=yc.t[:], in1=bc(hpv(9)), op=ALU.add)
                e.tensor_tensor(out=yc.t[:], in0=yc.t[:], in1=bon.t[:], op=ALU.add)
                return e.tensor_tensor(out=og.t[:], in0=yc.t[:], in1=gg.t[:], op=ALU.mult)
            P.op("dve", fin, reads=[rsd.b, yc.b, bon.b, gg.b] + CR, writes=[rsd.b, yc.b, og.b])
            yield
            P.dma("sp", lambda e: e.dma_start(out=MDST(512, 512, t0, TS).rearrange("(h p) t -> p h t", h=8), in_=og.t[:]),
                  "ostg%d" % i2, reads=[og.b])
            yield

        conv = list(conv)
        per = -(-len(conv) // NS) if conv else 0
        cvi = 0
        def drive(items):
            alive = list(items)
            while alive:
                for it in list(alive):
                    st["pool"] = it[0]
                    try:
                        next(it[1])
                    except StopIteration:
                        alive.remove(it)
        sxs = {}
        for s_i in range(NS + 1):
            items = []
            if s_i < NS:
                sxs[s_i] = {}
                items.append((0, super_pre(s_i, sxs[s_i])))
            if s_i >= 1:
                items.append((1, super_post(s_i - 1, sxs.pop(s_i - 1))))
            drive(items)
            if s_i >= NS:
                continue
            for _ in range(per):
                if cvi < len(conv):
                    def cvt(k):
                        dst, src = conv[k]
                        P.dma("pool", lambda e: e.dma_start(out=dst, in_=src), "cv%d" % (k % 8))
                    cvt(cvi)
                    cvi += 1
        P.run_block()


def _consts_A(qg, kg):
    c = np.zeros((128, 8 * 128 + 256 + 4), np.float32)
    p = np.arange(128)
    c[:, 0:128] = 1.0
    c[:, 128:256] = (p[:, None] // 64 == p[None, :] // 64).astype(np.float32)
    rot = np.zeros((128, 128), np.float32)
    for m in range(128):
        if m % 64 < 32:
            rot[m + 32, m] = -1.0
        else:
            rot[m - 32, m] = 1.0
    c[:, 256:384] = rot
    c[:, 384:512] = np.eye(128, dtype=np.float32)
    i = (p % 64)[:, None]
    t = np.arange(64)[None, :]
    c[:, 512:576] = (i < t).astype(np.float32)
    c[:, 576:640] = (i <= t).astype(np.float32)
    kk = p[:, None]
    qq = np.arange(256)[None, :]
    dist = qq - kk
    c[:, 1024:1280] = np.where((dist >= 0) & (dist <= 128), 0.0, -262144.0)
    c[:, 1280] = np.tile(qg, 2)
    c[:, 1281] = np.tile(kg, 2)
    c[:, 1282] = (np.float32(10000.0) ** (-(np.arange(32, dtype=np.float32)) / np.float32(32)))[p % 32]
    return c


def _consts_64():
    c = np.zeros((64, 64 + 8 * TS), np.float32)
    t = np.arange(64)
    c[:, 0:64] = (t[:, None] > t[None, :]).astype(np.float32)
    m = np.ones((8, TS), np.float32)
    m[:, ::CH] = 0.0
    c[:, 64:] = m.reshape(1, -1)
    return c


def _wchunk(w, cols):
    blk = np.zeros((4096, 128), np.float32)
    blk[:, :len(cols)] = w[:, cols]
    return np.ascontiguousarray(blk.reshape(32, 128, 128).transpose(1, 0, 2)).reshape(128, 32 * 128)


def prep_A(inp, S):
    x = inp["x"]
    w_in = inp["w_in"][0]
    mu = inp["rwkv_mu"][0]
    maps = []
    xTs = [np.ascontiguousarray(x[b].T.reshape(32, 128, S).transpose(1, 0, 2)) for b in range(2)]
    cA = _consts_A(inp["q_norm_g"][0], inp["k_norm_g"][0])
    c64 = _consts_64()
    gA = np.ascontiguousarray(inp["attn_norm_g"][0].reshape(32, 128).T)
    RB = 3 * ATTN_W
    for core in range(8):
        b, g = core // 4, core % 4
        cols = []
        for base in (0, 2048, 4096, RB, RB + 2048, RB + 4096):
            for jj in range(4):
                cols.append(np.arange(base + 512 * g + 128 * jj, base + 512 * g + 128 * jj + 128))
        cols.append(np.arange(RB + 6144, RB + 6272))
        cols.append(np.arange(RB + 6272, RB + 6400))
        for jj in range(4):
            lo = RB + 6400 + 128 * jj
            cols.append(np.arange(lo, min(lo + 128, RB + 6880)))
        wA = np.stack([_wchunk(w_in, cc) for cc in cols])

        def hsl(v):
            return v[512 * g:512 * g + 512].reshape(8, 64).T
        hp = np.zeros((64, 10, 8), np.float32)
        hp[:, 0] = hsl(mu[0:2048])
        hp[:, 1] = hsl(mu[2048:4096])
        hp[:, 2] = hsl(mu[4096:6144])
        hp[:, 3] = hsl(inp["w0"][0])
        hp[:, 4] = hsl(inp["a0"][0])
        hp[:, 5] = hsl(inp["k_k"][0])
        hp[:, 6] = hsl(inp["k_a"][0])
        hp[:, 7] = hsl(inp["r_k"][0].reshape(-1))
        hp[:, 8] = hsl(inp["ln_x_w"][0])
        hp[:, 9] = hsl(inp["ln_x_b"][0])
        lp = np.zeros((128, 6), np.float32)
        lp[:, 0] = mu[6144:6272]
        lp[:, 1] = mu[6272:6400]
        mg = np.zeros(512, np.float32)
        mg[:480] = mu[6400:6880]
        lp[:, 2:6] = mg.reshape(4, 128).T
        wg = np.zeros((512, 512), np.float32)
        wg[:480] = inp["w_gate_up"][0][:, 512 * g:512 * g + 512]
        maps.append(dict(
            xT=xTs[b], pos=np.ascontiguousarray(inp["positions"][b][None, :].astype(np.int32)),
            wA=wA, gA=gA, c128=cA, hp=np.ascontiguousarray(hp.reshape(64, 80)), lp=lp,
            wdu=np.ascontiguousarray(inp["w_decay_up"][0][:, 512 * g:512 * g + 512]),
            wau=np.ascontiguousarray(inp["w_iclr_up"][0][:, 512 * g:512 * g + 512]),
            wgu=np.ascontiguousarray(wg.reshape(4, 128, 512).transpose(1, 0, 2)).reshape(128, 2048),
            c64=c64))
    return maps


def gather_mix(resA, S):
    mixT = np.zeros((2, 4096, S), np.float32)
    for core in range(8):
        b, g = core // 4, core % 4
        m = resA[core]["mixA"]
        mixT[b, 512 * g:512 * g + 512] = m[0:512]
        mixT[b, 2048 + 512 * g:2048 + 512 * g + 512] = m[512:1024]
    return mixT


NWB = 3
MPAD = 64
NFF = D_FF // 128


def build_B(NTOK, mix_dram=None, B=None, fused=False, mixx=None, wdecl=None):
    own = B is None
    if own:
        B = Builder()
    nc, P = B.nc, B.P
    T = 512
    NTB = NTOK // T
    IN, OUT = "ExternalInput", "ExternalOutput"
    if fused:
        qm_d = B.dram("qmask", [128, 4], F32, IN)
    elif mix_dram is None:
        mix_dram = B.dram("mixin", [4, 1024, NTOK + 2], F32, IN)
    xT = B.dram("xTb", [128, 32, NTOK + 2], F32, IN)
    pT = B.dram("pTb", [128, 2, NTOK], F32, IN)
    if wdecl is None:
        wdecl = declare_B_weights(B)[0]
    wout, wup, wdn, wple, wgt = (wdecl[k] for k in ("wout", "wup", "wdn", "wple", "wgt"))
    WQ = "sp" if fused else "pool"
    SQ = "pool" if fused else "sp"
    cst = B.dram("cstb", [128, 64 + 2 * NFF * 4 + 128], F32, IN)
    outT = B.dram("outT", [128, 32, NTOK], F32, OUT)
    NU = 2 * NFF

    with contextlib.ExitStack() as es:
        c_sb = B.sb(es, "cstb", [128, 64 + NU * 4 + 128], F32)
        ones_b = B.sb(es, "onesb", [128, 128], BF16)
        P.dma("sp", lambda e: e.dma_start(out=c_sb.t[:], in_=cst), "cstb", writes=[c_sb.b])
        P.dma("pool", lambda e: e.dma_start(out=ones_b.t[:], in_=cst[:, 64 + NU * 4:64 + NU * 4 + 128]), "onesb", writes=[ones_b.b])
        GM = c_sb.t[:, 0:32]
        GP = c_sb.t[:, 32:64]
        CW = c_sb.t[:, 64:64 + NU * 3].rearrange("p (u j) -> p u j", j=3)
        CB = c_sb.t[:, 64 + NU * 3:64 + NU * 4]
        CR = [c_sb.b, ones_b.b]

        h = B.sb(es, "h", [128, 32, T], F32)
        a16 = B.sb(es, "a16", [128, 32, T], BF16)
        hh = B.sb(es, "hh", [128, 32, 2], F32)
        a16h = B.sb(es, "a16h", [128, 32, 2], BF16)
        wr = [B.sb(es, "wr%d" % i, [128, 32, 128], BF16) for i in range(NWB)]
        wd = [B.sb(es, "wd%d" % i, [128, 4096], BF16) for i in range(2)]
        ue = [[B.sb(es, "ue%d%d" % (i, j), [128, T + 2], F32) for j in range(2)] for i in range(2)]
        cv = [[B.sb(es, "cv%d%d" % (i, j), [128, T], F32) for j in range(2)] for i in range(2)]
        actg = [B.sb(es, "actg%d" % i, [128, 2, T], BF16) for i in range(2)]
        uhalo = B.sb(es, "uhalo", [128, NU, 2], F32)
        rstd = B.sb(es, "rstdb", [128, T], F32)
        rstdh = B.sb(es, "rstdh", [128, 2], F32)
        rstde = B.sb(es, "rstde", [128, T], F32)
        p16 = B.sb(es, "p16", [128, 2, T], BF16)
        sqb = [B.sb(es, "sqbb%d" % i, [128, T], BF16) for i in range(2)]
        sqh = B.sb(es, "sqh", [128, 2], BF16)
        sg = [B.sb(es, "sg%d" % i, [128, T], F32) for i in range(2)]
        te = [B.sb(es, "te%d" % i, [128, T], F32) for i in range(2)]
        hparts = [Buf() for _ in range(32)]
        mixg_b = Buf("mixg")
        if fused:
            qm = B.sb(es, "qm", [128, 4], F32)
            cand = [B.sb(es, "cand%d" % i, [128, 4, T], BF16) for i in range(4)]
            candh = [B.sb(es, "candh%d" % i, [128, 32, 2], BF16) for i in range(4)]
            P.dma("sp", lambda e: e.dma_start(out=qm.t[:], in_=qm_d), "qm", writes=[qm.b])
            NCHK = 4 * NTOK // 512
            chunk_b = [Buf("mixg%d" % i) for i in range(NCHK)]
            order = [qq * (NTOK // 512) + tt for tt in range(NTOK // 512) for qq in range(4)]
            for ci in order:
                def ag(ci):
                    P.dma("pool", lambda e: e.collective_compute("AllGather", ALU.bypass, replica_groups=[[0, 1, 2, 3], [4, 5, 6, 7]],
                                                                 ins=[mixx[ci]], outs=[mix_dram[ci]]), "ag", writes=[chunk_b[ci]], inc=1)
                ag(ci)
        psm = [B.psb[i] for i in range(6)]
        pss = [B.psb[6], B.psb[7]]
        cnt = dict(w=0, d=0, m=0, s=0, sq=0, ue=0, ag=0, sg=0)

        def wslot():
            s_ = wr[cnt["w"] % NWB]
            key = "wr%d" % (cnt["w"] % NWB)
            cnt["w"] += 1
            return s_, key

        def pmain():
            p = psm[cnt["m"] % 6]
            cnt["m"] += 1
            return p

        def psmall():
            p = pss[cnt["s"] % 2]
            cnt["s"] += 1
            return p

        def load_w(src_ap):
            s_, key = wslot()
            P.dma(WQ, lambda e: e.dma_start(out=s_.t[:].rearrange("p k c -> p (k c)"), in_=src_ap), key, writes=[s_.b])
            return s_

        def mm32(ps_ap, s_, rhs_fn):
            def f(e):
                for kc in range(32):
                    r = e.matmul(ps_ap, s_.t[:, kc, :], rhs_fn(kc), start=(kc == 0), stop=(kc == 31))
                return r
            return f

        def rms(src, src_tokens, n, dst16, g_ap, rs, halo):
            ssp = psmall()
            for kc in range(32):
                def one(kc):
                    sq = sqh if halo else sqb[cnt["sq"] % 2]
                    cnt["sq"] += 1
                    sqa = sq.t[:, 0:n]
                    P.op("act", lambda e: e.activation(out=sqa, in_=src.t[:, kc, 0:n], func=AF.Square), reads=[src_tokens[kc]], writes=[sq.b])
                    P.op("pe", lambda e: e.matmul(ssp.t[:, 0:n], ones_b.t[:], sqa, start=(kc == 0), stop=(kc == 31)), reads=[sq.b] + CR, writes=[ssp.b])
                one(kc)
            P.op("act", lambda e: e.activation(out=rs.t[:, 0:n], in_=ssp.t[:, 0:n], func=AF.Sqrt, bias=NORM_EPS, scale=1.0 / D_MODEL), reads=[ssp.b], writes=[rs.b])
            P.op("dve", lambda e: e.reciprocal(out=rs.t[:, 0:n], in_=rs.t[:, 0:n]), reads=[rs.b], writes=[rs.b])
            for kc in range(32):
                def two(kc):
                    P.op("dve", lambda e: e.scalar_tensor_tensor(out=dst16.t[:, kc, 0:n], in0=src.t[:, kc, 0:n], scalar=g_ap[:, kc:kc + 1], in1=rs.t[:, 0:n],
                                                                 op0=ALU.mult, op1=ALU.mult),
                         reads=[src_tokens[kc], rs.b] + CR, writes=[dst16.b])
                two(kc)

        def do_tile(tt):
            t0 = tt * T
            first = tt == 0
            hhp = [hh.b] * 32
            for q4 in range(4):
                def ld(q4):
                    P.dma("sp", lambda e: e.dma_start(out=h.t[:, 8 * q4:8 * q4 + 8, :], in_=xT[:, 8 * q4:8 * q4 + 8, 2 + t0:2 + t0 + T]),
                          "hx%d" % q4, writes=hparts[8 * q4:8 * q4 + 8])
                    if not fused:
                        P.dma("pool", lambda e: e.dma_start(out=a16.t[:, 8 * q4:8 * q4 + 8, :],
                                                            in_=mix_dram[q4, :, 2 + t0:2 + t0 + T].rearrange("(k p) t -> p k t", p=128)),
                              "mx%d" % q4, writes=[a16.b])
                ld(q4)
            if fused:
                for g8 in range(8):
                    def selg(g8):
                        for qq in range(4):
                            def ldc(qq):
                                ci = (qq * NTOK + t0) // 512
                                P.dma("sp", lambda e: e.dma_start(out=cand[qq].t[:], in_=mix_dram[ci, g8 * 512:(g8 + 1) * 512, :].rearrange("(k p) t -> p k t", p=128)),
                                      "cd%d" % qq, reads=[chunk_b[ci]], writes=[cand[qq].b])
                            ldc(qq)
                        dst = a16.t[:, 4 * g8:4 * g8 + 4, :]

                        def sel(e):
                            r = e.tensor_scalar(out=dst, in0=cand[0].t[:], scalar1=qm.t[:, 0:1], scalar2=None, op0=ALU.mult)
                            for qq in range(1, 4):
                                r = e.scalar_tensor_tensor(out=dst, in0=cand[qq].t[:], scalar=qm.t[:, qq:qq + 1], in1=dst, op0=ALU.mult, op1=ALU.add)
                            return r
                        P.op("dve", sel, reads=[c_.b for c_ in cand] + [qm.b], writes=[a16.b])
                    selg(g8)
            P.dma("pool", lambda e: e.dma_start(out=p16.t[:], in_=pT[:, :, t0:t0 + T]), "p16", writes=[p16.b])
            if first:
                P.dma("sp", lambda e: e.dma_start(out=hh.t[:], in_=xT[:, :, 0:2]), "hhx", writes=[hh.b])
                if fused:
                    P.op("pool", lambda e: e.memset(candh[0].t[:], 0.0), writes=[candh[0].b])
                    for qq in range(1, 4):
                        def ldch(qq):
                            ci = qq * NTOK // 512 - 1
                            P.dma("sp", lambda e: e.dma_start(out=candh[qq].t[:], in_=mix_dram[ci, :, 510:512].rearrange("(k p) t -> p k t", p=128)),
                                  "cdh%d" % qq, reads=[chunk_b[ci]], writes=[candh[qq].b])
                        ldch(qq)

                    def selh(e):
                        r = e.tensor_scalar(out=a16h.t[:], in0=candh[0].t[:], scalar1=qm.t[:, 0:1], scalar2=None, op0=ALU.mult)
                        for qq in range(1, 4):
                            r = e.scalar_tensor_tensor(out=a16h.t[:], in0=candh[qq].t[:], scalar=qm.t[:, qq:qq + 1], in1=a16h.t[:], op0=ALU.mult, op1=ALU.add)
                        return r
                    P.op("dve", selh, reads=[c_.b for c_ in candh] + [qm.b], writes=[a16h.b])
                else:
                    for q4 in range(4):
                        def ldh(q4):
                            P.dma("pool", lambda e: e.dma_start(out=a16h.t[:, 8 * q4:8 * q4 + 8, :],
                                                                in_=mix_dram[q4, :, 0:2].rearrange("(k p) t -> p k t", p=128)),
                                  "mxh%d" % q4, writes=[a16h.b])
                        ldh(q4)
            for c in range(32):
                def oc(c):
                    s_ = load_w(wout[c])
                    ps = pmain()
                    P.op("pe", mm32(ps.t[:], s_, lambda kc: a16.t[:, kc, :]), reads=[s_.b, a16.b], writes=[ps.b])
                    P.op("dve", lambda e: e.tensor_tensor(out=h.t[:, c, :], in0=ps.t[:], in1=h.t[:, c, :], op=ALU.add), reads=[ps.b, hparts[c]], writes=[hparts[c]])
                    if first:
                        ph = psmall()
                        P.op("pe", mm32(ph.t[:, 0:2], s_, lambda kc: a16h.t[:, kc, :]), reads=[s_.b, a16h.b], writes=[ph.b])
                        P.op("dve", lambda e: e.tensor_tensor(out=hh.t[:, c, :], in0=ph.t[:, 0:2], in1=hh.t[:, c, :], op=ALU.add), reads=[ph.b, hh.b], writes=[hh.b])
                oc(c)
            BD_ = int(os.environ.get('B_DBG', '100'))

            def dump():
                for c in range(32):
                    P.dma("sp", _bind(lambda e, c: e.dma_start(out=outT[:, c, t0:t0 + T], in_=h.t[:, c, :]), c), "oc%d" % (c % 8), reads=[hparts[c]])
            if BD_ == 1:
                dump()
                return
            rms(h, hparts, T, a16, GM, rstd, False)
            if first:
                rms(hh, hhp, 2, a16h, GM, rstdh, True)
            def do_pairf(f0):
                ag = actg[cnt["ag"] % 2]
                cnt["ag"] += 1
                for fi in range(2):
                    def ff(f, fi):
                        cvs = []
                        for gi in range(2):
                            def gu(gi):
                                uc = gi * NFF + f
                                s_ = load_w(wup[uc])
                                ps = pmain()
                                P.op("pe", mm32(ps.t[:], s_, lambda kc: a16.t[:, kc, :]), reads=[s_.b, a16.b], writes=[ps.b])
                                u = ue[gi][cnt["ue"] % 2]
                                c1 = cv[gi][cnt["ue"] % 2]
                                if first:
                                    ph = psmall()
                                    P.op("pe", mm32(ph.t[:, 0:2], s_, lambda kc: a16h.t[:, kc, :]), reads=[s_.b, a16h.b], writes=[ph.b])
                                    P.op("dve", lambda e: e.tensor_copy(out=u.t[:, 0:2], in_=ph.t[:, 0:2]), reads=[ph.b], writes=[u.b])
                                else:
                                    P.op("dve", lambda e: e.tensor_copy(out=u.t[:, 0:2], in_=uhalo.t[:, uc, :]), reads=[uhalo.b], writes=[u.b])
                                P.op("act", lambda e: e.copy(out=u.t[:, 2:T + 2], in_=ps.t[:]), reads=[ps.b], writes=[u.b])
                                P.op("dve", lambda e: e.tensor_copy(out=uhalo.t[:, uc, :], in_=u.t[:, T:T + 2]), reads=[u.b], writes=[uhalo.b])
                                P.op("act", lambda e: e.activation(out=c1.t[:], in_=u.t[:, 2:T + 2], func=AF.Identity, bias=CB[:, uc:uc + 1], scale=CW[:, uc, 2:3]),
                                     reads=[u.b] + CR, writes=[c1.b])

                                def cvf(e):
                                    e.scalar_tensor_tensor(out=c1.t[:], in0=u.t[:, 1:T + 1], scalar=CW[:, uc, 1:2], in1=c1.t[:], op0=ALU.mult, op1=ALU.add)
                                    return e.scalar_tensor_tensor(out=c1.t[:], in0=u.t[:, 0:T], scalar=CW[:, uc, 0:1], in1=c1.t[:], op0=ALU.mult, op1=ALU.add)
                                P.op("dve", cvf, reads=[u.b, c1.b] + CR, writes=[c1.b])
                                cvs.append(c1)
                            gu(gi)
                        cnt["ue"] += 1
                        sgt = sg[cnt["sg"] % 2]
                        cnt["sg"] += 1
                        P.op("act", lambda e: e.activation(out=sgt.t[:], in_=cvs[0].t[:], func=AF.Silu), reads=[cvs[0].b], writes=[sgt.b])
                        P.op("dve", lambda e: e.tensor_tensor(out=ag.t[:, fi, :], in0=sgt.t[:], in1=cvs[1].t[:], op=ALU.mult), reads=[sgt.b, cvs[1].b], writes=[ag.b])
                    ff(f0 + fi, fi)
                wds = []
                for fi in range(2):
                    def ldd(fi):
                        s_ = wd[fi]
                        P.dma(WQ, lambda e: e.dma_start(out=s_.t[:], in_=wdn[f0 + fi]), "wd%d" % fi, writes=[s_.b])
                        wds.append(s_)
                    ldd(fi)
                for c in range(32):
                    def dc(c):
                        ps = pmain()

                        def f(e):
                            e.matmul(ps.t[:], wds[0].t[:, c * 128:(c + 1) * 128], ag.t[:, 0, :], start=True, stop=False)
                            return e.matmul(ps.t[:], wds[1].t[:, c * 128:(c + 1) * 128], ag.t[:, 1, :], start=False, stop=True)
                        P.op("pe", f, reads=[wds[0].b, wds[1].b, ag.b], writes=[ps.b])
                        P.op("dve", lambda e: e.tensor_tensor(out=h.t[:, c, :], in0=ps.t[:], in1=h.t[:, c, :], op=ALU.add), reads=[ps.b, hparts[c]], writes=[hparts[c]])
                    dc(c)
            for f0 in range(0, NFF, 2):
                do_pairf(f0)
            if BD_ == 2:
                dump()
                return
            for c in range(32):
                P.op("act", _bind(lambda e, c: e.copy(out=a16.t[:, c, :], in_=h.t[:, c, :]), c), reads=[hparts[c]], writes=[a16.b])
            P.dma(WQ, lambda e: e.dma_start(out=wd[0].t[:], in_=wple[:, 0:4096]), "wd0", writes=[wd[0].b])
            P.dma(WQ, lambda e: e.dma_start(out=wd[1].t[:], in_=wple[:, 4096:8192]), "wd1", writes=[wd[1].b])
            ssp = psmall()
            for c in range(32):
                def e1(c):
                    ps = pmain()

                    def f(e):
                        e.matmul(ps.t[:], wd[0].t[:, c * 128:(c + 1) * 128], p16.t[:, 0, :], start=True, stop=False)
                        return e.matmul(ps.t[:], wd[1].t[:, c * 128:(c + 1) * 128], p16.t[:, 1, :], start=False, stop=True)
                    P.op("pe", f, reads=[wd[0].b, wd[1].b, p16.b], writes=[ps.b])
                    sq = sqb[cnt["sq"] % 2]
                    cnt["sq"] += 1
                    P.op("act", lambda e: e.activation(out=sq.t[:], in_=ps.t[:], func=AF.Square), reads=[ps.b], writes=[sq.b])
                    P.op("pe", lambda e: e.matmul(ssp.t[:], ones_b.t[:], sq.t[:], start=(c == 0), stop=(c == 31)), reads=[sq.b] + CR, writes=[ssp.b])
                e1(c)
            P.op("act", lambda e: e.activation(out=rstde.t[:], in_=ssp.t[:], func=AF.Sqrt, bias=NORM_EPS, scale=1.0 / D_MODEL), reads=[ssp.b], writes=[rstde.b])
            P.op("dve", lambda e: e.reciprocal(out=rstde.t[:], in_=rstde.t[:]), reads=[rstde.b], writes=[rstde.b])
            for c in range(32):
                def e2(c):
                    s_ = load_w(wgt[c])
                    pg = pmain()
                    P.op("pe", mm32(pg.t[:], s_, lambda kc: a16.t[:, kc, :]), reads=[s_.b, a16.b], writes=[pg.b])
                    pe_ = pmain()

                    def f(e):
                        e.matmul(pe_.t[:], wd[0].t[:, c * 128:(c + 1) * 128], p16.t[:, 0, :], start=True, stop=False)
                        return e.matmul(pe_.t[:], wd[1].t[:, c * 128:(c + 1) * 128], p16.t[:, 1, :], start=False, stop=True)
                    P.op("pe", f, reads=[wd[0].b, wd[1].b, p16.b], writes=[pe_.b])
                    sgt = sg[cnt["sg"] % 2]
                    tet = te[cnt["sg"] % 2]
                    cnt["sg"] += 1
                    P.op("act", lambda e: e.activation(out=sgt.t[:], in_=pg.t[:], func=AF.Sigmoid), reads=[pg.b], writes=[sgt.b])

                    def comb(e):
                        e.scalar_tensor_tensor(out=tet.t[:], in0=pe_.t[:], scalar=GP[:, c:c + 1], in1=rstde.t[:], op0=ALU.mult, op1=ALU.mult)
                        e.tensor_tensor(out=tet.t[:], in0=tet.t[:], in1=sgt.t[:], op=ALU.mult)
                        return e.tensor_tensor(out=h.t[:, c, :], in0=h.t[:, c, :], in1=tet.t[:], op=ALU.add)
                    P.op("dve", comb, reads=[pe_.b, rstde.b, sgt.b, hparts[c]] + CR, writes=[tet.b, hparts[c]])
                    P.dma(SQ, lambda e: e.dma_start(out=outT[:, c, t0:t0 + T], in_=h.t[:, c, :]), "oc%d" % (c % 8), reads=[hparts[c]])
                e2(c)

        for tt in range(NTB):
            do_tile(tt)
        P.run_block()
    if own:
        B.finish()
    return nc


def _wchunks_all(w, nchunk):
    return np.ascontiguousarray(w.reshape(32, 128, nchunk, 128).transpose(2, 1, 0, 3)).reshape(nchunk, 128, 32 * 128)


def prep_B_weights(inp):
    perm = np.concatenate([np.concatenate([np.arange(512 * g, 512 * g + 512), np.arange(2048 + 512 * g, 2048 + 512 * g + 512)]) for g in range(4)])
    wout = _wchunks_all(inp["w_out"][0][perm, :], 32)
    wup = _wchunks_all(inp["w_mlp_up"][0], 2 * NFF)
    wdn = np.ascontiguousarray(inp["w_mlp_down"][0].reshape(NFF, 128, 4096))
    wple = np.ascontiguousarray(inp["w_ple_proj"][0].reshape(2, 128, 4096).transpose(1, 0, 2)).reshape(128, 8192)
    wgt = _wchunks_all(inp["w_ple_gate"][0], 32)
    NU = 2 * NFF
    cst = np.zeros((128, 64 + NU * 4 + 128), np.float32)
    cst[:, 0:32] = inp["mlp_norm_g"][0].reshape(32, 128).T
    cst[:, 32:64] = inp["ple_norm_g"][0].reshape(32, 128).T
    cw = inp["conv_w"][0].reshape(3, NU, 128).transpose(2, 1, 0)
    cst[:, 64:64 + NU * 3] = cw.reshape(128, NU * 3)
    cst[:, 64 + NU * 3:64 + NU * 4] = inp["conv_b"][0].reshape(NU, 128).T
    cst[:, 64 + NU * 4:] = 1.0
    return dict(wout=wout, wup=wup, wdn=wdn, wple=wple, wgt=wgt, cstb=cst)


def prep_B_acts(inp, S, core):
    NTOK = S // 4
    b, q = core // 4, core % 4
    lo = q * NTOK
    xe = np.zeros((NTOK + 2, 4096), np.float32)
    xe[2:] = inp["x"][b, lo:lo + NTOK]
    if q > 0:
        xe[0:2] = inp["x"][b, lo - 2:lo]
    xTb = np.ascontiguousarray(xe.T.reshape(32, 128, NTOK + 2).transpose(1, 0, 2))
    pTb = np.ascontiguousarray(inp["p"][0, b, lo:lo + NTOK].T.reshape(2, 128, NTOK).transpose(1, 0, 2))
    return dict(xTb=xTb, pTb=pTb)


def mix_for_core(resA, S, core):
    NTOK = S // 4
    b, q = core // 4, core % 4
    lo = q * NTOK
    m = np.zeros((4, 1024, NTOK + 2), np.float32)
    for g in range(4):
        src = resA[4 * b + g]["mixA"]
        m[g, :, 2:] = src[:, lo:lo + NTOK]
        if q > 0:
            m[g, :, 0:2] = src[:, lo - 2:lo]
    return m


_CACHE = {}


def kernel_unfused(**inp):
    inp = {k: np.asarray(v) for k, v in inp.items()}
    S = inp["x"].shape[1]
    NTOK = S // 4
    if ("A", S) not in _CACHE:
        _CACHE[("A", S)] = build_A(S)
        _CACHE[("B", S)] = build_B(NTOK)
    ncA, ncB = _CACHE[("A", S)], _CACHE[("B", S)]
    mapsA = prep_A(inp, S)
    resA = run_bass_kernel_spmd(ncA, mapsA, core_ids=list(range(8))).results
    del mapsA
    wB = prep_B_weights(inp)
    mapsB = []
    for core in range(8):
        d = dict(wB)
        d.update(prep_B_acts(inp, S, core))
        d["mixin"] = mix_for_core(resA, S, core)
        mapsB.append(d)
    resB = run_bass_kernel_spmd(ncB, mapsB, core_ids=list(range(8))).results
    out = np.zeros((2, S, 4096), np.float32)
    for core in range(8):
        b, q = core // 4, core % 4
        o = resB[core]["outT"]
        out[b, q * NTOK:(q + 1) * NTOK] = o.transpose(2, 1, 0).reshape(NTOK, 4096)
    return out


def declare_B_weights(B, bf16_copy=False):
    IN = "ExternalInput"
    d = dict(wout=B.dram("wout", [32, 128, 32 * 128], F32, IN),
             wup=B.dram("wup", [2 * NFF, 128, 32 * 128], F32, IN),
             wdn=B.dram("wdn", [NFF, 128, 4096], F32, IN),
             wple=B.dram("wple", [128, 2 * 4096], F32, IN),
             wgt=B.dram("wgt", [32, 128, 32 * 128], F32, IN))
    if not bf16_copy:
        return d, ()
    c = dict(wout=B.dram("wout_b", [32, 128, 4096], BF16, "Internal"),
             wup=B.dram("wup_b", [2 * NFF, 128, 4096], BF16, "Internal"),
             wdn=B.dram("wdn_b", [NFF, 128, 4096], BF16, "Internal"),
             wple=B.dram("wple_b", [128, 8192], BF16, "Internal"),
             wgt=B.dram("wgt_b", [32, 128, 4096], BF16, "Internal"))
    conv = [(c["wout"][i], d["wout"][i]) for i in range(32)]
    for f in range(NFF):
        conv += [(c["wup"][f], d["wup"][f]), (c["wup"][NFF + f], d["wup"][NFF + f])]
        if f % 2 == 1:
            conv += [(c["wdn"][f - 1], d["wdn"][f - 1]), (c["wdn"][f], d["wdn"][f])]
    conv += [(c["wple"][:, 0:4096], d["wple"][:, 0:4096]), (c["wple"][:, 4096:8192], d["wple"][:, 4096:8192])]
    conv += [(c["wgt"][i], d["wgt"][i]) for i in range(32)]
    return c, conv


def build_fused(S):
    B = Builder()
    wdecl, conv = declare_B_weights(B, bf16_copy=True)
    mixx = build_A(S, B=B, fused=True, conv=conv)
    mixg = B.dram("mixg", [S // 512, 4096, 512], BF16, "Internal")
    build_B(S // 4, mix_dram=mixg, B=B, fused=True, mixx=mixx, wdecl=wdecl)
    B.finish()
    return B.nc


def kernel(**inp):
    inp = {k: np.asarray(v) for k, v in inp.items()}
    S = inp["x"].shape[1]
    NTOK = S // 4
    if ("F", S) not in _CACHE:
        _CACHE[("F", S)] = build_fused(S)
    nc = _CACHE[("F", S)]
    maps = prep_A(inp, S)
    wB = prep_B_weights(inp)
    for core in range(8):
        maps[core].update(wB)
        maps[core].update(prep_B_acts(inp, S, core))
        qm = np.zeros((128, 4), np.float32)
        qm[:, core % 4] = 1.0
        maps[core]["qmask"] = qm
    res = run_bass_kernel_spmd(nc, maps, core_ids=list(range(8))).results
    out = np.zeros((2, S, 4096), np.float32)
    for core in range(8):
        b, q = core // 4, core % 4
        o = res[core]["outT"]
        out[b, q * NTOK:(q + 1) * NTOK] = o.transpose(2, 1, 0).reshape(NTOK, 4096)
    return out
```

```python
import contextlib
import math
import os
import numpy as np
import concourse.bass as bass
import concourse.mybir as mybir
from concourse.bass_utils import run_bass_kernel_spmd

F32 = mybir.dt.float32
BF16 = mybir.dt.bfloat16
I32 = mybir.dt.int32
AF = mybir.ActivationFunctionType
ALU = mybir.AluOpType

D_MODEL = 4096
HEAD_DIM = 64
ATTN_W = 2048
RWKV_W = 2048
D_FF = 11008
PLE_DIM = 256
NORM_EPS = 1e-6
GN_EPS = 64e-5
C0 = math.exp(-0.5)

ENGS = ("pe", "act", "dve", "pool", "sp")
EPOCH = 8000


class Buf:
    __slots__ = ("name", "w", "r")

    def __init__(self, name=""):
        self.name = name
        self.w = None
        self.r = []


class Op:
    __slots__ = ("eng", "fn", "deps", "dma", "sig", "idx", "dsem", "dcnt", "blk", "inc")

    def __init__(self, eng, fn, dma=False):
        self.eng = eng
        self.fn = fn
        self.deps = []
        self.dma = dma
        self.sig = False
        self.idx = None
        self.dsem = None
        self.dcnt = None
        self.blk = 0


class Prog:
    def __init__(self, nc):
        self.nc = nc
        self.ops = {e: [] for e in ENGS}
        self.sigcount = {e: 0 for e in ENGS}
        self.esems = {e: [] for e in ENGS}
        self.dma_sems = {}
        self.waited = {e: {} for e in ENGS}
        self._ctx = []
        self.blk = 0
        self.keybufs = {}

    def _sem(self, name):
        cm = self.nc.semaphore(name)
        s = cm.__enter__()
        self._ctx.append(cm)
        return s

    def dsem(self, key):
        if key not in self.dma_sems:
            self.dma_sems[key] = [self._sem("d_" + str(key)), 0]
        return self.dma_sems[key]

    def close(self):
        for cm in reversed(self._ctx):
            cm.__exit__(None, None, None)

    def _deps(self, op, reads, writes):
        deps = set()
        for b in reads:
            if b.w is not None:
                deps.add(b.w)
        for b in writes:
            if b.w is not None:
                deps.add(b.w)
            for r in b.r:
                deps.add(r)
        deps.discard(op)
        for d in deps:
            if op.dma or d.dma or d.eng != op.eng:
                d.sig = True
        op.deps = list(deps)
        for b in reads:
            b.r.append(op)
        for b in writes:
            b.w = op
            b.r = []

    def op(self, eng, fn, reads=(), writes=()):
        o = Op(eng, fn)
        o.blk = self.blk
        self._deps(o, reads, writes)
        self.ops[eng].append(o)
        return o

    def dma(self, eng, fn, semkey, reads=(), writes=(), inc=16):
        o = Op(eng, fn, dma=True)
        o.blk = self.blk
        o.inc = inc
        ds = self.dsem(semkey)
        ds[1] += inc
        o.dsem, o.dcnt = ds[0], ds[1]
        kb = self.keybufs.setdefault(semkey, Buf(semkey))
        self._deps(o, reads, list(writes) + [kb])
        self.ops[eng].append(o)
        return o

    def _assign(self):
        for e in ENGS:
            for o in self.ops[e]:
                if o.sig and not o.dma and o.idx is None:
                    self.sigcount[e] += 1
                    o.idx = self.sigcount[e]
            need = (self.sigcount[e] + EPOCH - 1) // EPOCH
            while len(self.esems[e]) < need:
                self.esems[e].append(self._sem("e_%s_%d" % (e, len(self.esems[e]))))

    def _wait_for(self, eng_name, eng, d):
        w = self.waited[eng_name]
        if d.blk < self.blk:
            return
        if d.dma:
            key = ("d", id(d.dsem))
            if w.get(key, 0) >= d.dcnt:
                return
            w[key] = d.dcnt
            eng.wait_ge(d.dsem, d.dcnt)
        else:
            ep, v = divmod(d.idx - 1, EPOCH)
            key = (d.eng, ep)
            if w.get(key, 0) >= v + 1:
                return
            w[key] = v + 1
            for e2 in range(ep):
                w[(d.eng, e2)] = EPOCH
            eng.wait_ge(self.esems[d.eng][ep], v + 1)

    def emit_engine(self, eng_name, eng):
        for o in self.ops[eng_name]:
            for d in sorted(o.deps, key=lambda x: (x.dma, x.idx or 0, x.dcnt or 0)):
                if (not o.dma) and (not d.dma) and d.eng == eng_name:
                    continue
                self._wait_for(eng_name, eng, d)
            ins = o.fn(eng)
            if o.dma:
                ins.then_inc(o.dsem, o.inc)
            elif o.sig:
                ep, v = divmod(o.idx - 1, EPOCH)
                ins.then_inc(self.esems[eng_name][ep], 1)

    def finish_waits(self, eng_name, eng):
        for e in ENGS:
            if self.sigcount[e] > 0:
                ep, v = divmod(self.sigcount[e] - 1, EPOCH)
                key = (e, ep)
                if self.waited[eng_name].get(key, 0) < v + 1:
                    self.waited[eng_name][key] = v + 1
                    eng.wait_ge(self.esems[e][ep], v + 1)
        for key, (sem, cnt) in self.dma_sems.items():
            k = ("d", id(sem))
            if cnt > 0 and self.waited[eng_name].get(k, 0) < cnt:
                self.waited[eng_name][k] = cnt
                eng.wait_ge(sem, cnt)

    def run_block(self):
        nc = self.nc
        for e in ENGS:
            lst = [o for o in self.ops[e] if not o.dma]
            if lst:
                lst[-1].sig = True
        self._assign()
        with nc.Block() as block:
            @block.tensor
            def _(t):
                self.emit_engine("pe", t)
                self.finish_waits("pe", t)

            @block.scalar
            def _(a):
                self.emit_engine("act", a)
                self.finish_waits("act", a)

            @block.vector
            def _(v):
                self.emit_engine("dve", v)
                self.finish_waits("dve", v)

            @block.gpsimd
            def _(g):
                self.emit_engine("pool", g)
                self.finish_waits("pool", g)

            @block.sync
            def _(s):
                self.emit_engine("sp", s)
                self.finish_waits("sp", s)
        for e in ENGS:
            self.ops[e] = []
        self.blk += 1


class TB:
    __slots__ = ("t", "b")

    def __init__(self, t, name=""):
        self.t = t
        self.b = Buf(name)


class Builder:
    def __init__(self):
        self.nc = bass.Bass("TRN2", target_bir_lowering=False)
        self.P = Prog(self.nc)
        self.es = contextlib.ExitStack()
        self.n = 0
        self.psb = []
        for i in range(8):
            t = self.es.enter_context(self.nc.psum_tensor("psb%d" % i, [128, 512], F32))
            self.psb.append(TB(t, "psb%d" % i))

    def dram(self, name, shape, dt, kind):
        return self.nc.dram_tensor(name, list(shape), dt, kind=kind).ap()

    def sb(self, es, name, shape, dt=F32):
        self.n += 1
        t = es.enter_context(self.nc.sbuf_tensor("%s_%d" % (name, self.n), list(shape), dt))
        return TB(t, name)

    def finish(self):
        self.P.close()
        self.es.close()


def _bind(f, *a):
    return lambda e: f(e, *a)


NW_A = 3
TS = 64
CH = 64


def build_A(S, phases=("A1", "A2", "A3"), debug=False, B=None, fused=False, conv=()):
    own = B is None
    if own:
        B = Builder()
    nc, P = B.nc, B.P
    T = 512
    NT = S // T
    IN, OUT, INT = "ExternalInput", "ExternalOutput", "Internal"
    xT = B.dram("xT", [128, 32, S], F32, IN)
    pos = B.dram("pos", [1, S], I32, IN)
    wA = B.dram("wA", [30, 128, 32 * 128], F32, IN)
    gA = B.dram("gA", [128, 32], F32, IN)
    c128 = B.dram("c128", [128, 8 * 128 + 256 + 4], F32, IN)
    hp_d = B.dram("hp", [64, 10 * 8], F32, IN)
    lp_d = B.dram("lp", [128, 6], F32, IN)
    wdu_d = B.dram("wdu", [128, 512], F32, IN)
    wau_d = B.dram("wau", [128, 512], F32, IN)
    wgu_d = B.dram("wgu", [128, 4 * 512], F32, IN)
    c64_d = B.dram("c64", [64, 64 + 8 * TS], F32, IN)
    if fused:
        mixA = B.dram("mixx", [S // 512, 1024, 512], BF16, INT)
        MIX = dict(ap=mixA, off=0, dt=BF16, dst=lambda r0, nr, t0, n: mixA[t0 // 512, r0:r0 + nr, t0 % 512:t0 % 512 + n])
    else:
        mixA = B.dram("mixA", [1024, S], F32, OUT)
        MIX = dict(ap=mixA, off=0, dt=F32, dst=lambda r0, nr, t0, n: mixA[r0:r0 + nr, t0:t0 + n])
    SK = OUT if debug else INT
    qk_s = B.dram("qk_s", [8, 128, S], BF16, SK)
    vv_s = B.dram("vv_s", [4, 128, S], BF16, SK)
    pr_s = B.dram("pr_s", [18, 128, S], F32, SK)

    with contextlib.ExitStack() as es0:
        cf = B.sb(es0, "cf", [128, 8 * 128 + 256 + 4], F32)
        cb = B.sb(es0, "cb", [128, 128 + 128 + 256], BF16)
        hp = B.sb(es0, "hp", [64, 10, 8], F32)
        lp = B.sb(es0, "lp", [128, 6], F32)
        c64 = B.sb(es0, "c64", [64, 64 + 8 * TS], F32)
        P.dma("sp", lambda e: e.dma_start(out=cf.t[:], in_=c128), "cf", writes=[cf.b])
        P.dma("sp", lambda e: e.dma_start(out=hp.t[:].rearrange("p a h -> p (a h)"), in_=hp_d), "hp", writes=[hp.b])
        P.dma("sp", lambda e: e.dma_start(out=lp.t[:], in_=lp_d), "lp", writes=[lp.b])
        P.dma("sp", lambda e: e.dma_start(out=c64.t[:], in_=c64_d), "c64", writes=[c64.b])
        P.dma("pool", lambda e: e.dma_start(out=cb.t[:, 0:128], in_=c128[:, 0:128]), "cb0", writes=[cb.b])
        P.dma("pool", lambda e: e.dma_start(out=cb.t[:, 128:256], in_=c128[:, 384:512]), "cb1", writes=[cb.b])
        P.dma("pool", lambda e: e.dma_start(out=cb.t[:, 256:512], in_=c128[:, 1024:1280]), "cb2", writes=[cb.b])
        ONESF = cf.t[:, 0:128]
        BD = cf.t[:, 128:256]
        ROTT = cf.t[:, 256:384]
        IDENTF = cf.t[:, 384:512]
        MASKG = cf.t[:, 512:640]
        GQ = cf.t[:, 1280:1281]
        GK = cf.t[:, 1281:1282]
        INVF = cf.t[:, 1282:1283]
        ONESB = cb.t[:, 0:128]
        IDENTB = cb.t[:, 128:256]
        MASKB = cb.t[:, 256:512]
        MASKN = c64.t[:, 0:64]
        RESETM = c64.t[:, 64:64 + 8 * TS]
        CR = [cf.b, cb.b, hp.b, lp.b, c64.b]

        if "A1" in phases:
          phase_A1(B, S, T, NT, xT, pos, wA, gA, qk_s, vv_s, pr_s,
                 dict(ONESB=ONESB, BD=BD, ROTT=ROTT, GQ=GQ, GK=GK, INVF=INVF, CR=CR))
        if "A2" in phases:
          phase_A2(B, S, qk_s, vv_s, MIX,
                 dict(ONESF=ONESF, IDENTB=IDENTB, MASKB=MASKB, CR=CR))
        if "A3" in phases:
          phase_A3(B, S, pr_s, MIX, wdu_d, wau_d, wgu_d,
                 dict(ONESF=ONESF, IDENTF=IDENTF, MASKG=MASKG, MASKN=MASKN, RESETM=RESETM,
                      hp=hp, lp=lp, CR=CR), conv=conv)
    if own:
        B.finish()
        return nc
    return mixA


def phase_A1(B, S, T, NT, xT, pos, wA, gA, qk_s, vv_s, pr_s, K):
    nc, P = B.nc, B.P
    CR = K["CR"]
    with contextlib.ExitStack() as es:
        x_sb = B.sb(es, "x", [128, 32, T], F32)
        xn = [B.sb(es, "xn%d" % i, [128, 32, T], BF16) for i in range(2)]
        w_sb = [B.sb(es, "w%d" % i, [128, 32, 128], BF16) for i in range(NW_A)]
        sqb = [B.sb(es, "sqb%d" % i, [128, T], BF16) for i in range(2)]
        rstd = B.sb(es, "rstd", [128, T], F32)
        g_sb = B.sb(es, "g", [128, 32], F32)
        posi = B.sb(es, "posi", [128, T], I32)
        cs = [dict(sin=B.sb(es, "sin%d" % i, [128, T], F32), cos=B.sb(es, "cos%d" % i, [128, T], F32)) for i in range(2)]
        tri = B.sb(es, "tri", [128, T], I32)
        sqf = [B.sb(es, "sqf%d" % i, [128, T], F32) for i in range(2)]
        rs2 = [B.sb(es, "rs2%d" % i, [128, T], F32) for i in range(2)]
        qn = [B.sb(es, "qn%d" % i, [128, T], F32) for i in range(2)]
        t1 = [B.sb(es, "t1%d" % i, [128, T], F32) for i in range(2)]
        tr = [t1[0], t1[1], rs2[0], rs2[1]]
        stb = [B.sb(es, "stb%d" % i, [128, T], BF16) for i in range(3)]
        stf = [B.sb(es, "stf%d" % i, [128, T], F32) for i in range(2)]
        P.dma("sp", lambda e: e.dma_start(out=g_sb.t[:], in_=gA), "gA", writes=[g_sb.b])
        xparts = [Buf() for _ in range(4)]

        psm = [B.psb[i] for i in range(4)]
        pss = [B.psb[i] for i in range(4, 8)]
        cnt = dict(w=0, m=0, s=0, sq=0, stb=0, stf=0, q=0)

        def prep(tt):
            t0 = tt * T
            xnb = xn[tt % 2]
            csb = cs[tt % 2]
            for q4 in range(4):
                P.dma("sp", _bind(lambda e, q4: e.dma_start(out=x_sb.t[:, 8 * q4:8 * q4 + 8, :], in_=xT[:, 8 * q4:8 * q4 + 8, t0:t0 + T]), q4),
                      "x%d" % q4, writes=[xparts[q4]])
            P.dma("sp", lambda e: e.dma_start(out=posi.t[:], in_=pos[:, t0:t0 + T].partition_broadcast(128)), "posi", writes=[posi.b])
            ssp = pss[cnt["s"] % 4]
            cnt["s"] += 1
            for kc in range(32):
                sq = sqb[cnt["sq"] % 2]
                cnt["sq"] += 1
                P.op("act", _bind(lambda e, sq, kc: e.activation(out=sq.t[:], in_=x_sb.t[:, kc, :], func=AF.Square), sq, kc),
                     reads=[xparts[kc // 8]], writes=[sq.b])
                P.op("pe", _bind(lambda e, sq, kc: e.matmul(ssp.t[:], K["ONESB"], sq.t[:], start=(kc == 0), stop=(kc == 31)), sq, kc),
                     reads=[sq.b] + CR, writes=[ssp.b])
            P.op("act", lambda e: e.activation(out=rstd.t[:], in_=ssp.t[:], func=AF.Sqrt, bias=NORM_EPS, scale=1.0 / D_MODEL),
                 reads=[ssp.b], writes=[rstd.b])
            P.op("dve", lambda e: e.reciprocal(out=rstd.t[:], in_=rstd.t[:]), reads=[rstd.b], writes=[rstd.b])
            for kc in range(32):
                P.op("dve", _bind(lambda e, kc: e.scalar_tensor_tensor(out=xnb.t[:, kc, :], in0=x_sb.t[:, kc, :], scalar=g_sb.t[:, kc:kc + 1],
                                                                       in1=rstd.t[:], op0=ALU.mult, op1=ALU.mult), kc),
                     reads=[xparts[kc // 8], rstd.b, g_sb.b], writes=[xnb.b])
            a, k_, r_, rc = tr
            TWO_PI = 2.0 * math.pi
            C1 = 6.28125
            C2 = 0.0019340515136718750
            C3 = TWO_PI - C1 - C2

            def trig(e):
                e.tensor_copy(out=a.t[:], in_=posi.t[:])
                e.tensor_scalar(out=a.t[:], in0=a.t[:], scalar1=K["INVF"], scalar2=None, op0=ALU.mult)
                e.tensor_scalar(out=k_.t[:], in0=a.t[:], scalar1=1.0 / TWO_PI, scalar2=None, op0=ALU.mult)
                e.tensor_copy(out=tri.t[:], in_=k_.t[:])
                e.tensor_copy(out=k_.t[:], in_=tri.t[:])
                e.scalar_tensor_tensor(out=r_.t[:], in0=k_.t[:], scalar=-C1, in1=a.t[:], op0=ALU.mult, op1=ALU.add)
                e.scalar_tensor_tensor(out=r_.t[:], in0=k_.t[:], scalar=-C2, in1=r_.t[:], op0=ALU.mult, op1=ALU.add)
                e.scalar_tensor_tensor(out=r_.t[:], in0=k_.t[:], scalar=-C3, in1=r_.t[:], op0=ALU.mult, op1=ALU.add)
                e.tensor_scalar(out=r_.t[:], in0=r_.t[:], scalar1=-math.pi, scalar2=math.pi, op0=ALU.max, op1=ALU.min)
                e.tensor_scalar(out=rc.t[:], in0=r_.t[:], scalar1=math.pi / 2, scalar2=None, op0=ALU.add)
                e.tensor_scalar(out=k_.t[:], in0=rc.t[:], scalar1=math.pi, scalar2=-TWO_PI, op0=ALU.is_gt, op1=ALU.mult)
                e.tensor_tensor(out=rc.t[:], in0=rc.t[:], in1=k_.t[:], op=ALU.add)
                return e.tensor_scalar(out=rc.t[:], in0=rc.t[:], scalar1=-math.pi, scalar2=math.pi, op0=ALU.max, op1=ALU.min)
            P.op("dve", trig, reads=[posi.b] + CR, writes=[tb.b for tb in tr] + [])
            P.op("act", lambda e: e.activation(out=csb["sin"].t[:], in_=r_.t[:], func=AF.Sin), reads=[r_.b], writes=[csb["sin"].b])
            P.op("act", lambda e: e.activation(out=csb["cos"].t[:], in_=rc.t[:], func=AF.Sin), reads=[rc.b], writes=[csb["cos"].b])

        pending = []

        def flush(now):
            keep = []
            for due, fn in pending:
                if due <= now:
                    fn()
                else:
                    keep.append((due, fn))
            pending[:] = keep

        def chunk(tt, j, seq):
            t0 = tt * T
            xnb = xn[tt % 2]
            csb = cs[tt % 2]
            slot = w_sb[cnt["w"] % NW_A]
            cnt["w"] += 1
            P.dma("pool", _bind(lambda e, slot, j: e.dma_start(out=slot.t[:].rearrange("p k c -> p (k c)"), in_=wA[j]), slot, j),
                  "w%d" % (cnt["w"] % NW_A), writes=[slot.b])
            ps = psm[cnt["m"] % 4]
            cnt["m"] += 1

            def mm(e, slot=slot, ps=ps):
                for kc in range(32):
                    r = e.matmul(ps.t[:], slot.t[:, kc, :], xnb.t[:, kc, :], start=(kc == 0), stop=(kc == 31))
                return r
            P.op("pe", mm, reads=[slot.b, xnb.b], writes=[ps.b])
            if j < 8:
                gv = K["GQ"] if j < 4 else K["GK"]
                i2 = cnt["q"] % 2
                cnt["q"] += 1
                sq, r2, qq, tt1 = sqf[i2], rs2[i2], qn[i2], t1[i2]
                P.op("act", lambda e: e.activation(out=sq.t[:], in_=ps.t[:], func=AF.Square), reads=[ps.b], writes=[sq.b])

                def st2():
                    hs = pss[cnt["s"] % 4]
                    cnt["s"] += 1
                    P.op("pe", lambda e: e.matmul(hs.t[:], K["BD"], sq.t[:], start=True, stop=True), reads=[sq.b] + CR, writes=[hs.b])
                    P.op("act", lambda e: e.activation(out=r2.t[:], in_=hs.t[:], func=AF.Sqrt, bias=NORM_EPS, scale=1.0 / HEAD_DIM),
                         reads=[hs.b], writes=[r2.b])
                    P.op("dve", lambda e: e.reciprocal(out=r2.t[:], in_=r2.t[:]), reads=[r2.b], writes=[r2.b])
                    P.op("dve", lambda e: e.scalar_tensor_tensor(out=qq.t[:], in0=ps.t[:], scalar=gv, in1=r2.t[:], op0=ALU.mult, op1=ALU.mult),
                         reads=[ps.b, r2.b] + CR, writes=[qq.b])

                    def st3():
                        rp = pss[cnt["s"] % 4]
                        cnt["s"] += 1
                        sb_ = stb[cnt["stb"] % 3]
                        cnt["stb"] += 1
                        P.op("pe", lambda e: e.matmul(rp.t[:], K["ROTT"], qq.t[:], start=True, stop=True), reads=[qq.b] + CR, writes=[rp.b])
                        P.op("dve", lambda e: e.tensor_tensor(out=tt1.t[:], in0=qq.t[:], in1=csb["cos"].t[:], op=ALU.mult),
                             reads=[qq.b, csb["cos"].b], writes=[tt1.b])
                        P.op("dve", lambda e: e.tensor_tensor(out=qq.t[:], in0=rp.t[:], in1=csb["sin"].t[:], op=ALU.mult),
                             reads=[rp.b, csb["sin"].b], writes=[qq.b])
                        P.op("dve", lambda e: e.tensor_tensor(out=sb_.t[:], in0=tt1.t[:], in1=qq.t[:], op=ALU.add),
                             reads=[tt1.b, qq.b], writes=[sb_.b])
                        P.dma("sp", lambda e: e.dma_start(out=qk_s[j, :, t0:t0 + T], in_=sb_.t[:]), "stb%d" % (cnt["stb"] % 3), reads=[sb_.b])
                    pending.append((seq + 2, st3))
                pending.append((seq + 1, st2))
            elif j < 12:
                sb_ = stb[cnt["stb"] % 3]
                cnt["stb"] += 1
                P.op("act", lambda e: e.copy(out=sb_.t[:], in_=ps.t[:]), reads=[ps.b], writes=[sb_.b])
                P.dma("sp", lambda e: e.dma_start(out=vv_s[j - 8, :, t0:t0 + T], in_=sb_.t[:]), "stb%d" % (cnt["stb"] % 3), reads=[sb_.b])
            else:
                sf = stf[cnt["stf"] % 2]
                cnt["stf"] += 1
                P.op("act", lambda e: e.copy(out=sf.t[:], in_=ps.t[:]), reads=[ps.b], writes=[sf.b])
                P.dma("sp", lambda e: e.dma_start(out=pr_s[j - 12, :, t0:t0 + T], in_=sf.t[:]), "stf%d" % (cnt["stf"] % 2), reads=[sf.b])

        prep(0)
        seq = 0
        for tt in range(NT):
            for j in range(30):
                chunk(tt, j, seq)
                seq += 1
                flush(seq)
                if j == 14 and tt + 1 < NT:
                    prep(tt + 1)
        flush(seq + 10)
        P.run_block()


def phase_A2(B, S, qk_s, vv_s, MIX, K):
    nc, P = B.nc, B.P
    CR = K["CR"]
    with contextlib.ExitStack() as es:
        qn_ = B.sb(es, "qn", [128, S], BF16)
        kn_ = B.sb(es, "kn", [128, S], BF16)
        vn_ = B.sb(es, "vn", [128, S], BF16)
        qd_ = B.sb(es, "qd", [128, S], BF16)
        kd_ = B.sb(es, "kd", [128, S], BF16)
        vd_ = B.sb(es, "vd", [128, S], BF16)
        acc = [B.sb(es, "acc%d" % h, [65, S], F32) for h in range(2)]
        vp = [B.sb(es, "vp%d" % i, [128, 2, 65], BF16) for i in range(3)]
        pT = [B.sb(es, "pT%d" % i, [128, 256], BF16) for i in range(4)]
        rec = [B.sb(es, "rec%d" % i, [64, 512], F32) for i in range(2)]
        ost = [B.sb(es, "ost%d" % i, [64, 512], MIX["dt"]) for i in range(2)]
        MDST = MIX["dst"]
        for v_ in vp:
            P.op("pool", _bind(lambda e, v_: e.memset(v_.t[:], 1.0), v_), writes=[v_.b])
        sc_ps = [B.psb[0], B.psb[1]]
        o_ps = [[B.psb[2 + h * 2 + i] for i in range(2)] for h in range(2)]
        vt_ps = [B.psb[6], B.psb[7]]
        fin_ps = [B.psb[0], B.psb[1]]
        c = dict(vp=0, pT=0, sc=0, vt=0, fin=0, kb=0)

        def o_ap(h, i):
            return o_ps[h][i].t[0:65, 0:128]

        def vt_ap(i):
            return vt_ps[i].t[:, 0:64].bitcast(BF16)

        def do_kb(hp_i, bi, d, r, kb, nb, M, q3, k3, v3):
            nq = 256 if kb < nb - 1 else 128
            k0 = r * M + kb * 128
            vti = c["vt"] % 2
            c["vt"] += 1
            vtb = vt_ps[vti]
            vpt = vp[c["vp"] % 3]
            c["vp"] += 1
            P.op("pe", lambda e: e.transpose(out=vt_ap(vti), in_=v3.t[:, k0:k0 + 128], identity=K["IDENTB"]),
                 reads=[v3.b] + CR, writes=[vtb.b])
            P.op("dve", lambda e: e.tensor_copy(out=vpt.t[:, :, 0:64], in_=vt_ap(vti).rearrange("p (h c) -> p h c", h=2)),
                 reads=[vtb.b], writes=[vpt.b])
            for h in range(2):
                do_head(bi, d, r, kb, nq, k0, h, q3, k3, vpt)

        def do_head(bi, d, r, kb, nq, k0, h, q3, k3, vpt):
            hs = slice(h * 64, (h + 1) * 64)
            sc = sc_ps[c["sc"] % 2]
            c["sc"] += 1
            pt = pT[c["pT"] % 4]
            c["pT"] += 1

            def scf(e):
                e.matmul(sc.t[:, 0:nq], k3.t[hs, k0:k0 + 128], q3.t[hs, k0:k0 + nq], start=True, stop=False)
                return e.matmul(sc.t[:, 0:nq], K["IDENTB"], K["MASKB"][:, 0:nq], start=False, stop=True)
            P.op("pe", scf, reads=[q3.b, k3.b] + CR, writes=[sc.b])
            P.op("act", lambda e: e.activation(out=pt.t[:, 0:nq], in_=sc.t[:, 0:nq], func=AF.Exp, scale=0.125),
                 reads=[sc.b], writes=[pt.b])
            oa = o_ps[h][kb % 2]
            ob = o_ps[h][(kb + 1) % 2]

            def pv(e):
                r_ = e.matmul(o_ap(h, kb % 2), vpt.t[:, h, :], pt.t[:, 0:128], start=(kb == 0), stop=True, skip_group_check=True)
                if nq == 256:
                    r_ = e.matmul(o_ap(h, (kb + 1) % 2), vpt.t[:, h, :], pt.t[:, 128:256], start=True, stop=False, skip_group_check=True)
                return r_
            P.op("pe", pv, reads=[pt.b, vpt.b, oa.b], writes=[oa.b] + ([ob.b] if nq == 256 else []))
            tpos = (kb * 128) * d + r

            def ev(e):
                dst = acc[h].t[:, tpos:tpos + 127 * d + 1:d] if d > 1 else acc[h].t[:, tpos:tpos + 128]
                if bi == 0:
                    return e.tensor_copy(out=dst, in_=o_ap(h, kb % 2))
                return e.tensor_tensor(out=dst, in0=dst, in1=o_ap(h, kb % 2), op=ALU.add)
            P.op("dve", ev, reads=[oa.b, acc[h].b], writes=[acc[h].b, oa.b])

        def do_branch(hp_i, bi, d):
            M = S // d
            nb = M // 128
            if d == 1:
                q3, k3, v3 = qn_, kn_, vn_
            else:
                def cp(src, dst, eng):
                    P.op(eng, lambda e: e.tensor_copy(out=dst.t[:].rearrange("p (r m) -> p r m", r=d),
                                                      in_=src.t[:].rearrange("p (m r) -> p r m", r=d)),
                         reads=[src.b], writes=[dst.b])
                cp(qn_, qd_, "dve")
                cp(kn_, kd_, "pool")
                cp(vn_, vd_, "dve")
                q3, k3, v3 = qd_, kd_, vd_
            for r in range(d):
                for kb in range(nb):
                    do_kb(hp_i, bi, d, r, kb, nb, M, q3, k3, v3)

        def do_fin(hp_i, h, s0):
            fp = fin_ps[c["fin"] % 2]
            rc_ = rec[c["fin"] % 2]
            os_ = ost[c["fin"] % 2]
            key = "ost%d" % (c["fin"] % 2)
            c["fin"] += 1
            P.op("pe", lambda e: e.matmul(fp.t[0:64, :], K["ONESF"][64:65, 0:64], acc[h].t[64:65, s0:s0 + 512], start=True, stop=True),
                 reads=[acc[h].b] + CR, writes=[fp.b])
            P.op("dve", lambda e: e.reciprocal(out=rc_.t[:], in_=fp.t[0:64, :]), reads=[fp.b], writes=[rc_.b])
            P.op("pool", lambda e: e.tensor_tensor(out=os_.t[:], in0=acc[h].t[0:64, s0:s0 + 512], in1=rc_.t[:], op=ALU.mult),
                 reads=[rc_.b, acc[h].b], writes=[os_.b])
            row = (hp_i * 2 + h) * 64
            P.dma("sp", lambda e: e.dma_start(out=MDST(row, 64, s0, 512), in_=os_.t[:]), key, reads=[os_.b])

        def do_pair(hp_i):
            P.dma("sp", lambda e: e.dma_start(out=qn_.t[:], in_=qk_s[hp_i]), "a2q", writes=[qn_.b])
            P.dma("sp", lambda e: e.dma_start(out=kn_.t[:], in_=qk_s[4 + hp_i]), "a2k", writes=[kn_.b])
            P.dma("sp", lambda e: e.dma_start(out=vn_.t[:], in_=vv_s[hp_i]), "a2v", writes=[vn_.b])
            for bi, d in enumerate((1, 4, 16)):
                do_branch(hp_i, bi, d)
            for h in range(2):
                for s0 in range(0, S, 512):
                    do_fin(hp_i, h, s0)

        for hp_i in range(4):
            do_pair(hp_i)
        P.run_block()


def phase_A3(B, S, pr_s, MIX, wdu_d, wau_d, wgu_d, K, conv=()):
    nc, P = B.nc, B.P
    CR = K["CR"]
    hp, lp = K["hp"], K["lp"]
    NS = S // TS
    NC = TS // CH
    W = TS + 1
    with contextlib.ExitStack() as es:
        def H(name, n=TS, extra=()):
            return B.sb(es, name, [64, 8] + list(extra) + [n], F32)

        def S64(name, rows=64):
            return B.sb(es, name, [rows, 8, 64], F32)
        wdu = B.sb(es, "wdu", [128, 512], F32)
        wau = B.sb(es, "wau", [128, 512], F32)
        wgu = B.sb(es, "wgu", [128, 4, 512], F32)
        P.dma("sp", lambda e: e.dma_start(out=wdu.t[:], in_=wdu_d), "wdu", writes=[wdu.b])
        P.dma("sp", lambda e: e.dma_start(out=wau.t[:], in_=wau_d), "wau", writes=[wau.b])
        P.dma("sp", lambda e: e.dma_start(out=wgu.t[:].rearrange("p k c -> p (k c)"), in_=wgu_d), "wgu", writes=[wgu.b])
        RX = [H("RX%d" % i, W) for i in range(2)]
        KX = [H("KX%d" % i, W) for i in range(2)]
        VX = [H("VX%d" % i, W) for i in range(2)]
        WX = [B.sb(es, "WX%d" % i, [128, W], F32) for i in range(2)]
        AXl = [B.sb(es, "AX%d" % i, [128, W], F32) for i in range(2)]
        GX = [B.sb(es, "GX%d" % i, [128, 4, W], F32) for i in range(2)]
        parts = {}

        def part(tb, i):
            k = (id(tb), i)
            if k not in parts:
                parts[k] = Buf()
            return parts[k]
        D1 = H("D1")
        r_ = H("r")
        k_ = H("k")
        VZ = [H("VZ%d" % i, TS, extra=(2,)) for i in range(2)]
        wdm = B.sb(es, "wdm", [128, TS], F32)
        adm = B.sb(es, "adm", [128, TS], F32)
        gdm = B.sb(es, "gdm", [128, 4, TS], F32)
        dl = B.sb(es, "dl", [128, 4, TS], F32)
        sw = H("sw")
        a_ = H("a")
        g_ = [H("g%d" % i) for i in range(3)]
        kk = H("kk")
        sq = H("sq")
        kmod = H("kmod")
        ba = H("ba")
        cs_ = H("cs")
        E1 = [H("E1%d" % i) for i in range(3)]
        E2 = H("E2")
        E3 = H("E3")
        E4 = H("E4")
        tmpH = H("tmpH")
        AR = [H("AR%d" % i, TS, extra=(2,)) for i in range(3)]
        BK = [H("BK%d" % i, TS, extra=(2,)) for i in range(2)]
        BKh = [H("BKh%d" % i, TS, extra=(2,)) for i in range(2)]
        bonus = [H("bon%d" % i) for i in range(3)]
        Y = [H("Y%d" % i) for i in range(2)]
        Gm = [B.sb(es, "Gm%d" % i, [128, 8, 128], F32) for i in range(2)]
        QN0 = [S64("QN0%d" % i) for i in range(2)]
        QP = [S64("QP%d" % i) for i in range(2)]
        QtP = [S64("QtP%d" % i) for i in range(2)]
        IQ = S64("IQ")
        X = [S64("X%d" % i) for i in range(2)]
        Atm = S64("Atm")
        BKt = [S64("BKt%d" % i, 128) for i in range(2)]
        UV = [S64("UV%d" % i, 128) for i in range(2)]
        Wsb = S64("Wsb")
        Uhat = [S64("Uhat%d" % i) for i in range(2)]
        AhT = [S64("AhT%d" % i) for i in range(2)]
        ST = [S64("ST%d" % i) for i in range(2)]
        STd = S64("STd")
        yc = H("yc")
        ysq = H("ysq")
        rsd = H("rsd")
        ostg = [B.sb(es, "ostg%d" % i, [64, 8, TS], MIX["dt"]) for i in range(2)]
        MDST = MIX["dst"]
        P.op("pool", lambda e: e.memset(ST[0].t[:], 0.0), writes=[ST[0].b])
        P.op("pool", lambda e: e.memset(VZ[0].t[:], 0.0), writes=[VZ[0].b])
        P.op("pool", lambda e: e.memset(VZ[1].t[:], 0.0), writes=[VZ[1].b])
        P.op("pool", lambda e: e.memset(UV[0].t[:], 0.0), writes=[UV[0].b])
        P.op("pool", lambda e: e.memset(UV[1].t[:], 0.0), writes=[UV[1].b])

        ONES64 = K["ONESF"][0:64, 0:64]
        ID64 = K["IDENTF"][0:64, 0:64]
        ID64B = ID64.unsqueeze(1).to_broadcast([64, 8, 64])
        MASKNB = K["MASKN"].unsqueeze(1).to_broadcast([64, 8, 64])
        MASKGB = K["MASKG"].unsqueeze(1).to_broadcast([128, 4, 128])
        st = dict(ps=0, sti=0, chunk=0)

        def nps():
            pool = st.get("pool", 0)
            key = "ps%d" % pool
            base, size = ((0, 3), (3, 3), (6, 2))[pool]
            p = B.psb[base + st.get(key, 0) % size]
            st[key] = st.get(key, 0) + 1
            return p

        def hpv(i):
            return hp.t[:, i, :].unsqueeze(2)

        def bc(ap, n=TS):
            return ap.to_broadcast([64, 8, n])

        def pv8(p, rows=64):
            return p.t[0:rows, :].rearrange("p (h t) -> p h t", h=8)

        def mm8(out_fn, lhs_fn, rhs_fn):
            def f(e):
                for h in range(8):
                    r = e.matmul(out_fn(h), lhs_fn(h), rhs_fn(h), start=True, stop=True)
                return r
            return f

        def sum8(src, consume):
            pk = nps()
            P.op("pe", mm8(lambda h: pv8(pk)[:, h, :], lambda h: ONES64, lambda h: src.t[:, h, :]), reads=[src.b] + CR, writes=[pk.b])
            consume(pk)

        def chunk_pre(s_i, c, ar, bk, bkh, vz, e1, yy, cx):
            cs0 = c * CH
            csl = slice(cs0, cs0 + CH)
            ci = st["chunk"] % 2
            st["chunk"] += 1
            gm, qn0, bkt, uv, uh, aht = Gm[ci], QN0[ci], BKt[ci], UV[ci], Uhat[ci], AhT[ci]
            for half in range(2):
                def ghalf(half):
                    pg_ = nps()

                    def gmm(e):
                        for hh in range(4):
                            h = half * 4 + hh
                            r = e.matmul(pg_.t[:, hh * 128:(hh + 1) * 128], bk.t[:, h, :, csl], ar.t[:, h, :, csl], start=True, stop=True)
                        return r
                    P.op("pe", gmm, reads=[bk.b, ar.b], writes=[pg_.b])
                    P.op("dve", lambda e: e.tensor_tensor(out=gm.t[:, half * 4:half * 4 + 4, :], in0=pg_.t[:].rearrange("p (h t) -> p h t", h=4),
                                                          in1=MASKGB, op=ALU.mult),
                         reads=[pg_.b] + CR, writes=[gm.b])
                ghalf(half)
                yield
            pn = nps()
            P.op("pe", mm8(lambda h: pv8(pn)[:, h, :], lambda h: ar.t[:, h, 0, csl], lambda h: bk.t[:, h, 0, csl]), reads=[ar.b, bk.b], writes=[pn.b])
            yield
            P.op("dve", lambda e: e.tensor_tensor(out=qn0.t[:], in0=pv8(pn), in1=MASKNB, op=ALU.mult), reads=[pn.b] + CR, writes=[qn0.b])
            yield
            P.op("pool", lambda e: e.tensor_tensor(out=X[0].t[:], in0=gm.t[0:64, :, 0:64], in1=ID64B, op=ALU.add), reads=[gm.b] + CR, writes=[X[0].b])
            yield

            def level(lvl, q_cur, qt_ap, qt_b, x_cur):
                pq = nps()
                P.op("pe", mm8(lambda h: pv8(pq)[:, h, :], qt_ap, lambda h: q_cur.t[:, h, :]), reads=[qt_b, q_cur.b], writes=[pq.b])
                yield
                q_new = QP[lvl % 2]
                qt_new = QtP[lvl % 2]
                if lvl < 5:
                    pqt = nps()
                    P.op("pe", mm8(lambda h: pv8(pqt)[:, h, :], lambda h: q_cur.t[:, h, :], qt_ap), reads=[qt_b, q_cur.b], writes=[pqt.b])
                    P.op("act", lambda e: e.copy(out=q_new.t[:], in_=pv8(pq)), reads=[pq.b], writes=[q_new.b])
                    P.op("dve", lambda e: e.tensor_copy(out=qt_new.t[:], in_=pv8(pqt)), reads=[pqt.b], writes=[qt_new.b])
                    P.op("pool", lambda e: e.tensor_tensor(out=IQ.t[:], in0=q_new.t[:], in1=ID64B, op=ALU.add), reads=[q_new.b] + CR, writes=[IQ.b])
                else:
                    P.op("dve", lambda e: e.tensor_tensor(out=IQ.t[:], in0=pv8(pq), in1=ID64B, op=ALU.add), reads=[pq.b] + CR, writes=[IQ.b])
                px = nps()
                x_new = X[lvl % 2]
                P.op("pe", mm8(lambda h: pv8(px)[:, h, :], lambda h: IQ.t[:, h, :], lambda h: x_cur.t[:, h, :]), reads=[IQ.b, x_cur.b], writes=[px.b])
                yield
                P.op("act", lambda e: e.copy(out=x_new.t[:], in_=pv8(px)), reads=[px.b], writes=[x_new.b])
                yield
                return q_new, (lambda h: qt_new.t[:, h, :]), qt_new.b, x_new

            q_cur, qt_ap, qt_b, x_cur = qn0, (lambda h: gm.t[0:64, h, 0:64]), gm.b, X[0]
            for lvl in range(1, 6):
                q_cur, qt_ap, qt_b, x_cur = yield from level(lvl, q_cur, qt_ap, qt_b, x_cur)
            TT = x_cur
            pa_, pb_, pv_ = nps(), nps(), nps()

            def trs(e):
                for h in range(8):
                    e.transpose(out=pv8(pa_)[:, h, :], in_=ar.t[:, h, 0, csl], identity=ID64)
                for h in range(8):
                    e.transpose(out=pv8(pb_, 128)[:, h, :], in_=bkh.t[:, h, :, csl], identity=ID64)
                for h in range(8):
                    r = e.transpose(out=pv8(pv_, 128)[:, h, :], in_=vz.t[:, h, :, csl], identity=ID64)
                return r
            P.op("pe", trs, reads=[ar.b, bkh.b, vz.b] + CR, writes=[pa_.b, pb_.b, pv_.b])
            yield
            P.op("act", lambda e: e.copy(out=Atm.t[:], in_=pv8(pa_)), reads=[pa_.b], writes=[Atm.b])
            yield
            P.op("dve", lambda e: e.tensor_copy(out=bkt.t[:], in_=pv8(pb_, 128)), reads=[pb_.b], writes=[bkt.b])
            yield
            P.op("act", lambda e: e.copy(out=uv.t[64:128, :, :], in_=pv8(pv_, 128)[64:128, :, :]), reads=[pv_.b], writes=[uv.b])
            yield
            pw_ = nps()
            P.op("pe", mm8(lambda h: pv8(pw_)[:, h, :], lambda h: gm.t[64:128, h, 0:64], lambda h: uv.t[64:128, h, :]), reads=[gm.b, uv.b], writes=[pw_.b])
            yield
            P.op("act", lambda e: e.copy(out=Wsb.t[:], in_=pv8(pw_)), reads=[pw_.b], writes=[Wsb.b])
            yield
            pu_, ph_ = nps(), nps()
            P.op("pe", mm8(lambda h: pv8(pu_)[:, h, :], lambda h: TT.t[:, h, :], lambda h: Wsb.t[:, h, :]), reads=[TT.b, Wsb.b], writes=[pu_.b])
            yield
            P.op("pe", mm8(lambda h: pv8(ph_)[:, h, :], lambda h: Atm.t[:, h, :], lambda h: TT.t[:, h, :]), reads=[TT.b, Atm.b], writes=[ph_.b])
            yield
            P.op("act", lambda e: e.copy(out=uh.t[:], in_=pv8(pu_)), reads=[pu_.b], writes=[uh.b])
            yield
            P.op("dve", lambda e: e.tensor_copy(out=aht.t[:], in_=pv8(ph_)), reads=[ph_.b], writes=[aht.b])
            yield
            cx.update(gm=gm, bkt=bkt, uv=uv, uh=uh, aht=aht, csl=csl, cs0=cs0)

        def chunk_scan(cx, ar, e1, yy):
            gm, bkt, uv, uh, aht, csl, cs0 = (cx[k] for k in ("gm", "bkt", "uv", "uh", "aht", "csl", "cs0"))
            st_old = ST[st["sti"] % 2]
            st_new = ST[(st["sti"] + 1) % 2]
            st["sti"] += 1
            pc_ap = e1.t[:, :, cs0 + CH - 1:cs0 + CH]
            P.op("pool", lambda e: e.tensor_tensor(out=STd.t[:], in0=st_old.t[:], in1=pc_ap.to_broadcast([64, 8, 64]), op=ALU.mult),
                 reads=[st_old.b, e1.b], writes=[STd.b])
            yield
            pU = nps()
            P.op("pe", mm8(lambda h: pv8(pU)[:, h, :], lambda h: aht.t[:, h, :], lambda h: st_old.t[:, h, :]), reads=[aht.b, st_old.b], writes=[pU.b])
            yield
            P.op("dve", lambda e: e.tensor_tensor(out=uv.t[0:64, :, :], in0=pv8(pU), in1=uh.t[:], op=ALU.add), reads=[pU.b, uh.b], writes=[uv.b])
            yield
            pS = nps()
            P.op("pe", mm8(lambda h: pv8(pS)[:, h, :], lambda h: bkt.t[:, h, :], lambda h: uv.t[:, h, :]), reads=[bkt.b, uv.b], writes=[pS.b])
            yield
            P.op("dve", lambda e: e.tensor_tensor(out=st_new.t[:], in0=pv8(pS), in1=STd.t[:], op=ALU.add), reads=[pS.b, STd.b], writes=[st_new.b])
            yield
            pY = nps()

            def y1(e):
                for h in range(8):
                    e.matmul(pv8(pY)[:, h, :], st_old.t[:, h, :], ar.t[:, h, 1, csl], start=True, stop=False)
                    r = e.matmul(pv8(pY)[:, h, :], uv.t[:, h, :], gm.t[:, h, 64:128], start=False, stop=True)
                return r
            P.op("pe", y1, reads=[st_old.b, ar.b, uv.b, gm.b], writes=[pY.b])
            yield
            P.op("act", lambda e: e.copy(out=yy.t[:, :, csl], in_=pv8(pY)), reads=[pY.b], writes=[yy.b])
            yield

        def super_pre(s_i, sx):
            t0 = s_i * TS
            i2 = s_i % 2
            rx, kx, vx, wx, ax, gx = RX[i2], KX[i2], VX[i2], WX[i2], AXl[i2], GX[i2]
            i3 = s_i % 3
            vz, ar, bk, bkh, e1, bon, gg, yy = VZ[i2], AR[i3], BK[i2], BKh[i2], E1[i3], bonus[i3], g_[i3], Y[i2]
            lo = 1 if s_i == 0 else 0
            src0 = t0 - 1 + lo
            n = W - lo

            def ldH(dst, c0, key):
                if s_i == 0:
                    P.op("pool", lambda e: e.memset(dst.t[:, :, 0:1], 0.0), writes=[part(dst, cc) for cc in range(4)])
                for cc in range(4):
                    def one(cc):
                        P.dma("sp", lambda e: e.dma_start(out=dst.t[:, 2 * cc:2 * cc + 2, lo:W],
                                                          in_=pr_s[c0 + cc, :, src0:src0 + n].rearrange("(h p) t -> p h t", h=2)),
                              "%s%d" % (key, i2), writes=[part(dst, cc)])
                    one(cc)
            ldH(rx, 0, "rx")
            ldH(kx, 4, "kx")
            ldH(vx, 8, "vx")
            if s_i == 0:
                P.op("pool", lambda e: e.memset(wx.t[:, 0:1], 0.0), writes=[wx.b])
                P.op("pool", lambda e: e.memset(ax.t[:, 0:1], 0.0), writes=[ax.b])
                P.op("pool", lambda e: e.memset(gx.t[:, :, 0:1], 0.0), writes=[part(gx, cc) for cc in range(4)])
            P.dma("sp", lambda e: e.dma_start(out=wx.t[:, lo:W], in_=pr_s[12, :, src0:src0 + n]), "wx%d" % i2, writes=[wx.b])
            yield
            P.dma("sp", lambda e: e.dma_start(out=ax.t[:, lo:W], in_=pr_s[13, :, src0:src0 + n]), "ax%d" % i2, writes=[ax.b])
            yield
            for cc in range(4):
                def oneg(cc):
                    P.dma("sp", lambda e: e.dma_start(out=gx.t[:, cc, lo:W], in_=pr_s[14 + cc, :, src0:src0 + n]),
                          "gx%d" % i2, writes=[part(gx, cc)])
                oneg(cc)
            allp = lambda tb: [part(tb, cc) for cc in range(4)]

            def mixH(src, dst_ap, mi):
                def f(e):
                    e.tensor_tensor(out=D1.t[:], in0=src.t[:, :, 0:TS], in1=src.t[:, :, 1:W], op=ALU.subtract)
                    e.tensor_tensor(out=D1.t[:], in0=D1.t[:], in1=bc(hpv(mi)), op=ALU.mult)
                    return e.tensor_tensor(out=dst_ap, in0=D1.t[:], in1=src.t[:, :, 1:W], op=ALU.add)
                return f
            P.op("dve", mixH(rx, r_.t[:], 0), reads=allp(rx) + CR, writes=[D1.b, r_.b])
            yield
            P.op("dve", mixH(kx, k_.t[:], 1), reads=allp(kx) + CR, writes=[D1.b, k_.b])
            yield
            P.op("dve", mixH(vx, vz.t[:, :, 1, :], 2), reads=allp(vx) + CR, writes=[D1.b, vz.b])
            yield

            def mixL(e):
                e.tensor_tensor(out=dl.t[:, 0, :], in0=wx.t[:, 0:TS], in1=wx.t[:, 1:W], op=ALU.subtract)
                e.scalar_tensor_tensor(out=wdm.t[:], in0=dl.t[:, 0, :], scalar=lp.t[:, 0:1], in1=wx.t[:, 1:W], op0=ALU.mult, op1=ALU.add)
                e.tensor_tensor(out=dl.t[:, 0, :], in0=ax.t[:, 0:TS], in1=ax.t[:, 1:W], op=ALU.subtract)
                e.scalar_tensor_tensor(out=adm.t[:], in0=dl.t[:, 0, :], scalar=lp.t[:, 1:2], in1=ax.t[:, 1:W], op0=ALU.mult, op1=ALU.add)
                e.tensor_tensor(out=dl.t[:], in0=gx.t[:, :, 0:TS], in1=gx.t[:, :, 1:W], op=ALU.subtract)
                for cc in range(4):
                    r = e.scalar_tensor_tensor(out=gdm.t[:, cc, :], in0=dl.t[:, cc, :], scalar=lp.t[:, 2 + cc:3 + cc], in1=gx.t[:, cc, 1:W],
                                               op0=ALU.mult, op1=ALU.add)
                return r
            P.op("dve", mixL, reads=[wx.b, ax.b] + allp(gx) + CR, writes=[dl.b, wdm.b, adm.b, gdm.b])
            yield
            P.op("act", lambda e: e.activation(out=wdm.t[:], in_=wdm.t[:], func=AF.Tanh), reads=[wdm.b], writes=[wdm.b])
            yield
            P.op("act", lambda e: e.activation(out=gdm.t[:], in_=gdm.t[:], func=AF.Sigmoid), reads=[gdm.b], writes=[gdm.b])
            yield

            def lora_all():
                pw, pa, pg = nps(), nps(), nps()

                def lora(e):
                    for h in range(8):
                        e.matmul(pv8(pw)[:, h, :], wdu.t[:, h * 64:(h + 1) * 64], wdm.t[:], start=True, stop=True)
                    for h in range(8):
                        e.matmul(pv8(pa)[:, h, :], wau.t[:, h * 64:(h + 1) * 64], adm.t[:], start=True, stop=True)
                    for h in range(8):
                        for cc in range(4):
                            r = e.matmul(pv8(pg)[:, h, :], wgu.t[:, cc, h * 64:(h + 1) * 64], gdm.t[:, cc, :], start=(cc == 0), stop=(cc == 3))
                    return r
                P.op("pe", lora, reads=[wdu.b, wau.b, wgu.b, wdm.b, adm.b, gdm.b], writes=[pw.b, pa.b, pg.b])

                def sig(e):
                    for h in range(8):
                        e.activation(out=sw.t[:, h, :], in_=pv8(pw)[:, h, :], func=AF.Sigmoid, bias=hp.t[:, 3, h:h + 1], scale=1.0)
                    for h in range(8):
                        r = e.activation(out=a_.t[:, h, :], in_=pv8(pa)[:, h, :], func=AF.Sigmoid, bias=hp.t[:, 4, h:h + 1], scale=1.0)
                    return r
                P.op("act", sig, reads=[pw.b, pa.b] + CR, writes=[sw.b, a_.b])
                P.op("act", lambda e: e.copy(out=gg.t[:], in_=pv8(pg)), reads=[pg.b], writes=[gg.b])
            lora_all()
            yield
            P.op("dve", lambda e: e.tensor_tensor(out=kk.t[:], in0=k_.t[:], in1=bc(hpv(5)), op=ALU.mult), reads=[k_.b] + CR, writes=[kk.b])
            yield
            P.op("act", lambda e: e.activation(out=sq.t[:], in_=kk.t[:], func=AF.Square), reads=[kk.b], writes=[sq.b])
            yield

            sum8(sq, lambda pk: P.op("act", lambda e: e.activation(out=tmpH.t[:], in_=pv8(pk), func=AF.Sqrt), reads=[pk.b], writes=[tmpH.b]))
            yield

            def kkn(e):
                e.tensor_scalar(out=tmpH.t[:], in0=tmpH.t[:], scalar1=1e-12, scalar2=None, op0=ALU.max)
                e.reciprocal(out=tmpH.t[:], in_=tmpH.t[:])
                e.tensor_tensor(out=kk.t[:], in0=kk.t[:], in1=tmpH.t[:], op=ALU.mult)
                e.scalar_tensor_tensor(out=tmpH.t[:], in0=a_.t[:], scalar=-1.0, in1=bc(hpv(6)), op0=ALU.add, op1=ALU.mult)
                e.scalar_tensor_tensor(out=kmod.t[:], in0=tmpH.t[:], scalar=1.0, in1=k_.t[:], op0=ALU.add, op1=ALU.mult)
                return e.tensor_tensor(out=ba.t[:], in0=kk.t[:], in1=a_.t[:], op=ALU.mult)
            P.op("dve", kkn, reads=[tmpH.b, kk.b, a_.b, k_.b] + CR, writes=[tmpH.b, kk.b, kmod.b, ba.b])
            yield
            flat = lambda t: t.t[:].rearrange("p h t -> p (h t)")
            P.op("dve", lambda e: e.tensor_tensor_scan(out=flat(cs_), data0=K["RESETM"], data1=flat(sw), initial=0.0, op0=ALU.mult, op1=ALU.add),
                 reads=[sw.b] + CR, writes=[cs_.b])
            yield
            P.op("act", lambda e: e.activation(out=e1.t[:], in_=cs_.t[:], func=AF.Exp, scale=-C0), reads=[cs_.b], writes=[e1.b])
            yield
            P.op("act", lambda e: e.activation(out=E2.t[:], in_=cs_.t[:], func=AF.Exp, scale=C0), reads=[cs_.b], writes=[E2.b])
            yield
            P.op("pool", lambda e: e.tensor_tensor(out=E3.t[:], in0=cs_.t[:], in1=sw.t[:], op=ALU.subtract), reads=[cs_.b, sw.b], writes=[E3.b])
            yield
            P.op("act", lambda e: e.activation(out=E3.t[:], in_=E3.t[:], func=AF.Exp, scale=-C0), reads=[E3.b], writes=[E3.b])
            yield

            def e4f(e):
                c4 = cs_.t[:].rearrange("p h (c t) -> p h c t", t=CH)
                return e.tensor_tensor(out=E4.t[:].rearrange("p h (c t) -> p h c t", t=CH), in0=c4,
                                       in1=c4[:, :, :, CH - 1:CH].to_broadcast([64, 8, NC, CH]), op=ALU.subtract)
            P.op("pool", e4f, reads=[cs_.b], writes=[E4.b])
            yield
            P.op("act", lambda e: e.activation(out=E4.t[:], in_=E4.t[:], func=AF.Exp, scale=C0), reads=[E4.b], writes=[E4.b])
            yield

            def tild(e):
                e.tensor_tensor(out=ar.t[:, :, 1, :], in0=r_.t[:], in1=e1.t[:], op=ALU.mult)
                e.scalar_tensor_tensor(out=ar.t[:, :, 0, :], in0=kk.t[:], scalar=-1.0, in1=E3.t[:], op0=ALU.mult, op1=ALU.mult)
                e.tensor_tensor(out=bk.t[:, :, 0, :], in0=ba.t[:], in1=E2.t[:], op=ALU.mult)
                return e.tensor_tensor(out=bk.t[:, :, 1, :], in0=kmod.t[:], in1=E2.t[:], op=ALU.mult)
            P.op("dve", tild, reads=[r_.b, e1.b, kk.b, E3.b, ba.b, E2.b, kmod.b], writes=[ar.b, bk.b])
            yield

            def hatf(e):
                e.tensor_tensor(out=bkh.t[:, :, 0, :], in0=ba.t[:], in1=E4.t[:], op=ALU.mult)
                return e.tensor_tensor(out=bkh.t[:, :, 1, :], in0=kmod.t[:], in1=E4.t[:], op=ALU.mult)
            P.op("pool", hatf, reads=[ba.b, kmod.b, E4.b], writes=[bkh.b])
            yield

            def rkf(e):
                e.tensor_tensor(out=tmpH.t[:], in0=r_.t[:], in1=kmod.t[:], op=ALU.mult)
                return e.tensor_tensor(out=sq.t[:], in0=tmpH.t[:], in1=bc(hpv(7)), op=ALU.mult)
            P.op("pool", rkf, reads=[r_.b, kmod.b, tmpH.b, sq.b] + CR, writes=[tmpH.b, sq.b])
            yield
            sum8(sq, lambda pb: P.op("dve", lambda e: e.tensor_tensor(out=bon.t[:], in0=pv8(pb), in1=vz.t[:, :, 1, :], op=ALU.mult),
                                     reads=[pb.b, vz.b], writes=[bon.b]))
            yield
            sx.update(ar=ar, bk=bk, bkh=bkh, vz=vz, e1=e1, yy=yy, bon=bon, gg=gg, i2=i2, t0=t0)

        def super_cpre(s_i, sx):
            ar, bk, bkh, vz, e1, yy = (sx[k] for k in ("ar", "bk", "bkh", "vz", "e1", "yy"))
            cxs = []
            for c in range(NC):
                cx = {}
                yield from chunk_pre(s_i, c, ar, bk, bkh, vz, e1, yy, cx)
                cxs.append(cx)
            sx.update(cxs=cxs)

        def super_post(s_i, sx):
            ar, e1, yy, bon, gg, i2, t0 = (sx[k] for k in ("ar", "e1", "yy", "bon", "gg", "i2", "t0"))
            for cx in sx["cxs"]:
                yield from chunk_scan(cx, ar, e1, yy)
            sum8(yy, lambda pm: P.op("dve", lambda e: e.scalar_tensor_tensor(out=yc.t[:], in0=pv8(pm), scalar=-1.0 / 64, in1=yy.t[:],
                                                                             op0=ALU.mult, op1=ALU.add),
                                     reads=[pm.b, yy.b], writes=[yc.b]))
            yield
            P.op("act", lambda e: e.activation(out=ysq.t[:], in_=yc.t[:], func=AF.Square), reads=[yc.b], writes=[ysq.b])
            yield
            sum8(ysq, lambda pvv: P.op("act", lambda e: e.activation(out=rsd.t[:], in_=pv8(pvv), func=AF.Sqrt, bias=GN_EPS, scale=1.0 / 64),
                                       reads=[pvv.b], writes=[rsd.b]))
            yield
            og = ostg[i2]

            def fin(e):
                e.reciprocal(out=rsd.t[:], in_=rsd.t[:])
                e.tensor_tensor(out=yc.t[:], in0=yc.t[:], in1=rsd.t[:], op=ALU.mult)
                e.tensor_tensor(out=yc.t[:], in0=yc.t[:], in1=bc(hpv(8)), op=ALU.mult)
                e.tensor_tensor(out=yc.t[:], in0=yc.t[:], in1=bc(hpv(9)), op=ALU.add)
                e.tensor_tensor(out=yc.t[:], in0=yc.t[:], in1=bon.t[:], op=ALU.add)
                return e.tensor_tensor(out=og.t[:], in0=yc.t[:], in1=gg.t[:], op=ALU.mult)
            P.op("dve", fin, reads=[rsd.b, yc.b, bon.b, gg.b] + CR, writes=[rsd.b, yc.b, og.b])
            yield
            P.dma("sp", lambda e: e.dma_start(out=MDST(512, 512, t0, TS).rearrange("(h p) t -> p h t", h=8), in_=og.t[:]),
                  "ostg%d" % i2, reads=[og.b])
            yield

        conv = list(conv)
        per = -(-len(conv) // NS) if conv else 0
        cvi = 0
        def drive(items):
            alive = list(items)
            while alive:
                for it in list(alive):
                    st["pool"] = it[0]
                    try:
                        next(it[1])
                    except StopIteration:
                        alive.remove(it)
        sxs = {}
        for s_i in range(NS + 2):
            items = []
            if s_i < NS:
                sxs[s_i] = {}
                items.append((0, super_pre(s_i, sxs[s_i])))
            if 0 <= s_i - 1 < NS:
                items.append((1, super_cpre(s_i - 1, sxs[s_i - 1])))
            if 0 <= s_i - 2 < NS:
                items.append((2, super_post(s_i - 2, sxs.pop(s_i - 2))))
            drive(items)
            if s_i >= NS:
                continue
            for _ in range(per):
                if cvi < len(conv):
                    def cvt(k):
                        dst, src = conv[k]
                        P.dma("pool", lambda e: e.dma_start(out=dst, in_=src), "cv%d" % (k % 8))
                    cvt(cvi)
                    cvi += 1
        P.run_block()


def _consts_A(qg, kg):
    c = np.zeros((128, 8 * 128 + 256 + 4), np.float32)
    p = np.arange(128)
    c[:, 0:128] = 1.0
    c[:, 128:256] = (p[:, None] // 64 == p[None, :] // 64).astype(np.float32)
    rot = np.zeros((128, 128), np.float32)
    for m in range(128):
        if m % 64 < 32:
            rot[m + 32, m] = -1.0
        else:
            rot[m - 32, m] = 1.0
    c[:, 256:384] = rot
    c[:, 384:512] = np.eye(128, dtype=np.float32)
    i = (p % 64)[:, None]
    t = np.arange(64)[None, :]
    c[:, 512:576] = (i < t).astype(np.float32)
    c[:, 576:640] = (i <= t).astype(np.float32)
    kk = p[:, None]
    qq = np.arange(256)[None, :]
    dist = qq - kk
    c[:, 1024:1280] = np.where((dist >= 0) & (dist <= 128), 0.0, -262144.0)
    c[:, 1280] = np.tile(qg, 2)
    c[:, 1281] = np.tile(kg, 2)
    c[:, 1282] = (np.float32(10000.0) ** (-(np.arange(32, dtype=np.float32)) / np.float32(32)))[p % 32]
    return c


def _consts_64():
    c = np.zeros((64, 64 + 8 * TS), np.float32)
    t = np.arange(64)
    c[:, 0:64] = (t[:, None] > t[None, :]).astype(np.float32)
    m = np.ones((8, TS), np.float32)
    m[:, ::CH] = 0.0
    c[:, 64:] = m.reshape(1, -1)
    return c


def _wchunk(w, cols):
    blk = np.zeros((4096, 128), np.float32)
    blk[:, :len(cols)] = w[:, cols]
    return np.ascontiguousarray(blk.reshape(32, 128, 128).transpose(1, 0, 2)).reshape(128, 32 * 128)


def prep_A(inp, S):
    x = inp["x"]
    w_in = inp["w_in"][0]
    mu = inp["rwkv_mu"][0]
    maps = []
    xTs = [np.ascontiguousarray(x[b].T.reshape(32, 128, S).transpose(1, 0, 2)) for b in range(2)]
    cA = _consts_A(inp["q_norm_g"][0], inp["k_norm_g"][0])
    c64 = _consts_64()
    gA = np.ascontiguousarray(inp["attn_norm_g"][0].reshape(32, 128).T)
    RB = 3 * ATTN_W
    for core in range(8):
        b, g = core // 4, core % 4
        cols = []
        for base in (0, 2048, 4096, RB, RB + 2048, RB + 4096):
            for jj in range(4):
                cols.append(np.arange(base + 512 * g + 128 * jj, base + 512 * g + 128 * jj + 128))
        cols.append(np.arange(RB + 6144, RB + 6272))
        cols.append(np.arange(RB + 6272, RB + 6400))
        for jj in range(4):
            lo = RB + 6400 + 128 * jj
            cols.append(np.arange(lo, min(lo + 128, RB + 6880)))
        wA = np.stack([_wchunk(w_in, cc) for cc in cols])

        def hsl(v):
            return v[512 * g:512 * g + 512].reshape(8, 64).T
        hp = np.zeros((64, 10, 8), np.float32)
        hp[:, 0] = hsl(mu[0:2048])
        hp[:, 1] = hsl(mu[2048:4096])
        hp[:, 2] = hsl(mu[4096:6144])
        hp[:, 3] = hsl(inp["w0"][0])
        hp[:, 4] = hsl(inp["a0"][0])
        hp[:, 5] = hsl(inp["k_k"][0])
        hp[:, 6] = hsl(inp["k_a"][0])
        hp[:, 7] = hsl(inp["r_k"][0].reshape(-1))
        hp[:, 8] = hsl(inp["ln_x_w"][0])
        hp[:, 9] = hsl(inp["ln_x_b"][0])
        lp = np.zeros((128, 6), np.float32)
        lp[:, 0] = mu[6144:6272]
        lp[:, 1] = mu[6272:6400]
        mg = np.zeros(512, np.float32)
        mg[:480] = mu[6400:6880]
        lp[:, 2:6] = mg.reshape(4, 128).T
        wg = np.zeros((512, 512), np.float32)
        wg[:480] = inp["w_gate_up"][0][:, 512 * g:512 * g + 512]
        maps.append(dict(
            xT=xTs[b], pos=np.ascontiguousarray(inp["positions"][b][None, :].astype(np.int32)),
            wA=wA, gA=gA, c128=cA, hp=np.ascontiguousarray(hp.reshape(64, 80)), lp=lp,
            wdu=np.ascontiguousarray(inp["w_decay_up"][0][:, 512 * g:512 * g + 512]),
            wau=np.ascontiguousarray(inp["w_iclr_up"][0][:, 512 * g:512 * g + 512]),
            wgu=np.ascontiguousarray(wg.reshape(4, 128, 512).transpose(1, 0, 2)).reshape(128, 2048),
            c64=c64))
    return maps


def gather_mix(resA, S):
    mixT = np.zeros((2, 4096, S), np.float32)
    for core in range(8):
        b, g = core // 4, core % 4
        m = resA[core]["mixA"]
        mixT[b, 512 * g:512 * g + 512] = m[0:512]
        mixT[b, 2048 + 512 * g:2048 + 512 * g + 512] = m[512:1024]
    return mixT


NWB = 3
MPAD = 64
NFF = D_FF // 128


def build_B(NTOK, mix_dram=None, B=None, fused=False, mixx=None, wdecl=None):
    own = B is None
    if own:
        B = Builder()
    nc, P = B.nc, B.P
    T = 512
    NTB = NTOK // T
    IN, OUT = "ExternalInput", "ExternalOutput"
    if fused:
        qm_d = B.dram("qmask", [128, 4], F32, IN)
    elif mix_dram is None:
        mix_dram = B.dram("mixin", [4, 1024, NTOK + 2], F32, IN)
    xT = B.dram("xTb", [128, 32, NTOK + 2], F32, IN)
    pT = B.dram("pTb", [128, 2, NTOK], F32, IN)
    if wdecl is None:
        wdecl = declare_B_weights(B)[0]
    wout, wup, wdn, wple, wgt = (wdecl[k] for k in ("wout", "wup", "wdn", "wple", "wgt"))
    WQ = "sp" if fused else "pool"
    SQ = "pool" if fused else "sp"
    cst = B.dram("cstb", [128, 64 + 2 * NFF * 4 + 128], F32, IN)
    outT = B.dram("outT", [128, 32, NTOK], F32, OUT)
    NU = 2 * NFF

    with contextlib.ExitStack() as es:
        c_sb = B.sb(es, "cstb", [128, 64 + NU * 4 + 128], F32)
        ones_b = B.sb(es, "onesb", [128, 128], BF16)
        P.dma("sp", lambda e: e.dma_start(out=c_sb.t[:], in_=cst), "cstb", writes=[c_sb.b])
        P.dma("pool", lambda e: e.dma_start(out=ones_b.t[:], in_=cst[:, 64 + NU * 4:64 + NU * 4 + 128]), "onesb", writes=[ones_b.b])
        GM = c_sb.t[:, 0:32]
        GP = c_sb.t[:, 32:64]
        CW = c_sb.t[:, 64:64 + NU * 3].rearrange("p (u j) -> p u j", j=3)
        CB = c_sb.t[:, 64 + NU * 3:64 + NU * 4]
        CR = [c_sb.b, ones_b.b]

        h = B.sb(es, "h", [128, 32, T], F32)
        a16 = B.sb(es, "a16", [128, 32, T], BF16)
        hh = B.sb(es, "hh", [128, 32, 2], F32)
        a16h = B.sb(es, "a16h", [128, 32, 2], BF16)
        wr = [B.sb(es, "wr%d" % i, [128, 32, 128], BF16) for i in range(NWB)]
        wd = [B.sb(es, "wd%d" % i, [128, 4096], BF16) for i in range(2)]
        ue = [[B.sb(es, "ue%d%d" % (i, j), [128, T + 2], F32) for j in range(2)] for i in range(2)]
        cv = [[B.sb(es, "cv%d%d" % (i, j), [128, T], F32) for j in range(2)] for i in range(2)]
        actg = [B.sb(es, "actg%d" % i, [128, 2, T], BF16) for i in range(2)]
        uhalo = B.sb(es, "uhalo", [128, NU, 2], F32)
        rstd = B.sb(es, "rstdb", [128, T], F32)
        rstdh = B.sb(es, "rstdh", [128, 2], F32)
        rstde = B.sb(es, "rstde", [128, T], F32)
        p16 = B.sb(es, "p16", [128, 2, T], BF16)
        sqb = [B.sb(es, "sqbb%d" % i, [128, T], BF16) for i in range(2)]
        sqh = B.sb(es, "sqh", [128, 2], BF16)
        sg = [B.sb(es, "sg%d" % i, [128, T], F32) for i in range(2)]
        te = [B.sb(es, "te%d" % i, [128, T], F32) for i in range(2)]
        hparts = [Buf() for _ in range(32)]
        mixg_b = Buf("mixg")
        if fused:
            qm = B.sb(es, "qm", [128, 4], F32)
            cand = [B.sb(es, "cand%d" % i, [128, 4, T], BF16) for i in range(4)]
            candh = [B.sb(es, "candh%d" % i, [128, 32, 2], BF16) for i in range(4)]
            P.dma("sp", lambda e: e.dma_start(out=qm.t[:], in_=qm_d), "qm", writes=[qm.b])
            NCHK = 4 * NTOK // 512
            chunk_b = [Buf("mixg%d" % i) for i in range(NCHK)]
            order = [qq * (NTOK // 512) + tt for tt in range(NTOK // 512) for qq in range(4)]
            for ci in order:
                def ag(ci):
                    P.dma("pool", lambda e: e.collective_compute("AllGather", ALU.bypass, replica_groups=[[0, 1, 2, 3], [4, 5, 6, 7]],
                                                                 ins=[mixx[ci]], outs=[mix_dram[ci]]), "ag", writes=[chunk_b[ci]], inc=1)
                ag(ci)
        psm = [B.psb[i] for i in range(6)]
        pss = [B.psb[6], B.psb[7]]
        cnt = dict(w=0, d=0, m=0, s=0, sq=0, ue=0, ag=0, sg=0)

        def wslot():
            s_ = wr[cnt["w"] % NWB]
            key = "wr%d" % (cnt["w"] % NWB)
            cnt["w"] += 1
            return s_, key

        def pmain():
            p = psm[cnt["m"] % 6]
            cnt["m"] += 1
            return p

        def psmall():
            p = pss[cnt["s"] % 2]
            cnt["s"] += 1
            return p

        def load_w(src_ap):
            s_, key = wslot()
            P.dma(WQ, lambda e: e.dma_start(out=s_.t[:].rearrange("p k c -> p (k c)"), in_=src_ap), key, writes=[s_.b])
            return s_

        def mm32(ps_ap, s_, rhs_fn):
            def f(e):
                for kc in range(32):
                    r = e.matmul(ps_ap, s_.t[:, kc, :], rhs_fn(kc), start=(kc == 0), stop=(kc == 31))
                return r
            return f

        def rms(src, src_tokens, n, dst16, g_ap, rs, halo):
            ssp = psmall()
            for kc in range(32):
                def one(kc):
                    sq = sqh if halo else sqb[cnt["sq"] % 2]
                    cnt["sq"] += 1
                    sqa = sq.t[:, 0:n]
                    P.op("act", lambda e: e.activation(out=sqa, in_=src.t[:, kc, 0:n], func=AF.Square), reads=[src_tokens[kc]], writes=[sq.b])
                    P.op("pe", lambda e: e.matmul(ssp.t[:, 0:n], ones_b.t[:], sqa, start=(kc == 0), stop=(kc == 31)), reads=[sq.b] + CR, writes=[ssp.b])
                one(kc)
            P.op("act", lambda e: e.activation(out=rs.t[:, 0:n], in_=ssp.t[:, 0:n], func=AF.Sqrt, bias=NORM_EPS, scale=1.0 / D_MODEL), reads=[ssp.b], writes=[rs.b])
            P.op("dve", lambda e: e.reciprocal(out=rs.t[:, 0:n], in_=rs.t[:, 0:n]), reads=[rs.b], writes=[rs.b])
            for kc in range(32):
                def two(kc):
                    P.op("dve", lambda e: e.scalar_tensor_tensor(out=dst16.t[:, kc, 0:n], in0=src.t[:, kc, 0:n], scalar=g_ap[:, kc:kc + 1], in1=rs.t[:, 0:n],
                                                                 op0=ALU.mult, op1=ALU.mult),
                         reads=[src_tokens[kc], rs.b] + CR, writes=[dst16.b])
                two(kc)

        def do_tile(tt):
            t0 = tt * T
            first = tt == 0
            hhp = [hh.b] * 32
            for q4 in range(4):
                def ld(q4):
                    P.dma("sp", lambda e: e.dma_start(out=h.t[:, 8 * q4:8 * q4 + 8, :], in_=xT[:, 8 * q4:8 * q4 + 8, 2 + t0:2 + t0 + T]),
                          "hx%d" % q4, writes=hparts[8 * q4:8 * q4 + 8])
                    if not fused:
                        P.dma("pool", lambda e: e.dma_start(out=a16.t[:, 8 * q4:8 * q4 + 8, :],
                                                            in_=mix_dram[q4, :, 2 + t0:2 + t0 + T].rearrange("(k p) t -> p k t", p=128)),
                              "mx%d" % q4, writes=[a16.b])
                ld(q4)
            if fused:
                for g8 in range(8):
                    def selg(g8):
                        for qq in range(4):
                            def ldc(qq):
                                ci = (qq * NTOK + t0) // 512
                                P.dma("sp", lambda e: e.dma_start(out=cand[qq].t[:], in_=mix_dram[ci, g8 * 512:(g8 + 1) * 512, :].rearrange("(k p) t -> p k t", p=128)),
                                      "cd%d" % qq, reads=[chunk_b[ci]], writes=[cand[qq].b])
                            ldc(qq)
                        dst = a16.t[:, 4 * g8:4 * g8 + 4, :]

                        def sel(e):
                            r = e.tensor_scalar(out=dst, in0=cand[0].t[:], scalar1=qm.t[:, 0:1], scalar2=None, op0=ALU.mult)
                            for qq in range(1, 4):
                                r = e.scalar_tensor_tensor(out=dst, in0=cand[qq].t[:], scalar=qm.t[:, qq:qq + 1], in1=dst, op0=ALU.mult, op1=ALU.add)
                            return r
                        P.op("dve", sel, reads=[c_.b for c_ in cand] + [qm.b], writes=[a16.b])
                    selg(g8)
            P.dma("pool", lambda e: e.dma_start(out=p16.t[:], in_=pT[:, :, t0:t0 + T]), "p16", writes=[p16.b])
            if first:
                P.dma("sp", lambda e: e.dma_start(out=hh.t[:], in_=xT[:, :, 0:2]), "hhx", writes=[hh.b])
                if fused:
                    P.op("pool", lambda e: e.memset(candh[0].t[:], 0.0), writes=[candh[0].b])
                    for qq in range(1, 4):
                        def ldch(qq):
                            ci = qq * NTOK // 512 - 1
                            P.dma("sp", lambda e: e.dma_start(out=candh[qq].t[:], in_=mix_dram[ci, :, 510:512].rearrange("(k p) t -> p k t", p=128)),
                                  "cdh%d" % qq, reads=[chunk_b[ci]], writes=[candh[qq].b])
                        ldch(qq)

                    def selh(e):
                        r = e.tensor_scalar(out=a16h.t[:], in0=candh[0].t[:], scalar1=qm.t[:, 0:1], scalar2=None, op0=ALU.mult)
                        for qq in range(1, 4):
                            r = e.scalar_tensor_tensor(out=a16h.t[:], in0=candh[qq].t[:], scalar=qm.t[:, qq:qq + 1], in1=a16h.t[:], op0=ALU.mult, op1=ALU.add)
                        return r
                    P.op("dve", selh, reads=[c_.b for c_ in candh] + [qm.b], writes=[a16h.b])
                else:
                    for q4 in range(4):
                        def ldh(q4):
                            P.dma("pool", lambda e: e.dma_start(out=a16h.t[:, 8 * q4:8 * q4 + 8, :],
                                                                in_=mix_dram[q4, :, 0:2].rearrange("(k p) t -> p k t", p=128)),
                                  "mxh%d" % q4, writes=[a16h.b])
                        ldh(q4)
            for c in range(32):
                def oc(c):
                    s_ = load_w(wout[c])
                    ps = pmain()
                    P.op("pe", mm32(ps.t[:], s_, lambda kc: a16.t[:, kc, :]), reads=[s_.b, a16.b], writes=[ps.b])
                    P.op("dve", lambda e: e.tensor_tensor(out=h.t[:, c, :], in0=ps.t[:], in1=h.t[:, c, :], op=ALU.add), reads=[ps.b, hparts[c]], writes=[hparts[c]])
                    if first:
                        ph = psmall()
                        P.op("pe", mm32(ph.t[:, 0:2], s_, lambda kc: a16h.t[:, kc, :]), reads=[s_.b, a16h.b], writes=[ph.b])
                        P.op("dve", lambda e: e.tensor_tensor(out=hh.t[:, c, :], in0=ph.t[:, 0:2], in1=hh.t[:, c, :], op=ALU.add), reads=[ph.b, hh.b], writes=[hh.b])
                oc(c)
            BD_ = int(os.environ.get('B_DBG', '100'))

            def dump():
                for c in range(32):
                    P.dma("sp", _bind(lambda e, c: e.dma_start(out=outT[:, c, t0:t0 + T], in_=h.t[:, c, :]), c), "oc%d" % (c % 8), reads=[hparts[c]])
            if BD_ == 1:
                dump()
                return
            rms(h, hparts, T, a16, GM, rstd, False)
            if first:
                rms(hh, hhp, 2, a16h, GM, rstdh, True)
            def do_pairf(f0):
                ag = actg[cnt["ag"] % 2]
                cnt["ag"] += 1
                for fi in range(2):
                    def ff(f, fi):
                        cvs = []
                        for gi in range(2):
                            def gu(gi):
                                uc = gi * NFF + f
                                s_ = load_w(wup[uc])
                                ps = pmain()
                                P.op("pe", mm32(ps.t[:], s_, lambda kc: a16.t[:, kc, :]), reads=[s_.b, a16.b], writes=[ps.b])
                                u = ue[gi][cnt["ue"] % 2]
                                c1 = cv[gi][cnt["ue"] % 2]
                                if first:
                                    ph = psmall()
                                    P.op("pe", mm32(ph.t[:, 0:2], s_, lambda kc: a16h.t[:, kc, :]), reads=[s_.b, a16h.b], writes=[ph.b])
                                    P.op("dve", lambda e: e.tensor_copy(out=u.t[:, 0:2], in_=ph.t[:, 0:2]), reads=[ph.b], writes=[u.b])
                                else:
                                    P.op("dve", lambda e: e.tensor_copy(out=u.t[:, 0:2], in_=uhalo.t[:, uc, :]), reads=[uhalo.b], writes=[u.b])
                                P.op("act", lambda e: e.copy(out=u.t[:, 2:T + 2], in_=ps.t[:]), reads=[ps.b], writes=[u.b])
                                P.op("dve", lambda e: e.tensor_copy(out=uhalo.t[:, uc, :], in_=u.t[:, T:T + 2]), reads=[u.b], writes=[uhalo.b])
                                P.op("act", lambda e: e.activation(out=c1.t[:], in_=u.t[:, 2:T + 2], func=AF.Identity, bias=CB[:, uc:uc + 1], scale=CW[:, uc, 2:3]),
                                     reads=[u.b] + CR, writes=[c1.b])

                                def cvf(e):
                                    e.scalar_tensor_tensor(out=c1.t[:], in0=u.t[:, 1:T + 1], scalar=CW[:, uc, 1:2], in1=c1.t[:], op0=ALU.mult, op1=ALU.add)
                                    return e.scalar_tensor_tensor(out=c1.t[:], in0=u.t[:, 0:T], scalar=CW[:, uc, 0:1], in1=c1.t[:], op0=ALU.mult, op1=ALU.add)
                                P.op("dve", cvf, reads=[u.b, c1.b] + CR, writes=[c1.b])
                                cvs.append(c1)
                            gu(gi)
                        cnt["ue"] += 1
                        sgt = sg[cnt["sg"] % 2]
                        cnt["sg"] += 1
                        P.op("act", lambda e: e.activation(out=sgt.t[:], in_=cvs[0].t[:], func=AF.Silu), reads=[cvs[0].b], writes=[sgt.b])
                        P.op("dve", lambda e: e.tensor_tensor(out=ag.t[:, fi, :], in0=sgt.t[:], in1=cvs[1].t[:], op=ALU.mult), reads=[sgt.b, cvs[1].b], writes=[ag.b])
                    ff(f0 + fi, fi)
                wds = []
                for fi in range(2):
                    def ldd(fi):
                        s_ = wd[fi]
                        P.dma(WQ, lambda e: e.dma_start(out=s_.t[:], in_=wdn[f0 + fi]), "wd%d" % fi, writes=[s_.b])
                        wds.append(s_)
                    ldd(fi)
                for c in range(32):
                    def dc(c):
                        ps = pmain()

                        def f(e):
                            e.matmul(ps.t[:], wds[0].t[:, c * 128:(c + 1) * 128], ag.t[:, 0, :], start=True, stop=False)
                            return e.matmul(ps.t[:], wds[1].t[:, c * 128:(c + 1) * 128], ag.t[:, 1, :], start=False, stop=True)
                        P.op("pe", f, reads=[wds[0].b, wds[1].b, ag.b], writes=[ps.b])
                        P.op("dve", lambda e: e.tensor_tensor(out=h.t[:, c, :], in0=ps.t[:], in1=h.t[:, c, :], op=ALU.add), reads=[ps.b, hparts[c]], writes=[hparts[c]])
                    dc(c)
            for f0 in range(0, NFF, 2):
                do_pairf(f0)
            if BD_ == 2:
                dump()
                return
            for c in range(32):
                P.op("act", _bind(lambda e, c: e.copy(out=a16.t[:, c, :], in_=h.t[:, c, :]), c), reads=[hparts[c]], writes=[a16.b])
            P.dma(WQ, lambda e: e.dma_start(out=wd[0].t[:], in_=wple[:, 0:4096]), "wd0", writes=[wd[0].b])
            P.dma(WQ, lambda e: e.dma_start(out=wd[1].t[:], in_=wple[:, 4096:8192]), "wd1", writes=[wd[1].b])
            ssp = psmall()
            for c in range(32):
                def e1(c):
                    ps = pmain()

                    def f(e):
                        e.matmul(ps.t[:], wd[0].t[:, c * 128:(c + 1) * 128], p16.t[:, 0, :], start=True, stop=False)
                        return e.matmul(ps.t[:], wd[1].t[:, c * 128:(c + 1) * 128], p16.t[:, 1, :], start=False, stop=True)
                    P.op("pe", f, reads=[wd[0].b, wd[1].b, p16.b], writes=[ps.b])
                    sq = sqb[cnt["sq"] % 2]
                    cnt["sq"] += 1
                    P.op("act", lambda e: e.activation(out=sq.t[:], in_=ps.t[:], func=AF.Square), reads=[ps.b], writes=[sq.b])
                    P.op("pe", lambda e: e.matmul(ssp.t[:], ones_b.t[:], sq.t[:], start=(c == 0), stop=(c == 31)), reads=[sq.b] + CR, writes=[ssp.b])
                e1(c)
            P.op("act", lambda e: e.activation(out=rstde.t[:], in_=ssp.t[:], func=AF.Sqrt, bias=NORM_EPS, scale=1.0 / D_MODEL), reads=[ssp.b], writes=[rstde.b])
            P.op("dve", lambda e: e.reciprocal(out=rstde.t[:], in_=rstde.t[:]), reads=[rstde.b], writes=[rstde.b])
            for c in range(32):
                def e2(c):
                    s_ = load_w(wgt[c])
                    pg = pmain()
                    P.op("pe", mm32(pg.t[:], s_, lambda kc: a16.t[:, kc, :]), reads=[s_.b, a16.b], writes=[pg.b])
                    pe_ = pmain()

                    def f(e):
                        e.matmul(pe_.t[:], wd[0].t[:, c * 128:(c + 1) * 128], p16.t[:, 0, :], start=True, stop=False)
                        return e.matmul(pe_.t[:], wd[1].t[:, c * 128:(c + 1) * 128], p16.t[:, 1, :], start=False, stop=True)
                    P.op("pe", f, reads=[wd[0].b, wd[1].b, p16.b], writes=[pe_.b])
                    sgt = sg[cnt["sg"] % 2]
                    tet = te[cnt["sg"] % 2]
                    cnt["sg"] += 1
                    P.op("act", lambda e: e.activation(out=sgt.t[:], in_=pg.t[:], func=AF.Sigmoid), reads=[pg.b], writes=[sgt.b])

                    def comb(e):
                        e.scalar_tensor_tensor(out=tet.t[:], in0=pe_.t[:], scalar=GP[:, c:c + 1], in1=rstde.t[:], op0=ALU.mult, op1=ALU.mult)
                        e.tensor_tensor(out=tet.t[:], in0=tet.t[:], in1=sgt.t[:], op=ALU.mult)
                        return e.tensor_tensor(out=h.t[:, c, :], in0=h.t[:, c, :], in1=tet.t[:], op=ALU.add)
                    P.op("dve", comb, reads=[pe_.b, rstde.b, sgt.b, hparts[c]] + CR, writes=[tet.b, hparts[c]])
                    P.dma(SQ, lambda e: e.dma_start(out=outT[:, c, t0:t0 + T], in_=h.t[:, c, :]), "oc%d" % (c % 8), reads=[hparts[c]])
                e2(c)

        for tt in range(NTB):
            do_tile(tt)
        P.run_block()
    if own:
        B.finish()
    return nc


def _wchunks_all(w, nchunk):
    return np.ascontiguousarray(w.reshape(32, 128, nchunk, 128).transpose(2, 1, 0, 3)).reshape(nchunk, 128, 32 * 128)


def prep_B_weights(inp):
    perm = np.concatenate([np.concatenate([np.arange(512 * g, 512 * g + 512), np.arange(2048 + 512 * g, 2048 + 512 * g + 512)]) for g in range(4)])
    wout = _wchunks_all(inp["w_out"][0][perm, :], 32)
    wup = _wchunks_all(inp["w_mlp_up"][0], 2 * NFF)
    wdn = np.ascontiguousarray(inp["w_mlp_down"][0].reshape(NFF, 128, 4096))
    wple = np.ascontiguousarray(inp["w_ple_proj"][0].reshape(2, 128, 4096).transpose(1, 0, 2)).reshape(128, 8192)
    wgt = _wchunks_all(inp["w_ple_gate"][0], 32)
    NU = 2 * NFF
    cst = np.zeros((128, 64 + NU * 4 + 128), np.float32)
    cst[:, 0:32] = inp["mlp_norm_g"][0].reshape(32, 128).T
    cst[:, 32:64] = inp["ple_norm_g"][0].reshape(32, 128).T
    cw = inp["conv_w"][0].reshape(3, NU, 128).transpose(2, 1, 0)
    cst[:, 64:64 + NU * 3] = cw.reshape(128, NU * 3)
    cst[:, 64 + NU * 3:64 + NU * 4] = inp["conv_b"][0].reshape(NU, 128).T
    cst[:, 64 + NU * 4:] = 1.0
    return dict(wout=wout, wup=wup, wdn=wdn, wple=wple, wgt=wgt, cstb=cst)


def prep_B_acts(inp, S, core):
    NTOK = S // 4
    b, q = core // 4, core % 4
    lo = q * NTOK
    xe = np.zeros((NTOK + 2, 4096), np.float32)
    xe[2:] = inp["x"][b, lo:lo + NTOK]
    if q > 0:
        xe[0:2] = inp["x"][b, lo - 2:lo]
    xTb = np.ascontiguousarray(xe.T.reshape(32, 128, NTOK + 2).transpose(1, 0, 2))
    pTb = np.ascontiguousarray(inp["p"][0, b, lo:lo + NTOK].T.reshape(2, 128, NTOK).transpose(1, 0, 2))
    return dict(xTb=xTb, pTb=pTb)


def mix_for_core(resA, S, core):
    NTOK = S // 4
    b, q = core // 4, core % 4
    lo = q * NTOK
    m = np.zeros((4, 1024, NTOK + 2), np.float32)
    for g in range(4):
        src = resA[4 * b + g]["mixA"]
        m[g, :, 2:] = src[:, lo:lo + NTOK]
        if q > 0:
            m[g, :, 0:2] = src[:, lo - 2:lo]
    return m


_CACHE = {}


def kernel_unfused(**inp):
    inp = {k: np.asarray(v) for k, v in inp.items()}
    S = inp["x"].shape[1]
    NTOK = S // 4
    if ("A", S) not in _CACHE:
        _CACHE[("A", S)] = build_A(S)
        _CACHE[("B", S)] = build_B(NTOK)
    ncA, ncB = _CACHE[("A", S)], _CACHE[("B", S)]
    mapsA = prep_A(inp, S)
    resA = run_bass_kernel_spmd(ncA, mapsA, core_ids=list(range(8))).results
    del mapsA
    wB = prep_B_weights(inp)
    mapsB = []
    for core in range(8):
        d = dict(wB)
        d.update(prep_B_acts(inp, S, core))
        d["mixin"] = mix_for_core(resA, S, core)
        mapsB.append(d)
    resB = run_bass_kernel_spmd(ncB, mapsB, core_ids=list(range(8))).results
    out = np.zeros((2, S, 4096), np.float32)
    for core in range(8):
        b, q = core // 4, core % 4
        o = resB[core]["outT"]
        out[b, q * NTOK:(q + 1) * NTOK] = o.transpose(2, 1, 0).reshape(NTOK, 4096)
    return out


def declare_B_weights(B, bf16_copy=False):
    IN = "ExternalInput"
    d = dict(wout=B.dram("wout", [32, 128, 32 * 128], F32, IN),
             wup=B.dram("wup", [2 * NFF, 128, 32 * 128], F32, IN),
             wdn=B.dram("wdn", [NFF, 128, 4096], F32, IN),
             wple=B.dram("wple", [128, 2 * 4096], F32, IN),
             wgt=B.dram("wgt", [32, 128, 32 * 128], F32, IN))
    if not bf16_copy:
        return d, ()
    c = dict(wout=B.dram("wout_b", [32, 128, 4096], BF16, "Internal"),
             wup=B.dram("wup_b", [2 * NFF, 128, 4096], BF16, "Internal"),
             wdn=B.dram("wdn_b", [NFF, 128, 4096], BF16, "Internal"),
             wple=B.dram("wple_b", [128, 8192], BF16, "Internal"),
             wgt=B.dram("wgt_b", [32, 128, 4096], BF16, "Internal"))
    conv = [(c["wout"][i], d["wout"][i]) for i in range(32)]
    for f in range(NFF):
        conv += [(c["wup"][f], d["wup"][f]), (c["wup"][NFF + f], d["wup"][NFF + f])]
        if f % 2 == 1:
            conv += [(c["wdn"][f - 1], d["wdn"][f - 1]), (c["wdn"][f], d["wdn"][f])]
    conv += [(c["wple"][:, 0:4096], d["wple"][:, 0:4096]), (c["wple"][:, 4096:8192], d["wple"][:, 4096:8192])]
    conv += [(c["wgt"][i], d["wgt"][i]) for i in range(32)]
    return c, conv


def build_fused(S):
    B = Builder()
    wdecl, conv = declare_B_weights(B, bf16_copy=True)
    mixx = build_A(S, B=B, fused=True, conv=conv)
    mixg = B.dram("mixg", [S // 512, 4096, 512], BF16, "Internal")
    build_B(S // 4, mix_dram=mixg, B=B, fused=True, mixx=mixx, wdecl=wdecl)
    B.finish()
    return B.nc


def kernel(**inp):
    inp = {k: np.asarray(v) for k, v in inp.items()}
    S = inp["x"].shape[1]
    NTOK = S // 4
    if ("F", S) not in _CACHE:
        _CACHE[("F", S)] = build_fused(S)
    nc = _CACHE[("F", S)]
    maps = prep_A(inp, S)
    wB = prep_B_weights(inp)
    for core in range(8):
        maps[core].update(wB)
        maps[core].update(prep_B_acts(inp, S, core))
        qm = np.zeros((128, 4), np.float32)
        qm[:, core % 4] = 1.0
        maps[core]["qmask"] = qm
    res = run_bass_kernel_spmd(nc, maps, core_ids=list(range(8))).results
    out = np.zeros((2, S, 4096), np.float32)
    for core in range(8):
        b, q = core // 4, core % 4
        o = res[core]["outT"]
        out[b, q * NTOK:(q + 1) * NTOK] = o.transpose(2, 1, 0).reshape(NTOK, 4096)
    return out
```

```python
import contextlib
import math
import os
import numpy as np
import concourse.bass as bass
import concourse.mybir as mybir
from concourse.bass_utils import run_bass_kernel_spmd

F32 = mybir.dt.float32
BF16 = mybir.dt.bfloat16
I32 = mybir.dt.int32
AF = mybir.ActivationFunctionType
ALU = mybir.AluOpType

D_MODEL = 4096
HEAD_DIM = 64
ATTN_W = 2048
RWKV_W = 2048
D_FF = 11008
PLE_DIM = 256
NORM_EPS = 1e-6
GN_EPS = 64e-5
C0 = math.exp(-0.5)

ENGS = ("pe", "act", "dve", "pool", "sp")
EPOCH = 8000


class Buf:
    __slots__ = ("name", "w", "r")

    def __init__(self, name=""):
        self.name = name
        self.w = None
        self.r = []


class Op:
    __slots__ = ("eng", "fn", "deps", "dma", "sig", "idx", "dsem", "dcnt", "blk", "inc")

    def __init__(self, eng, fn, dma=False):
        self.eng = eng
        self.fn = fn
        self.deps = []
        self.dma = dma
        self.sig = False
        self.idx = None
        self.dsem = None
        self.dcnt = None
        self.blk = 0


class Prog:
    def __init__(self, nc):
        self.nc = nc
        self.ops = {e: [] for e in ENGS}
        self.sigcount = {e: 0 for e in ENGS}
        self.esems = {e: [] for e in ENGS}
        self.dma_sems = {}
        self.waited = {e: {} for e in ENGS}
        self._ctx = []
        self.blk = 0
        self.keybufs = {}

    def _sem(self, name):
        cm = self.nc.semaphore(name)
        s = cm.__enter__()
        self._ctx.append(cm)
        return s

    def dsem(self, key):
        if key not in self.dma_sems:
            self.dma_sems[key] = [self._sem("d_" + str(key)), 0]
        return self.dma_sems[key]

    def close(self):
        for cm in reversed(self._ctx):
            cm.__exit__(None, None, None)

    def _deps(self, op, reads, writes):
        deps = set()
        for b in reads:
            if b.w is not None:
                deps.add(b.w)
        for b in writes:
            if b.w is not None:
                deps.add(b.w)
            for r in b.r:
                deps.add(r)
        deps.discard(op)
        for d in deps:
            if op.dma or d.dma or d.eng != op.eng:
                d.sig = True
        op.deps = list(deps)
        for b in reads:
            b.r.append(op)
        for b in writes:
            b.w = op
            b.r = []

    def op(self, eng, fn, reads=(), writes=()):
        o = Op(eng, fn)
        o.blk = self.blk
        self._deps(o, reads, writes)
        self.ops[eng].append(o)
        return o

    def dma(self, eng, fn, semkey, reads=(), writes=(), inc=16):
        o = Op(eng, fn, dma=True)
        o.blk = self.blk
        o.inc = inc
        ds = self.dsem(semkey)
        ds[1] += inc
        o.dsem, o.dcnt = ds[0], ds[1]
        kb = self.keybufs.setdefault(semkey, Buf(semkey))
        self._deps(o, reads, list(writes) + [kb])
        self.ops[eng].append(o)
        return o

    def _assign(self):
        for e in ENGS:
            for o in self.ops[e]:
                if o.sig and not o.dma and o.idx is None:
                    self.sigcount[e] += 1
                    o.idx = self.sigcount[e]
            need = (self.sigcount[e] + EPOCH - 1) // EPOCH
            while len(self.esems[e]) < need:
                self.esems[e].append(self._sem("e_%s_%d" % (e, len(self.esems[e]))))

    def _wait_for(self, eng_name, eng, d):
        w = self.waited[eng_name]
        if d.blk < self.blk:
            return
        if d.dma:
            key = ("d", id(d.dsem))
            if w.get(key, 0) >= d.dcnt:
                return
            w[key] = d.dcnt
            eng.wait_ge(d.dsem, d.dcnt)
        else:
            ep, v = divmod(d.idx - 1, EPOCH)
            key = (d.eng, ep)
            if w.get(key, 0) >= v + 1:
                return
            w[key] = v + 1
            for e2 in range(ep):
                w[(d.eng, e2)] = EPOCH
            eng.wait_ge(self.esems[d.eng][ep], v + 1)

    def emit_engine(self, eng_name, eng):
        for o in self.ops[eng_name]:
            for d in sorted(o.deps, key=lambda x: (x.dma, x.idx or 0, x.dcnt or 0)):
                if (not o.dma) and (not d.dma) and d.eng == eng_name:
                    continue
                self._wait_for(eng_name, eng, d)
            ins = o.fn(eng)
            if o.dma:
                ins.then_inc(o.dsem, o.inc)
            elif o.sig:
                ep, v = divmod(o.idx - 1, EPOCH)
                ins.then_inc(self.esems[eng_name][ep], 1)

    def finish_waits(self, eng_name, eng):
        for e in ENGS:
            if self.sigcount[e] > 0:
                ep, v = divmod(self.sigcount[e] - 1, EPOCH)
                key = (e, ep)
                if self.waited[eng_name].get(key, 0) < v + 1:
                    self.waited[eng_name][key] = v + 1
                    eng.wait_ge(self.esems[e][ep], v + 1)
        for key, (sem, cnt) in self.dma_sems.items():
            k = ("d", id(sem))
            if cnt > 0 and self.waited[eng_name].get(k, 0) < cnt:
                self.waited[eng_name][k] = cnt
                eng.wait_ge(sem, cnt)

    def run_block(self):
        nc = self.nc
        for e in ENGS:
            lst = [o for o in self.ops[e] if not o.dma]
            if lst:
                lst[-1].sig = True
        self._assign()
        with nc.Block() as block:
            @block.tensor
            def _(t):
                self.emit_engine("pe", t)
                self.finish_waits("pe", t)

            @block.scalar
            def _(a):
                self.emit_engine("act", a)
                self.finish_waits("act", a)

            @block.vector
            def _(v):
                self.emit_engine("dve", v)
                self.finish_waits("dve", v)

            @block.gpsimd
            def _(g):
                self.emit_engine("pool", g)
                self.finish_waits("pool", g)

            @block.sync
            def _(s):
                self.emit_engine("sp", s)
                self.finish_waits("sp", s)
        for e in ENGS:
            self.ops[e] = []
        self.blk += 1


class TB:
    __slots__ = ("t", "b")

    def __init__(self, t, name=""):
        self.t = t
        self.b = Buf(name)


class Builder:
    def __init__(self):
        self.nc = bass.Bass("TRN2", target_bir_lowering=False)
        self.P = Prog(self.nc)
        self.es = contextlib.ExitStack()
        self.n = 0
        self.psb = []
        for i in range(8):
            t = self.es.enter_context(self.nc.psum_tensor("psb%d" % i, [128, 512], F32))
            self.psb.append(TB(t, "psb%d" % i))

    def dram(self, name, shape, dt, kind):
        return self.nc.dram_tensor(name, list(shape), dt, kind=kind).ap()

    def sb(self, es, name, shape, dt=F32):
        self.n += 1
        t = es.enter_context(self.nc.sbuf_tensor("%s_%d" % (name, self.n), list(shape), dt))
        return TB(t, name)

    def finish(self):
        self.P.close()
        self.es.close()


def _bind(f, *a):
    return lambda e: f(e, *a)


NW_A = 3
TS = 64
CH = 64


def build_A(S, phases=("A1", "A2", "A3"), debug=False, B=None, fused=False, conv=()):
    own = B is None
    if own:
        B = Builder()
    nc, P = B.nc, B.P
    T = 512
    NT = S // T
    IN, OUT, INT = "ExternalInput", "ExternalOutput", "Internal"
    xT = B.dram("xT", [128, 32, S], F32, IN)
    pos = B.dram("pos", [1, S], I32, IN)
    wA = B.dram("wA", [30, 128, 32 * 128], F32, IN)
    gA = B.dram("gA", [128, 32], F32, IN)
    c128 = B.dram("c128", [128, 8 * 128 + 256 + 4], F32, IN)
    hp_d = B.dram("hp", [64, 10 * 8], F32, IN)
    lp_d = B.dram("lp", [128, 6], F32, IN)
    wdu_d = B.dram("wdu", [128, 512], F32, IN)
    wau_d = B.dram("wau", [128, 512], F32, IN)
    wgu_d = B.dram("wgu", [128, 4 * 512], F32, IN)
    c64_d = B.dram("c64", [64, 64 + 8 * TS], F32, IN)
    if fused:
        mixA = B.dram("mixx", [S // 512, 1024, 512], BF16, INT)
        MIX = dict(ap=mixA, off=0, dt=BF16, dst=lambda r0, nr, t0, n: mixA[t0 // 512, r0:r0 + nr, t0 % 512:t0 % 512 + n])
    else:
        mixA = B.dram("mixA", [1024, S], F32, OUT)
        MIX = dict(ap=mixA, off=0, dt=F32, dst=lambda r0, nr, t0, n: mixA[r0:r0 + nr, t0:t0 + n])
    SK = OUT if debug else INT
    qk_s = B.dram("qk_s", [8, 128, S], BF16, SK)
    vv_s = B.dram("vv_s", [4, 128, S], BF16, SK)
    pr_s = B.dram("pr_s", [18, 128, S], F32, SK)

    with contextlib.ExitStack() as es0:
        cf = B.sb(es0, "cf", [128, 8 * 128 + 256 + 4], F32)
        cb = B.sb(es0, "cb", [128, 128 + 128 + 256], BF16)
        hp = B.sb(es0, "hp", [64, 10, 8], F32)
        lp = B.sb(es0, "lp", [128, 6], F32)
        c64 = B.sb(es0, "c64", [64, 64 + 8 * TS], F32)
        P.dma("sp", lambda e: e.dma_start(out=cf.t[:], in_=c128), "cf", writes=[cf.b])
        P.dma("sp", lambda e: e.dma_start(out=hp.t[:].rearrange("p a h -> p (a h)"), in_=hp_d), "hp", writes=[hp.b])
        P.dma("sp", lambda e: e.dma_start(out=lp.t[:], in_=lp_d), "lp", writes=[lp.b])
        P.dma("sp", lambda e: e.dma_start(out=c64.t[:], in_=c64_d), "c64", writes=[c64.b])
        P.dma("pool", lambda e: e.dma_start(out=cb.t[:, 0:128], in_=c128[:, 0:128]), "cb0", writes=[cb.b])
        P.dma("pool", lambda e: e.dma_start(out=cb.t[:, 128:256], in_=c128[:, 384:512]), "cb1", writes=[cb.b])
        P.dma("pool", lambda e: e.dma_start(out=cb.t[:, 256:512], in_=c128[:, 1024:1280]), "cb2", writes=[cb.b])
        ONESF = cf.t[:, 0:128]
        BD = cf.t[:, 128:256]
        ROTT = cf.t[:, 256:384]
        IDENTF = cf.t[:, 384:512]
        MASKG = cf.t[:, 512:640]
        GQ = cf.t[:, 1280:1281]
        GK = cf.t[:, 1281:1282]
        INVF = cf.t[:, 1282:1283]
        ONESB = cb.t[:, 0:128]
        IDENTB = cb.t[:, 128:256]
        MASKB = cb.t[:, 256:512]
        MASKN = c64.t[:, 0:64]
        RESETM = c64.t[:, 64:64 + 8 * TS]
        CR = [cf.b, cb.b, hp.b, lp.b, c64.b]

        if "A1" in phases:
          phase_A1(B, S, T, NT, xT, pos, wA, gA, qk_s, vv_s, pr_s,
                 dict(ONESB=ONESB, BD=BD, ROTT=ROTT, GQ=GQ, GK=GK, INVF=INVF, CR=CR))
        if "A2" in phases:
          phase_A2(B, S, qk_s, vv_s, MIX,
                 dict(ONESF=ONESF, IDENTB=IDENTB, MASKB=MASKB, CR=CR))
        if "A3" in phases:
          phase_A3(B, S, pr_s, MIX, wdu_d, wau_d, wgu_d,
                 dict(ONESF=ONESF, IDENTF=IDENTF, MASKG=MASKG, MASKN=MASKN, RESETM=RESETM,
                      hp=hp, lp=lp, CR=CR), conv=conv)
    if own:
        B.finish()
        return nc
    return mixA


def phase_A1(B, S, T, NT, xT, pos, wA, gA, qk_s, vv_s, pr_s, K):
    nc, P = B.nc, B.P
    CR = K["CR"]
    with contextlib.ExitStack() as es:
        x_sb = B.sb(es, "x", [128, 32, T], F32)
        xn = [B.sb(es, "xn%d" % i, [128, 32, T], BF16) for i in range(2)]
        w_sb = [B.sb(es, "w%d" % i, [128, 32, 128], BF16) for i in range(NW_A)]
        sqb = [B.sb(es, "sqb%d" % i, [128, T], BF16) for i in range(2)]
        rstd = B.sb(es, "rstd", [128, T], F32)
        g_sb = B.sb(es, "g", [128, 32], F32)
        posi = B.sb(es, "posi", [128, T], I32)
        cs = [dict(sin=B.sb(es, "sin%d" % i, [128, T], F32), cos=B.sb(es, "cos%d" % i, [128, T], F32)) for i in range(2)]
        tri = B.sb(es, "tri", [128, T], I32)
        sqf = [B.sb(es, "sqf%d" % i, [128, T], F32) for i in range(2)]
        rs2 = [B.sb(es, "rs2%d" % i, [128, T], F32) for i in range(2)]
        qn = [B.sb(es, "qn%d" % i, [128, T], F32) for i in range(2)]
        t1 = [B.sb(es, "t1%d" % i, [128, T], F32) for i in range(2)]
        tr = [t1[0], t1[1], rs2[0], rs2[1]]
        stb = [B.sb(es, "stb%d" % i, [128, T], BF16) for i in range(3)]
        stf = [B.sb(es, "stf%d" % i, [128, T], F32) for i in range(2)]
        P.dma("sp", lambda e: e.dma_start(out=g_sb.t[:], in_=gA), "gA", writes=[g_sb.b])
        xparts = [Buf() for _ in range(4)]

        psm = [B.psb[i] for i in range(4)]
        pss = [B.psb[i] for i in range(4, 8)]
        cnt = dict(w=0, m=0, s=0, sq=0, stb=0, stf=0, q=0)

        def prep(tt):
            t0 = tt * T
            xnb = xn[tt % 2]
            csb = cs[tt % 2]
            for q4 in range(4):
                P.dma("sp", _bind(lambda e, q4: e.dma_start(out=x_sb.t[:, 8 * q4:8 * q4 + 8, :], in_=xT[:, 8 * q4:8 * q4 + 8, t0:t0 + T]), q4),
                      "x%d" % q4, writes=[xparts[q4]])
            P.dma("sp", lambda e: e.dma_start(out=posi.t[:], in_=pos[:, t0:t0 + T].partition_broadcast(128)), "posi", writes=[posi.b])
            ssp = pss[cnt["s"] % 4]
            cnt["s"] += 1
            for kc in range(32):
                sq = sqb[cnt["sq"] % 2]
                cnt["sq"] += 1
                P.op("act", _bind(lambda e, sq, kc: e.activation(out=sq.t[:], in_=x_sb.t[:, kc, :], func=AF.Square), sq, kc),
                     reads=[xparts[kc // 8]], writes=[sq.b])
                P.op("pe", _bind(lambda e, sq, kc: e.matmul(ssp.t[:], K["ONESB"], sq.t[:], start=(kc == 0), stop=(kc == 31)), sq, kc),
                     reads=[sq.b] + CR, writes=[ssp.b])
            P.op("act", lambda e: e.activation(out=rstd.t[:], in_=ssp.t[:], func=AF.Sqrt, bias=NORM_EPS, scale=1.0 / D_MODEL),
                 reads=[ssp.b], writes=[rstd.b])
            P.op("dve", lambda e: e.reciprocal(out=rstd.t[:], in_=rstd.t[:]), reads=[rstd.b], writes=[rstd.b])
            for kc in range(32):
                P.op("dve", _bind(lambda e, kc: e.scalar_tensor_tensor(out=xnb.t[:, kc, :], in0=x_sb.t[:, kc, :], scalar=g_sb.t[:, kc:kc + 1],
                                                                       in1=rstd.t[:], op0=ALU.mult, op1=ALU.mult), kc),
                     reads=[xparts[kc // 8], rstd.b, g_sb.b], writes=[xnb.b])
            a, k_, r_, rc = tr
            TWO_PI = 2.0 * math.pi
            C1 = 6.28125
            C2 = 0.0019340515136718750
            C3 = TWO_PI - C1 - C2

            def trig(e):
                e.tensor_copy(out=a.t[:], in_=posi.t[:])
                e.tensor_scalar(out=a.t[:], in0=a.t[:], scalar1=K["INVF"], scalar2=None, op0=ALU.mult)
                e.tensor_scalar(out=k_.t[:], in0=a.t[:], scalar1=1.0 / TWO_PI, scalar2=None, op0=ALU.mult)
                e.tensor_copy(out=tri.t[:], in_=k_.t[:])
                e.tensor_copy(out=k_.t[:], in_=tri.t[:])
                e.scalar_tensor_tensor(out=r_.t[:], in0=k_.t[:], scalar=-C1, in1=a.t[:], op0=ALU.mult, op1=ALU.add)
                e.scalar_tensor_tensor(out=r_.t[:], in0=k_.t[:], scalar=-C2, in1=r_.t[:], op0=ALU.mult, op1=ALU.add)
                e.scalar_tensor_tensor(out=r_.t[:], in0=k_.t[:], scalar=-C3, in1=r_.t[:], op0=ALU.mult, op1=ALU.add)
                e.tensor_scalar(out=r_.t[:], in0=r_.t[:], scalar1=-math.pi, scalar2=math.pi, op0=ALU.max, op1=ALU.min)
                e.tensor_scalar(out=rc.t[:], in0=r_.t[:], scalar1=math.pi / 2, scalar2=None, op0=ALU.add)
                e.tensor_scalar(out=k_.t[:], in0=rc.t[:], scalar1=math.pi, scalar2=-TWO_PI, op0=ALU.is_gt, op1=ALU.mult)
                e.tensor_tensor(out=rc.t[:], in0=rc.t[:], in1=k_.t[:], op=ALU.add)
                return e.tensor_scalar(out=rc.t[:], in0=rc.t[:], scalar1=-math.pi, scalar2=math.pi, op0=ALU.max, op1=ALU.min)
            P.op("dve", trig, reads=[posi.b] + CR, writes=[tb.b for tb in tr] + [])
            P.op("act", lambda e: e.activation(out=csb["sin"].t[:], in_=r_.t[:], func=AF.Sin), reads=[r_.b], writes=[csb["sin"].b])
            P.op("act", lambda e: e.activation(out=csb["cos"].t[:], in_=rc.t[:], func=AF.Sin), reads=[rc.b], writes=[csb["cos"].b])

        pending = []

        def flush(now):
            keep = []
            for due, fn in pending:
                if due <= now:
                    fn()
                else:
                    keep.append((due, fn))
            pending[:] = keep

        def chunk(tt, j, seq):
            t0 = tt * T
            xnb = xn[tt % 2]
            csb = cs[tt % 2]
            slot = w_sb[cnt["w"] % NW_A]
            cnt["w"] += 1
            P.dma("pool", _bind(lambda e, slot, j: e.dma_start(out=slot.t[:].rearrange("p k c -> p (k c)"), in_=wA[j]), slot, j),
                  "w%d" % (cnt["w"] % NW_A), writes=[slot.b])
            ps = psm[cnt["m"] % 4]
            cnt["m"] += 1

            def mm(e, slot=slot, ps=ps):
                for kc in range(32):
                    r = e.matmul(ps.t[:], slot.t[:, kc, :], xnb.t[:, kc, :], start=(kc == 0), stop=(kc == 31))
                return r
            P.op("pe", mm, reads=[slot.b, xnb.b], writes=[ps.b])
            if j < 8:
                gv = K["GQ"] if j < 4 else K["GK"]
                i2 = cnt["q"] % 2
                cnt["q"] += 1
                sq, r2, qq, tt1 = sqf[i2], rs2[i2], qn[i2], t1[i2]
                P.op("act", lambda e: e.activation(out=sq.t[:], in_=ps.t[:], func=AF.Square), reads=[ps.b], writes=[sq.b])

                def st2():
                    hs = pss[cnt["s"] % 4]
                    cnt["s"] += 1
                    P.op("pe", lambda e: e.matmul(hs.t[:], K["BD"], sq.t[:], start=True, stop=True), reads=[sq.b] + CR, writes=[hs.b])
                    P.op("act", lambda e: e.activation(out=r2.t[:], in_=hs.t[:], func=AF.Sqrt, bias=NORM_EPS, scale=1.0 / HEAD_DIM),
                         reads=[hs.b], writes=[r2.b])
                    P.op("dve", lambda e: e.reciprocal(out=r2.t[:], in_=r2.t[:]), reads=[r2.b], writes=[r2.b])
                    P.op("dve", lambda e: e.scalar_tensor_tensor(out=qq.t[:], in0=ps.t[:], scalar=gv, in1=r2.t[:], op0=ALU.mult, op1=ALU.mult),
                         reads=[ps.b, r2.b] + CR, writes=[qq.b])

                    def st3():
                        rp = pss[cnt["s"] % 4]
                        cnt["s"] += 1
                        sb_ = stb[cnt["stb"] % 3]
                        cnt["stb"] += 1
                        P.op("pe", lambda e: e.matmul(rp.t[:], K["ROTT"], qq.t[:], start=True, stop=True), reads=[qq.b] + CR, writes=[rp.b])
                        P.op("dve", lambda e: e.tensor_tensor(out=tt1.t[:], in0=qq.t[:], in1=csb["cos"].t[:], op=ALU.mult),
                             reads=[qq.b, csb["cos"].b], writes=[tt1.b])
                        P.op("dve", lambda e: e.tensor_tensor(out=qq.t[:], in0=rp.t[:], in1=csb["sin"].t[:], op=ALU.mult),
                             reads=[rp.b, csb["sin"].b], writes=[qq.b])
                        P.op("dve", lambda e: e.tensor_tensor(out=sb_.t[:], in0=tt1.t[:], in1=qq.t[:], op=ALU.add),
                             reads=[tt1.b, qq.b], writes=[sb_.b])
                        P.dma("sp", lambda e: e.dma_start(out=qk_s[j, :, t0:t0 + T], in_=sb_.t[:]), "stb%d" % (cnt["stb"] % 3), reads=[sb_.b])
                    pending.append((seq + 2, st3))
                pending.append((seq + 1, st2))
            elif j < 12:
                sb_ = stb[cnt["stb"] % 3]
                cnt["stb"] += 1
                P.op("act", lambda e: e.copy(out=sb_.t[:], in_=ps.t[:]), reads=[ps.b], writes=[sb_.b])
                P.dma("sp", lambda e: e.dma_start(out=vv_s[j - 8, :, t0:t0 + T], in_=sb_.t[:]), "stb%d" % (cnt["stb"] % 3), reads=[sb_.b])
            else:
                sf = stf[cnt["stf"] % 2]
                cnt["stf"] += 1
                P.op("act", lambda e: e.copy(out=sf.t[:], in_=ps.t[:]), reads=[ps.b], writes=[sf.b])
                P.dma("sp", lambda e: e.dma_start(out=pr_s[j - 12, :, t0:t0 + T], in_=sf.t[:]), "stf%d" % (cnt["stf"] % 2), reads=[sf.b])

        prep(0)
        seq = 0
        for tt in range(NT):
            for j in range(30):
                chunk(tt, j, seq)
                seq += 1
                flush(seq)
                if j == 14 and tt + 1 < NT:
                    prep(tt + 1)
        flush(seq + 10)
        P.run_block()


def phase_A2(B, S, qk_s, vv_s, MIX, K):
    nc, P = B.nc, B.P
    CR = K["CR"]
    with contextlib.ExitStack() as es:
        qn_ = B.sb(es, "qn", [128, S], BF16)
        kn_ = B.sb(es, "kn", [128, S], BF16)
        vn_ = B.sb(es, "vn", [128, S], BF16)
        qd_ = B.sb(es, "qd", [128, S], BF16)
        kd_ = B.sb(es, "kd", [128, S], BF16)
        vd_ = B.sb(es, "vd", [128, S], BF16)
        acc = [B.sb(es, "acc%d" % h, [65, S], F32) for h in range(2)]
        vp = [B.sb(es, "vp%d" % i, [128, 2, 65], BF16) for i in range(3)]
        pT = [B.sb(es, "pT%d" % i, [128, 256], BF16) for i in range(4)]
        rec = [B.sb(es, "rec%d" % i, [64, 512], F32) for i in range(2)]
        ost = [B.sb(es, "ost%d" % i, [64, 512], MIX["dt"]) for i in range(2)]
        MDST = MIX["dst"]
        for v_ in vp:
            P.op("pool", _bind(lambda e, v_: e.memset(v_.t[:], 1.0), v_), writes=[v_.b])
        sc_ps = [B.psb[0], B.psb[1]]
        o_ps = [[B.psb[2 + h * 2 + i] for i in range(2)] for h in range(2)]
        vt_ps = [B.psb[6], B.psb[7]]
        fin_ps = [B.psb[0], B.psb[1]]
        c = dict(vp=0, pT=0, sc=0, vt=0, fin=0, kb=0)

        def o_ap(h, i):
            return o_ps[h][i].t[0:65, 0:128]

        def vt_ap(i):
            return vt_ps[i].t[:, 0:64].bitcast(BF16)

        def do_kb(hp_i, bi, d, r, kb, nb, M, q3, k3, v3):
            nq = 256 if kb < nb - 1 else 128
            k0 = r * M + kb * 128
            vti = c["vt"] % 2
            c["vt"] += 1
            vtb = vt_ps[vti]
            vpt = vp[c["vp"] % 3]
            c["vp"] += 1
            P.op("pe", lambda e: e.transpose(out=vt_ap(vti), in_=v3.t[:, k0:k0 + 128], identity=K["IDENTB"]),
                 reads=[v3.b] + CR, writes=[vtb.b])
            P.op("dve", lambda e: e.tensor_copy(out=vpt.t[:, :, 0:64], in_=vt_ap(vti).rearrange("p (h c) -> p h c", h=2)),
                 reads=[vtb.b], writes=[vpt.b])
            cur = [do_head_a(nq, k0, h, q3, k3) for h in range(2)]
            flush_pv()
            pend.append((bi, d, r, kb, nq, vpt, cur))

        pend = []

        def flush_pv():
            while pend:
                bi, d, r, kb, nq, vpt, cur = pend.pop(0)
                for h in range(2):
                    do_head_b(bi, d, r, kb, nq, h, vpt, cur[h])

        def do_head_a(nq, k0, h, q3, k3):
            hs = slice(h * 64, (h + 1) * 64)
            sc = sc_ps[c["sc"] % 2]
            c["sc"] += 1
            pt = pT[c["pT"] % 4]
            c["pT"] += 1

            def scf(e):
                e.matmul(sc.t[:, 0:nq], k3.t[hs, k0:k0 + 128], q3.t[hs, k0:k0 + nq], start=True, stop=False)
                return e.matmul(sc.t[:, 0:nq], K["IDENTB"], K["MASKB"][:, 0:nq], start=False, stop=True)
            P.op("pe", scf, reads=[q3.b, k3.b] + CR, writes=[sc.b])
            P.op("act", lambda e: e.activation(out=pt.t[:, 0:nq], in_=sc.t[:, 0:nq], func=AF.Exp, scale=0.125),
                 reads=[sc.b], writes=[pt.b])
            return pt

        def do_head_b(bi, d, r, kb, nq, h, vpt, pt):
            oa = o_ps[h][kb % 2]
            ob = o_ps[h][(kb + 1) % 2]

            def pv(e):
                r_ = e.matmul(o_ap(h, kb % 2), vpt.t[:, h, :], pt.t[:, 0:128], start=(kb == 0), stop=True, skip_group_check=True)
                if nq == 256:
                    r_ = e.matmul(o_ap(h, (kb + 1) % 2), vpt.t[:, h, :], pt.t[:, 128:256], start=True, stop=False, skip_group_check=True)
                return r_
            P.op("pe", pv, reads=[pt.b, vpt.b, oa.b], writes=[oa.b] + ([ob.b] if nq == 256 else []))
            tpos = (kb * 128) * d + r

            def ev(e):
                dst = acc[h].t[:, tpos:tpos + 127 * d + 1:d] if d > 1 else acc[h].t[:, tpos:tpos + 128]
                if bi == 0:
                    return e.tensor_copy(out=dst, in_=o_ap(h, kb % 2))
                return e.tensor_tensor(out=dst, in0=dst, in1=o_ap(h, kb % 2), op=ALU.add)
            P.op("dve", ev, reads=[oa.b, acc[h].b], writes=[acc[h].b, oa.b])

        def do_branch(hp_i, bi, d):
            M = S // d
            nb = M // 128
            if d == 1:
                q3, k3, v3 = qn_, kn_, vn_
            else:
                def cp(src, dst, eng):
                    P.op(eng, lambda e: e.tensor_copy(out=dst.t[:].rearrange("p (r m) -> p r m", r=d),
                                                      in_=src.t[:].rearrange("p (m r) -> p r m", r=d)),
                         reads=[src.b], writes=[dst.b])
                cp(qn_, qd_, "dve")
                cp(kn_, kd_, "pool")
                cp(vn_, vd_, "dve")
                q3, k3, v3 = qd_, kd_, vd_
            for r in range(d):
                for kb in range(nb):
                    do_kb(hp_i, bi, d, r, kb, nb, M, q3, k3, v3)
                flush_pv()

        def do_fin(hp_i, h, s0):
            fp = fin_ps[c["fin"] % 2]
            rc_ = rec[c["fin"] % 2]
            os_ = ost[c["fin"] % 2]
            key = "ost%d" % (c["fin"] % 2)
            c["fin"] += 1
            P.op("pe", lambda e: e.matmul(fp.t[0:64, :], K["ONESF"][64:65, 0:64], acc[h].t[64:65, s0:s0 + 512], start=True, stop=True),
                 reads=[acc[h].b] + CR, writes=[fp.b])
            P.op("dve", lambda e: e.reciprocal(out=rc_.t[:], in_=fp.t[0:64, :]), reads=[fp.b], writes=[rc_.b])
            P.op("pool", lambda e: e.tensor_tensor(out=os_.t[:], in0=acc[h].t[0:64, s0:s0 + 512], in1=rc_.t[:], op=ALU.mult),
                 reads=[rc_.b, acc[h].b], writes=[os_.b])
            row = (hp_i * 2 + h) * 64
            P.dma("sp", lambda e: e.dma_start(out=MDST(row, 64, s0, 512), in_=os_.t[:]), key, reads=[os_.b])

        def do_pair(hp_i):
            P.dma("sp", lambda e: e.dma_start(out=qn_.t[:], in_=qk_s[hp_i]), "a2q", writes=[qn_.b])
            P.dma("sp", lambda e: e.dma_start(out=kn_.t[:], in_=qk_s[4 + hp_i]), "a2k", writes=[kn_.b])
            P.dma("sp", lambda e: e.dma_start(out=vn_.t[:], in_=vv_s[hp_i]), "a2v", writes=[vn_.b])
            for bi, d in enumerate((1, 4, 16)):
                do_branch(hp_i, bi, d)
            for h in range(2):
                for s0 in range(0, S, 512):
                    do_fin(hp_i, h, s0)

        for hp_i in range(4):
            do_pair(hp_i)
        P.run_block()


def phase_A3(B, S, pr_s, MIX, wdu_d, wau_d, wgu_d, K, conv=()):
    nc, P = B.nc, B.P
    CR = K["CR"]
    hp, lp = K["hp"], K["lp"]
    NS = S // TS
    NC = TS // CH
    W = TS + 1
    with contextlib.ExitStack() as es:
        def H(name, n=TS, extra=()):
            return B.sb(es, name, [64, 8] + list(extra) + [n], F32)

        def S64(name, rows=64):
            return B.sb(es, name, [rows, 8, 64], F32)
        wdu = B.sb(es, "wdu", [128, 512], F32)
        wau = B.sb(es, "wau", [128, 512], F32)
        wgu = B.sb(es, "wgu", [128, 4, 512], F32)
        P.dma("sp", lambda e: e.dma_start(out=wdu.t[:], in_=wdu_d), "wdu", writes=[wdu.b])
        P.dma("sp", lambda e: e.dma_start(out=wau.t[:], in_=wau_d), "wau", writes=[wau.b])
        P.dma("sp", lambda e: e.dma_start(out=wgu.t[:].rearrange("p k c -> p (k c)"), in_=wgu_d), "wgu", writes=[wgu.b])
        RX = [H("RX%d" % i, W) for i in range(2)]
        KX = [H("KX%d" % i, W) for i in range(2)]
        VX = [H("VX%d" % i, W) for i in range(2)]
        WX = [B.sb(es, "WX%d" % i, [128, W], F32) for i in range(2)]
        AXl = [B.sb(es, "AX%d" % i, [128, W], F32) for i in range(2)]
        GX = [B.sb(es, "GX%d" % i, [128, 4, W], F32) for i in range(2)]
        parts = {}

        def part(tb, i):
            k = (id(tb), i)
            if k not in parts:
                parts[k] = Buf()
            return parts[k]
        D1 = H("D1")
        r_ = H("r")
        k_ = H("k")
        VZ = [H("VZ%d" % i, TS, extra=(2,)) for i in range(2)]
        wdm = B.sb(es, "wdm", [128, TS], F32)
        adm = B.sb(es, "adm", [128, TS], F32)
        gdm = B.sb(es, "gdm", [128, 4, TS], F32)
        dl = B.sb(es, "dl", [128, 4, TS], F32)
        sw = H("sw")
        a_ = H("a")
        g_ = [H("g%d" % i) for i in range(3)]
        kk = H("kk")
        sq = H("sq")
        kmod = H("kmod")
        ba = H("ba")
        cs_ = H("cs")
        E1 = [H("E1%d" % i) for i in range(3)]
        E2 = H("E2")
        E3 = H("E3")
        E4 = H("E4")
        tmpH = H("tmpH")
        AR = [H("AR%d" % i, TS, extra=(2,)) for i in range(3)]
        BK = [H("BK%d" % i, TS, extra=(2,)) for i in range(2)]
        BKh = [H("BKh%d" % i, TS, extra=(2,)) for i in range(2)]
        bonus = [H("bon%d" % i) for i in range(3)]
        Y = [H("Y%d" % i) for i in range(2)]
        Gm = [B.sb(es, "Gm%d" % i, [128, 8, 128], F32) for i in range(2)]
        QN0 = [S64("QN0%d" % i) for i in range(2)]
        QP = [S64("QP%d" % i) for i in range(2)]
        QtP = [S64("QtP%d" % i) for i in range(2)]
        IQ = S64("IQ")
        X = [S64("X%d" % i) for i in range(2)]
        Atm = S64("Atm")
        BKt = [S64("BKt%d" % i, 128) for i in range(2)]
        UV = [S64("UV%d" % i, 128) for i in range(2)]
        Wsb = S64("Wsb")
        Uhat = [S64("Uhat%d" % i) for i in range(2)]
        AhT = [S64("AhT%d" % i) for i in range(2)]
        ST = [S64("ST%d" % i) for i in range(2)]
        STd = S64("STd")
        yc = H("yc")
        ysq = H("ysq")
        rsd = H("rsd")
        ostg = [B.sb(es, "ostg%d" % i, [64, 8, TS], MIX["dt"]) for i in range(2)]
        MDST = MIX["dst"]
        P.op("pool", lambda e: e.memset(ST[0].t[:], 0.0), writes=[ST[0].b])
        P.op("pool", lambda e: e.memset(VZ[0].t[:], 0.0), writes=[VZ[0].b])
        P.op("pool", lambda e: e.memset(VZ[1].t[:], 0.0), writes=[VZ[1].b])
        P.op("pool", lambda e: e.memset(UV[0].t[:], 0.0), writes=[UV[0].b])
        P.op("pool", lambda e: e.memset(UV[1].t[:], 0.0), writes=[UV[1].b])

        ONES64 = K["ONESF"][0:64, 0:64]
        ID64 = K["IDENTF"][0:64, 0:64]
        ID64B = ID64.unsqueeze(1).to_broadcast([64, 8, 64])
        MASKNB = K["MASKN"].unsqueeze(1).to_broadcast([64, 8, 64])
        MASKGB = K["MASKG"].unsqueeze(1).to_broadcast([128, 4, 128])
        st = dict(ps=0, sti=0, chunk=0)

        def nps():
            pool = st.get("pool", 0)
            key = "ps%d" % pool
            base, size = ((0, 3), (3, 3), (6, 2))[pool]
            p = B.psb[base + st.get(key, 0) % size]
            st[key] = st.get(key, 0) + 1
            return p

        def hpv(i):
            return hp.t[:, i, :].unsqueeze(2)

        def bc(ap, n=TS):
            return ap.to_broadcast([64, 8, n])

        def pv8(p, rows=64):
            return p.t[0:rows, :].rearrange("p (h t) -> p h t", h=8)

        def mm8(out_fn, lhs_fn, rhs_fn):
            def f(e):
                for h in range(8):
                    r = e.matmul(out_fn(h), lhs_fn(h), rhs_fn(h), start=True, stop=True)
                return r
            return f

        def sum8(src, consume):
            pk = nps()
            P.op("pe", mm8(lambda h: pv8(pk)[:, h, :], lambda h: ONES64, lambda h: src.t[:, h, :]), reads=[src.b] + CR, writes=[pk.b])
            consume(pk)

        def chunk_pre(s_i, c, ar, bk, bkh, vz, e1, yy, cx):
            cs0 = c * CH
            csl = slice(cs0, cs0 + CH)
            ci = st["chunk"] % 2
            st["chunk"] += 1
            gm, qn0, bkt, uv, uh, aht = Gm[ci], QN0[ci], BKt[ci], UV[ci], Uhat[ci], AhT[ci]
            for half in range(2):
                def ghalf(half):
                    pg_ = nps()

                    def gmm(e):
                        for hh in range(4):
                            h = half * 4 + hh
                            r = e.matmul(pg_.t[:, hh * 128:(hh + 1) * 128], bk.t[:, h, :, csl], ar.t[:, h, :, csl], start=True, stop=True)
                        return r
                    P.op("pe", gmm, reads=[bk.b, ar.b], writes=[pg_.b])
                    P.op("dve", lambda e: e.tensor_tensor(out=gm.t[:, half * 4:half * 4 + 4, :], in0=pg_.t[:].rearrange("p (h t) -> p h t", h=4),
                                                          in1=MASKGB, op=ALU.mult),
                         reads=[pg_.b] + CR, writes=[gm.b])
                ghalf(half)
                yield
            pn = nps()
            P.op("pe", mm8(lambda h: pv8(pn)[:, h, :], lambda h: ar.t[:, h, 0, csl], lambda h: bk.t[:, h, 0, csl]), reads=[ar.b, bk.b], writes=[pn.b])
            yield
            P.op("dve", lambda e: e.tensor_tensor(out=qn0.t[:], in0=pv8(pn), in1=MASKNB, op=ALU.mult), reads=[pn.b] + CR, writes=[qn0.b])
            yield
            P.op("pool", lambda e: e.tensor_tensor(out=X[0].t[:], in0=gm.t[0:64, :, 0:64], in1=ID64B, op=ALU.add), reads=[gm.b] + CR, writes=[X[0].b])
            yield

            def level(lvl, q_cur, qt_ap, qt_b, x_cur):
                pq = nps()
                P.op("pe", mm8(lambda h: pv8(pq)[:, h, :], qt_ap, lambda h: q_cur.t[:, h, :]), reads=[qt_b, q_cur.b], writes=[pq.b])
                yield
                q_new = QP[lvl % 2]
                qt_new = QtP[lvl % 2]
                if lvl < 5:
                    pqt = nps()
                    P.op("pe", mm8(lambda h: pv8(pqt)[:, h, :], lambda h: q_cur.t[:, h, :], qt_ap), reads=[qt_b, q_cur.b], writes=[pqt.b])
                    P.op("act", lambda e: e.copy(out=q_new.t[:], in_=pv8(pq)), reads=[pq.b], writes=[q_new.b])
                    P.op("dve", lambda e: e.tensor_copy(out=qt_new.t[:], in_=pv8(pqt)), reads=[pqt.b], writes=[qt_new.b])
                    P.op("pool", lambda e: e.tensor_tensor(out=IQ.t[:], in0=q_new.t[:], in1=ID64B, op=ALU.add), reads=[q_new.b] + CR, writes=[IQ.b])
                else:
                    P.op("dve", lambda e: e.tensor_tensor(out=IQ.t[:], in0=pv8(pq), in1=ID64B, op=ALU.add), reads=[pq.b] + CR, writes=[IQ.b])
                px = nps()
                x_new = X[lvl % 2]
                P.op("pe", mm8(lambda h: pv8(px)[:, h, :], lambda h: IQ.t[:, h, :], lambda h: x_cur.t[:, h, :]), reads=[IQ.b, x_cur.b], writes=[px.b])
                yield
                P.op("act", lambda e: e.copy(out=x_new.t[:], in_=pv8(px)), reads=[px.b], writes=[x_new.b])
                yield
                return q_new, (lambda h: qt_new.t[:, h, :]), qt_new.b, x_new

            q_cur, qt_ap, qt_b, x_cur = qn0, (lambda h: gm.t[0:64, h, 0:64]), gm.b, X[0]
            for lvl in range(1, 6):
                q_cur, qt_ap, qt_b, x_cur = yield from level(lvl, q_cur, qt_ap, qt_b, x_cur)
            TT = x_cur
            pa_, pb_, pv_ = nps(), nps(), nps()

            def trs(e):
                for h in range(8):
                    e.transpose(out=pv8(pa_)[:, h, :], in_=ar.t[:, h, 0, csl], identity=ID64)
                for h in range(8):
                    e.transpose(out=pv8(pb_, 128)[:, h, :], in_=bkh.t[:, h, :, csl], identity=ID64)
                for h in range(8):
                    r = e.transpose(out=pv8(pv_, 128)[:, h, :], in_=vz.t[:, h, :, csl], identity=ID64)
                return r
            P.op("pe", trs, reads=[ar.b, bkh.b, vz.b] + CR, writes=[pa_.b, pb_.b, pv_.b])
            yield
            P.op("act", lambda e: e.copy(out=Atm.t[:], in_=pv8(pa_)), reads=[pa_.b], writes=[Atm.b])
            yield
            P.op("dve", lambda e: e.tensor_copy(out=bkt.t[:], in_=pv8(pb_, 128)), reads=[pb_.b], writes=[bkt.b])
            yield
            P.op("act", lambda e: e.copy(out=uv.t[64:128, :, :], in_=pv8(pv_, 128)[64:128, :, :]), reads=[pv_.b], writes=[uv.b])
            yield
            pw_ = nps()
            P.op("pe", mm8(lambda h: pv8(pw_)[:, h, :], lambda h: gm.t[64:128, h, 0:64], lambda h: uv.t[64:128, h, :]), reads=[gm.b, uv.b], writes=[pw_.b])
            yield
            P.op("act", lambda e: e.copy(out=Wsb.t[:], in_=pv8(pw_)), reads=[pw_.b], writes=[Wsb.b])
            yield
            pu_, ph_ = nps(), nps()
            P.op("pe", mm8(lambda h: pv8(pu_)[:, h, :], lambda h: TT.t[:, h, :], lambda h: Wsb.t[:, h, :]), reads=[TT.b, Wsb.b], writes=[pu_.b])
            yield
            P.op("pe", mm8(lambda h: pv8(ph_)[:, h, :], lambda h: Atm.t[:, h, :], lambda h: TT.t[:, h, :]), reads=[TT.b, Atm.b], writes=[ph_.b])
            yield
            P.op("act", lambda e: e.copy(out=uh.t[:], in_=pv8(pu_)), reads=[pu_.b], writes=[uh.b])
            yield
            P.op("dve", lambda e: e.tensor_copy(out=aht.t[:], in_=pv8(ph_)), reads=[ph_.b], writes=[aht.b])
            yield
            cx.update(gm=gm, bkt=bkt, uv=uv, uh=uh, aht=aht, csl=csl, cs0=cs0)

        def chunk_scan(cx, ar, e1, yy):
            gm, bkt, uv, uh, aht, csl, cs0 = (cx[k] for k in ("gm", "bkt", "uv", "uh", "aht", "csl", "cs0"))
            st_old = ST[st["sti"] % 2]
            st_new = ST[(st["sti"] + 1) % 2]
            st["sti"] += 1
            pc_ap = e1.t[:, :, cs0 + CH - 1:cs0 + CH]
            P.op("pool", lambda e: e.tensor_tensor(out=STd.t[:], in0=st_old.t[:], in1=pc_ap.to_broadcast([64, 8, 64]), op=ALU.mult),
                 reads=[st_old.b, e1.b], writes=[STd.b])
            yield
            pU = nps()
            P.op("pe", mm8(lambda h: pv8(pU)[:, h, :], lambda h: aht.t[:, h, :], lambda h: st_old.t[:, h, :]), reads=[aht.b, st_old.b], writes=[pU.b])
            yield
            P.op("dve", lambda e: e.tensor_tensor(out=uv.t[0:64, :, :], in0=pv8(pU), in1=uh.t[:], op=ALU.add), reads=[pU.b, uh.b], writes=[uv.b])
            yield
            pS = nps()
            P.op("pe", mm8(lambda h: pv8(pS)[:, h, :], lambda h: bkt.t[:, h, :], lambda h: uv.t[:, h, :]), reads=[bkt.b, uv.b], writes=[pS.b])
            yield
            P.op("dve", lambda e: e.tensor_tensor(out=st_new.t[:], in0=pv8(pS), in1=STd.t[:], op=ALU.add), reads=[pS.b, STd.b], writes=[st_new.b])
            yield
            pY = nps()

            def y1(e):
                for h in range(8):
                    e.matmul(pv8(pY)[:, h, :], st_old.t[:, h, :], ar.t[:, h, 1, csl], start=True, stop=False)
                    r = e.matmul(pv8(pY)[:, h, :], uv.t[:, h, :], gm.t[:, h, 64:128], start=False, stop=True)
                return r
            P.op("pe", y1, reads=[st_old.b, ar.b, uv.b, gm.b], writes=[pY.b])
            yield
            P.op("act", lambda e: e.copy(out=yy.t[:, :, csl], in_=pv8(pY)), reads=[pY.b], writes=[yy.b])
            yield

        def super_pre(s_i, sx):
            t0 = s_i * TS
            i2 = s_i % 2
            rx, kx, vx, wx, ax, gx = RX[i2], KX[i2], VX[i2], WX[i2], AXl[i2], GX[i2]
            i3 = s_i % 3
            vz, ar, bk, bkh, e1, bon, gg, yy = VZ[i2], AR[i3], BK[i2], BKh[i2], E1[i3], bonus[i3], g_[i3], Y[i2]
            lo = 1 if s_i == 0 else 0
            src0 = t0 - 1 + lo
            n = W - lo

            def ldH(dst, c0, key):
                if s_i == 0:
                    P.op("pool", lambda e: e.memset(dst.t[:, :, 0:1], 0.0), writes=[part(dst, cc) for cc in range(4)])
                for cc in range(4):
                    def one(cc):
                        P.dma("sp", lambda e: e.dma_start(out=dst.t[:, 2 * cc:2 * cc + 2, lo:W],
                                                          in_=pr_s[c0 + cc, :, src0:src0 + n].rearrange("(h p) t -> p h t", h=2)),
                              "%s%d" % (key, i2), writes=[part(dst, cc)])
                    one(cc)
            ldH(rx, 0, "rx")
            ldH(kx, 4, "kx")
            ldH(vx, 8, "vx")
            if s_i == 0:
                P.op("pool", lambda e: e.memset(wx.t[:, 0:1], 0.0), writes=[wx.b])
                P.op("pool", lambda e: e.memset(ax.t[:, 0:1], 0.0), writes=[ax.b])
                P.op("pool", lambda e: e.memset(gx.t[:, :, 0:1], 0.0), writes=[part(gx, cc) for cc in range(4)])
            P.dma("sp", lambda e: e.dma_start(out=wx.t[:, lo:W], in_=pr_s[12, :, src0:src0 + n]), "wx%d" % i2, writes=[wx.b])
            yield
            P.dma("sp", lambda e: e.dma_start(out=ax.t[:, lo:W], in_=pr_s[13, :, src0:src0 + n]), "ax%d" % i2, writes=[ax.b])
            yield
            for cc in range(4):
                def oneg(cc):
                    P.dma("sp", lambda e: e.dma_start(out=gx.t[:, cc, lo:W], in_=pr_s[14 + cc, :, src0:src0 + n]),
                          "gx%d" % i2, writes=[part(gx, cc)])
                oneg(cc)
            allp = lambda tb: [part(tb, cc) for cc in range(4)]

            def mixH(src, dst_ap, mi):
                def f(e):
                    e.tensor_tensor(out=D1.t[:], in0=src.t[:, :, 0:TS], in1=src.t[:, :, 1:W], op=ALU.subtract)
                    e.tensor_tensor(out=D1.t[:], in0=D1.t[:], in1=bc(hpv(mi)), op=ALU.mult)
                    return e.tensor_tensor(out=dst_ap, in0=D1.t[:], in1=src.t[:, :, 1:W], op=ALU.add)
                return f
            P.op("dve", mixH(rx, r_.t[:], 0), reads=allp(rx) + CR, writes=[D1.b, r_.b])
            yield
            P.op("dve", mixH(kx, k_.t[:], 1), reads=allp(kx) + CR, writes=[D1.b, k_.b])
            yield
            P.op("dve", mixH(vx, vz.t[:, :, 1, :], 2), reads=allp(vx) + CR, writes=[D1.b, vz.b])
            yield

            def mixL(e):
                e.tensor_tensor(out=dl.t[:, 0, :], in0=wx.t[:, 0:TS], in1=wx.t[:, 1:W], op=ALU.subtract)
                e.scalar_tensor_tensor(out=wdm.t[:], in0=dl.t[:, 0, :], scalar=lp.t[:, 0:1], in1=wx.t[:, 1:W], op0=ALU.mult, op1=ALU.add)
                e.tensor_tensor(out=dl.t[:, 0, :], in0=ax.t[:, 0:TS], in1=ax.t[:, 1:W], op=ALU.subtract)
                e.scalar_tensor_tensor(out=adm.t[:], in0=dl.t[:, 0, :], scalar=lp.t[:, 1:2], in1=ax.t[:, 1:W], op0=ALU.mult, op1=ALU.add)
                e.tensor_tensor(out=dl.t[:], in0=gx.t[:, :, 0:TS], in1=gx.t[:, :, 1:W], op=ALU.subtract)
                for cc in range(4):
                    r = e.scalar_tensor_tensor(out=gdm.t[:, cc, :], in0=dl.t[:, cc, :], scalar=lp.t[:, 2 + cc:3 + cc], in1=gx.t[:, cc, 1:W],
                                               op0=ALU.mult, op1=ALU.add)
                return r
            P.op("dve", mixL, reads=[wx.b, ax.b] + allp(gx) + CR, writes=[dl.b, wdm.b, adm.b, gdm.b])
            yield
            P.op("act", lambda e: e.activation(out=wdm.t[:], in_=wdm.t[:], func=AF.Tanh), reads=[wdm.b], writes=[wdm.b])
            yield
            P.op("act", lambda e: e.activation(out=gdm.t[:], in_=gdm.t[:], func=AF.Sigmoid), reads=[gdm.b], writes=[gdm.b])
            yield

            def lora_all():
                pw, pa, pg = nps(), nps(), nps()

                def lora(e):
                    for h in range(8):
                        e.matmul(pv8(pw)[:, h, :], wdu.t[:, h * 64:(h + 1) * 64], wdm.t[:], start=True, stop=True)
                    for h in range(8):
                        e.matmul(pv8(pa)[:, h, :], wau.t[:, h * 64:(h + 1) * 64], adm.t[:], start=True, stop=True)
                    for h in range(8):
                        for cc in range(4):
                            r = e.matmul(pv8(pg)[:, h, :], wgu.t[:, cc, h * 64:(h + 1) * 64], gdm.t[:, cc, :], start=(cc == 0), stop=(cc == 3))
                    return r
                P.op("pe", lora, reads=[wdu.b, wau.b, wgu.b, wdm.b, adm.b, gdm.b], writes=[pw.b, pa.b, pg.b])

                def sig(e):
                    for h in range(8):
                        e.activation(out=sw.t[:, h, :], in_=pv8(pw)[:, h, :], func=AF.Sigmoid, bias=hp.t[:, 3, h:h + 1], scale=1.0)
                    for h in range(8):
                        r = e.activation(out=a_.t[:, h, :], in_=pv8(pa)[:, h, :], func=AF.Sigmoid, bias=hp.t[:, 4, h:h + 1], scale=1.0)
                    return r
                P.op("act", sig, reads=[pw.b, pa.b] + CR, writes=[sw.b, a_.b])
                P.op("act", lambda e: e.copy(out=gg.t[:], in_=pv8(pg)), reads=[pg.b], writes=[gg.b])
            lora_all()
            yield
            P.op("dve", lambda e: e.tensor_tensor(out=kk.t[:], in0=k_.t[:], in1=bc(hpv(5)), op=ALU.mult), reads=[k_.b] + CR, writes=[kk.b])
            yield
            P.op("act", lambda e: e.activation(out=sq.t[:], in_=kk.t[:], func=AF.Square), reads=[kk.b], writes=[sq.b])
            yield

            sum8(sq, lambda pk: P.op("act", lambda e: e.activation(out=tmpH.t[:], in_=pv8(pk), func=AF.Sqrt), reads=[pk.b], writes=[tmpH.b]))
            yield

            def kkn(e):
                e.tensor_scalar(out=tmpH.t[:], in0=tmpH.t[:], scalar1=1e-12, scalar2=None, op0=ALU.max)
                e.reciprocal(out=tmpH.t[:], in_=tmpH.t[:])
                e.tensor_tensor(out=kk.t[:], in0=kk.t[:], in1=tmpH.t[:], op=ALU.mult)
                e.scalar_tensor_tensor(out=tmpH.t[:], in0=a_.t[:], scalar=-1.0, in1=bc(hpv(6)), op0=ALU.add, op1=ALU.mult)
                e.scalar_tensor_tensor(out=kmod.t[:], in0=tmpH.t[:], scalar=1.0, in1=k_.t[:], op0=ALU.add, op1=ALU.mult)
                return e.tensor_tensor(out=ba.t[:], in0=kk.t[:], in1=a_.t[:], op=ALU.mult)
            P.op("dve", kkn, reads=[tmpH.b, kk.b, a_.b, k_.b] + CR, writes=[tmpH.b, kk.b, kmod.b, ba.b])
            yield
            flat = lambda t: t.t[:].rearrange("p h t -> p (h t)")
            P.op("dve", lambda e: e.tensor_tensor_scan(out=flat(cs_), data0=K["RESETM"], data1=flat(sw), initial=0.0, op0=ALU.mult, op1=ALU.add),
                 reads=[sw.b] + CR, writes=[cs_.b])
            yield
            P.op("act", lambda e: e.activation(out=e1.t[:], in_=cs_.t[:], func=AF.Exp, scale=-C0), reads=[cs_.b], writes=[e1.b])
            yield
            P.op("act", lambda e: e.activation(out=E2.t[:], in_=cs_.t[:], func=AF.Exp, scale=C0), reads=[cs_.b], writes=[E2.b])
            yield
            P.op("pool", lambda e: e.tensor_tensor(out=E3.t[:], in0=cs_.t[:], in1=sw.t[:], op=ALU.subtract), reads=[cs_.b, sw.b], writes=[E3.b])
            yield
            P.op("act", lambda e: e.activation(out=E3.t[:], in_=E3.t[:], func=AF.Exp, scale=-C0), reads=[E3.b], writes=[E3.b])
            yield

            def e4f(e):
                c4 = cs_.t[:].rearrange("p h (c t) -> p h c t", t=CH)
                return e.tensor_tensor(out=E4.t[:].rearrange("p h (c t) -> p h c t", t=CH), in0=c4,
                                       in1=c4[:, :, :, CH - 1:CH].to_broadcast([64, 8, NC, CH]), op=ALU.subtract)
            P.op("pool", e4f, reads=[cs_.b], writes=[E4.b])
            yield
            P.op("act", lambda e: e.activation(out=E4.t[:], in_=E4.t[:], func=AF.Exp, scale=C0), reads=[E4.b], writes=[E4.b])
            yield

            def tild(e):
                e.tensor_tensor(out=ar.t[:, :, 1, :], in0=r_.t[:], in1=e1.t[:], op=ALU.mult)
                e.scalar_tensor_tensor(out=ar.t[:, :, 0, :], in0=kk.t[:], scalar=-1.0, in1=E3.t[:], op0=ALU.mult, op1=ALU.mult)
                e.tensor_tensor(out=bk.t[:, :, 0, :], in0=ba.t[:], in1=E2.t[:], op=ALU.mult)
                return e.tensor_tensor(out=bk.t[:, :, 1, :], in0=kmod.t[:], in1=E2.t[:], op=ALU.mult)
            P.op("dve", tild, reads=[r_.b, e1.b, kk.b, E3.b, ba.b, E2.b, kmod.b], writes=[ar.b, bk.b])
            yield

            def hatf(e):
                e.tensor_tensor(out=bkh.t[:, :, 0, :], in0=ba.t[:], in1=E4.t[:], op=ALU.mult)
                return e.tensor_tensor(out=bkh.t[:, :, 1, :], in0=kmod.t[:], in1=E4.t[:], op=ALU.mult)
            P.op("pool", hatf, reads=[ba.b, kmod.b, E4.b], writes=[bkh.b])
            yield

            def rkf(e):
                e.tensor_tensor(out=tmpH.t[:], in0=r_.t[:], in1=kmod.t[:], op=ALU.mult)
                return e.tensor_tensor(out=sq.t[:], in0=tmpH.t[:], in1=bc(hpv(7)), op=ALU.mult)
            P.op("pool", rkf, reads=[r_.b, kmod.b, tmpH.b, sq.b] + CR, writes=[tmpH.b, sq.b])
            yield
            sum8(sq, lambda pb: P.op("dve", lambda e: e.tensor_tensor(out=bon.t[:], in0=pv8(pb), in1=vz.t[:, :, 1, :], op=ALU.mult),
                                     reads=[pb.b, vz.b], writes=[bon.b]))
            yield
            sx.update(ar=ar, bk=bk, bkh=bkh, vz=vz, e1=e1, yy=yy, bon=bon, gg=gg, i2=i2, t0=t0)

        def super_cpre(s_i, sx):
            ar, bk, bkh, vz, e1, yy = (sx[k] for k in ("ar", "bk", "bkh", "vz", "e1", "yy"))
            cxs = []
            for c in range(NC):
                cx = {}
                yield from chunk_pre(s_i, c, ar, bk, bkh, vz, e1, yy, cx)
                cxs.append(cx)
            sx.update(cxs=cxs)

        def super_post(s_i, sx):
            ar, e1, yy, bon, gg, i2, t0 = (sx[k] for k in ("ar", "e1", "yy", "bon", "gg", "i2", "t0"))
            for cx in sx["cxs"]:
                yield from chunk_scan(cx, ar, e1, yy)
            sum8(yy, lambda pm: P.op("dve", lambda e: e.scalar_tensor_tensor(out=yc.t[:], in0=pv8(pm), scalar=-1.0 / 64, in1=yy.t[:],
                                                                             op0=ALU.mult, op1=ALU.add),
                                     reads=[pm.b, yy.b], writes=[yc.b]))
            yield
            P.op("act", lambda e: e.activation(out=ysq.t[:], in_=yc.t[:], func=AF.Square), reads=[yc.b], writes=[ysq.b])
            yield
            sum8(ysq, lambda pvv: P.op("act", lambda e: e.activation(out=rsd.t[:], in_=pv8(pvv), func=AF.Sqrt, bias=GN_EPS, scale=1.0 / 64),
                                       reads=[pvv.b], writes=[rsd.b]))
            yield
            og = ostg[i2]

            def fin(e):
                e.reciprocal(out=rsd.t[:], in_=rsd.t[:])
                e.tensor_tensor(out=yc.t[:], in0=yc.t[:], in1=rsd.t[:], op=ALU.mult)
                e.tensor_tensor(out=yc.t[:], in0=yc.t[:], in1=bc(hpv(8)), op=ALU.mult)
                e.tensor_tensor(out=yc.t[:], in0=yc.t[:], in1=bc(hpv(9)), op=ALU.add)
                e.tensor_tensor(out=yc.t[:], in0=yc.t[:], in1=bon.t[:], op=ALU.add)
                return e.tensor_tensor(out=og.t[:], in0=yc.t[:], in1=gg.t[:], op=ALU.mult)
            P.op("dve", fin, reads=[rsd.b, yc.b, bon.b, gg.b] + CR, writes=[rsd.b, yc.b, og.b])
            yield
            P.dma("sp", lambda e: e.dma_start(out=MDST(512, 512, t0, TS).rearrange("(h p) t -> p h t", h=8), in_=og.t[:]),
                  "ostg%d" % i2, reads=[og.b])
            yield

        conv = list(conv)
        per = -(-len(conv) // NS) if conv else 0
        cvi = 0
        def drive(items):
            alive = list(items)
            while alive:
                for it in list(alive):
                    st["pool"] = it[0]
                    try:
                        next(it[1])
                    except StopIteration:
                        alive.remove(it)
        sxs = {}
        for s_i in range(NS + 2):
            items = []
            if s_i < NS:
                sxs[s_i] = {}
                items.append((0, super_pre(s_i, sxs[s_i])))
            if 0 <= s_i - 1 < NS:
                items.append((1, super_cpre(s_i - 1, sxs[s_i - 1])))
            if 0 <= s_i - 2 < NS:
                items.append((2, super_post(s_i - 2, sxs.pop(s_i - 2))))
            drive(items)
            if s_i >= NS:
                continue
            for _ in range(per):
                if cvi < len(conv):
                    def cvt(k):
                        dst, src = conv[k]
                        P.dma("pool", lambda e: e.dma_start(out=dst, in_=src), "cv%d" % (k % 8))
                    cvt(cvi)
                    cvi += 1
        P.run_block()


def _consts_A(qg, kg):
    c = np.zeros((128, 8 * 128 + 256 + 4), np.float32)
    p = np.arange(128)
    c[:, 0:128] = 1.0
    c[:, 128:256] = (p[:, None] // 64 == p[None, :] // 64).astype(np.float32)
    rot = np.zeros((128, 128), np.float32)
    for m in range(128):
        if m % 64 < 32:
            rot[m + 32, m] = -1.0
        else:
            rot[m - 32, m] = 1.0
    c[:, 256:384] = rot
    c[:, 384:512] = np.eye(128, dtype=np.float32)
    i = (p % 64)[:, None]
    t = np.arange(64)[None, :]
    c[:, 512:576] = (i < t).astype(np.float32)
    c[:, 576:640] = (i <= t).astype(np.float32)
    kk = p[:, None]
    qq = np.arange(256)[None, :]
    dist = qq - kk
    c[:, 1024:1280] = np.where((dist >= 0) & (dist <= 128), 0.0, -262144.0)
    c[:, 1280] = np.tile(qg, 2)
    c[:, 1281] = np.tile(kg, 2)
    c[:, 1282] = (np.float32(10000.0) ** (-(np.arange(32, dtype=np.float32)) / np.float32(32)))[p % 32]
    return c


def _consts_64():
    c = np.zeros((64, 64 + 8 * TS), np.float32)
    t = np.arange(64)
    c[:, 0:64] = (t[:, None] > t[None, :]).astype(np.float32)
    m = np.ones((8, TS), np.float32)
    m[:, ::CH] = 0.0
    c[:, 64:] = m.reshape(1, -1)
    return c


def _wchunk(w, cols):
    blk = np.zeros((4096, 128), np.float32)
    blk[:, :len(cols)] = w[:, cols]
    return np.ascontiguousarray(blk.reshape(32, 128, 128).transpose(1, 0, 2)).reshape(128, 32 * 128)


def prep_A(inp, S):
    x = inp["x"]
    w_in = inp["w_in"][0]
    mu = inp["rwkv_mu"][0]
    maps = []
    xTs = [np.ascontiguousarray(x[b].T.reshape(32, 128, S).transpose(1, 0, 2)) for b in range(2)]
    cA = _consts_A(inp["q_norm_g"][0], inp["k_norm_g"][0])
    c64 = _consts_64()
    gA = np.ascontiguousarray(inp["attn_norm_g"][0].reshape(32, 128).T)
    RB = 3 * ATTN_W
    for core in range(8):
        b, g = core // 4, core % 4
        cols = []
        for base in (0, 2048, 4096, RB, RB + 2048, RB + 4096):
            for jj in range(4):
                cols.append(np.arange(base + 512 * g + 128 * jj, base + 512 * g + 128 * jj + 128))
        cols.append(np.arange(RB + 6144, RB + 6272))
        cols.append(np.arange(RB + 6272, RB + 6400))
        for jj in range(4):
            lo = RB + 6400 + 128 * jj
            cols.append(np.arange(lo, min(lo + 128, RB + 6880)))
        wA = np.stack([_wchunk(w_in, cc) for cc in cols])

        def hsl(v):
            return v[512 * g:512 * g + 512].reshape(8, 64).T
        hp = np.zeros((64, 10, 8), np.float32)
        hp[:, 0] = hsl(mu[0:2048])
        hp[:, 1] = hsl(mu[2048:4096])
        hp[:, 2] = hsl(mu[4096:6144])
        hp[:, 3] = hsl(inp["w0"][0])
        hp[:, 4] = hsl(inp["a0"][0])
        hp[:, 5] = hsl(inp["k_k"][0])
        hp[:, 6] = hsl(inp["k_a"][0])
        hp[:, 7] = hsl(inp["r_k"][0].reshape(-1))
        hp[:, 8] = hsl(inp["ln_x_w"][0])
        hp[:, 9] = hsl(inp["ln_x_b"][0])
        lp = np.zeros((128, 6), np.float32)
        lp[:, 0] = mu[6144:6272]
        lp[:, 1] = mu[6272:6400]
        mg = np.zeros(512, np.float32)
        mg[:480] = mu[6400:6880]
        lp[:, 2:6] = mg.reshape(4, 128).T
        wg = np.zeros((512, 512), np.float32)
        wg[:480] = inp["w_gate_up"][0][:, 512 * g:512 * g + 512]
        maps.append(dict(
            xT=xTs[b], pos=np.ascontiguousarray(inp["positions"][b][None, :].astype(np.int32)),
            wA=wA, gA=gA, c128=cA, hp=np.ascontiguousarray(hp.reshape(64, 80)), lp=lp,
            wdu=np.ascontiguousarray(inp["w_decay_up"][0][:, 512 * g:512 * g + 512]),
            wau=np.ascontiguousarray(inp["w_iclr_up"][0][:, 512 * g:512 * g + 512]),
            wgu=np.ascontiguousarray(wg.reshape(4, 128, 512).transpose(1, 0, 2)).reshape(128, 2048),
            c64=c64))
    return maps


def gather_mix(resA, S):
    mixT = np.zeros((2, 4096, S), np.float32)
    for core in range(8):
        b, g = core // 4, core % 4
        m = resA[core]["mixA"]
        mixT[b, 512 * g:512 * g + 512] = m[0:512]
        mixT[b, 2048 + 512 * g:2048 + 512 * g + 512] = m[512:1024]
    return mixT


NWB = 3
MPAD = 64
NFF = D_FF // 128


def build_B(NTOK, mix_dram=None, B=None, fused=False, mixx=None, wdecl=None):
    own = B is None
    if own:
        B = Builder()
    nc, P = B.nc, B.P
    T = 512
    NTB = NTOK // T
    IN, OUT = "ExternalInput", "ExternalOutput"
    if fused:
        qm_d = B.dram("qmask", [128, 4], F32, IN)
    elif mix_dram is None:
        mix_dram = B.dram("mixin", [4, 1024, NTOK + 2], F32, IN)
    xT = B.dram("xTb", [128, 32, NTOK + 2], F32, IN)
    pT = B.dram("pTb", [128, 2, NTOK], F32, IN)
    if wdecl is None:
        wdecl = declare_B_weights(B)[0]
    wout, wup, wdn, wple, wgt = (wdecl[k] for k in ("wout", "wup", "wdn", "wple", "wgt"))
    WQ = "sp" if fused else "pool"
    SQ = "pool" if fused else "sp"
    cst = B.dram("cstb", [128, 64 + 2 * NFF * 4 + 128], F32, IN)
    outT = B.dram("outT", [128, 32, NTOK], F32, OUT)
    NU = 2 * NFF

    with contextlib.ExitStack() as es:
        c_sb = B.sb(es, "cstb", [128, 64 + NU * 4 + 128], F32)
        ones_b = B.sb(es, "onesb", [128, 128], BF16)
        P.dma("sp", lambda e: e.dma_start(out=c_sb.t[:], in_=cst), "cstb", writes=[c_sb.b])
        P.dma("pool", lambda e: e.dma_start(out=ones_b.t[:], in_=cst[:, 64 + NU * 4:64 + NU * 4 + 128]), "onesb", writes=[ones_b.b])
        GM = c_sb.t[:, 0:32]
        GP = c_sb.t[:, 32:64]
        CW = c_sb.t[:, 64:64 + NU * 3].rearrange("p (u j) -> p u j", j=3)
        CB = c_sb.t[:, 64 + NU * 3:64 + NU * 4]
        CR = [c_sb.b, ones_b.b]

        h = B.sb(es, "h", [128, 32, T], F32)
        a16 = B.sb(es, "a16", [128, 32, T], BF16)
        hh = B.sb(es, "hh", [128, 32, 2], F32)
        a16h = B.sb(es, "a16h", [128, 32, 2], BF16)
        wr = [B.sb(es, "wr%d" % i, [128, 32, 128], BF16) for i in range(NWB)]
        wd = [B.sb(es, "wd%d" % i, [128, 4096], BF16) for i in range(2)]
        ue = [[B.sb(es, "ue%d%d" % (i, j), [128, T + 2], F32) for j in range(2)] for i in range(2)]
        cv = [[B.sb(es, "cv%d%d" % (i, j), [128, T], F32) for j in range(2)] for i in range(2)]
        actg = [B.sb(es, "actg%d" % i, [128, 2, T], BF16) for i in range(2)]
        uhalo = B.sb(es, "uhalo", [128, NU, 2], F32)
        rstd = B.sb(es, "rstdb", [128, T], F32)
        rstdh = B.sb(es, "rstdh", [128, 2], F32)
        rstde = B.sb(es, "rstde", [128, T], F32)
        p16 = B.sb(es, "p16", [128, 2, T], BF16)
        sqb = [B.sb(es, "sqbb%d" % i, [128, T], BF16) for i in range(2)]
        sqh = B.sb(es, "sqh", [128, 2], BF16)
        sg = [B.sb(es, "sg%d" % i, [128, T], F32) for i in range(2)]
        te = [B.sb(es, "te%d" % i, [128, T], F32) for i in range(2)]
        hparts = [Buf() for _ in range(32)]
        mixg_b = Buf("mixg")
        if fused:
            qm = B.sb(es, "qm", [128, 4], F32)
            cand = [B.sb(es, "cand%d" % i, [128, 4, T], BF16) for i in range(4)]
            candh = [B.sb(es, "candh%d" % i, [128, 32, 2], BF16) for i in range(4)]
            P.dma("sp", lambda e: e.dma_start(out=qm.t[:], in_=qm_d), "qm", writes=[qm.b])
            NCHK = 4 * NTOK // 512
            chunk_b = [Buf("mixg%d" % i) for i in range(NCHK)]
            order = [qq * (NTOK // 512) + tt for tt in range(NTOK // 512) for qq in range(4)]
            for ci in order:
                def ag(ci):
                    P.dma("pool", lambda e: e.collective_compute("AllGather", ALU.bypass, replica_groups=[[0, 1, 2, 3], [4, 5, 6, 7]],
                                                                 ins=[mixx[ci]], outs=[mix_dram[ci]]), "ag", writes=[chunk_b[ci]], inc=1)
                ag(ci)
        psm = [B.psb[i] for i in range(6)]
        pss = [B.psb[6], B.psb[7]]
        cnt = dict(w=0, d=0, m=0, s=0, sq=0, ue=0, ag=0, sg=0)

        def wslot():
            s_ = wr[cnt["w"] % NWB]
            key = "wr%d" % (cnt["w"] % NWB)
            cnt["w"] += 1
            return s_, key

        def pmain():
            p = psm[cnt["m"] % 6]
            cnt["m"] += 1
            return p

        def psmall():
            p = pss[cnt["s"] % 2]
            cnt["s"] += 1
            return p

        def load_w(src_ap):
            s_, key = wslot()
            P.dma(WQ, lambda e: e.dma_start(out=s_.t[:].rearrange("p k c -> p (k c)"), in_=src_ap), key, writes=[s_.b])
            return s_

        def mm32(ps_ap, s_, rhs_fn):
            def f(e):
                for kc in range(32):
                    r = e.matmul(ps_ap, s_.t[:, kc, :], rhs_fn(kc), start=(kc == 0), stop=(kc == 31))
                return r
            return f

        def rms(src, src_tokens, n, dst16, g_ap, rs, halo):
            ssp = psmall()
            for kc in range(32):
                def one(kc):
                    sq = sqh if halo else sqb[cnt["sq"] % 2]
                    cnt["sq"] += 1
                    sqa = sq.t[:, 0:n]
                    P.op("act", lambda e: e.activation(out=sqa, in_=src.t[:, kc, 0:n], func=AF.Square), reads=[src_tokens[kc]], writes=[sq.b])
                    P.op("pe", lambda e: e.matmul(ssp.t[:, 0:n], ones_b.t[:], sqa, start=(kc == 0), stop=(kc == 31)), reads=[sq.b] + CR, writes=[ssp.b])
                one(kc)
            P.op("act", lambda e: e.activation(out=rs.t[:, 0:n], in_=ssp.t[:, 0:n], func=AF.Sqrt, bias=NORM_EPS, scale=1.0 / D_MODEL), reads=[ssp.b], writes=[rs.b])
            P.op("dve", lambda e: e.reciprocal(out=rs.t[:, 0:n], in_=rs.t[:, 0:n]), reads=[rs.b], writes=[rs.b])
            for kc in range(32):
                def two(kc):
                    P.op("dve", lambda e: e.scalar_tensor_tensor(out=dst16.t[:, kc, 0:n], in0=src.t[:, kc, 0:n], scalar=g_ap[:, kc:kc + 1], in1=rs.t[:, 0:n],
                                                                 op0=ALU.mult, op1=ALU.mult),
                         reads=[src_tokens[kc], rs.b] + CR, writes=[dst16.b])
                two(kc)

        def do_tile(tt):
            t0 = tt * T
            first = tt == 0
            hhp = [hh.b] * 32
            for q4 in range(4):
                def ld(q4):
                    P.dma("sp", lambda e: e.dma_start(out=h.t[:, 8 * q4:8 * q4 + 8, :], in_=xT[:, 8 * q4:8 * q4 + 8, 2 + t0:2 + t0 + T]),
                          "hx%d" % q4, writes=hparts[8 * q4:8 * q4 + 8])
                    if not fused:
                        P.dma("pool", lambda e: e.dma_start(out=a16.t[:, 8 * q4:8 * q4 + 8, :],
                                                            in_=mix_dram[q4, :, 2 + t0:2 + t0 + T].rearrange("(k p) t -> p k t", p=128)),
                              "mx%d" % q4, writes=[a16.b])
                ld(q4)
            if fused:
                for g8 in range(8):
                    def selg(g8):
                        for qq in range(4):
                            def ldc(qq):
                                ci = (qq * NTOK + t0) // 512
                                P.dma("sp", lambda e: e.dma_start(out=cand[qq].t[:], in_=mix_dram[ci, g8 * 512:(g8 + 1) * 512, :].rearrange("(k p) t -> p k t", p=128)),
                                      "cd%d" % qq, reads=[chunk_b[ci]], writes=[cand[qq].b])
                            ldc(qq)
                        dst = a16.t[:, 4 * g8:4 * g8 + 4, :]

                        def sel(e):
                            r = e.tensor_scalar(out=dst, in0=cand[0].t[:], scalar1=qm.t[:, 0:1], scalar2=None, op0=ALU.mult)
                            for qq in range(1, 4):
                                r = e.scalar_tensor_tensor(out=dst, in0=cand[qq].t[:], scalar=qm.t[:, qq:qq + 1], in1=dst, op0=ALU.mult, op1=ALU.add)
                            return r
                        P.op("dve", sel, reads=[c_.b for c_ in cand] + [qm.b], writes=[a16.b])
                    selg(g8)
            P.dma("pool", lambda e: e.dma_start(out=p16.t[:], in_=pT[:, :, t0:t0 + T]), "p16", writes=[p16.b])
            if first:
                P.dma("sp", lambda e: e.dma_start(out=hh.t[:], in_=xT[:, :, 0:2]), "hhx", writes=[hh.b])
                if fused:
                    P.op("pool", lambda e: e.memset(candh[0].t[:], 0.0), writes=[candh[0].b])
                    for qq in range(1, 4):
                        def ldch(qq):
                            ci = qq * NTOK // 512 - 1
                            P.dma("sp", lambda e: e.dma_start(out=candh[qq].t[:], in_=mix_dram[ci, :, 510:512].rearrange("(k p) t -> p k t", p=128)),
                                  "cdh%d" % qq, reads=[chunk_b[ci]], writes=[candh[qq].b])
                        ldch(qq)

                    def selh(e):
                        r = e.tensor_scalar(out=a16h.t[:], in0=candh[0].t[:], scalar1=qm.t[:, 0:1], scalar2=None, op0=ALU.mult)
                        for qq in range(1, 4):
                            r = e.scalar_tensor_tensor(out=a16h.t[:], in0=candh[qq].t[:], scalar=qm.t[:, qq:qq + 1], in1=a16h.t[:], op0=ALU.mult, op1=ALU.add)
                        return r
                    P.op("dve", selh, reads=[c_.b for c_ in candh] + [qm.b], writes=[a16h.b])
                else:
                    for q4 in range(4):
                        def ldh(q4):
                            P.dma("pool", lambda e: e.dma_start(out=a16h.t[:, 8 * q4:8 * q4 + 8, :],
                                                                in_=mix_dram[q4, :, 0:2].rearrange("(k p) t -> p k t", p=128)),
                                  "mxh%d" % q4, writes=[a16h.b])
                        ldh(q4)
            for c in range(32):
                def oc(c):
                    s_ = load_w(wout[c])
                    ps = pmain()
                    P.op("pe", mm32(ps.t[:], s_, lambda kc: a16.t[:, kc, :]), reads=[s_.b, a16.b], writes=[ps.b])
                    P.op("dve", lambda e: e.tensor_tensor(out=h.t[:, c, :], in0=ps.t[:], in1=h.t[:, c, :], op=ALU.add), reads=[ps.b, hparts[c]], writes=[hparts[c]])
                    if first:
                        ph = psmall()
                        P.op("pe", mm32(ph.t[:, 0:2], s_, lambda kc: a16h.t[:, kc, :]), reads=[s_.b, a16h.b], writes=[ph.b])
                        P.op("dve", lambda e: e.tensor_tensor(out=hh.t[:, c, :], in0=ph.t[:, 0:2], in1=hh.t[:, c, :], op=ALU.add), reads=[ph.b, hh.b], writes=[hh.b])
                oc(c)
            BD_ = int(os.environ.get('B_DBG', '100'))

            def dump():
                for c in range(32):
                    P.dma("sp", _bind(lambda e, c: e.dma_start(out=outT[:, c, t0:t0 + T], in_=h.t[:, c, :]), c), "oc%d" % (c % 8), reads=[hparts[c]])
            if BD_ == 1:
                dump()
                return
            rms(h, hparts, T, a16, GM, rstd, False)
            if first:
                rms(hh, hhp, 2, a16h, GM, rstdh, True)
            def do_pairf(f0):
                ag = actg[cnt["ag"] % 2]
                cnt["ag"] += 1
                for fi in range(2):
                    def ff(f, fi):
                        cvs = []
                        for gi in range(2):
                            def gu(gi):
                                uc = gi * NFF + f
                                s_ = load_w(wup[uc])
                                ps = pmain()
                                P.op("pe", mm32(ps.t[:], s_, lambda kc: a16.t[:, kc, :]), reads=[s_.b, a16.b], writes=[ps.b])
                                u = ue[gi][cnt["ue"] % 2]
                                c1 = cv[gi][cnt["ue"] % 2]
                                if first:
                                    ph = psmall()
                                    P.op("pe", mm32(ph.t[:, 0:2], s_, lambda kc: a16h.t[:, kc, :]), reads=[s_.b, a16h.b], writes=[ph.b])
                                    P.op("dve", lambda e: e.tensor_copy(out=u.t[:, 0:2], in_=ph.t[:, 0:2]), reads=[ph.b], writes=[u.b])
                                else:
                                    P.op("dve", lambda e: e.tensor_copy(out=u.t[:, 0:2], in_=uhalo.t[:, uc, :]), reads=[uhalo.b], writes=[u.b])
                                P.op("act", lambda e: e.copy(out=u.t[:, 2:T + 2], in_=ps.t[:]), reads=[ps.b], writes=[u.b])
                                P.op("dve", lambda e: e.tensor_copy(out=uhalo.t[:, uc, :], in_=u.t[:, T:T + 2]), reads=[u.b], writes=[uhalo.b])
                                P.op("act", lambda e: e.activation(out=c1.t[:], in_=u.t[:, 2:T + 2], func=AF.Identity, bias=CB[:, uc:uc + 1], scale=CW[:, uc, 2:3]),
                                     reads=[u.b] + CR, writes=[c1.b])

                                def cvf(e):
                                    e.scalar_tensor_tensor(out=c1.t[:], in0=u.t[:, 1:T + 1], scalar=CW[:, uc, 1:2], in1=c1.t[:], op0=ALU.mult, op1=ALU.add)
                                    return e.scalar_tensor_tensor(out=c1.t[:], in0=u.t[:, 0:T], scalar=CW[:, uc, 0:1], in1=c1.t[:], op0=ALU.mult, op1=ALU.add)
                                P.op("dve", cvf, reads=[u.b, c1.b] + CR, writes=[c1.b])
                                cvs.append(c1)
                            gu(gi)
                        cnt["ue"] += 1
                        sgt = sg[cnt["sg"] % 2]
                        cnt["sg"] += 1
                        P.op("act", lambda e: e.activation(out=sgt.t[:], in_=cvs[0].t[:], func=AF.Silu), reads=[cvs[0].b], writes=[sgt.b])
                        P.op("dve", lambda e: e.tensor_tensor(out=ag.t[:, fi, :], in0=sgt.t[:], in1=cvs[1].t[:], op=ALU.mult), reads=[sgt.b, cvs[1].b], writes=[ag.b])
                    ff(f0 + fi, fi)
                wds = []
                for fi in range(2):
                    def ldd(fi):
                        s_ = wd[fi]
                        P.dma(WQ, lambda e: e.dma_start(out=s_.t[:], in_=wdn[f0 + fi]), "wd%d" % fi, writes=[s_.b])
                        wds.append(s_)
                    ldd(fi)
                for c in range(32):
                    def dc(c):
                        ps = pmain()

                        def f(e):
                            e.matmul(ps.t[:], wds[0].t[:, c * 128:(c + 1) * 128], ag.t[:, 0, :], start=True, stop=False)
                            return e.matmul(ps.t[:], wds[1].t[:, c * 128:(c + 1) * 128], ag.t[:, 1, :], start=False, stop=True)
                        P.op("pe", f, reads=[wds[0].b, wds[1].b, ag.b], writes=[ps.b])
                        P.op("dve", lambda e: e.tensor_tensor(out=h.t[:, c, :], in0=ps.t[:], in1=h.t[:, c, :], op=ALU.add), reads=[ps.b, hparts[c]], writes=[hparts[c]])
                    dc(c)
            for f0 in range(0, NFF, 2):
                do_pairf(f0)
            if BD_ == 2:
                dump()
                return
            for c in range(32):
                P.op("act", _bind(lambda e, c: e.copy(out=a16.t[:, c, :], in_=h.t[:, c, :]), c), reads=[hparts[c]], writes=[a16.b])
            P.dma(WQ, lambda e: e.dma_start(out=wd[0].t[:], in_=wple[:, 0:4096]), "wd0", writes=[wd[0].b])
            P.dma(WQ, lambda e: e.dma_start(out=wd[1].t[:], in_=wple[:, 4096:8192]), "wd1", writes=[wd[1].b])
            ssp = psmall()
            for c in range(32):
                def e1(c):
                    ps = pmain()

                    def f(e):
                        e.matmul(ps.t[:], wd[0].t[:, c * 128:(c + 1) * 128], p16.t[:, 0, :], start=True, stop=False)
                        return e.matmul(ps.t[:], wd[1].t[:, c * 128:(c + 1) * 128], p16.t[:, 1, :], start=False, stop=True)
                    P.op("pe", f, reads=[wd[0].b, wd[1].b, p16.b], writes=[ps.b])
                    sq = sqb[cnt["sq"] % 2]
                    cnt["sq"] += 1
                    P.op("act", lambda e: e.activation(out=sq.t[:], in_=ps.t[:], func=AF.Square), reads=[ps.b], writes=[sq.b])
                    P.op("pe", lambda e: e.matmul(ssp.t[:], ones_b.t[:], sq.t[:], start=(c == 0), stop=(c == 31)), reads=[sq.b] + CR, writes=[ssp.b])
                e1(c)
            P.op("act", lambda e: e.activation(out=rstde.t[:], in_=ssp.t[:], func=AF.Sqrt, bias=NORM_EPS, scale=1.0 / D_MODEL), reads=[ssp.b], writes=[rstde.b])
            P.op("dve", lambda e: e.reciprocal(out=rstde.t[:], in_=rstde.t[:]), reads=[rstde.b], writes=[rstde.b])
            for c in range(32):
                def e2(c):
                    s_ = load_w(wgt[c])
                    pg = pmain()
                    P.op("pe", mm32(pg.t[:], s_, lambda kc: a16.t[:, kc, :]), reads=[s_.b, a16.b], writes=[pg.b])
                    pe_ = pmain()

                    def f(e):
                        e.matmul(pe_.t[:], wd[0].t[:, c * 128:(c + 1) * 128], p16.t[:, 0, :], start=True, stop=False)
                        return e.matmul(pe_.t[:], wd[1].t[:, c * 128:(c + 1) * 128], p16.t[:, 1, :], start=False, stop=True)
                    P.op("pe", f, reads=[wd[0].b, wd[1].b, p16.b], writes=[pe_.b])
                    sgt = sg[cnt["sg"] % 2]
                    tet = te[cnt["sg"] % 2]
                    cnt["sg"] += 1
                    P.op("act", lambda e: e.activation(out=sgt.t[:], in_=pg.t[:], func=AF.Sigmoid), reads=[pg.b], writes=[sgt.b])

                    def comb(e):
                        e.scalar_tensor_tensor(out=tet.t[:], in0=pe_.t[:], scalar=GP[:, c:c + 1], in1=rstde.t[:], op0=ALU.mult, op1=ALU.mult)
                        e.tensor_tensor(out=tet.t[:], in0=tet.t[:], in1=sgt.t[:], op=ALU.mult)
                        return e.tensor_tensor(out=h.t[:, c, :], in0=h.t[:, c, :], in1=tet.t[:], op=ALU.add)
                    P.op("dve", comb, reads=[pe_.b, rstde.b, sgt.b, hparts[c]] + CR, writes=[tet.b, hparts[c]])
                    P.dma(SQ, lambda e: e.dma_start(out=outT[:, c, t0:t0 + T], in_=h.t[:, c, :]), "oc%d" % (c % 8), reads=[hparts[c]])
                e2(c)

        for tt in range(NTB):
            do_tile(tt)
        P.run_block()
    if own:
        B.finish()
    return nc


def _wchunks_all(w, nchunk):
    return np.ascontiguousarray(w.reshape(32, 128, nchunk, 128).transpose(2, 1, 0, 3)).reshape(nchunk, 128, 32 * 128)


def prep_B_weights(inp):
    perm = np.concatenate([np.concatenate([np.arange(512 * g, 512 * g + 512), np.arange(2048 + 512 * g, 2048 + 512 * g + 512)]) for g in range(4)])
    wout = _wchunks_all(inp["w_out"][0][perm, :], 32)
    wup = _wchunks_all(inp["w_mlp_up"][0], 2 * NFF)
    wdn = np.ascontiguousarray(inp["w_mlp_down"][0].reshape(NFF, 128, 4096))
    wple = np.ascontiguousarray(inp["w_ple_proj"][0].reshape(2, 128, 4096).transpose(1, 0, 2)).reshape(128, 8192)
    wgt = _wchunks_all(inp["w_ple_gate"][0], 32)
    NU = 2 * NFF
    cst = np.zeros((128, 64 + NU * 4 + 128), np.float32)
    cst[:, 0:32] = inp["mlp_norm_g"][0].reshape(32, 128).T
    cst[:, 32:64] = inp["ple_norm_g"][0].reshape(32, 128).T
    cw = inp["conv_w"][0].reshape(3, NU, 128).transpose(2, 1, 0)
    cst[:, 64:64 + NU * 3] = cw.reshape(128, NU * 3)
    cst[:, 64 + NU * 3:64 + NU * 4] = inp["conv_b"][0].reshape(NU, 128).T
    cst[:, 64 + NU * 4:] = 1.0
    return dict(wout=wout, wup=wup, wdn=wdn, wple=wple, wgt=wgt, cstb=cst)


def prep_B_acts(inp, S, core):
    NTOK = S // 4
    b, q = core // 4, core % 4
    lo = q * NTOK
    xe = np.zeros((NTOK + 2, 4096), np.float32)
    xe[2:] = inp["x"][b, lo:lo + NTOK]
    if q > 0:
        xe[0:2] = inp["x"][b, lo - 2:lo]
    xTb = np.ascontiguousarray(xe.T.reshape(32, 128, NTOK + 2).transpose(1, 0, 2))
    pTb = np.ascontiguousarray(inp["p"][0, b, lo:lo + NTOK].T.reshape(2, 128, NTOK).transpose(1, 0, 2))
    return dict(xTb=xTb, pTb=pTb)


def mix_for_core(resA, S, core):
    NTOK = S // 4
    b, q = core // 4, core % 4
    lo = q * NTOK
    m = np.zeros((4, 1024, NTOK + 2), np.float32)
    for g in range(4):
        src = resA[4 * b + g]["mixA"]
        m[g, :, 2:] = src[:, lo:lo + NTOK]
        if q > 0:
            m[g, :, 0:2] = src[:, lo - 2:lo]
    return m


_CACHE = {}


def kernel_unfused(**inp):
    inp = {k: np.asarray(v) for k, v in inp.items()}
    S = inp["x"].shape[1]
    NTOK = S // 4
    if ("A", S) not in _CACHE:
        _CACHE[("A", S)] = build_A(S)
        _CACHE[("B", S)] = build_B(NTOK)
    ncA, ncB = _CACHE[("A", S)], _CACHE[("B", S)]
    mapsA = prep_A(inp, S)
    resA = run_bass_kernel_spmd(ncA, mapsA, core_ids=list(range(8))).results
    del mapsA
    wB = prep_B_weights(inp)
    mapsB = []
    for core in range(8):
        d = dict(wB)
        d.update(prep_B_acts(inp, S, core))
        d["mixin"] = mix_for_core(resA, S, core)
        mapsB.append(d)
    resB = run_bass_kernel_spmd(ncB, mapsB, core_ids=list(range(8))).results
    out = np.zeros((2, S, 4096), np.float32)
    for core in range(8):
        b, q = core // 4, core % 4
        o = resB[core]["outT"]
        out[b, q * NTOK:(q + 1) * NTOK] = o.transpose(2, 1, 0).reshape(NTOK, 4096)
    return out


def declare_B_weights(B, bf16_copy=False):
    IN = "ExternalInput"
    d = dict(wout=B.dram("wout", [32, 128, 32 * 128], F32, IN),
             wup=B.dram("wup", [2 * NFF, 128, 32 * 128], F32, IN),
             wdn=B.dram("wdn", [NFF, 128, 4096], F32, IN),
             wple=B.dram("wple", [128, 2 * 4096], F32, IN),
             wgt=B.dram("wgt", [32, 128, 32 * 128], F32, IN))
    if not bf16_copy:
        return d, ()
    c = dict(wout=B.dram("wout_b", [32, 128, 4096], BF16, "Internal"),
             wup=B.dram("wup_b", [2 * NFF, 128, 4096], BF16, "Internal"),
             wdn=B.dram("wdn_b", [NFF, 128, 4096], BF16, "Internal"),
             wple=B.dram("wple_b", [128, 8192], BF16, "Internal"),
             wgt=B.dram("wgt_b", [32, 128, 4096], BF16, "Internal"))
    conv = [(c["wout"][i], d["wout"][i]) for i in range(32)]
    for f in range(NFF):
        conv += [(c["wup"][f], d["wup"][f]), (c["wup"][NFF + f], d["wup"][NFF + f])]
        if f % 2 == 1:
            conv += [(c["wdn"][f - 1], d["wdn"][f - 1]), (c["wdn"][f], d["wdn"][f])]
    conv += [(c["wple"][:, 0:4096], d["wple"][:, 0:4096]), (c["wple"][:, 4096:8192], d["wple"][:, 4096:8192])]
    conv += [(c["wgt"][i], d["wgt"][i]) for i in range(32)]
    return c, conv


def build_fused(S):
    B = Builder()
    wdecl, conv = declare_B_weights(B, bf16_copy=True)
    mixx = build_A(S, B=B, fused=True, conv=conv)
    mixg = B.dram("mixg", [S // 512, 4096, 512], BF16, "Internal")
    build_B(S // 4, mix_dram=mixg, B=B, fused=True, mixx=mixx, wdecl=wdecl)
    B.finish()
    return B.nc


def kernel(**inp):
    inp = {k: np.asarray(v) for k, v in inp.items()}
    S = inp["x"].shape[1]
    NTOK = S // 4
    if ("F", S) not in _CACHE:
        _CACHE[("F", S)] = build_fused(S)
    nc = _CACHE[("F", S)]
    maps = prep_A(inp, S)
    wB = prep_B_weights(inp)
    for core in range(8):
        maps[core].update(wB)
        maps[core].update(prep_B_acts(inp, S, core))
        qm = np.zeros((128, 4), np.float32)
        qm[:, core % 4] = 1.0
        maps[core]["qmask"] = qm
    res = run_bass_kernel_spmd(nc, maps, core_ids=list(range(8))).results
    out = np.zeros((2, S, 4096), np.float32)
    for core in range(8):
        b, q = core // 4, core % 4
        o = res[core]["outT"]
        out[b, q * NTOK:(q + 1) * NTOK] = o.transpose(2, 1, 0).reshape(NTOK, 4096)
    return out
```

```python
import contextlib
import math
import os
import numpy as np
import concourse.bass as bass
import concourse.mybir as mybir
from concourse.bass_utils import run_bass_kernel_spmd

F32 = mybir.dt.float32
BF16 = mybir.dt.bfloat16
I32 = mybir.dt.int32
AF = mybir.ActivationFunctionType
ALU = mybir.AluOpType

D_MODEL = 4096
HEAD_DIM = 64
ATTN_W = 2048
RWKV_W = 2048
D_FF = 11008
PLE_DIM = 256
NORM_EPS = 1e-6
GN_EPS = 64e-5
C0 = math.exp(-0.5)

ENGS = ("pe", "act", "dve", "pool", "sp")
EPOCH = 8000


class Buf:
    __slots__ = ("name", "w", "r")

    def __init__(self, name=""):
        self.name = name
        self.w = None
        self.r = []


class Op:
    __slots__ = ("eng", "fn", "deps", "dma", "sig", "idx", "dsem", "dcnt", "blk", "inc")

    def __init__(self, eng, fn, dma=False):
        self.eng = eng
        self.fn = fn
        self.deps = []
        self.dma = dma
        self.sig = False
        self.idx = None
        self.dsem = None
        self.dcnt = None
        self.blk = 0


class Prog:
    def __init__(self, nc):
        self.nc = nc
        self.ops = {e: [] for e in ENGS}
        self.sigcount = {e: 0 for e in ENGS}
        self.esems = {e: [] for e in ENGS}
        self.dma_sems = {}
        self.waited = {e: {} for e in ENGS}
        self._ctx = []
        self.blk = 0
        self.keybufs = {}

    def _sem(self, name):
        cm = self.nc.semaphore(name)
        s = cm.__enter__()
        self._ctx.append(cm)
        return s

    def dsem(self, key):
        if key not in self.dma_sems:
            self.dma_sems[key] = [self._sem("d_" + str(key)), 0]
        return self.dma_sems[key]

    def close(self):
        for cm in reversed(self._ctx):
            cm.__exit__(None, None, None)

    def _deps(self, op, reads, writes):
        deps = set()
        for b in reads:
            if b.w is not None:
                deps.add(b.w)
        for b in writes:
            if b.w is not None:
                deps.add(b.w)
            for r in b.r:
                deps.add(r)
        deps.discard(op)
        for d in deps:
            if op.dma or d.dma or d.eng != op.eng:
                d.sig = True
        op.deps = list(deps)
        for b in reads:
            b.r.append(op)
        for b in writes:
            b.w = op
            b.r = []

    def op(self, eng, fn, reads=(), writes=()):
        o = Op(eng, fn)
        o.blk = self.blk
        self._deps(o, reads, writes)
        self.ops[eng].append(o)
        return o

    def dma(self, eng, fn, semkey, reads=(), writes=(), inc=16):
        o = Op(eng, fn, dma=True)
        o.blk = self.blk
        o.inc = inc
        ds = self.dsem(semkey)
        ds[1] += inc
        o.dsem, o.dcnt = ds[0], ds[1]
        kb = self.keybufs.setdefault(semkey, Buf(semkey))
        self._deps(o, reads, list(writes) + [kb])
        self.ops[eng].append(o)
        return o

    def _assign(self):
        for e in ENGS:
            for o in self.ops[e]:
                if o.sig and not o.dma and o.idx is None:
                    self.sigcount[e] += 1
                    o.idx = self.sigcount[e]
            need = (self.sigcount[e] + EPOCH - 1) // EPOCH
            while len(self.esems[e]) < need:
                self.esems[e].append(self._sem("e_%s_%d" % (e, len(self.esems[e]))))

    def _wait_for(self, eng_name, eng, d):
        w = self.waited[eng_name]
        if d.blk < self.blk:
            return
        if d.dma:
            key = ("d", id(d.dsem))
            if w.get(key, 0) >= d.dcnt:
                return
            w[key] = d.dcnt
            eng.wait_ge(d.dsem, d.dcnt)
        else:
            ep, v = divmod(d.idx - 1, EPOCH)
            key = (d.eng, ep)
            if w.get(key, 0) >= v + 1:
                return
            w[key] = v + 1
            for e2 in range(ep):
                w[(d.eng, e2)] = EPOCH
            eng.wait_ge(self.esems[d.eng][ep], v + 1)

    def emit_engine(self, eng_name, eng):
        for o in self.ops[eng_name]:
            for d in sorted(o.deps, key=lambda x: (x.dma, x.idx or 0, x.dcnt or 0)):
                if (not o.dma) and (not d.dma) and d.eng == eng_name:
                    continue
                self._wait_for(eng_name, eng, d)
            ins = o.fn(eng)
            if o.dma:
                ins.then_inc(o.dsem, o.inc)
            elif o.sig:
                ep, v = divmod(o.idx - 1, EPOCH)
                ins.then_inc(self.esems[eng_name][ep], 1)

    def finish_waits(self, eng_name, eng):
        for e in ENGS:
            if self.sigcount[e] > 0:
                ep, v = divmod(self.sigcount[e] - 1, EPOCH)
                key = (e, ep)
                if self.waited[eng_name].get(key, 0) < v + 1:
                    self.waited[eng_name][key] = v + 1
                    eng.wait_ge(self.esems[e][ep], v + 1)
        for key, (sem, cnt) in self.dma_sems.items():
            k = ("d", id(sem))
            if cnt > 0 and self.waited[eng_name].get(k, 0) < cnt:
                self.waited[eng_name][k] = cnt
                eng.wait_ge(sem, cnt)

    def run_block(self):
        nc = self.nc
        for e in ENGS:
            lst = [o for o in self.ops[e] if not o.dma]
            if lst:
                lst[-1].sig = True
        self._assign()
        with nc.Block() as block:
            @block.tensor
            def _(t):
                self.emit_engine("pe", t)
                self.finish_waits("pe", t)

            @block.scalar
            def _(a):
                self.emit_engine("act", a)
                self.finish_waits("act", a)

            @block.vector
            def _(v):
                self.emit_engine("dve", v)
                self.finish_waits("dve", v)

            @block.gpsimd
            def _(g):
                self.emit_engine("pool", g)
                self.finish_waits("pool", g)

            @block.sync
            def _(s):
                self.emit_engine("sp", s)
                self.finish_waits("sp", s)
        for e in ENGS:
            self.ops[e] = []
        self.blk += 1


class TB:
    __slots__ = ("t", "b")

    def __init__(self, t, name=""):
        self.t = t
        self.b = Buf(name)


class Builder:
    def __init__(self):
        self.nc = bass.Bass("TRN2", target_bir_lowering=False)
        self.P = Prog(self.nc)
        self.es = contextlib.ExitStack()
        self.n = 0
        self.psb = []
        for i in range(8):
            t = self.es.enter_context(self.nc.psum_tensor("psb%d" % i, [128, 512], F32))
            self.psb.append(TB(t, "psb%d" % i))

    def dram(self, name, shape, dt, kind):
        return self.nc.dram_tensor(name, list(shape), dt, kind=kind).ap()

    def sb(self, es, name, shape, dt=F32):
        self.n += 1
        t = es.enter_context(self.nc.sbuf_tensor("%s_%d" % (name, self.n), list(shape), dt))
        return TB(t, name)

    def finish(self):
        self.P.close()
        self.es.close()


def _bind(f, *a):
    return lambda e: f(e, *a)


NW_A = 3
TS = 64
CH = 64


def build_A(S, phases=("A1", "A2", "A3"), debug=False, B=None, fused=False, conv=()):
    own = B is None
    if own:
        B = Builder()
    nc, P = B.nc, B.P
    T = 512
    NT = S // T
    IN, OUT, INT = "ExternalInput", "ExternalOutput", "Internal"
    xT = B.dram("xT", [128, 32, S], F32, IN)
    pos = B.dram("pos", [1, S], I32, IN)
    wA = B.dram("wA", [30, 128, 32 * 128], F32, IN)
    gA = B.dram("gA", [128, 32], F32, IN)
    c128 = B.dram("c128", [128, 8 * 128 + 256 + 4], F32, IN)
    hp_d = B.dram("hp", [64, 10 * 8], F32, IN)
    lp_d = B.dram("lp", [128, 6], F32, IN)
    wdu_d = B.dram("wdu", [128, 512], F32, IN)
    wau_d = B.dram("wau", [128, 512], F32, IN)
    wgu_d = B.dram("wgu", [128, 4 * 512], F32, IN)
    c64_d = B.dram("c64", [64, 64 + 8 * TS], F32, IN)
    if fused:
        mixA = B.dram("mixx", [S // 512, 1024, 512], BF16, INT)
        MIX = dict(ap=mixA, off=0, dt=BF16, dst=lambda r0, nr, t0, n: mixA[t0 // 512, r0:r0 + nr, t0 % 512:t0 % 512 + n])
    else:
        mixA = B.dram("mixA", [1024, S], F32, OUT)
        MIX = dict(ap=mixA, off=0, dt=F32, dst=lambda r0, nr, t0, n: mixA[r0:r0 + nr, t0:t0 + n])
    SK = OUT if debug else INT
    qk_s = B.dram("qk_s", [8, 128, S], BF16, SK)
    vv_s = B.dram("vv_s", [4, 128, S], BF16, SK)
    pr_s = B.dram("pr_s", [18, 128, S], F32, SK)

    with contextlib.ExitStack() as es0:
        cf = B.sb(es0, "cf", [128, 8 * 128 + 256 + 4], F32)
        cb = B.sb(es0, "cb", [128, 128 + 128 + 256], BF16)
        hp = B.sb(es0, "hp", [64, 10, 8], F32)
        lp = B.sb(es0, "lp", [128, 6], F32)
        c64 = B.sb(es0, "c64", [64, 64 + 8 * TS], F32)
        P.dma("sp", lambda e: e.dma_start(out=cf.t[:], in_=c128), "cf", writes=[cf.b])
        P.dma("sp", lambda e: e.dma_start(out=hp.t[:].rearrange("p a h -> p (a h)"), in_=hp_d), "hp", writes=[hp.b])
        P.dma("sp", lambda e: e.dma_start(out=lp.t[:], in_=lp_d), "lp", writes=[lp.b])
        P.dma("sp", lambda e: e.dma_start(out=c64.t[:], in_=c64_d), "c64", writes=[c64.b])
        P.dma("pool", lambda e: e.dma_start(out=cb.t[:, 0:128], in_=c128[:, 0:128]), "cb0", writes=[cb.b])
        P.dma("pool", lambda e: e.dma_start(out=cb.t[:, 128:256], in_=c128[:, 384:512]), "cb1", writes=[cb.b])
        P.dma("pool", lambda e: e.dma_start(out=cb.t[:, 256:512], in_=c128[:, 1024:1280]), "cb2", writes=[cb.b])
        ONESF = cf.t[:, 0:128]
        BD = cf.t[:, 128:256]
        ROTT = cf.t[:, 256:384]
        IDENTF = cf.t[:, 384:512]
        MASKG = cf.t[:, 512:640]
        GQ = cf.t[:, 1280:1281]
        GK = cf.t[:, 1281:1282]
        INVF = cf.t[:, 1282:1283]
        ONESB = cb.t[:, 0:128]
        IDENTB = cb.t[:, 128:256]
        MASKB = cb.t[:, 256:512]
        MASKN = c64.t[:, 0:64]
        RESETM = c64.t[:, 64:64 + 8 * TS]
        CR = [cf.b, cb.b, hp.b, lp.b, c64.b]

        if "A1" in phases:
          phase_A1(B, S, T, NT, xT, pos, wA, gA, qk_s, vv_s, pr_s,
                 dict(ONESB=ONESB, BD=BD, ROTT=ROTT, GQ=GQ, GK=GK, INVF=INVF, CR=CR))
        if "A2" in phases:
          phase_A2(B, S, qk_s, vv_s, MIX,
                 dict(ONESF=ONESF, IDENTB=IDENTB, MASKB=MASKB, CR=CR))
        if "A3" in phases:
          phase_A3(B, S, pr_s, MIX, wdu_d, wau_d, wgu_d,
                 dict(ONESF=ONESF, IDENTF=IDENTF, MASKG=MASKG, MASKN=MASKN, RESETM=RESETM,
                      hp=hp, lp=lp, CR=CR), conv=conv)
    if own:
        B.finish()
        return nc
    return mixA


def phase_A1(B, S, T, NT, xT, pos, wA, gA, qk_s, vv_s, pr_s, K):
    nc, P = B.nc, B.P
    CR = K["CR"]
    with contextlib.ExitStack() as es:
        x_sb = B.sb(es, "x", [128, 32, T], F32)
        xn = [B.sb(es, "xn%d" % i, [128, 32, T], BF16) for i in range(2)]
        w_sb = [B.sb(es, "w%d" % i, [128, 32, 128], BF16) for i in range(NW_A)]
        sqb = [B.sb(es, "sqb%d" % i, [128, T], BF16) for i in range(2)]
        rstd = B.sb(es, "rstd", [128, T], F32)
        g_sb = B.sb(es, "g", [128, 32], F32)
        posi = B.sb(es, "posi", [128, T], I32)
        cs = [dict(sin=B.sb(es, "sin%d" % i, [128, T], F32), cos=B.sb(es, "cos%d" % i, [128, T], F32)) for i in range(2)]
        tri = B.sb(es, "tri", [128, T], I32)
        sqf = [B.sb(es, "sqf%d" % i, [128, T], F32) for i in range(2)]
        rs2 = [B.sb(es, "rs2%d" % i, [128, T], F32) for i in range(2)]
        qn = [B.sb(es, "qn%d" % i, [128, T], F32) for i in range(2)]
        t1 = [B.sb(es, "t1%d" % i, [128, T], F32) for i in range(2)]
        tr = [t1[0], t1[1], rs2[0], rs2[1]]
        stb = [B.sb(es, "stb%d" % i, [128, T], BF16) for i in range(3)]
        stf = [B.sb(es, "stf%d" % i, [128, T], F32) for i in range(2)]
        P.dma("sp", lambda e: e.dma_start(out=g_sb.t[:], in_=gA), "gA", writes=[g_sb.b])
        xparts = [Buf() for _ in range(4)]

        psm = [B.psb[i] for i in range(4)]
        pss = [B.psb[i] for i in range(4, 8)]
        cnt = dict(w=0, m=0, s=0, sq=0, stb=0, stf=0, q=0)

        def prep(tt):
            t0 = tt * T
            xnb = xn[tt % 2]
            csb = cs[tt % 2]
            for q4 in range(4):
                P.dma("sp", _bind(lambda e, q4: e.dma_start(out=x_sb.t[:, 8 * q4:8 * q4 + 8, :], in_=xT[:, 8 * q4:8 * q4 + 8, t0:t0 + T]), q4),
                      "x%d" % q4, writes=[xparts[q4]])
            P.dma("sp", lambda e: e.dma_start(out=posi.t[:], in_=pos[:, t0:t0 + T].partition_broadcast(128)), "posi", writes=[posi.b])
            ssp = pss[cnt["s"] % 4]
            cnt["s"] += 1
            for kc in range(32):
                sq = sqb[cnt["sq"] % 2]
                cnt["sq"] += 1
                P.op("act", _bind(lambda e, sq, kc: e.activation(out=sq.t[:], in_=x_sb.t[:, kc, :], func=AF.Square), sq, kc),
                     reads=[xparts[kc // 8]], writes=[sq.b])
                P.op("pe", _bind(lambda e, sq, kc: e.matmul(ssp.t[:], K["ONESB"], sq.t[:], start=(kc == 0), stop=(kc == 31)), sq, kc),
                     reads=[sq.b] + CR, writes=[ssp.b])
            P.op("act", lambda e: e.activation(out=rstd.t[:], in_=ssp.t[:], func=AF.Sqrt, bias=NORM_EPS, scale=1.0 / D_MODEL),
                 reads=[ssp.b], writes=[rstd.b])
            P.op("dve", lambda e: e.reciprocal(out=rstd.t[:], in_=rstd.t[:]), reads=[rstd.b], writes=[rstd.b])
            for kc in range(32):
                P.op("dve", _bind(lambda e, kc: e.scalar_tensor_tensor(out=xnb.t[:, kc, :], in0=x_sb.t[:, kc, :], scalar=g_sb.t[:, kc:kc + 1],
                                                                       in1=rstd.t[:], op0=ALU.mult, op1=ALU.mult), kc),
                     reads=[xparts[kc // 8], rstd.b, g_sb.b], writes=[xnb.b])
            a, k_, r_, rc = tr
            TWO_PI = 2.0 * math.pi
            C1 = 6.28125
            C2 = 0.0019340515136718750
            C3 = TWO_PI - C1 - C2

            def trig(e):
                e.tensor_copy(out=a.t[:], in_=posi.t[:])
                e.tensor_scalar(out=a.t[:], in0=a.t[:], scalar1=K["INVF"], scalar2=None, op0=ALU.mult)
                e.tensor_scalar(out=k_.t[:], in0=a.t[:], scalar1=1.0 / TWO_PI, scalar2=None, op0=ALU.mult)
                e.tensor_copy(out=tri.t[:], in_=k_.t[:])
                e.tensor_copy(out=k_.t[:], in_=tri.t[:])
                e.scalar_tensor_tensor(out=r_.t[:], in0=k_.t[:], scalar=-C1, in1=a.t[:], op0=ALU.mult, op1=ALU.add)
                e.scalar_tensor_tensor(out=r_.t[:], in0=k_.t[:], scalar=-C2, in1=r_.t[:], op0=ALU.mult, op1=ALU.add)
                e.scalar_tensor_tensor(out=r_.t[:], in0=k_.t[:], scalar=-C3, in1=r_.t[:], op0=ALU.mult, op1=ALU.add)
                e.tensor_scalar(out=r_.t[:], in0=r_.t[:], scalar1=-math.pi, scalar2=math.pi, op0=ALU.max, op1=ALU.min)
                e.tensor_scalar(out=rc.t[:], in0=r_.t[:], scalar1=math.pi / 2, scalar2=None, op0=ALU.add)
                e.tensor_scalar(out=k_.t[:], in0=rc.t[:], scalar1=math.pi, scalar2=-TWO_PI, op0=ALU.is_gt, op1=ALU.mult)
                e.tensor_tensor(out=rc.t[:], in0=rc.t[:], in1=k_.t[:], op=ALU.add)
                return e.tensor_scalar(out=rc.t[:], in0=rc.t[:], scalar1=-math.pi, scalar2=math.pi, op0=ALU.max, op1=ALU.min)
            P.op("dve", trig, reads=[posi.b] + CR, writes=[tb.b for tb in tr] + [])
            P.op("act", lambda e: e.activation(out=csb["sin"].t[:], in_=r_.t[:], func=AF.Sin), reads=[r_.b], writes=[csb["sin"].b])
            P.op("act", lambda e: e.activation(out=csb["cos"].t[:], in_=rc.t[:], func=AF.Sin), reads=[rc.b], writes=[csb["cos"].b])

        pending = []

        def flush(now):
            keep = []
            for due, fn in pending:
                if due <= now:
                    fn()
                else:
                    keep.append((due, fn))
            pending[:] = keep

        def chunk(tt, j, seq):
            t0 = tt * T
            xnb = xn[tt % 2]
            csb = cs[tt % 2]
            slot = w_sb[cnt["w"] % NW_A]
            cnt["w"] += 1
            P.dma("pool", _bind(lambda e, slot, j: e.dma_start(out=slot.t[:].rearrange("p k c -> p (k c)"), in_=wA[j]), slot, j),
                  "w%d" % (cnt["w"] % NW_A), writes=[slot.b])
            ps = psm[cnt["m"] % 4]
            cnt["m"] += 1

            def mm(e, slot=slot, ps=ps):
                for kc in range(32):
                    r = e.matmul(ps.t[:], slot.t[:, kc, :], xnb.t[:, kc, :], start=(kc == 0), stop=(kc == 31))
                return r
            P.op("pe", mm, reads=[slot.b, xnb.b], writes=[ps.b])
            if j < 8:
                gv = K["GQ"] if j < 4 else K["GK"]
                i2 = cnt["q"] % 2
                cnt["q"] += 1
                sq, r2, qq, tt1 = sqf[i2], rs2[i2], qn[i2], t1[i2]
                P.op("act", lambda e: e.activation(out=sq.t[:], in_=ps.t[:], func=AF.Square), reads=[ps.b], writes=[sq.b])

                def st2():
                    hs = pss[cnt["s"] % 4]
                    cnt["s"] += 1
                    P.op("pe", lambda e: e.matmul(hs.t[:], K["BD"], sq.t[:], start=True, stop=True), reads=[sq.b] + CR, writes=[hs.b])
                    P.op("act", lambda e: e.activation(out=r2.t[:], in_=hs.t[:], func=AF.Sqrt, bias=NORM_EPS, scale=1.0 / HEAD_DIM),
                         reads=[hs.b], writes=[r2.b])
                    P.op("dve", lambda e: e.reciprocal(out=r2.t[:], in_=r2.t[:]), reads=[r2.b], writes=[r2.b])
                    P.op("dve", lambda e: e.scalar_tensor_tensor(out=qq.t[:], in0=ps.t[:], scalar=gv, in1=r2.t[:], op0=ALU.mult, op1=ALU.mult),
                         reads=[ps.b, r2.b] + CR, writes=[qq.b])

                    def st3():
                        rp = pss[cnt["s"] % 4]
                        cnt["s"] += 1
                        sb_ = stb[cnt["stb"] % 3]
                        cnt["stb"] += 1
                        P.op("pe", lambda e: e.matmul(rp.t[:], K["ROTT"], qq.t[:], start=True, stop=True), reads=[qq.b] + CR, writes=[rp.b])
                        P.op("dve", lambda e: e.tensor_tensor(out=tt1.t[:], in0=qq.t[:], in1=csb["cos"].t[:], op=ALU.mult),
                             reads=[qq.b, csb["cos"].b], writes=[tt1.b])
                        P.op("dve", lambda e: e.tensor_tensor(out=qq.t[:], in0=rp.t[:], in1=csb["sin"].t[:], op=ALU.mult),
                             reads=[rp.b, csb["sin"].b], writes=[qq.b])
                        P.op("dve", lambda e: e.tensor_tensor(out=sb_.t[:], in0=tt1.t[:], in1=qq.t[:], op=ALU.add),
                             reads=[tt1.b, qq.b], writes=[sb_.b])
                        P.dma("sp", lambda e: e.dma_start(out=qk_s[j, :, t0:t0 + T], in_=sb_.t[:]), "stb%d" % (cnt["stb"] % 3), reads=[sb_.b])
                    pending.append((seq + 2, st3))
                pending.append((seq + 1, st2))
            elif j < 12:
                sb_ = stb[cnt["stb"] % 3]
                cnt["stb"] += 1
                P.op("act", lambda e: e.copy(out=sb_.t[:], in_=ps.t[:]), reads=[ps.b], writes=[sb_.b])
                P.dma("sp", lambda e: e.dma_start(out=vv_s[j - 8, :, t0:t0 + T], in_=sb_.t[:]), "stb%d" % (cnt["stb"] % 3), reads=[sb_.b])
            else:
                sf = stf[cnt["stf"] % 2]
                cnt["stf"] += 1
                P.op("act", lambda e: e.copy(out=sf.t[:], in_=ps.t[:]), reads=[ps.b], writes=[sf.b])
                P.dma("sp", lambda e: e.dma_start(out=pr_s[j - 12, :, t0:t0 + T], in_=sf.t[:]), "stf%d" % (cnt["stf"] % 2), reads=[sf.b])

        prep(0)
        seq = 0
        for tt in range(NT):
            for j in range(30):
                chunk(tt, j, seq)
                seq += 1
                flush(seq)
                if j == 14 and tt + 1 < NT:
                    prep(tt + 1)
        flush(seq + 10)
        P.run_block()


def phase_A2(B, S, qk_s, vv_s, MIX, K):
    nc, P = B.nc, B.P
    CR = K["CR"]
    with contextlib.ExitStack() as es:
        qn_ = B.sb(es, "qn", [128, S], BF16)
        kn_ = B.sb(es, "kn", [128, S], BF16)
        vn_ = B.sb(es, "vn", [128, S], BF16)
        qd_ = B.sb(es, "qd", [128, S], BF16)
        kd_ = B.sb(es, "kd", [128, S], BF16)
        vd_ = B.sb(es, "vd", [128, S], BF16)
        acc = [B.sb(es, "acc%d" % h, [65, S], F32) for h in range(2)]
        vp = [B.sb(es, "vp%d" % i, [128, 2, 65], BF16) for i in range(3)]
        pT = [B.sb(es, "pT%d" % i, [128, 256], BF16) for i in range(4)]
        rec = [B.sb(es, "rec%d" % i, [64, 512], F32) for i in range(2)]
        ost = [B.sb(es, "ost%d" % i, [64, 512], MIX["dt"]) for i in range(2)]
        MDST = MIX["dst"]
        for v_ in vp:
            P.op("pool", _bind(lambda e, v_: e.memset(v_.t[:], 1.0), v_), writes=[v_.b])
        sc_ps = [B.psb[0], B.psb[1]]
        o_ps = [[B.psb[2 + h * 2 + i] for i in range(2)] for h in range(2)]
        vt_ps = [B.psb[6], B.psb[7]]
        fin_ps = [B.psb[0], B.psb[1]]
        c = dict(vp=0, pT=0, sc=0, vt=0, fin=0, kb=0)

        def o_ap(h, i):
            return o_ps[h][i].t[0:65, 0:128]

        def vt_ap(i):
            return vt_ps[i].t[:, 0:64].bitcast(BF16)

        def do_kb(hp_i, bi, d, r, kb, nb, M, q3, k3, v3):
            nq = 256 if kb < nb - 1 else 128
            k0 = r * M + kb * 128
            vti = c["vt"] % 2
            c["vt"] += 1
            vtb = vt_ps[vti]
            vpt = vp[c["vp"] % 3]
            c["vp"] += 1
            P.op("pe", lambda e: e.transpose(out=vt_ap(vti), in_=v3.t[:, k0:k0 + 128], identity=K["IDENTB"]),
                 reads=[v3.b] + CR, writes=[vtb.b])
            P.op("dve", lambda e: e.tensor_copy(out=vpt.t[:, :, 0:64], in_=vt_ap(vti).rearrange("p (h c) -> p h c", h=2)),
                 reads=[vtb.b], writes=[vpt.b])
            cur = [do_head_a(nq, k0, h, q3, k3) for h in range(2)]
            flush_pv()
            pend.append((bi, d, r, kb, nq, vpt, cur))

        pend = []

        def flush_pv():
            while pend:
                bi, d, r, kb, nq, vpt, cur = pend.pop(0)
                for h in range(2):
                    do_head_b(bi, d, r, kb, nq, h, vpt, cur[h])

        def do_head_a(nq, k0, h, q3, k3):
            hs = slice(h * 64, (h + 1) * 64)
            sc = sc_ps[c["sc"] % 2]
            c["sc"] += 1
            pt = pT[c["pT"] % 4]
            c["pT"] += 1

            def scf(e):
                e.matmul(sc.t[:, 0:nq], k3.t[hs, k0:k0 + 128], q3.t[hs, k0:k0 + nq], start=True, stop=False)
                return e.matmul(sc.t[:, 0:nq], K["IDENTB"], K["MASKB"][:, 0:nq], start=False, stop=True)
            P.op("pe", scf, reads=[q3.b, k3.b] + CR, writes=[sc.b])
            P.op("act", lambda e: e.activation(out=pt.t[:, 0:nq], in_=sc.t[:, 0:nq], func=AF.Exp, scale=0.125),
                 reads=[sc.b], writes=[pt.b])
            return pt

        def do_head_b(bi, d, r, kb, nq, h, vpt, pt):
            oa = o_ps[h][kb % 2]
            ob = o_ps[h][(kb + 1) % 2]

            def pv(e):
                r_ = e.matmul(o_ap(h, kb % 2), vpt.t[:, h, :], pt.t[:, 0:128], start=(kb == 0), stop=True, skip_group_check=True)
                if nq == 256:
                    r_ = e.matmul(o_ap(h, (kb + 1) % 2), vpt.t[:, h, :], pt.t[:, 128:256], start=True, stop=False, skip_group_check=True)
                return r_
            P.op("pe", pv, reads=[pt.b, vpt.b, oa.b], writes=[oa.b] + ([ob.b] if nq == 256 else []))
            tpos = (kb * 128) * d + r

            def ev(e):
                dst = acc[h].t[:, tpos:tpos + 127 * d + 1:d] if d > 1 else acc[h].t[:, tpos:tpos + 128]
                if bi == 0:
                    return e.tensor_copy(out=dst, in_=o_ap(h, kb % 2))
                return e.tensor_tensor(out=dst, in0=dst, in1=o_ap(h, kb % 2), op=ALU.add)
            P.op("dve", ev, reads=[oa.b, acc[h].b], writes=[acc[h].b, oa.b])

        def do_branch(hp_i, bi, d):
            M = S // d
            nb = M // 128
            if d == 1:
                q3, k3, v3 = qn_, kn_, vn_
            else:
                def cp(src, dst, eng):
                    P.op(eng, lambda e: e.tensor_copy(out=dst.t[:].rearrange("p (r m) -> p r m", r=d),
                                                      in_=src.t[:].rearrange("p (m r) -> p r m", r=d)),
                         reads=[src.b], writes=[dst.b])
                cp(qn_, qd_, "dve")
                cp(kn_, kd_, "pool")
                cp(vn_, vd_, "dve")
                q3, k3, v3 = qd_, kd_, vd_
            for r in range(d):
                for kb in range(nb):
                    do_kb(hp_i, bi, d, r, kb, nb, M, q3, k3, v3)
                flush_pv()

        def do_fin(hp_i, h, s0):
            fp = fin_ps[c["fin"] % 2]
            rc_ = rec[c["fin"] % 2]
            os_ = ost[c["fin"] % 2]
            key = "ost%d" % (c["fin"] % 2)
            c["fin"] += 1
            P.op("pe", lambda e: e.matmul(fp.t[0:64, :], K["ONESF"][64:65, 0:64], acc[h].t[64:65, s0:s0 + 512], start=True, stop=True),
                 reads=[acc[h].b] + CR, writes=[fp.b])
            P.op("dve", lambda e: e.reciprocal(out=rc_.t[:], in_=fp.t[0:64, :]), reads=[fp.b], writes=[rc_.b])
            P.op("pool", lambda e: e.tensor_tensor(out=os_.t[:], in0=acc[h].t[0:64, s0:s0 + 512], in1=rc_.t[:], op=ALU.mult),
                 reads=[rc_.b, acc[h].b], writes=[os_.b])
            row = (hp_i * 2 + h) * 64
            P.dma("sp", lambda e: e.dma_start(out=MDST(row, 64, s0, 512), in_=os_.t[:]), key, reads=[os_.b])

        def do_pair(hp_i):
            P.dma("sp", lambda e: e.dma_start(out=qn_.t[:], in_=qk_s[hp_i]), "a2q", writes=[qn_.b])
            P.dma("sp", lambda e: e.dma_start(out=kn_.t[:], in_=qk_s[4 + hp_i]), "a2k", writes=[kn_.b])
            P.dma("sp", lambda e: e.dma_start(out=vn_.t[:], in_=vv_s[hp_i]), "a2v", writes=[vn_.b])
            for bi, d in enumerate((1, 4, 16)):
                do_branch(hp_i, bi, d)
            for h in range(2):
                for s0 in range(0, S, 512):
                    do_fin(hp_i, h, s0)

        for hp_i in range(4):
            do_pair(hp_i)
        P.run_block()


def phase_A3(B, S, pr_s, MIX, wdu_d, wau_d, wgu_d, K, conv=()):
    nc, P = B.nc, B.P
    CR = K["CR"]
    hp, lp = K["hp"], K["lp"]
    NS = S // TS
    NC = TS // CH
    W = TS + 1
    with contextlib.ExitStack() as es:
        def H(name, n=TS, extra=()):
            return B.sb(es, name, [64, 8] + list(extra) + [n], F32)

        def S64(name, rows=64):
            return B.sb(es, name, [rows, 8, 64], F32)
        wdu = B.sb(es, "wdu", [128, 512], F32)
        wau = B.sb(es, "wau", [128, 512], F32)
        wgu = B.sb(es, "wgu", [128, 4, 512], F32)
        P.dma("sp", lambda e: e.dma_start(out=wdu.t[:], in_=wdu_d), "wdu", writes=[wdu.b])
        P.dma("sp", lambda e: e.dma_start(out=wau.t[:], in_=wau_d), "wau", writes=[wau.b])
        P.dma("sp", lambda e: e.dma_start(out=wgu.t[:].rearrange("p k c -> p (k c)"), in_=wgu_d), "wgu", writes=[wgu.b])
        RX = [H("RX%d" % i, W) for i in range(2)]
        KX = [H("KX%d" % i, W) for i in range(2)]
        VX = [H("VX%d" % i, W) for i in range(2)]
        WX = [B.sb(es, "WX%d" % i, [128, W], F32) for i in range(2)]
        AXl = [B.sb(es, "AX%d" % i, [128, W], F32) for i in range(2)]
        GX = [B.sb(es, "GX%d" % i, [128, 4, W], F32) for i in range(2)]
        parts = {}

        def part(tb, i):
            k = (id(tb), i)
            if k not in parts:
                parts[k] = Buf()
            return parts[k]
        D1 = H("D1")
        r_ = H("r")
        k_ = H("k")
        VZ = [H("VZ%d" % i, TS, extra=(2,)) for i in range(2)]
        wdm = B.sb(es, "wdm", [128, TS], F32)
        adm = B.sb(es, "adm", [128, TS], F32)
        gdm = B.sb(es, "gdm", [128, 4, TS], F32)
        dl = B.sb(es, "dl", [128, 4, TS], F32)
        sw = H("sw")
        a_ = H("a")
        g_ = [H("g%d" % i) for i in range(3)]
        kk = H("kk")
        sq = H("sq")
        kmod = H("kmod")
        ba = H("ba")
        cs_ = H("cs")
        E1 = [H("E1%d" % i) for i in range(3)]
        E2 = H("E2")
        E3 = H("E3")
        E4 = H("E4")
        tmpH = H("tmpH")
        AR = [H("AR%d" % i, TS, extra=(2,)) for i in range(3)]
        BK = [H("BK%d" % i, TS, extra=(2,)) for i in range(2)]
        BKh = [H("BKh%d" % i, TS, extra=(2,)) for i in range(2)]
        bonus = [H("bon%d" % i) for i in range(3)]
        Y = [H("Y%d" % i) for i in range(2)]
        Gm = [B.sb(es, "Gm%d" % i, [128, 8, 128], F32) for i in range(2)]
        QN0 = [S64("QN0%d" % i) for i in range(2)]
        QP = [S64("QP%d" % i) for i in range(2)]
        QtP = [S64("QtP%d" % i) for i in range(2)]
        IQ = S64("IQ")
        X = [S64("X%d" % i) for i in range(2)]
        Atm = S64("Atm")
        BKt = [S64("BKt%d" % i, 128) for i in range(2)]
        UV = [S64("UV%d" % i, 128) for i in range(2)]
        Wsb = S64("Wsb")
        Uhat = [S64("Uhat%d" % i) for i in range(2)]
        AhT = [S64("AhT%d" % i) for i in range(2)]
        ST = [S64("ST%d" % i) for i in range(2)]
        STd = S64("STd")
        yc = H("yc")
        ysq = H("ysq")
        rsd = H("rsd")
        ostg = [B.sb(es, "ostg%d" % i, [64, 8, TS], MIX["dt"]) for i in range(2)]
        MDST = MIX["dst"]
        P.op("pool", lambda e: e.memset(ST[0].t[:], 0.0), writes=[ST[0].b])
        P.op("pool", lambda e: e.memset(VZ[0].t[:], 0.0), writes=[VZ[0].b])
        P.op("pool", lambda e: e.memset(VZ[1].t[:], 0.0), writes=[VZ[1].b])
        P.op("pool", lambda e: e.memset(UV[0].t[:], 0.0), writes=[UV[0].b])
        P.op("pool", lambda e: e.memset(UV[1].t[:], 0.0), writes=[UV[1].b])

        ONES64 = K["ONESF"][0:64, 0:64]
        ID64 = K["IDENTF"][0:64, 0:64]
        ID64B = ID64.unsqueeze(1).to_broadcast([64, 8, 64])
        MASKNB = K["MASKN"].unsqueeze(1).to_broadcast([64, 8, 64])
        MASKGB = K["MASKG"].unsqueeze(1).to_broadcast([128, 4, 128])
        st = dict(ps=0, sti=0, chunk=0)

        def nps():
            pool = st.get("pool", 0)
            key = "ps%d" % pool
            base, size = ((0, 3), (3, 3), (6, 2))[pool]
            p = B.psb[base + st.get(key, 0) % size]
            st[key] = st.get(key, 0) + 1
            return p

        def hpv(i):
            return hp.t[:, i, :].unsqueeze(2)

        def bc(ap, n=TS):
            return ap.to_broadcast([64, 8, n])

        def pv8(p, rows=64):
            return p.t[0:rows, :].rearrange("p (h t) -> p h t", h=8)

        def mm8(out_fn, lhs_fn, rhs_fn):
            def f(e):
                for h in range(8):
                    r = e.matmul(out_fn(h), lhs_fn(h), rhs_fn(h), start=True, stop=True)
                return r
            return f

        def sum8(src, consume):
            pk = nps()
            P.op("pe", mm8(lambda h: pv8(pk)[:, h, :], lambda h: ONES64, lambda h: src.t[:, h, :]), reads=[src.b] + CR, writes=[pk.b])
            consume(pk)

        def chunk_pre(s_i, c, ar, bk, bkh, vz, e1, yy, cx):
            cs0 = c * CH
            csl = slice(cs0, cs0 + CH)
            ci = st["chunk"] % 2
            st["chunk"] += 1
            gm, qn0, bkt, uv, uh, aht = Gm[ci], QN0[ci], BKt[ci], UV[ci], Uhat[ci], AhT[ci]
            for half in range(2):
                def ghalf(half):
                    pg_ = nps()

                    def gmm(e):
                        for hh in range(4):
                            h = half * 4 + hh
                            r = e.matmul(pg_.t[:, hh * 128:(hh + 1) * 128], bk.t[:, h, :, csl], ar.t[:, h, :, csl], start=True, stop=True)
                        return r
                    P.op("pe", gmm, reads=[bk.b, ar.b], writes=[pg_.b])
                    P.op("dve", lambda e: e.tensor_tensor(out=gm.t[:, half * 4:half * 4 + 4, :], in0=pg_.t[:].rearrange("p (h t) -> p h t", h=4),
                                                          in1=MASKGB, op=ALU.mult),
                         reads=[pg_.b] + CR, writes=[gm.b])
                ghalf(half)
                yield
            pn = nps()
            P.op("pe", mm8(lambda h: pv8(pn)[:, h, :], lambda h: ar.t[:, h, 0, csl], lambda h: bk.t[:, h, 0, csl]), reads=[ar.b, bk.b], writes=[pn.b])
            yield
            P.op("dve", lambda e: e.tensor_tensor(out=qn0.t[:], in0=pv8(pn), in1=MASKNB, op=ALU.mult), reads=[pn.b] + CR, writes=[qn0.b])
            yield
            P.op("pool", lambda e: e.tensor_tensor(out=X[0].t[:], in0=gm.t[0:64, :, 0:64], in1=ID64B, op=ALU.add), reads=[gm.b] + CR, writes=[X[0].b])
            yield

            def level(lvl, q_cur, qt_ap, qt_b, x_cur):
                pq = nps()
                P.op("pe", mm8(lambda h: pv8(pq)[:, h, :], qt_ap, lambda h: q_cur.t[:, h, :]), reads=[qt_b, q_cur.b], writes=[pq.b])
                yield
                q_new = QP[lvl % 2]
                qt_new = QtP[lvl % 2]
                if lvl < 5:
                    pqt = nps()
                    P.op("pe", mm8(lambda h: pv8(pqt)[:, h, :], lambda h: q_cur.t[:, h, :], qt_ap), reads=[qt_b, q_cur.b], writes=[pqt.b])
                    P.op("act", lambda e: e.copy(out=q_new.t[:], in_=pv8(pq)), reads=[pq.b], writes=[q_new.b])
                    P.op("dve", lambda e: e.tensor_copy(out=qt_new.t[:], in_=pv8(pqt)), reads=[pqt.b], writes=[qt_new.b])
                    P.op("pool", lambda e: e.tensor_tensor(out=IQ.t[:], in0=q_new.t[:], in1=ID64B, op=ALU.add), reads=[q_new.b] + CR, writes=[IQ.b])
                else:
                    P.op("dve", lambda e: e.tensor_tensor(out=IQ.t[:], in0=pv8(pq), in1=ID64B, op=ALU.add), reads=[pq.b] + CR, writes=[IQ.b])
                px = nps()
                x_new = X[lvl % 2]
                P.op("pe", mm8(lambda h: pv8(px)[:, h, :], lambda h: IQ.t[:, h, :], lambda h: x_cur.t[:, h, :]), reads=[IQ.b, x_cur.b], writes=[px.b])
                yield
                P.op("act", lambda e: e.copy(out=x_new.t[:], in_=pv8(px)), reads=[px.b], writes=[x_new.b])
                yield
                return q_new, (lambda h: qt_new.t[:, h, :]), qt_new.b, x_new

            q_cur, qt_ap, qt_b, x_cur = qn0, (lambda h: gm.t[0:64, h, 0:64]), gm.b, X[0]
            for lvl in range(1, 6):
                q_cur, qt_ap, qt_b, x_cur = yield from level(lvl, q_cur, qt_ap, qt_b, x_cur)
            TT = x_cur
            pa_, pb_, pv_ = nps(), nps(), nps()

            def trs(e):
                for h in range(8):
                    e.transpose(out=pv8(pa_)[:, h, :], in_=ar.t[:, h, 0, csl], identity=ID64)
                for h in range(8):
                    e.transpose(out=pv8(pb_, 128)[:, h, :], in_=bkh.t[:, h, :, csl], identity=ID64)
                for h in range(8):
                    r = e.transpose(out=pv8(pv_, 128)[:, h, :], in_=vz.t[:, h, :, csl], identity=ID64)
                return r
            P.op("pe", trs, reads=[ar.b, bkh.b, vz.b] + CR, writes=[pa_.b, pb_.b, pv_.b])
            yield
            P.op("act", lambda e: e.copy(out=Atm.t[:], in_=pv8(pa_)), reads=[pa_.b], writes=[Atm.b])
            yield
            P.op("dve", lambda e: e.tensor_copy(out=bkt.t[:], in_=pv8(pb_, 128)), reads=[pb_.b], writes=[bkt.b])
            yield
            P.op("act", lambda e: e.copy(out=uv.t[64:128, :, :], in_=pv8(pv_, 128)[64:128, :, :]), reads=[pv_.b], writes=[uv.b])
            yield
            pw_ = nps()
            P.op("pe", mm8(lambda h: pv8(pw_)[:, h, :], lambda h: gm.t[64:128, h, 0:64], lambda h: uv.t[64:128, h, :]), reads=[gm.b, uv.b], writes=[pw_.b])
            yield
            P.op("act", lambda e: e.copy(out=Wsb.t[:], in_=pv8(pw_)), reads=[pw_.b], writes=[Wsb.b])
            yield
            pu_, ph_ = nps(), nps()
            P.op("pe", mm8(lambda h: pv8(pu_)[:, h, :], lambda h: TT.t[:, h, :], lambda h: Wsb.t[:, h, :]), reads=[TT.b, Wsb.b], writes=[pu_.b])
            yield
            P.op("pe", mm8(lambda h: pv8(ph_)[:, h, :], lambda h: Atm.t[:, h, :], lambda h: TT.t[:, h, :]), reads=[TT.b, Atm.b], writes=[ph_.b])
            yield
            P.op("act", lambda e: e.copy(out=uh.t[:], in_=pv8(pu_)), reads=[pu_.b], writes=[uh.b])
            yield
            P.op("dve", lambda e: e.tensor_copy(out=aht.t[:], in_=pv8(ph_)), reads=[ph_.b], writes=[aht.b])
            yield
            cx.update(gm=gm, bkt=bkt, uv=uv, uh=uh, aht=aht, csl=csl, cs0=cs0)

        def chunk_scan(cx, ar, e1, yy):
            gm, bkt, uv, uh, aht, csl, cs0 = (cx[k] for k in ("gm", "bkt", "uv", "uh", "aht", "csl", "cs0"))
            st_old = ST[st["sti"] % 2]
            st_new = ST[(st["sti"] + 1) % 2]
            st["sti"] += 1
            pc_ap = e1.t[:, :, cs0 + CH - 1:cs0 + CH]
            P.op("pool", lambda e: e.tensor_tensor(out=STd.t[:], in0=st_old.t[:], in1=pc_ap.to_broadcast([64, 8, 64]), op=ALU.mult),
                 reads=[st_old.b, e1.b], writes=[STd.b])
            yield
            pU = nps()
            P.op("pe", mm8(lambda h: pv8(pU)[:, h, :], lambda h: aht.t[:, h, :], lambda h: st_old.t[:, h, :]), reads=[aht.b, st_old.b], writes=[pU.b])
            yield
            P.op("dve", lambda e: e.tensor_tensor(out=uv.t[0:64, :, :], in0=pv8(pU), in1=uh.t[:], op=ALU.add), reads=[pU.b, uh.b], writes=[uv.b])
            yield
            pS = nps()
            P.op("pe", mm8(lambda h: pv8(pS)[:, h, :], lambda h: bkt.t[:, h, :], lambda h: uv.t[:, h, :]), reads=[bkt.b, uv.b], writes=[pS.b])
            yield
            P.op("dve", lambda e: e.tensor_tensor(out=st_new.t[:], in0=pv8(pS), in1=STd.t[:], op=ALU.add), reads=[pS.b, STd.b], writes=[st_new.b])
            yield
            pY = nps()

            def y1(e):
                for h in range(8):
                    e.matmul(pv8(pY)[:, h, :], st_old.t[:, h, :], ar.t[:, h, 1, csl], start=True, stop=False)
                    r = e.matmul(pv8(pY)[:, h, :], uv.t[:, h, :], gm.t[:, h, 64:128], start=False, stop=True)
                return r
            P.op("pe", y1, reads=[st_old.b, ar.b, uv.b, gm.b], writes=[pY.b])
            yield
            P.op("act", lambda e: e.copy(out=yy.t[:, :, csl], in_=pv8(pY)), reads=[pY.b], writes=[yy.b])
            yield

        def super_pre(s_i, sx):
            t0 = s_i * TS
            i2 = s_i % 2
            rx, kx, vx, wx, ax, gx = RX[i2], KX[i2], VX[i2], WX[i2], AXl[i2], GX[i2]
            i3 = s_i % 3
            vz, ar, bk, bkh, e1, bon, gg, yy = VZ[i2], AR[i3], BK[i2], BKh[i2], E1[i3], bonus[i3], g_[i3], Y[i2]
            lo = 1 if s_i == 0 else 0
            src0 = t0 - 1 + lo
            n = W - lo

            def ldH(dst, c0, key):
                if s_i == 0:
                    P.op("pool", lambda e: e.memset(dst.t[:, :, 0:1], 0.0), writes=[part(dst, cc) for cc in range(4)])
                for cc in range(4):
                    def one(cc):
                        P.dma("sp", lambda e: e.dma_start(out=dst.t[:, 2 * cc:2 * cc + 2, lo:W],
                                                          in_=pr_s[c0 + cc, :, src0:src0 + n].rearrange("(h p) t -> p h t", h=2)),
                              "%s%d" % (key, i2), writes=[part(dst, cc)])
                    one(cc)
            ldH(rx, 0, "rx")
            ldH(kx, 4, "kx")
            ldH(vx, 8, "vx")
            if s_i == 0:
                P.op("pool", lambda e: e.memset(wx.t[:, 0:1], 0.0), writes=[wx.b])
                P.op("pool", lambda e: e.memset(ax.t[:, 0:1], 0.0), writes=[ax.b])
                P.op("pool", lambda e: e.memset(gx.t[:, :, 0:1], 0.0), writes=[part(gx, cc) for cc in range(4)])
            P.dma("sp", lambda e: e.dma_start(out=wx.t[:, lo:W], in_=pr_s[12, :, src0:src0 + n]), "wx%d" % i2, writes=[wx.b])
            yield
            P.dma("sp", lambda e: e.dma_start(out=ax.t[:, lo:W], in_=pr_s[13, :, src0:src0 + n]), "ax%d" % i2, writes=[ax.b])
            yield
            for cc in range(4):
                def oneg(cc):
                    P.dma("sp", lambda e: e.dma_start(out=gx.t[:, cc, lo:W], in_=pr_s[14 + cc, :, src0:src0 + n]),
                          "gx%d" % i2, writes=[part(gx, cc)])
                oneg(cc)
            allp = lambda tb: [part(tb, cc) for cc in range(4)]

            def mixH(src, dst_ap, mi):
                def f(e):
                    e.tensor_tensor(out=D1.t[:], in0=src.t[:, :, 0:TS], in1=src.t[:, :, 1:W], op=ALU.subtract)
                    e.tensor_tensor(out=D1.t[:], in0=D1.t[:], in1=bc(hpv(mi)), op=ALU.mult)
                    return e.tensor_tensor(out=dst_ap, in0=D1.t[:], in1=src.t[:, :, 1:W], op=ALU.add)
                return f
            P.op("dve", mixH(rx, r_.t[:], 0), reads=allp(rx) + CR, writes=[D1.b, r_.b])
            yield
            P.op("dve", mixH(kx, k_.t[:], 1), reads=allp(kx) + CR, writes=[D1.b, k_.b])
            yield
            P.op("dve", mixH(vx, vz.t[:, :, 1, :], 2), reads=allp(vx) + CR, writes=[D1.b, vz.b])
            yield

            def mixL(e):
                e.tensor_tensor(out=dl.t[:, 0, :], in0=wx.t[:, 0:TS], in1=wx.t[:, 1:W], op=ALU.subtract)
                e.scalar_tensor_tensor(out=wdm.t[:], in0=dl.t[:, 0, :], scalar=lp.t[:, 0:1], in1=wx.t[:, 1:W], op0=ALU.mult, op1=ALU.add)
                e.tensor_tensor(out=dl.t[:, 0, :], in0=ax.t[:, 0:TS], in1=ax.t[:, 1:W], op=ALU.subtract)
                e.scalar_tensor_tensor(out=adm.t[:], in0=dl.t[:, 0, :], scalar=lp.t[:, 1:2], in1=ax.t[:, 1:W], op0=ALU.mult, op1=ALU.add)
                e.tensor_tensor(out=dl.t[:], in0=gx.t[:, :, 0:TS], in1=gx.t[:, :, 1:W], op=ALU.subtract)
                for cc in range(4):
                    r = e.scalar_tensor_tensor(out=gdm.t[:, cc, :], in0=dl.t[:, cc, :], scalar=lp.t[:, 2 + cc:3 + cc], in1=gx.t[:, cc, 1:W],
                                               op0=ALU.mult, op1=ALU.add)
                return r
            P.op("dve", mixL, reads=[wx.b, ax.b] + allp(gx) + CR, writes=[dl.b, wdm.b, adm.b, gdm.b])
            yield
            P.op("act", lambda e: e.activation(out=wdm.t[:], in_=wdm.t[:], func=AF.Tanh), reads=[wdm.b], writes=[wdm.b])
            yield
            P.op("act", lambda e: e.activation(out=gdm.t[:], in_=gdm.t[:], func=AF.Sigmoid), reads=[gdm.b], writes=[gdm.b])
            yield

            def lora_all():
                pw, pa, pg = nps(), nps(), nps()

                def lora(e):
                    for h in range(8):
                        e.matmul(pv8(pw)[:, h, :], wdu.t[:, h * 64:(h + 1) * 64], wdm.t[:], start=True, stop=True)
                    for h in range(8):
                        e.matmul(pv8(pa)[:, h, :], wau.t[:, h * 64:(h + 1) * 64], adm.t[:], start=True, stop=True)
                    for h in range(8):
                        for cc in range(4):
                            r = e.matmul(pv8(pg)[:, h, :], wgu.t[:, cc, h * 64:(h + 1) * 64], gdm.t[:, cc, :], start=(cc == 0), stop=(cc == 3))
                    return r
                P.op("pe", lora, reads=[wdu.b, wau.b, wgu.b, wdm.b, adm.b, gdm.b], writes=[pw.b, pa.b, pg.b])

                def sig(e):
                    for h in range(8):
                        e.activation(out=sw.t[:, h, :], in_=pv8(pw)[:, h, :], func=AF.Sigmoid, bias=hp.t[:, 3, h:h + 1], scale=1.0)
                    for h in range(8):
                        r = e.activation(out=a_.t[:, h, :], in_=pv8(pa)[:, h, :], func=AF.Sigmoid, bias=hp.t[:, 4, h:h + 1], scale=1.0)
                    return r
                P.op("act", sig, reads=[pw.b, pa.b] + CR, writes=[sw.b, a_.b])
                P.op("act", lambda e: e.copy(out=gg.t[:], in_=pv8(pg)), reads=[pg.b], writes=[gg.b])
            lora_all()
            yield
            P.op("dve", lambda e: e.tensor_tensor(out=kk.t[:], in0=k_.t[:], in1=bc(hpv(5)), op=ALU.mult), reads=[k_.b] + CR, writes=[kk.b])
            yield
            P.op("act", lambda e: e.activation(out=sq.t[:], in_=kk.t[:], func=AF.Square), reads=[kk.b], writes=[sq.b])
            yield

            sum8(sq, lambda pk: P.op("act", lambda e: e.activation(out=tmpH.t[:], in_=pv8(pk), func=AF.Sqrt), reads=[pk.b], writes=[tmpH.b]))
            yield

            def kkn(e):
                e.tensor_scalar(out=tmpH.t[:], in0=tmpH.t[:], scalar1=1e-12, scalar2=None, op0=ALU.max)
                e.reciprocal(out=tmpH.t[:], in_=tmpH.t[:])
                e.tensor_tensor(out=kk.t[:], in0=kk.t[:], in1=tmpH.t[:], op=ALU.mult)
                e.scalar_tensor_tensor(out=tmpH.t[:], in0=a_.t[:], scalar=-1.0, in1=bc(hpv(6)), op0=ALU.add, op1=ALU.mult)
                e.scalar_tensor_tensor(out=kmod.t[:], in0=tmpH.t[:], scalar=1.0, in1=k_.t[:], op0=ALU.add, op1=ALU.mult)
                return e.tensor_tensor(out=ba.t[:], in0=kk.t[:], in1=a_.t[:], op=ALU.mult)
            P.op("dve", kkn, reads=[tmpH.b, kk.b, a_.b, k_.b] + CR, writes=[tmpH.b, kk.b, kmod.b, ba.b])
            yield
            flat = lambda t: t.t[:].rearrange("p h t -> p (h t)")
            P.op("dve", lambda e: e.tensor_tensor_scan(out=flat(cs_), data0=K["RESETM"], data1=flat(sw), initial=0.0, op0=ALU.mult, op1=ALU.add),
                 reads=[sw.b] + CR, writes=[cs_.b])
            yield
            P.op("act", lambda e: e.activation(out=e1.t[:], in_=cs_.t[:], func=AF.Exp, scale=-C0), reads=[cs_.b], writes=[e1.b])
            yield
            P.op("act", lambda e: e.activation(out=E2.t[:], in_=cs_.t[:], func=AF.Exp, scale=C0), reads=[cs_.b], writes=[E2.b])
            yield
            P.op("pool", lambda e: e.tensor_tensor(out=E3.t[:], in0=cs_.t[:], in1=sw.t[:], op=ALU.subtract), reads=[cs_.b, sw.b], writes=[E3.b])
            yield
            P.op("act", lambda e: e.activation(out=E3.t[:], in_=E3.t[:], func=AF.Exp, scale=-C0), reads=[E3.b], writes=[E3.b])
            yield

            def e4f(e):
                c4 = cs_.t[:].rearrange("p h (c t) -> p h c t", t=CH)
                return e.tensor_tensor(out=E4.t[:].rearrange("p h (c t) -> p h c t", t=CH), in0=c4,
                                       in1=c4[:, :, :, CH - 1:CH].to_broadcast([64, 8, NC, CH]), op=ALU.subtract)
            P.op("pool", e4f, reads=[cs_.b], writes=[E4.b])
            yield
            P.op("act", lambda e: e.activation(out=E4.t[:], in_=E4.t[:], func=AF.Exp, scale=C0), reads=[E4.b], writes=[E4.b])
            yield

            def tild(e):
                e.tensor_tensor(out=ar.t[:, :, 1, :], in0=r_.t[:], in1=e1.t[:], op=ALU.mult)
                e.scalar_tensor_tensor(out=ar.t[:, :, 0, :], in0=kk.t[:], scalar=-1.0, in1=E3.t[:], op0=ALU.mult, op1=ALU.mult)
                e.tensor_tensor(out=bk.t[:, :, 0, :], in0=ba.t[:], in1=E2.t[:], op=ALU.mult)
                return e.tensor_tensor(out=bk.t[:, :, 1, :], in0=kmod.t[:], in1=E2.t[:], op=ALU.mult)
            P.op("dve", tild, reads=[r_.b, e1.b, kk.b, E3.b, ba.b, E2.b, kmod.b], writes=[ar.b, bk.b])
            yield

            def hatf(e):
                e.tensor_tensor(out=bkh.t[:, :, 0, :], in0=ba.t[:], in1=E4.t[:], op=ALU.mult)
                return e.tensor_tensor(out=bkh.t[:, :, 1, :], in0=kmod.t[:], in1=E4.t[:], op=ALU.mult)
            P.op("pool", hatf, reads=[ba.b, kmod.b, E4.b], writes=[bkh.b])
            yield

            def rkf(e):
                e.tensor_tensor(out=tmpH.t[:], in0=r_.t[:], in1=kmod.t[:], op=ALU.mult)
                return e.tensor_tensor(out=sq.t[:], in0=tmpH.t[:], in1=bc(hpv(7)), op=ALU.mult)
            P.op("pool", rkf, reads=[r_.b, kmod.b, tmpH.b, sq.b] + CR, writes=[tmpH.b, sq.b])
            yield
            sum8(sq, lambda pb: P.op("dve", lambda e: e.tensor_tensor(out=bon.t[:], in0=pv8(pb), in1=vz.t[:, :, 1, :], op=ALU.mult),
                                     reads=[pb.b, vz.b], writes=[bon.b]))
            yield
            sx.update(ar=ar, bk=bk, bkh=bkh, vz=vz, e1=e1, yy=yy, bon=bon, gg=gg, i2=i2, t0=t0)

        def super_cpre(s_i, sx):
            ar, bk, bkh, vz, e1, yy = (sx[k] for k in ("ar", "bk", "bkh", "vz", "e1", "yy"))
            cxs = []
            for c in range(NC):
                cx = {}
                yield from chunk_pre(s_i, c, ar, bk, bkh, vz, e1, yy, cx)
                cxs.append(cx)
            sx.update(cxs=cxs)

        def super_post(s_i, sx):
            ar, e1, yy, bon, gg, i2, t0 = (sx[k] for k in ("ar", "e1", "yy", "bon", "gg", "i2", "t0"))
            for cx in sx["cxs"]:
                yield from chunk_scan(cx, ar, e1, yy)
            sum8(yy, lambda pm: P.op("dve", lambda e: e.scalar_tensor_tensor(out=yc.t[:], in0=pv8(pm), scalar=-1.0 / 64, in1=yy.t[:],
                                                                             op0=ALU.mult, op1=ALU.add),
                                     reads=[pm.b, yy.b], writes=[yc.b]))
            yield
            P.op("act", lambda e: e.activation(out=ysq.t[:], in_=yc.t[:], func=AF.Square), reads=[yc.b], writes=[ysq.b])
            yield
            sum8(ysq, lambda pvv: P.op("act", lambda e: e.activation(out=rsd.t[:], in_=pv8(pvv), func=AF.Sqrt, bias=GN_EPS, scale=1.0 / 64),
                                       reads=[pvv.b], writes=[rsd.b]))
            yield
            og = ostg[i2]

            def fin(e):
                e.reciprocal(out=rsd.t[:], in_=rsd.t[:])
                e.tensor_tensor(out=yc.t[:], in0=yc.t[:], in1=rsd.t[:], op=ALU.mult)
                e.tensor_tensor(out=yc.t[:], in0=yc.t[:], in1=bc(hpv(8)), op=ALU.mult)
                e.tensor_tensor(out=yc.t[:], in0=yc.t[:], in1=bc(hpv(9)), op=ALU.add)
                e.tensor_tensor(out=yc.t[:], in0=yc.t[:], in1=bon.t[:], op=ALU.add)
                return e.tensor_tensor(out=og.t[:], in0=yc.t[:], in1=gg.t[:], op=ALU.mult)
            P.op("dve", fin, reads=[rsd.b, yc.b, bon.b, gg.b] + CR, writes=[rsd.b, yc.b, og.b])
            yield
            P.dma("sp", lambda e: e.dma_start(out=MDST(512, 512, t0, TS).rearrange("(h p) t -> p h t", h=8), in_=og.t[:]),
                  "ostg%d" % i2, reads=[og.b])
            yield

        conv = list(conv)
        per = -(-len(conv) // NS) if conv else 0
        cvi = 0
        def drive(items):
            alive = list(items)
            while alive:
                for it in list(alive):
                    st["pool"] = it[0]
                    try:
                        next(it[1])
                    except StopIteration:
                        alive.remove(it)
        sxs = {}
        for s_i in range(NS + 2):
            items = []
            if s_i < NS:
                sxs[s_i] = {}
                items.append((0, super_pre(s_i, sxs[s_i])))
            if 0 <= s_i - 1 < NS:
                items.append((1, super_cpre(s_i - 1, sxs[s_i - 1])))
            if 0 <= s_i - 2 < NS:
                items.append((2, super_post(s_i - 2, sxs.pop(s_i - 2))))
            drive(items)
            if s_i >= NS:
                continue
            for _ in range(per):
                if cvi < len(conv):
                    def cvt(k):
                        dst, src = conv[k]
                        P.dma("pool", lambda e: e.dma_start(out=dst, in_=src), "cv%d" % (k % 8))
                    cvt(cvi)
                    cvi += 1
        P.run_block()


def _consts_A(qg, kg):
    c = np.zeros((128, 8 * 128 + 256 + 4), np.float32)
    p = np.arange(128)
    c[:, 0:128] = 1.0
    c[:, 128:256] = (p[:, None] // 64 == p[None, :] // 64).astype(np.float32)
    rot = np.zeros((128, 128), np.float32)
    for m in range(128):
        if m % 64 < 32:
            rot[m + 32, m] = -1.0
        else:
            rot[m - 32, m] = 1.0
    c[:, 256:384] = rot
    c[:, 384:512] = np.eye(128, dtype=np.float32)
    i = (p % 64)[:, None]
    t = np.arange(64)[None, :]
    c[:, 512:576] = (i < t).astype(np.float32)
    c[:, 576:640] = (i <= t).astype(np.float32)
    kk = p[:, None]
    qq = np.arange(256)[None, :]
    dist = qq - kk
    c[:, 1024:1280] = np.where((dist >= 0) & (dist <= 128), 0.0, -262144.0)
    c[:, 1280] = np.tile(qg, 2)
    c[:, 1281] = np.tile(kg, 2)
    c[:, 1282] = (np.float32(10000.0) ** (-(np.arange(32, dtype=np.float32)) / np.float32(32)))[p % 32]
    return c


def _consts_64():
    c = np.zeros((64, 64 + 8 * TS), np.float32)
    t = np.arange(64)
    c[:, 0:64] = (t[:, None] > t[None, :]).astype(np.float32)
    m = np.ones((8, TS), np.float32)
    m[:, ::CH] = 0.0
    c[:, 64:] = m.reshape(1, -1)
    return c


def _wchunk(w, cols):
    blk = np.zeros((4096, 128), np.float32)
    blk[:, :len(cols)] = w[:, cols]
    return np.ascontiguousarray(blk.reshape(32, 128, 128).transpose(1, 0, 2)).reshape(128, 32 * 128)


def prep_A(inp, S):
    x = inp["x"]
    w_in = inp["w_in"][0]
    mu = inp["rwkv_mu"][0]
    maps = []
    xTs = [np.ascontiguousarray(x[b].T.reshape(32, 128, S).transpose(1, 0, 2)) for b in range(2)]
    cA = _consts_A(inp["q_norm_g"][0], inp["k_norm_g"][0])
    c64 = _consts_64()
    gA = np.ascontiguousarray(inp["attn_norm_g"][0].reshape(32, 128).T)
    RB = 3 * ATTN_W
    for core in range(8):
        b, g = core // 4, core % 4
        cols = []
        for base in (0, 2048, 4096, RB, RB + 2048, RB + 4096):
            for jj in range(4):
                cols.append(np.arange(base + 512 * g + 128 * jj, base + 512 * g + 128 * jj + 128))
        cols.append(np.arange(RB + 6144, RB + 6272))
        cols.append(np.arange(RB + 6272, RB + 6400))
        for jj in range(4):
            lo = RB + 6400 + 128 * jj
            cols.append(np.arange(lo, min(lo + 128, RB + 6880)))
        wA = np.stack([_wchunk(w_in, cc) for cc in cols])

        def hsl(v):
            return v[512 * g:512 * g + 512].reshape(8, 64).T
        hp = np.zeros((64, 10, 8), np.float32)
        hp[:, 0] = hsl(mu[0:2048])
        hp[:, 1] = hsl(mu[2048:4096])
        hp[:, 2] = hsl(mu[4096:6144])
        hp[:, 3] = hsl(inp["w0"][0])
        hp[:, 4] = hsl(inp["a0"][0])
        hp[:, 5] = hsl(inp["k_k"][0])
        hp[:, 6] = hsl(inp["k_a"][0])
        hp[:, 7] = hsl(inp["r_k"][0].reshape(-1))
        hp[:, 8] = hsl(inp["ln_x_w"][0])
        hp[:, 9] = hsl(inp["ln_x_b"][0])
        lp = np.zeros((128, 6), np.float32)
        lp[:, 0] = mu[6144:6272]
        lp[:, 1] = mu[6272:6400]
        mg = np.zeros(512, np.float32)
        mg[:480] = mu[6400:6880]
        lp[:, 2:6] = mg.reshape(4, 128).T
        wg = np.zeros((512, 512), np.float32)
        wg[:480] = inp["w_gate_up"][0][:, 512 * g:512 * g + 512]
        maps.append(dict(
            xT=xTs[b], pos=np.ascontiguousarray(inp["positions"][b][None, :].astype(np.int32)),
            wA=wA, gA=gA, c128=cA, hp=np.ascontiguousarray(hp.reshape(64, 80)), lp=lp,
            wdu=np.ascontiguousarray(inp["w_decay_up"][0][:, 512 * g:512 * g + 512]),
            wau=np.ascontiguousarray(inp["w_iclr_up"][0][:, 512 * g:512 * g + 512]),
            wgu=np.ascontiguousarray(wg.reshape(4, 128, 512).transpose(1, 0, 2)).reshape(128, 2048),
            c64=c64))
    return maps


def gather_mix(resA, S):
    mixT = np.zeros((2, 4096, S), np.float32)
    for core in range(8):
        b, g = core // 4, core % 4
        m = resA[core]["mixA"]
        mixT[b, 512 * g:512 * g + 512] = m[0:512]
        mixT[b, 2048 + 512 * g:2048 + 512 * g + 512] = m[512:1024]
    return mixT


NWB = 3
MPAD = 64
NFF = D_FF // 128


def build_B(NTOK, mix_dram=None, B=None, fused=False, mixx=None, wdecl=None):
    own = B is None
    if own:
        B = Builder()
    nc, P = B.nc, B.P
    T = 512
    NTB = NTOK // T
    IN, OUT = "ExternalInput", "ExternalOutput"
    if fused:
        qm_d = B.dram("qmask", [128, 4], F32, IN)
    elif mix_dram is None:
        mix_dram = B.dram("mixin", [4, 1024, NTOK + 2], F32, IN)
    xT = B.dram("xTb", [128, 32, NTOK + 2], F32, IN)
    pT = B.dram("pTb", [128, 2, NTOK], F32, IN)
    if wdecl is None:
        wdecl = declare_B_weights(B)[0]
    wout, wup, wdn, wple, wgt = (wdecl[k] for k in ("wout", "wup", "wdn", "wple", "wgt"))
    WQ = "sp" if fused else "pool"
    SQ = "pool" if fused else "sp"
    cst = B.dram("cstb", [128, 64 + 2 * NFF * 4 + 128], F32, IN)
    outT = B.dram("outT", [128, 32, NTOK], F32, OUT)
    NU = 2 * NFF

    with contextlib.ExitStack() as es:
        c_sb = B.sb(es, "cstb", [128, 64 + NU * 4 + 128], F32)
        ones_b = B.sb(es, "onesb", [128, 128], BF16)
        P.dma("sp", lambda e: e.dma_start(out=c_sb.t[:], in_=cst), "cstb", writes=[c_sb.b])
        P.dma("pool", lambda e: e.dma_start(out=ones_b.t[:], in_=cst[:, 64 + NU * 4:64 + NU * 4 + 128]), "onesb", writes=[ones_b.b])
        GM = c_sb.t[:, 0:32]
        GP = c_sb.t[:, 32:64]
        CW = c_sb.t[:, 64:64 + NU * 3].rearrange("p (u j) -> p u j", j=3)
        CB = c_sb.t[:, 64 + NU * 3:64 + NU * 4]
        CR = [c_sb.b, ones_b.b]

        h = B.sb(es, "h", [128, 32, T], F32)
        a16 = B.sb(es, "a16", [128, 32, T], BF16)
        hh = B.sb(es, "hh", [128, 32, 2], F32)
        a16h = B.sb(es, "a16h", [128, 32, 2], BF16)
        wr = [B.sb(es, "wr%d" % i, [128, 32, 128], BF16) for i in range(NWB)]
        wd = [B.sb(es, "wd%d" % i, [128, 4096], BF16) for i in range(2)]
        ue = [[B.sb(es, "ue%d%d" % (i, j), [128, T + 2], F32) for j in range(2)] for i in range(2)]
        cv = [[B.sb(es, "cv%d%d" % (i, j), [128, T], F32) for j in range(2)] for i in range(2)]
        actg = [B.sb(es, "actg%d" % i, [128, 2, T], BF16) for i in range(2)]
        uhalo = B.sb(es, "uhalo", [128, NU, 2], F32)
        rstd = B.sb(es, "rstdb", [128, T], F32)
        rstdh = B.sb(es, "rstdh", [128, 2], F32)
        rstde = B.sb(es, "rstde", [128, T], F32)
        p16 = B.sb(es, "p16", [128, 2, T], BF16)
        sqb = [B.sb(es, "sqbb%d" % i, [128, T], BF16) for i in range(2)]
        sqh = B.sb(es, "sqh", [128, 2], BF16)
        sg = [B.sb(es, "sg%d" % i, [128, T], F32) for i in range(2)]
        te = [B.sb(es, "te%d" % i, [128, T], F32) for i in range(2)]
        hparts = [Buf() for _ in range(32)]
        mixg_b = Buf("mixg")
        if fused:
            qm = B.sb(es, "qm", [128, 4], F32)
            cand2 = [[B.sb(es, "cand%d_%d" % (j, i), [128, 2, T], BF16) for i in range(4)] for j in range(2)]
            candh = [B.sb(es, "candh%d" % i, [128, 32, 2], BF16) for i in range(4)]
            P.dma("sp", lambda e: e.dma_start(out=qm.t[:], in_=qm_d), "qm", writes=[qm.b])
            NCHK = 4 * NTOK // 512
            chunk_b = [Buf("mixg%d" % i) for i in range(NCHK)]
            order = [qq * (NTOK // 512) + tt for tt in range(NTOK // 512) for qq in range(4)]
            for ci in order:
                def ag(ci):
                    P.dma("pool", lambda e: e.collective_compute("AllGather", ALU.bypass, replica_groups=[[0, 1, 2, 3], [4, 5, 6, 7]],
                                                                 ins=[mixx[ci]], outs=[mix_dram[ci]]), "ag", writes=[chunk_b[ci]], inc=1)
                ag(ci)
        psm = [B.psb[i] for i in range(6)]
        pss = [B.psb[6], B.psb[7]]
        cnt = dict(w=0, d=0, m=0, s=0, sq=0, ue=0, ag=0, sg=0)

        def wslot():
            s_ = wr[cnt["w"] % NWB]
            key = "wr%d" % (cnt["w"] % NWB)
            cnt["w"] += 1
            return s_, key

        def pmain():
            p = psm[cnt["m"] % 6]
            cnt["m"] += 1
            return p

        def psmall():
            p = pss[cnt["s"] % 2]
            cnt["s"] += 1
            return p

        def load_w(src_ap):
            s_, key = wslot()
            P.dma(WQ, lambda e: e.dma_start(out=s_.t[:].rearrange("p k c -> p (k c)"), in_=src_ap), key, writes=[s_.b])
            return s_

        def mm32(ps_ap, s_, rhs_fn):
            def f(e):
                for kc in range(32):
                    r = e.matmul(ps_ap, s_.t[:, kc, :], rhs_fn(kc), start=(kc == 0), stop=(kc == 31))
                return r
            return f

        def rms(src, src_tokens, n, dst16, g_ap, rs, halo):
            ssp = psmall()
            for kc in range(32):
                def one(kc):
                    sq = sqh if halo else sqb[cnt["sq"] % 2]
                    cnt["sq"] += 1
                    sqa = sq.t[:, 0:n]
                    P.op("act", lambda e: e.activation(out=sqa, in_=src.t[:, kc, 0:n], func=AF.Square), reads=[src_tokens[kc]], writes=[sq.b])
                    P.op("pe", lambda e: e.matmul(ssp.t[:, 0:n], ones_b.t[:], sqa, start=(kc == 0), stop=(kc == 31)), reads=[sq.b] + CR, writes=[ssp.b])
                one(kc)
            P.op("act", lambda e: e.activation(out=rs.t[:, 0:n], in_=ssp.t[:, 0:n], func=AF.Sqrt, bias=NORM_EPS, scale=1.0 / D_MODEL), reads=[ssp.b], writes=[rs.b])
            P.op("dve", lambda e: e.reciprocal(out=rs.t[:, 0:n], in_=rs.t[:, 0:n]), reads=[rs.b], writes=[rs.b])
            for kc in range(32):
                def two(kc):
                    P.op("dve", lambda e: e.scalar_tensor_tensor(out=dst16.t[:, kc, 0:n], in0=src.t[:, kc, 0:n], scalar=g_ap[:, kc:kc + 1], in1=rs.t[:, 0:n],
                                                                 op0=ALU.mult, op1=ALU.mult),
                         reads=[src_tokens[kc], rs.b] + CR, writes=[dst16.b])
                two(kc)

        def do_tile(tt):
            t0 = tt * T
            first = tt == 0
            hhp = [hh.b] * 32
            for q4 in range(4):
                def ld(q4):
                    P.dma("sp", lambda e: e.dma_start(out=h.t[:, 8 * q4:8 * q4 + 8, :], in_=xT[:, 8 * q4:8 * q4 + 8, 2 + t0:2 + t0 + T]),
                          "hx%d" % q4, writes=hparts[8 * q4:8 * q4 + 8])
                    if not fused:
                        P.dma("pool", lambda e: e.dma_start(out=a16.t[:, 8 * q4:8 * q4 + 8, :],
                                                            in_=mix_dram[q4, :, 2 + t0:2 + t0 + T].rearrange("(k p) t -> p k t", p=128)),
                              "mx%d" % q4, writes=[a16.b])
                ld(q4)
            if fused:
                for g8 in range(16):
                    def selg(g8):
                        cand = cand2[g8 % 2]
                        for qq in range(4):
                            def ldc(qq):
                                ci = (qq * NTOK + t0) // 512
                                P.dma("sp", lambda e: e.dma_start(out=cand[qq].t[:], in_=mix_dram[ci, g8 * 256:(g8 + 1) * 256, :].rearrange("(k p) t -> p k t", p=128)),
                                      "cd%d" % qq, reads=[chunk_b[ci]], writes=[cand[qq].b])
                            ldc(qq)
                        dst = a16.t[:, 2 * g8:2 * g8 + 2, :]

                        def sel(e):
                            r = e.tensor_scalar(out=dst, in0=cand[0].t[:], scalar1=qm.t[:, 0:1], scalar2=None, op0=ALU.mult)
                            for qq in range(1, 4):
                                r = e.scalar_tensor_tensor(out=dst, in0=cand[qq].t[:], scalar=qm.t[:, qq:qq + 1], in1=dst, op0=ALU.mult, op1=ALU.add)
                            return r
                        P.op("dve", sel, reads=[c_.b for c_ in cand] + [qm.b], writes=[a16.b])
                    selg(g8)
            P.dma("pool", lambda e: e.dma_start(out=p16.t[:], in_=pT[:, :, t0:t0 + T]), "p16", writes=[p16.b])
            if first:
                P.dma("sp", lambda e: e.dma_start(out=hh.t[:], in_=xT[:, :, 0:2]), "hhx", writes=[hh.b])
                if fused:
                    P.op("pool", lambda e: e.memset(candh[0].t[:], 0.0), writes=[candh[0].b])
                    for qq in range(1, 4):
                        def ldch(qq):
                            ci = qq * NTOK // 512 - 1
                            P.dma("sp", lambda e: e.dma_start(out=candh[qq].t[:], in_=mix_dram[ci, :, 510:512].rearrange("(k p) t -> p k t", p=128)),
                                  "cdh%d" % qq, reads=[chunk_b[ci]], writes=[candh[qq].b])
                        ldch(qq)

                    def selh(e):
                        r = e.tensor_scalar(out=a16h.t[:], in0=candh[0].t[:], scalar1=qm.t[:, 0:1], scalar2=None, op0=ALU.mult)
                        for qq in range(1, 4):
                            r = e.scalar_tensor_tensor(out=a16h.t[:], in0=candh[qq].t[:], scalar=qm.t[:, qq:qq + 1], in1=a16h.t[:], op0=ALU.mult, op1=ALU.add)
                        return r
                    P.op("dve", selh, reads=[c_.b for c_ in candh] + [qm.b], writes=[a16h.b])
                else:
                    for q4 in range(4):
                        def ldh(q4):
                            P.dma("pool", lambda e: e.dma_start(out=a16h.t[:, 8 * q4:8 * q4 + 8, :],
                                                                in_=mix_dram[q4, :, 0:2].rearrange("(k p) t -> p k t", p=128)),
                                  "mxh%d" % q4, writes=[a16h.b])
                        ldh(q4)
            for c in range(32):
                def oc(c):
                    s_ = load_w(wout[c])
                    ps = pmain()
                    P.op("pe", mm32(ps.t[:], s_, lambda kc: a16.t[:, kc, :]), reads=[s_.b, a16.b], writes=[ps.b])
                    P.op("dve", lambda e: e.tensor_tensor(out=h.t[:, c, :], in0=ps.t[:], in1=h.t[:, c, :], op=ALU.add), reads=[ps.b, hparts[c]], writes=[hparts[c]])
                    if first:
                        ph = psmall()
                        P.op("pe", mm32(ph.t[:, 0:2], s_, lambda kc: a16h.t[:, kc, :]), reads=[s_.b, a16h.b], writes=[ph.b])
                        P.op("dve", lambda e: e.tensor_tensor(out=hh.t[:, c, :], in0=ph.t[:, 0:2], in1=hh.t[:, c, :], op=ALU.add), reads=[ph.b, hh.b], writes=[hh.b])
                oc(c)
            BD_ = int(os.environ.get('B_DBG', '100'))

            def dump():
                for c in range(32):
                    P.dma("sp", _bind(lambda e, c: e.dma_start(out=outT[:, c, t0:t0 + T], in_=h.t[:, c, :]), c), "oc%d" % (c % 8), reads=[hparts[c]])
            if BD_ == 1:
                dump()
                return
            rms(h, hparts, T, a16, GM, rstd, False)
            if first:
                rms(hh, hhp, 2, a16h, GM, rstdh, True)
            def do_pairf(f0):
                ag = actg[cnt["ag"] % 2]
                cnt["ag"] += 1
                for fi in range(2):
                    def ff(f, fi):
                        cvs = []
                        for gi in range(2):
                            def gu(gi):
                                uc = gi * NFF + f
                                s_ = load_w(wup[uc])
                                ps = pmain()
                                P.op("pe", mm32(ps.t[:], s_, lambda kc: a16.t[:, kc, :]), reads=[s_.b, a16.b], writes=[ps.b])
                                u = ue[gi][cnt["ue"] % 2]
                                c1 = cv[gi][cnt["ue"] % 2]
                                if first:
                                    ph = psmall()
                                    P.op("pe", mm32(ph.t[:, 0:2], s_, lambda kc: a16h.t[:, kc, :]), reads=[s_.b, a16h.b], writes=[ph.b])
                                    P.op("dve", lambda e: e.tensor_copy(out=u.t[:, 0:2], in_=ph.t[:, 0:2]), reads=[ph.b], writes=[u.b])
                                else:
                                    P.op("dve", lambda e: e.tensor_copy(out=u.t[:, 0:2], in_=uhalo.t[:, uc, :]), reads=[uhalo.b], writes=[u.b])
                                P.op("act", lambda e: e.copy(out=u.t[:, 2:T + 2], in_=ps.t[:]), reads=[ps.b], writes=[u.b])
                                P.op("dve", lambda e: e.tensor_copy(out=uhalo.t[:, uc, :], in_=u.t[:, T:T + 2]), reads=[u.b], writes=[uhalo.b])
                                P.op("act", lambda e: e.activation(out=c1.t[:], in_=u.t[:, 2:T + 2], func=AF.Identity, bias=CB[:, uc:uc + 1], scale=CW[:, uc, 2:3]),
                                     reads=[u.b] + CR, writes=[c1.b])

                                def cvf(e):
                                    e.scalar_tensor_tensor(out=c1.t[:], in0=u.t[:, 1:T + 1], scalar=CW[:, uc, 1:2], in1=c1.t[:], op0=ALU.mult, op1=ALU.add)
                                    return e.scalar_tensor_tensor(out=c1.t[:], in0=u.t[:, 0:T], scalar=CW[:, uc, 0:1], in1=c1.t[:], op0=ALU.mult, op1=ALU.add)
                                P.op("dve", cvf, reads=[u.b, c1.b] + CR, writes=[c1.b])
                                cvs.append(c1)
                            gu(gi)
                        cnt["ue"] += 1
                        sgt = sg[cnt["sg"] % 2]
                        cnt["sg"] += 1
                        P.op("act", lambda e: e.activation(out=sgt.t[:], in_=cvs[0].t[:], func=AF.Silu), reads=[cvs[0].b], writes=[sgt.b])
                        P.op("dve", lambda e: e.tensor_tensor(out=ag.t[:, fi, :], in0=sgt.t[:], in1=cvs[1].t[:], op=ALU.mult), reads=[sgt.b, cvs[1].b], writes=[ag.b])
                    ff(f0 + fi, fi)
                wds = []
                for fi in range(2):
                    def ldd(fi):
                        s_ = wd[fi]
                        P.dma(WQ, lambda e: e.dma_start(out=s_.t[:], in_=wdn[f0 + fi]), "wd%d" % fi, writes=[s_.b])
                        wds.append(s_)
                    ldd(fi)
                for c in range(32):
                    def dc(c):
                        ps = pmain()

                        def f(e):
                            e.matmul(ps.t[:], wds[0].t[:, c * 128:(c + 1) * 128], ag.t[:, 0, :], start=True, stop=False)
                            return e.matmul(ps.t[:], wds[1].t[:, c * 128:(c + 1) * 128], ag.t[:, 1, :], start=False, stop=True)
                        P.op("pe", f, reads=[wds[0].b, wds[1].b, ag.b], writes=[ps.b])
                        P.op("dve", lambda e: e.tensor_tensor(out=h.t[:, c, :], in0=ps.t[:], in1=h.t[:, c, :], op=ALU.add), reads=[ps.b, hparts[c]], writes=[hparts[c]])
                    dc(c)
            for f0 in range(0, NFF, 2):
                do_pairf(f0)
            if BD_ == 2:
                dump()
                return
            for c in range(32):
                P.op("act", _bind(lambda e, c: e.copy(out=a16.t[:, c, :], in_=h.t[:, c, :]), c), reads=[hparts[c]], writes=[a16.b])
            P.dma(WQ, lambda e: e.dma_start(out=wd[0].t[:], in_=wple[:, 0:4096]), "wd0", writes=[wd[0].b])
            P.dma(WQ, lambda e: e.dma_start(out=wd[1].t[:], in_=wple[:, 4096:8192]), "wd1", writes=[wd[1].b])
            ssp = psmall()
            for c in range(32):
                def e1(c):
                    ps = pmain()

                    def f(e):
                        e.matmul(ps.t[:], wd[0].t[:, c * 128:(c + 1) * 128], p16.t[:, 0, :], start=True, stop=False)
                        return e.matmul(ps.t[:], wd[1].t[:, c * 128:(c + 1) * 128], p16.t[:, 1, :], start=False, stop=True)
                    P.op("pe", f, reads=[wd[0].b, wd[1].b, p16.b], writes=[ps.b])
                    sq = sqb[cnt["sq"] % 2]
                    cnt["sq"] += 1
                    P.op("act", lambda e: e.activation(out=sq.t[:], in_=ps.t[:], func=AF.Square), reads=[ps.b], writes=[sq.b])
                    P.op("pe", lambda e: e.matmul(ssp.t[:], ones_b.t[:], sq.t[:], start=(c == 0), stop=(c == 31)), reads=[sq.b] + CR, writes=[ssp.b])
                e1(c)
            P.op("act", lambda e: e.activation(out=rstde.t[:], in_=ssp.t[:], func=AF.Sqrt, bias=NORM_EPS, scale=1.0 / D_MODEL), reads=[ssp.b], writes=[rstde.b])
            P.op("dve", lambda e: e.reciprocal(out=rstde.t[:], in_=rstde.t[:]), reads=[rstde.b], writes=[rstde.b])
            for c in range(32):
                def e2(c):
                    s_ = load_w(wgt[c])
                    pg = pmain()
                    P.op("pe", mm32(pg.t[:], s_, lambda kc: a16.t[:, kc, :]), reads=[s_.b, a16.b], writes=[pg.b])
                    pe_ = pmain()

                    def f(e):
                        e.matmul(pe_.t[:], wd[0].t[:, c * 128:(c + 1) * 128], p16.t[:, 0, :], start=True, stop=False)
                        return e.matmul(pe_.t[:], wd[1].t[:, c * 128:(c + 1) * 128], p16.t[:, 1, :], start=False, stop=True)
                    P.op("pe", f, reads=[wd[0].b, wd[1].b, p16.b], writes=[pe_.b])
                    sgt = sg[cnt["sg"] % 2]
                    tet = te[cnt["sg"] % 2]
                    cnt["sg"] += 1
                    P.op("act", lambda e: e.activation(out=sgt.t[:], in_=pg.t[:], func=AF.Sigmoid), reads=[pg.b], writes=[sgt.b])

                    def comb(e):
                        e.scalar_tensor_tensor(out=tet.t[:], in0=pe_.t[:], scalar=GP[:, c:c + 1], in1=rstde.t[:], op0=ALU.mult, op1=ALU.mult)
                        e.tensor_tensor(out=tet.t[:], in0=tet.t[:], in1=sgt.t[:], op=ALU.mult)
                        return e.tensor_tensor(out=h.t[:, c, :], in0=h.t[:, c, :], in1=tet.t[:], op=ALU.add)
                    P.op("dve", comb, reads=[pe_.b, rstde.b, sgt.b, hparts[c]] + CR, writes=[tet.b, hparts[c]])
                    P.dma(SQ, lambda e: e.dma_start(out=outT[:, c, t0:t0 + T], in_=h.t[:, c, :]), "oc%d" % (c % 8), reads=[hparts[c]])
                e2(c)

        for tt in range(NTB):
            do_tile(tt)
        P.run_block()
    if own:
        B.finish()
    return nc


def _wchunks_all(w, nchunk):
    return np.ascontiguousarray(w.reshape(32, 128, nchunk, 128).transpose(2, 1, 0, 3)).reshape(nchunk, 128, 32 * 128)


def prep_B_weights(inp):
    perm = np.concatenate([np.concatenate([np.arange(512 * g, 512 * g + 512), np.arange(2048 + 512 * g, 2048 + 512 * g + 512)]) for g in range(4)])
    wout = _wchunks_all(inp["w_out"][0][perm, :], 32)
    wup = _wchunks_all(inp["w_mlp_up"][0], 2 * NFF)
    wdn = np.ascontiguousarray(inp["w_mlp_down"][0].reshape(NFF, 128, 4096))
    wple = np.ascontiguousarray(inp["w_ple_proj"][0].reshape(2, 128, 4096).transpose(1, 0, 2)).reshape(128, 8192)
    wgt = _wchunks_all(inp["w_ple_gate"][0], 32)
    NU = 2 * NFF
    cst = np.zeros((128, 64 + NU * 4 + 128), np.float32)
    cst[:, 0:32] = inp["mlp_norm_g"][0].reshape(32, 128).T
    cst[:, 32:64] = inp["ple_norm_g"][0].reshape(32, 128).T
    cw = inp["conv_w"][0].reshape(3, NU, 128).transpose(2, 1, 0)
    cst[:, 64:64 + NU * 3] = cw.reshape(128, NU * 3)
    cst[:, 64 + NU * 3:64 + NU * 4] = inp["conv_b"][0].reshape(NU, 128).T
    cst[:, 64 + NU * 4:] = 1.0
    return dict(wout=wout, wup=wup, wdn=wdn, wple=wple, wgt=wgt, cstb=cst)


def prep_B_acts(inp, S, core):
    NTOK = S // 4
    b, q = core // 4, core % 4
    lo = q * NTOK
    xe = np.zeros((NTOK + 2, 4096), np.float32)
    xe[2:] = inp["x"][b, lo:lo + NTOK]
    if q > 0:
        xe[0:2] = inp["x"][b, lo - 2:lo]
    xTb = np.ascontiguousarray(xe.T.reshape(32, 128, NTOK + 2).transpose(1, 0, 2))
    pTb = np.ascontiguousarray(inp["p"][0, b, lo:lo + NTOK].T.reshape(2, 128, NTOK).transpose(1, 0, 2))
    return dict(xTb=xTb, pTb=pTb)


def mix_for_core(resA, S, core):
    NTOK = S // 4
    b, q = core // 4, core % 4
    lo = q * NTOK
    m = np.zeros((4, 1024, NTOK + 2), np.float32)
    for g in range(4):
        src = resA[4 * b + g]["mixA"]
        m[g, :, 2:] = src[:, lo:lo + NTOK]
        if q > 0:
            m[g, :, 0:2] = src[:, lo - 2:lo]
    return m


_CACHE = {}


def kernel_unfused(**inp):
    inp = {k: np.asarray(v) for k, v in inp.items()}
    S = inp["x"].shape[1]
    NTOK = S // 4
    if ("A", S) not in _CACHE:
        _CACHE[("A", S)] = build_A(S)
        _CACHE[("B", S)] = build_B(NTOK)
    ncA, ncB = _CACHE[("A", S)], _CACHE[("B", S)]
    mapsA = prep_A(inp, S)
    resA = run_bass_kernel_spmd(ncA, mapsA, core_ids=list(range(8))).results
    del mapsA
    wB = prep_B_weights(inp)
    mapsB = []
    for core in range(8):
        d = dict(wB)
        d.update(prep_B_acts(inp, S, core))
        d["mixin"] = mix_for_core(resA, S, core)
        mapsB.append(d)
    resB = run_bass_kernel_spmd(ncB, mapsB, core_ids=list(range(8))).results
    out = np.zeros((2, S, 4096), np.float32)
    for core in range(8):
        b, q = core // 4, core % 4
        o = resB[core]["outT"]
        out[b, q * NTOK:(q + 1) * NTOK] = o.transpose(2, 1, 0).reshape(NTOK, 4096)
    return out


def declare_B_weights(B, bf16_copy=False):
    IN = "ExternalInput"
    d = dict(wout=B.dram("wout", [32, 128, 32 * 128], F32, IN),
             wup=B.dram("wup", [2 * NFF, 128, 32 * 128], F32, IN),
             wdn=B.dram("wdn", [NFF, 128, 4096], F32, IN),
             wple=B.dram("wple", [128, 2 * 4096], F32, IN),
             wgt=B.dram("wgt", [32, 128, 32 * 128], F32, IN))
    if not bf16_copy:
        return d, ()
    c = dict(wout=B.dram("wout_b", [32, 128, 4096], BF16, "Internal"),
             wup=B.dram("wup_b", [2 * NFF, 128, 4096], BF16, "Internal"),
             wdn=B.dram("wdn_b", [NFF, 128, 4096], BF16, "Internal"),
             wple=B.dram("wple_b", [128, 8192], BF16, "Internal"),
             wgt=B.dram("wgt_b", [32, 128, 4096], BF16, "Internal"))
    conv = [(c["wout"][i], d["wout"][i]) for i in range(32)]
    for f in range(NFF):
        conv += [(c["wup"][f], d["wup"][f]), (c["wup"][NFF + f], d["wup"][NFF + f])]
        if f % 2 == 1:
            conv += [(c["wdn"][f - 1], d["wdn"][f - 1]), (c["wdn"][f], d["wdn"][f])]
    conv += [(c["wple"][:, 0:4096], d["wple"][:, 0:4096]), (c["wple"][:, 4096:8192], d["wple"][:, 4096:8192])]
    conv += [(c["wgt"][i], d["wgt"][i]) for i in range(32)]
    return c, conv


def build_fused(S):
    B = Builder()
    wdecl, conv = declare_B_weights(B, bf16_copy=True)
    mixx = build_A(S, B=B, fused=True, conv=conv)
    mixg = B.dram("mixg", [S // 512, 4096, 512], BF16, "Internal")
    build_B(S // 4, mix_dram=mixg, B=B, fused=True, mixx=mixx, wdecl=wdecl)
    B.finish()
    return B.nc


def kernel(**inp):
    inp = {k: np.asarray(v) for k, v in inp.items()}
    S = inp["x"].shape[1]
    NTOK = S // 4
    if ("F", S) not in _CACHE:
        _CACHE[("F", S)] = build_fused(S)
    nc = _CACHE[("F", S)]
    maps = prep_A(inp, S)
    wB = prep_B_weights(inp)
    for core in range(8):
        maps[core].update(wB)
        maps[core].update(prep_B_acts(inp, S, core))
        qm = np.zeros((128, 4), np.float32)
        qm[:, core % 4] = 1.0
        maps[core]["qmask"] = qm
    res = run_bass_kernel_spmd(nc, maps, core_ids=list(range(8))).results
    out = np.zeros((2, S, 4096), np.float32)
    for core in range(8):
        b, q = core // 4, core % 4
        o = res[core]["outT"]
        out[b, q * NTOK:(q + 1) * NTOK] = o.transpose(2, 1, 0).reshape(NTOK, 4096)
    return out
```
